# Optimizing a Trainium2 kernel written in Bass

```python
import math
import jax, jax.numpy as jnp
from jax import lax
import numpy as np

D_MODEL = 1024
BATCH = 4
SEQ = 4096
DEPTH = 1

CTX_LEN = 256
GRID_W = 64
N_MOD = 9
D_FF = ((8 * D_MODEL // 3 + 127) // 128) * 128
NORM_EPS = 1e-6
RW_HEAD_DIM = 64
RW_HEADS = D_MODEL // RW_HEAD_DIM
D_RW = RW_HEADS * RW_HEAD_DIM
W_LORA = 64
A_LORA = 64
G_LORA = 128
LN_X_EPS = 64e-5
DECAY_SCALE = math.exp(-0.5)
D_LRU = D_MODEL
LRU_BLOCKS = 4
LRU_BLOCK = D_LRU // LRU_BLOCKS
LRU_CONV = 4
LRU_C = 8.0
RW_SIZES = (D_RW, D_RW, D_RW, W_LORA, W_LORA, A_LORA, A_LORA, G_LORA)
N_RW_SHIFT = sum(RW_SIZES)
IN_SIZES = (N_RW_SHIFT, D_LRU, D_LRU, D_MODEL, D_MODEL)
N_IN = sum(IN_SIZES)

kernel_name = "hybrid_rwkv7_rglru_prefix_dit_block"


def split_cols(p, sizes):
    idx = np.cumsum(np.array(sizes))[:-1].tolist()
    return jnp.split(p, idx, axis=-1)


def rms_norm(x, g):
    xf = x.astype(jnp.float32)
    y = xf * lax.rsqrt(jnp.mean(xf * xf, axis=-1, keepdims=True) + NORM_EPS)
    return (y * g.astype(jnp.float32)).astype(x.dtype)


def modulate(xn, shift, scale):
    return xn * (1 + scale) + shift


def swiglu(h, wg, wu, wd):
    return (jax.nn.silu(h @ wg) * (h @ wu)) @ wd


def ffn_half_step(h, mod, g, wg, wu, wd):
    shift, scale, gate = mod
    hn = modulate(rms_norm(h, g), shift, scale)
    return h + 0.5 * gate * swiglu(hn, wg, wu, wd)


def shift_seq(p):
    pad = jnp.pad(p, ((0, 0), (1, 1), (0, 0)))
    return 0.5 * (pad[:, :-2] + pad[:, 2:])


def shift_grid(p):
    bsz, t, ch = p.shape
    rows = t // GRID_W
    g = p.reshape(bsz, rows, GRID_W, ch)
    pad = jnp.pad(g, ((0, 0), (1, 1), (1, 1), (0, 0)))
    nb = 0.25 * (pad[:, :-2, 1:-1] + pad[:, 2:, 1:-1] + pad[:, 1:-1, :-2] + pad[:, 1:-1, 2:])
    return nb.reshape(bsz, t, ch)


def depthwise_conv_centred(x, w, b):
    k = w.shape[0]
    left = (k - 1) // 2
    right = k - 1 - left
    ch = x.shape[-1]
    y = lax.conv_general_dilated(
        x.astype(jnp.float32), w.astype(jnp.float32)[:, None, :], window_strides=(1,),
        padding=[(left, right)], dimension_numbers=('NWC', 'WIO', 'NWC'), feature_group_count=ch)
    return y + b.astype(jnp.float32)


def rwkv_step(s, inp):
    r, w, k, v, kk, kka = inp
    sa = jnp.einsum('bhvk,bhk->bhv', s, -kk)
    s = s * w[:, :, None, :] + sa[..., None] * kka[:, :, None, :] + v[..., None] * k[:, :, None, :]
    y = jnp.einsum('bhvk,bhk->bhv', s, r)
    return s, y


def rwkv_scan(r, w, k, v, kk, kka, s0, reverse):
    xs = tuple(jnp.moveaxis(u, 1, 0) for u in (r, w, k, v, kk, kka))
    s_fin, ys = lax.scan(rwkv_step, s0, xs, reverse=reverse)
    return jnp.moveaxis(ys, 0, 1), s_fin


def linear_scan(a, u, h0, reverse):
    def comb(e1, e2):
        a1, b1 = e1
        a2, b2 = e2
        return a1 * a2, a2 * b1 + b2
    if reverse:
        a, u = a[:, ::-1], u[:, ::-1]
    acum, bcum = lax.associative_scan(comb, (a, u), axis=1)
    h = bcum + acum * h0[:, None, :]
    h_fin = h[:, -1]
    if reverse:
        h = h[:, ::-1]
    return h, h_fin


def rwkv_branch(z, s0_f, s0_b, need_out, w0, w_up, a0, a_up, g_up, k_k, k_a, r_k, ln_w, ln_b):
    z = z.astype(jnp.float32)
    r, k, v, wd_f, wd_b, ad_f, ad_b, gd = split_cols(z, RW_SIZES)
    bsz, t = z.shape[0], z.shape[1]
    heads = lambda u: u.reshape(bsz, t, RW_HEADS, RW_HEAD_DIM)
    kk = heads(k * k_k)
    kk = kk / jnp.maximum(jnp.sqrt(jnp.sum(kk * kk, axis=-1, keepdims=True)), 1e-12)
    ys = []
    finals = []
    for d, (wd, ad, s0) in enumerate(((wd_f, ad_f, s0_f), (wd_b, ad_b, s0_b))):
        w = jnp.exp(-DECAY_SCALE * jax.nn.sigmoid(w0[d] + jnp.tanh(wd) @ w_up[d]))
        a = jax.nn.sigmoid(a0[d] + ad @ a_up[d])
        kd = k * (1 + (a - 1) * k_a)
        y, s_fin = rwkv_scan(heads(r), heads(w), heads(kd), heads(v), kk, kk * heads(a), s0, d == 1)
        ys.append(y)
        finals.append(s_fin)
    if not need_out:
        return None, finals
    y = ys[0] + ys[1]
    mu = jnp.mean(y, axis=-1, keepdims=True)
    var = jnp.mean(jnp.square(y - mu), axis=-1, keepdims=True)
    y = ((y - mu) * lax.rsqrt(var + LN_X_EPS)).reshape(bsz, t, D_RW) * ln_w + ln_b
    bonus = jnp.sum(heads(r) * heads(k) * r_k, axis=-1, keepdims=True) * heads(v)
    g = jax.nn.sigmoid(gd) @ g_up
    return (y + bonus.reshape(bsz, t, D_RW)) * g, finals


def lru_branch(xl, gl, h0_f, h0_b, need_out, conv_w, conv_b, lam, wa, ba, wx, bx):
    xc = depthwise_conv_centred(xl, conv_w, conv_b)
    bsz, t = xc.shape[0], xc.shape[1]
    xb = xc.reshape(bsz, t, LRU_BLOCKS, LRU_BLOCK)
    hs = []
    finals = []
    for d, h0 in enumerate((h0_f, h0_b)):
        gate_r = jax.nn.sigmoid(jnp.einsum('btni,nij->btnj', xb, wa[d]).reshape(bsz, t, D_LRU) + ba[d])
        gate_i = jax.nn.sigmoid(jnp.einsum('btni,nij->btnj', xb, wx[d]).reshape(bsz, t, D_LRU) + bx[d])
        log_a = LRU_C * gate_r * jax.nn.log_sigmoid(lam[d])
        a = jnp.exp(log_a)
        u = jnp.sqrt(-jnp.expm1(2 * log_a)) * (gate_i * xc)
        h, h_fin = linear_scan(a, u, h0, d == 1)
        hs.append(h)
        finals.append(h_fin)
    if not need_out:
        return None, finals
    return (hs[0] + hs[1]) * jax.nn.gelu(gl.astype(jnp.float32)), finals


def token_mixer(xn, shift_fn, init, need_out, w_in, rw_mu, rw_w0, rw_w_up, rw_a0, rw_a_up, rw_g_up,
                rw_k_k, rw_k_a, rw_r_k, rw_ln_w, rw_ln_b, w_proj_rw, lru_conv_w, lru_conv_b, lru_lam,
                lru_wa, lru_ba, lru_wx, lru_bx, w_proj_lru, w_out):
    p = xn @ w_in
    p_rw, p_lx, p_lg, p_mrw, p_mlru = split_cols(p, IN_SIZES)
    z = p_rw + (shift_fn(p_rw) - p_rw) * rw_mu
    s_f0, s_b0, h_f0, h_b0 = init
    y_rw, (s_f, s_b) = rwkv_branch(z, s_f0, s_b0, need_out, rw_w0, rw_w_up, rw_a0, rw_a_up, rw_g_up,
                                   rw_k_k, rw_k_a, rw_r_k, rw_ln_w, rw_ln_b)
    y_lru, (h_f, h_b) = lru_branch(p_lx, p_lg, h_f0, h_b0, need_out, lru_conv_w, lru_conv_b, lru_lam,
                                   lru_wa, lru_ba, lru_wx, lru_bx)
    final = (s_f, s_b, h_f, h_b)
    if not need_out:
        return None, final
    merged = (jax.nn.sigmoid(p_mrw) * (y_rw @ w_proj_rw)
              + jax.nn.sigmoid(p_mlru) * (y_lru @ w_proj_lru))
    return merged @ w_out, final


def setup_inputs(seed: int = 0) -> dict:
    key = jax.random.key(seed)
    kit = iter(jax.random.split(key, 48))
    L = DEPTH
    f32 = jnp.float32
    nrm = lambda shape, s: jax.random.normal(next(kit), shape, f32) * s
    uni = lambda shape, lo, hi: jax.random.uniform(next(kit), shape, f32, minval=lo, maxval=hi)
    d = D_MODEL
    u_lam = uni((L, 2, D_LRU), 0.9, 0.999)
    s_lam = u_lam ** (1.0 / LRU_C)
    return {
        "x": nrm((BATCH, SEQ, d), 1.0),
        "c": nrm((BATCH, d), 1.0),
        "ctx": nrm((BATCH, CTX_LEN, d), 1.0),
        "c_ctx": nrm((d,), 1.0),
        "w_mod": nrm((L, d, N_MOD * d), 0.5 * d ** -0.5),
        "b_mod": nrm((L, N_MOD * d), 0.01),
        "g_ffn1": 1.0 + nrm((L, d), 0.02),
        "ffn1_wg": nrm((L, d, D_FF), d ** -0.5),
        "ffn1_wu": nrm((L, d, D_FF), d ** -0.5),
        "ffn1_wd": nrm((L, D_FF, d), D_FF ** -0.5),
        "g_mix": 1.0 + nrm((L, d), 0.02),
        "w_in": nrm((L, d, N_IN), d ** -0.5),
        "rw_mu": uni((L, N_RW_SHIFT), 0.0, 1.0),
        "rw_w0": uni((L, 2, D_RW), -3.0, 1.0),
        "rw_w_up": nrm((L, 2, W_LORA, D_RW), 0.5 * W_LORA ** -0.5),
        "rw_a0": nrm((L, 2, D_RW), 0.5),
        "rw_a_up": nrm((L, 2, A_LORA, D_RW), 0.5 * A_LORA ** -0.5),
        "rw_g_up": nrm((L, G_LORA, D_RW), G_LORA ** -0.5),
        "rw_k_k": 0.85 + nrm((L, D_RW), 0.05),
        "rw_k_a": 1.0 + nrm((L, D_RW), 0.05),
        "rw_r_k": nrm((L, RW_HEADS, RW_HEAD_DIM), 0.1),
        "rw_ln_w": 1.0 + nrm((L, D_RW), 0.02),
        "rw_ln_b": nrm((L, D_RW), 0.01),
        "w_proj_rw": nrm((L, D_RW, d), D_RW ** -0.5),
        "lru_conv_w": nrm((L, LRU_CONV, D_LRU), LRU_CONV ** -0.5),
        "lru_conv_b": nrm((L, D_LRU), 0.01),
        "lru_lam": jnp.log(s_lam) - jnp.log1p(-s_lam),
        "lru_wa": nrm((L, 2, LRU_BLOCKS, LRU_BLOCK, LRU_BLOCK), LRU_BLOCK ** -0.5),
        "lru_ba": nrm((L, 2, D_LRU), 0.01),
        "lru_wx": nrm((L, 2, LRU_BLOCKS, LRU_BLOCK, LRU_BLOCK), LRU_BLOCK ** -0.5),
        "lru_bx": nrm((L, 2, D_LRU), 0.01),
        "w_proj_lru": nrm((L, D_LRU, d), D_LRU ** -0.5),
        "w_out": nrm((L, d, d), d ** -0.5),
        "g_ffn2": 1.0 + nrm((L, d), 0.02),
        "ffn2_wg": nrm((L, d, D_FF), d ** -0.5),
        "ffn2_wu": nrm((L, d, D_FF), d ** -0.5),
        "ffn2_wd": nrm((L, D_FF, d), D_FF ** -0.5),
        "g_final": 1.0 + nrm((d,), 0.02),
    }


def reference(x, c, ctx, c_ctx, w_mod, b_mod, g_ffn1, ffn1_wg, ffn1_wu, ffn1_wd, g_mix, w_in, rw_mu,
              rw_w0, rw_w_up, rw_a0, rw_a_up, rw_g_up, rw_k_k, rw_k_a, rw_r_k, rw_ln_w, rw_ln_b,
              w_proj_rw, lru_conv_w, lru_conv_b, lru_lam, lru_wa, lru_ba, lru_wx, lru_bx, w_proj_lru,
              w_out, g_ffn2, ffn2_wg, ffn2_wu, ffn2_wd, g_final):
    bsz = x.shape[0]
    for l in range(DEPTH):
        last = l == DEPTH - 1
        mod_x = jnp.split((jax.nn.silu(c) @ w_mod[l] + b_mod[l])[:, None, :], N_MOD, axis=-1)
        mod_c = jnp.split((jax.nn.silu(c_ctx) @ w_mod[l] + b_mod[l])[None, None, :], N_MOD, axis=-1)
        x = ffn_half_step(x, mod_x[0:3], g_ffn1[l], ffn1_wg[l], ffn1_wu[l], ffn1_wd[l])
        ctx = ffn_half_step(ctx, mod_c[0:3], g_ffn1[l], ffn1_wg[l], ffn1_wu[l], ffn1_wd[l])
        mixer_params = (w_in[l], rw_mu[l], rw_w0[l], rw_w_up[l], rw_a0[l], rw_a_up[l], rw_g_up[l],
                        rw_k_k[l], rw_k_a[l], rw_r_k[l], rw_ln_w[l], rw_ln_b[l], w_proj_rw[l],
                        lru_conv_w[l], lru_conv_b[l], lru_lam[l], lru_wa[l], lru_ba[l], lru_wx[l],
                        lru_bx[l], w_proj_lru[l], w_out[l])
        zero_state = (jnp.zeros((bsz, RW_HEADS, RW_HEAD_DIM, RW_HEAD_DIM), jnp.float32),
                      jnp.zeros((bsz, RW_HEADS, RW_HEAD_DIM, RW_HEAD_DIM), jnp.float32),
                      jnp.zeros((bsz, D_LRU), jnp.float32),
                      jnp.zeros((bsz, D_LRU), jnp.float32))
        cn = modulate(rms_norm(ctx, g_mix[l]), mod_c[3], mod_c[4])
        ctx_mix, ctx_state = token_mixer(cn, shift_seq, zero_state, not last, *mixer_params)
        xn = modulate(rms_norm(x, g_mix[l]), mod_x[3], mod_x[4])
        x_mix, _ = token_mixer(xn, shift_grid, ctx_state, True, *mixer_params)
        x = x + mod_x[5] * x_mix
        x = ffn_half_step(x, mod_x[6:9], g_ffn2[l], ffn2_wg[l], ffn2_wu[l], ffn2_wd[l])
        if not last:
            ctx = ctx + mod_c[5] * ctx_mix
            ctx = ffn_half_step(ctx, mod_c[6:9], g_ffn2[l], ffn2_wg[l], ffn2_wu[l], ffn2_wd[l])
    return rms_norm(x, g_final)
```

```python
import numpy as np
from contextlib import ExitStack
import concourse.bass as bass
import concourse.mybir as mybir
from concourse.bass_utils import run_bass_kernel_spmd

F32 = mybir.dt.float32
BF16 = mybir.dt.bfloat16
AF = mybir.ActivationFunctionType
ALU = mybir.AluOpType
AX = mybir.AxisListType

D = 1024
T = 4096
TC = 256
TT = T + TC
DFF = 2816
NJ = DFF // 128
NIN = 7552
NRW = 3456
EPS = 1e-6
N_CORES = 8


class Sched:
    R = 8

    def __init__(self, nc, es):
        self.nc = nc
        self.eng = {'pe': nc.tensor, 'act': nc.scalar, 'dve': nc.vector, 'pool': nc.gpsimd, 'sp': nc.sync}
        self.sem = {k: es.enter_context(nc.semaphore('s_' + k)) for k in ('pe', 'act', 'dve', 'pool')}
        self.cnt = {k: 0 for k in self.sem}
        self.dsem = {q: [es.enter_context(nc.semaphore('d_%s%d' % (q, i))) for i in range(self.R)]
                     for q in ('sp', 'pool')}
        self.dn = {q: 0 for q in self.dsem}
        self.waited = {e: {} for e in self.eng}
        self.lastw = {}
        self.readers = {}
        self.nops = 0
        self.pe_last = {}

    def _semof(self, k):
        return self.sem[k] if isinstance(k, str) else self.dsem[k[0]][k[1]]

    def _wait(self, e, k, val):
        if val <= 0 or self.waited[e].get(k, 0) >= val:
            return
        self.waited[e][k] = val
        self.eng[e].wait_ge(self._semof(k), val)

    def _deps(self, e, reads, writes):
        need = {}
        for r in reads:
            t = self.lastw.get(r)
            if t is not None and need.get(t[0], 0) < t[1]:
                need[t[0]] = t[1]
            if isinstance(r, str) and r.startswith('ps'):
                for k, v in self.readers.get(r, {}).items():
                    if k != e and need.get(k, 0) < v:
                        need[k] = v
        for w in writes:
            t = self.lastw.get(w)
            if t is not None and need.get(t[0], 0) < t[1]:
                need[t[0]] = t[1]
            for k, v in self.readers.get(w, {}).items():
                if need.get(k, 0) < v:
                    need[k] = v
        for k, v in need.items():
            if k == 'pe' and e == 'pe':
                continue
            self._wait(e, k, v)

    def _commit(self, tok, reads, writes):
        for w in writes:
            self.lastw[w] = tok
            self.readers[w] = {}
        ws = set(writes)
        for r in reads:
            if r in ws:
                continue
            d = self.readers.setdefault(r, {})
            if d.get(tok[0], 0) < tok[1]:
                d[tok[0]] = tok[1]

    def mm(self, out, lhsT, rhs, start=True, stop=True, reads=(), writes=(), inc=True):
        rg = lhsT.base_partition() if lhsT.partition_size() < 128 else None
        self.op('pe', lambda e: e.matmul(out, lhsT=lhsT, rhs=rhs, start=start, stop=stop), reads, writes, inc, rg=rg)

    def op(self, e, fn, reads=(), writes=(), inc=True, rg=None):
        if e == 'pe':
            if rg is not None:
                inc = True
                for w in writes:
                    pl = self.pe_last.get(w)
                    if pl is not None and pl[0] != rg:
                        self._wait('pe', 'pe', pl[1])
            else:
                for w in writes:
                    self.pe_last.pop(w, None)
        self._deps(e, reads, writes)
        ins = fn(self.eng[e])
        self.nops += 1
        if inc:
            self.cnt[e] += 1
            ins.then_inc(self.sem[e], 1)
            tok = (e, self.cnt[e])
        else:
            assert e == 'pe'
            tok = (e, self.cnt[e] + 1)
        if e == 'pe' and rg is not None:
            for w in writes:
                self.pe_last[w] = (rg, tok[1])
        self._commit(tok, reads, writes)

    def dma(self, q, out, in_, reads=(), writes=(), **kw):
        n = self.dn[q]
        i = n % self.R
        tgt = 16 * (n // self.R + 1)
        self._wait(q, (q, i), tgt - 16)
        self._deps(q, reads, writes)
        self.eng[q].dma_start(out=out, in_=in_, **kw).then_inc(self.dsem[q][i], 16)
        self.nops += 1
        self.dn[q] += 1
        self._commit(((q, i), tgt), reads, writes)

    def barrier(self):
        for e in self.eng:
            for k in self.sem:
                self._wait(e, k, self.cnt[k])
            for q in self.dsem:
                for i in range(self.R):
                    n_i = (self.dn[q] - i + self.R - 1) // self.R
                    self._wait(e, (q, i), 16 * n_i)
        self.lastw = {}
        self.readers = {}


VEC_ROWS = [("b_mod", 72), ("g_ffn1", 8), ("g_mix", 8), ("g_ffn2", 8), ("g_final", 8), ("rw_mu", 27),
            ("rw_w0", 16), ("rw_a0", 16), ("rw_k_k", 8), ("rw_k_a", 8), ("rw_r_k", 8), ("rw_ln_w", 8),
            ("rw_ln_b", 8), ("lru_conv_w", 32), ("lru_conv_b", 8), ("lru_lam", 16), ("lru_ba", 16),
            ("lru_bx", 16), ("c", 8), ("c_ctx", 8)]
VEC_OFF = {}
_o = 0
for _n, _r in VEC_ROWS:
    VEC_OFF[_n] = _o
    _o += _r
NVEC = _o
NVT = (NVEC + 127) // 128


def build(debug=False, phases=None):
    nc = bass.Bass("TRN2", target_bir_lowering=False)
    dram_in = {}

    def din(name, shape, dt=F32):
        dram_in[name] = nc.dram_tensor(name, list(shape), dt, kind="ExternalInput").ap()
        return dram_in[name]

    x_d = din("x", [T, D])
    ctx_d = din("ctx", [TC, D])
    for n, r in VEC_ROWS:
        din(n, [r, 128])
    w_mod = din("w_mod", [D, 9 * D])
    wts = {}
    for f in ("ffn1", "ffn2"):
        wts[f + "_wg"] = din(f + "_wg", [D, DFF])
        wts[f + "_wu"] = din(f + "_wu", [D, DFF])
        wts[f + "_wd"] = din(f + "_wd", [DFF, D])
    w_in_d = din("w_in", [D, NIN])
    lru_wa_d = din("lru_wa", [2, 4, 256, 256])
    lru_wx_d = din("lru_wx", [2, 4, 256, 256])
    wpr_d = din("w_proj_rw", [D, D])
    wpl_d = din("w_proj_lru", [D, D])
    wo_d = din("w_out", [D, D])
    rw_w_up_d = din("rw_w_up", [128, D])
    rw_a_up_d = din("rw_a_up", [128, D])
    rw_g_up_d = din("rw_g_up", [128, D])
    out_d = nc.dram_tensor("out", [T, D], F32, kind="ExternalOutput").ap()
    skind = "ExternalOutput" if debug else "Internal"
    X1 = nc.dram_tensor("X1", [D, TT], F32, kind=skind).ap()
    XN = nc.dram_tensor("XN", [D, TT], BF16, kind=skind).ap()
    MODD = nc.dram_tensor("MODD", [128, 144], F32, kind=skind).ap()
    P = nc.dram_tensor("P", [NIN, TT], BF16, kind=skind).ap()
    YL = nc.dram_tensor("YL", [D, T], BF16, kind=skind).ap()
    HSTD = nc.dram_tensor("HSTD", [128, 16], F32, kind=skind).ap()
    GS = nc.dram_tensor("GS", [D, T], BF16, kind=skind).ap()
    YRW = nc.dram_tensor("YRW", [D, T], BF16, kind=skind).ap()
    SFD = nc.dram_tensor("SFD", [128, 2, 8, 64], F32, kind=skind).ap()
    X2 = nc.dram_tensor("X2", [D, T], F32, kind=skind).ap()
    X2v = X2.rearrange("(c p) t -> p c t", p=128)
    GSv = GS.rearrange("(c p) t -> p c t", p=128)
    YRWv = YRW.rearrange("(c p) t -> p c t", p=128)
    Pv = P.rearrange("(c p) t -> p c t", p=128)
    YLv = YL.rearrange("(c p) t -> p c t", p=128)

    with ExitStack() as es:
        S = Sched(nc, es)
        ps = [es.enter_context(nc.psum_tensor("ps%d" % i, [128, 512], F32)) for i in range(8)]
        sb = lambda name, shape, dt=F32, st=es: st.enter_context(nc.sbuf_tensor(name, list(shape), dt))
        ident = sb("ident", [128, 128])
        ones = sb("ones", [128, 128])
        PV = sb("PV", [128, NVT * 128])
        MOD = sb("MOD", [128, 72, 2])
        SCAL = sb("SCAL", [128, 2, 11, 8])

        def pv(name, i=0):
            o = VEC_OFF[name] + i
            return PV[:, o:o + 1]

        S.op('pool', lambda e: e.memset(ident[:], 1.0), writes=['ident'])
        S.op('pool', lambda e: e.affine_select(out=ident[:], in_=ident[:], pattern=[[-1, 128]],
                                               compare_op=ALU.is_equal, fill=0.0, base=0, channel_multiplier=1),
             reads=['ident'], writes=['ident'])
        S.op('pool', lambda e: e.memset(ones[:], 1.0), writes=['ones'])

        with ExitStack() as p0:
            VST = sb("VST", [128, NVT, 128], st=p0)
            SC = sb("SC", [128, 8, 2], st=p0)
            slab = [sb("slab%d" % i, [128, 8, 1152], st=p0) for i in range(2)]
            S.op('dve', lambda e: e.memset(VST[:], 0.0), writes=['VST'])
            for n, r in VEC_ROWS:
                o = VEC_OFF[n]
                done = 0
                while done < r:
                    ti, ri = divmod(o + done, 128)
                    m = min(r - done, 128 - ri)
                    S.dma('sp', VST[ri:ri + m, ti, :], dram_in[n][done:done + m, :], reads=[], writes=['VST'])
                    done += m
            for ti in range(NVT):
                S.mm(ps[7][:, 0:128], lhsT=VST[:, ti, :], rhs=ident[:],
                                                     start=True, stop=True,
                     reads=['VST', 'ident'], writes=['ps7'])
                S.op('dve', lambda e, ti=ti: e.tensor_copy(out=PV[:, ti * 128:(ti + 1) * 128], in_=ps[7][:, 0:128]),
                     reads=['ps7'], writes=['PV'])
            oc_, occ = VEC_OFF["c"], VEC_OFF["c_ctx"]
            S.op('act', lambda e: e.activation(out=SC[:, :, 0], in_=PV[:, oc_:oc_ + 8], func=AF.Silu),
                 reads=['PV'], writes=['SC'])
            S.op('act', lambda e: e.activation(out=SC[:, :, 1], in_=PV[:, occ:occ + 8], func=AF.Silu),
                 reads=['PV'], writes=['SC'])
            wmv = w_mod.rearrange("(k p) n -> p k n", p=128)
            for si in range(8):
                sl = slab[si % 2]
                key = 'slab%d' % (si % 2)
                for k in range(8):
                    S.dma('sp' if k % 2 == 0 else 'pool', sl[:, k, :], wmv[:, k, si * 1152:(si + 1) * 1152],
                          writes=[key])
                for ol in range(9):
                    oc = si * 9 + ol
                    for k in range(8):
                        S.mm(
                            ps[6][:, oc * 2:oc * 2 + 2], lhsT=sl[:, k, ol * 128:(ol + 1) * 128], rhs=SC[:, k, :],
                            start=(k == 0), stop=(k == 7),
                            reads=[key, 'SC'], writes=['ps6'], inc=(k == 7))
            bo = VEC_OFF["b_mod"]
            psv = ps[6][:, 0:144].rearrange("p (a b) -> p a b", b=2)
            for s_ in range(2):
                S.op('dve', lambda e, s_=s_: e.tensor_tensor(out=MOD[:, :, s_], in0=psv[:, :, s_],
                                                             in1=PV[:, bo:bo + 72], op=ALU.add),
                     reads=['ps6', 'PV'], writes=['MOD'])
            for s_ in range(2):
                for gi, gname in enumerate(("g_ffn1", "g_mix", "g_ffn2")):
                    go = VEC_OFF[gname]
                    sh = MOD[:, (3 * gi) * 8:(3 * gi + 1) * 8, s_]
                    scl = MOD[:, (3 * gi + 1) * 8:(3 * gi + 2) * 8, s_]
                    gt = MOD[:, (3 * gi + 2) * 8:(3 * gi + 3) * 8, s_]
                    S.op('dve', lambda e, s_=s_, gi=gi, scl=scl, go=go: e.scalar_tensor_tensor(
                        out=SCAL[:, s_, 3 * gi, :], in0=scl, scalar=1.0, in1=PV[:, go:go + 8],
                        op0=ALU.add, op1=ALU.mult), reads=['MOD', 'PV'], writes=['SCAL'])
                    S.op('dve', lambda e, s_=s_, gi=gi, sh=sh: e.tensor_copy(out=SCAL[:, s_, 3 * gi + 1, :], in_=sh),
                         reads=['MOD'], writes=['SCAL'])
                    S.op('dve', lambda e, s_=s_, gi=gi, gt=gt: e.tensor_scalar(
                        out=SCAL[:, s_, 3 * gi + 2, :], in0=gt, scalar1=(1.0 if gi == 1 else 0.5), scalar2=None,
                        op0=ALU.mult), reads=['MOD'], writes=['SCAL'])
            gfo = VEC_OFF["g_final"]
            for s_ in range(2):
                S.op('dve', lambda e, s_=s_: e.tensor_copy(out=SCAL[:, s_, 9, :], in_=PV[:, gfo:gfo + 8]), reads=['PV'], writes=['SCAL'])
                S.op('dve', lambda e, s_=s_: e.memset(SCAL[:, s_, 10, :], 0.0), reads=['SCAL'], writes=['SCAL'])
            if debug:
                S.dma('sp', MODD[:, :], MOD[:].rearrange("p a b -> p (a b)"), reads=['MOD'], writes=['MODD'])
            S.barrier()

        def ffn_phase(which, tiles, load_tile, epilogue):
            with ExitStack() as ph:
                WG = sb("WG_" + which, [128, 8, DFF], BF16, st=ph)
                WU = sb("WU_" + which, [128, 8, DFF], BF16, st=ph)
                WD = sb("WD_" + which, [128, NJ, D], BF16, st=ph)
                SCR = sb("SCR_" + which, [128, 4096], st=ph)
                XT = sb("XT_" + which, [128, 8, 512], st=ph)
                HN = sb("HN_" + which, [128, 8, 512], BF16, st=ph)
                AA = sb("AA_" + which, [128, NJ, 512], BF16, st=ph)
                TM = [sb("TM_" + which + "%d" % i, [128, 512], st=ph) for i in range(2)]
                RS = sb("RS_" + which, [128, 512], st=ph)
                ci = [0]
                cast_eng = ('dve', 'pool', 'act')

                def load_cast(dst, src, w):
                    i = ci[0] % 2
                    ci[0] += 1
                    key = 'stg%d' % i
                    S.dma('sp' if i == 0 else 'pool', SCR[:, i * 2048:i * 2048 + w], src, writes=[key])
                    ce = cast_eng[ci[0] % 3]
                    if ce == 'act':
                        S.op('act', lambda e: e.copy(out=dst, in_=SCR[:, i * 2048:i * 2048 + w]), reads=[key])
                    else:
                        S.op(ce, lambda e: e.tensor_copy(out=dst, in_=SCR[:, i * 2048:i * 2048 + w]), reads=[key])
                for (Wt, wd_) in ((WG, wts[which + "_wg"]), (WU, wts[which + "_wu"])):
                    for k in range(8):
                        for h in range(2):
                            load_cast(Wt[:, k, h * 1408:(h + 1) * 1408], wd_[k * 128:(k + 1) * 128, h * 1408:(h + 1) * 1408],
                                      1408)
                wdd = wts[which + "_wd"]
                for j in range(NJ):
                    load_cast(WD[:, j, :], wdd[j * 128:(j + 1) * 128, :], 1024)
                S.barrier()

                def rmsnorm_mod(n, s_, gi, dst, ia=None, ib=None, dk='HN'):
                    ia = 3 * gi if ia is None else ia
                    ib = 3 * gi + 1 if ib is None else ib
                    for c in range(8):
                        tm = TM[c % 2]
                        S.op('act', lambda e, c=c, tm=tm: e.activation(out=tm[:, :n], in_=XT[:, c, :n], func=AF.Square),
                             reads=[('XT', c)], writes=['TM%d' % (c % 2)])
                        S.mm(ps[6][:, :n], lhsT=ones[:], rhs=tm[:, :n],
                                                                  start=(c == 0), stop=(c == 7),
                             reads=['TM%d' % (c % 2), 'ones'], writes=['ps6'], inc=True)
                    S.op('act', lambda e: e.activation(out=RS[:, :n], in_=ps[6][:, :n], func=AF.Sqrt, bias=EPSB[:, 0:1],
                                                       scale=1.0 / D), reads=['ps6'], writes=['RS'])
                    S.op('dve', lambda e: e.reciprocal(out=RS[:, :n], in_=RS[:, :n]), reads=['RS'], writes=['RS'])
                    for c in range(8):
                        tm = TM[c % 2]
                        S.op('dve', lambda e, c=c, tm=tm: e.scalar_tensor_tensor(
                            out=tm[:, :n], in0=XT[:, c, :n], scalar=SCAL[:, s_, ia, c:c + 1], in1=RS[:, :n],
                            op0=ALU.mult, op1=ALU.mult), reads=[('XT', c), 'RS'], writes=['TM%d' % (c % 2)])
                        S.op('act', lambda e, c=c, tm=tm: e.activation(
                            out=dst[:, c, :n], in_=tm[:, :n], func=AF.Identity, bias=SCAL[:, s_, ib, c:c + 1],
                            scale=1.0), reads=['TM%d' % (c % 2)], writes=[(dk, c)])

                gi = 0 if which == "ffn1" else 2
                for (s_, t0, n) in tiles:
                    load_tile(S, SCR, XT, s_, t0, n)
                    rmsnorm_mod(n, s_, gi, HN)
                    for j in range(NJ):
                        pg, pu = ps[(j % 2) * 2], ps[(j % 2) * 2 + 1]
                        kg, ku = 'ps%d' % ((j % 2) * 2), 'ps%d' % ((j % 2) * 2 + 1)
                        for (pp, kk_, Wt) in ((pg, kg, WG), (pu, ku, WU)):
                            for k in range(8):
                                S.mm(
                                    pp[:, :n], lhsT=Wt[:, k, j * 128:(j + 1) * 128], rhs=HN[:, k, :n],
                                    start=(k == 0), stop=(k == 7),
                                    reads=[('HN', k)], writes=[kk_], inc=(k == 7))
                        tm = TM[j % 2]
                        S.op('act', lambda e, pg=pg, tm=tm: e.activation(out=tm[:, :n], in_=pg[:, :n], func=AF.Silu),
                             reads=[kg], writes=['TM%d' % (j % 2)])
                        S.op('dve', lambda e, pu=pu, tm=tm, j=j: e.tensor_tensor(out=AA[:, j, :n], in0=pu[:, :n],
                                                                                 in1=tm[:, :n], op=ALU.mult),
                             reads=[ku, 'TM%d' % (j % 2)], writes=[('AA', j)])
                    for o in range(8):
                        pd = ps[4 + o % 2]
                        kd = 'ps%d' % (4 + o % 2)
                        for j in range(NJ):
                            S.mm(
                                pd[:, :n], lhsT=WD[:, j, o * 128:(o + 1) * 128], rhs=AA[:, j, :n],
                                start=(j == 0), stop=(j == NJ - 1),
                                reads=[('AA', j)], writes=[kd], inc=(j == NJ - 1))
                        S.op('dve', lambda e, pd=pd, o=o: e.scalar_tensor_tensor(
                            out=XT[:, o, :n], in0=pd[:, :n], scalar=SCAL[:, s_, 3 * gi + 2, o:o + 1], in1=XT[:, o, :n],
                            op0=ALU.mult, op1=ALU.add), reads=[kd, ('XT', o)], writes=[('XT', o)])
                    epilogue(S, XT, HN, TM, RS, rmsnorm_mod, s_, t0, n, SCR=SCR)
                S.barrier()

        EPSB = sb("EPSB", [128, 1])
        S.op('pool', lambda e: e.memset(EPSB[:], EPS), writes=['EPSB'])
        MUSG = sb("MUSG", [128, 3, 27])
        _mo = VEC_OFF["rw_mu"]
        S.op('dve', lambda e: e.tensor_scalar(out=MUSG[:, 0, :], in0=PV[:, _mo:_mo + 27], scalar1=-1.0, scalar2=1.0, op0=ALU.mult,
                                              op1=ALU.add), reads=['PV'], writes=['MUSG'])
        S.op('dve', lambda e: e.tensor_scalar(out=MUSG[:, 1, :], in0=PV[:, _mo:_mo + 27], scalar1=0.25, scalar2=None, op0=ALU.mult),
             reads=['PV'], writes=['MUSG'])
        S.op('dve', lambda e: e.tensor_scalar(out=MUSG[:, 2, :], in0=PV[:, _mo:_mo + 27], scalar1=0.5, scalar2=None, op0=ALU.mult),
             reads=['PV'], writes=['MUSG'])

        def load_tok_major(S, SCR, XT, s_, t0, n):
            src = x_d if s_ == 0 else ctx_d
            nb = n // 128
            for b_ in range(nb):
                S.dma('sp', SCR[:, b_ * 1024:(b_ + 1) * 1024], src[t0 + b_ * 128:t0 + (b_ + 1) * 128, :],
                      writes=[('XIN', b_)])
            for c in range(8):
                for b_ in range(nb):
                    S.mm(
                        ps[7][:, b_ * 128:(b_ + 1) * 128], lhsT=SCR[:, b_ * 1024 + c * 128:b_ * 1024 + (c + 1) * 128],
                        rhs=ident[:], start=True, stop=True, reads=[('XIN', b_), 'ident'], writes=['ps7'],
                        inc=(b_ == nb - 1))
                S.op('act' if c % 2 == 0 else 'dve',
                     (lambda e, c=c: e.copy(out=XT[:, c, :n], in_=ps[7][:, :n])) if c % 2 == 0 else
                     (lambda e, c=c: e.tensor_copy(out=XT[:, c, :n], in_=ps[7][:, :n])),
                     reads=['ps7'], writes=[('XT', c)])

        X1v = X1.rearrange("(c p) t -> p c t", p=128)
        XNv = XN.rearrange("(c p) t -> p c t", p=128)

        def epi_ffn1(S, XT, HN, TM, RS, rmsnorm_mod, s_, t0, n, SCR=None):
            g0 = (TC if s_ == 0 else 0) + t0
            for c in range(8):
                S.dma('pool', X1v[:, c, g0:g0 + n], XT[:, c, :n], reads=[('XT', c)], writes=[('X1', g0)])
            rmsnorm_mod(n, s_, 1, HN)
            for c in range(8):
                S.dma('pool', XNv[:, c, g0:g0 + n], HN[:, c, :n], reads=[('HN', c)], writes=[('XN', g0)])

        tiles1 = [(1, 0, TC)] + [(0, i * 512, 512) for i in range(T // 512)]
        if phases is None or "A" in phases:
            ffn_phase("ffn1", tiles1, load_tok_major, epi_ffn1)
        S.barrier()

        import os as _os
        RCUT = int(_os.environ.get("RCUT", "0"))

        class _Cut(Exception):
            pass

        def cut(k):
            if RCUT == k:
                raise _Cut()

        def seq_tiles(n):
            return [(t0, min(512, n - t0)) for t0 in range(0, n, 512)]

        def phase_inproj():
            with ExitStack() as ph:
                WIN = sb("WIN", [128, 8, NIN], BF16, st=ph)
                SCRB = sb("SCRB", [128, 4096], st=ph)
                XNH = [sb("XNH%d" % i, [128, 8, 640], BF16, st=ph) for i in range(2)]
                XS = [sb("XSn%d" % i, [128, 8, 512], BF16, st=ph) for i in range(2)]
                XSF = [sb("XSF%d" % i, [128, 512], st=ph) for i in range(2)]
                TZ = [sb("TZ%d" % i, [128, 512], st=ph) for i in range(2)]
                PO = [sb("PO%d" % i, [128, 512], BF16, st=ph) for i in range(6)]
                ci = 0
                for k in range(8):
                    for c0 in range(0, NIN, 2048):
                        w = min(2048, NIN - c0)
                        i = ci % 2
                        ci += 1
                        key = 'stg%d' % i
                        S.dma('sp' if i == 0 else 'pool', SCRB[:, i * 2048:i * 2048 + w],
                              w_in_d[k * 128:(k + 1) * 128, c0:c0 + w], writes=[key])
                        ce = ('dve', 'pool', 'act')[ci % 3]
                        if ce == 'act':
                            S.op('act', lambda e, i=i, w=w, k=k, c0=c0: e.copy(out=WIN[:, k, c0:c0 + w],
                                                                               in_=SCRB[:, i * 2048:i * 2048 + w]), reads=[key])
                        else:
                            S.op(ce, lambda e, i=i, w=w, k=k, c0=c0: e.tensor_copy(out=WIN[:, k, c0:c0 + w],
                                                                                   in_=SCRB[:, i * 2048:i * 2048 + w]), reads=[key])
                S.barrier()
                tl = [(1, 0, TC)] + [(0, t0, n) for (t0, n) in seq_tiles(T)]
                it = 0
                iz = 0
                for ti_, (s_, t0, n) in enumerate(tl):
                    g0 = (TC if s_ == 0 else 0) + t0
                    H = XNH[ti_ % 2]
                    Xs = XS[ti_ % 2]
                    kH = lambda c: ('XNH', ti_ % 2, c)
                    kX = lambda c: ('XS', ti_ % 2, c)
                    has_top = (s_ == 0 and t0 > 0)
                    has_bot = (s_ == 0 and t0 + n < T)
                    if not has_top:
                        S.op('pool', lambda e, H=H: e.memset(H[:, :, 0:64], 0.0), reads=[kH(c) for c in range(8)],
                             writes=[kH(c) for c in range(8)])
                    if not has_bot:
                        S.op('pool', lambda e, H=H: e.memset(H[:, :, 64 + n:128 + n], 0.0), reads=[kH(c) for c in range(8)],
                             writes=[kH(c) for c in range(8)])
                    for c in range(8):
                        lo = g0 - (64 if has_top else 0)
                        hi = g0 + n + (64 if has_bot else 0)
                        S.dma('sp', H[:, c, 64 - (g0 - lo):64 + n + (hi - g0 - n)], XNv[:, c, lo:hi], reads=[kH(c)], writes=[kH(c)])
                    for c in range(8):
                        if s_ == 1:
                            S.op('dve' if c % 2 else 'pool', lambda e, c=c: e.tensor_tensor(
                                out=Xs[:, c, :n], in0=H[:, c, 63:63 + n], in1=H[:, c, 65:65 + n], op=ALU.add),
                                reads=[kH(c)], writes=[kX(c)])
                        else:
                            xf = XSF[c % 2]
                            kf = 'XSF%d' % (c % 2)
                            xf3 = xf[:, :n].rearrange("p (r c) -> p r c", c=64)
                            c3 = H[:, c, 64:64 + n].rearrange("p (r c) -> p r c", c=64)
                            xs3 = Xs[:, c, :n].rearrange("p (r c) -> p r c", c=64)
                            S.op('pool', lambda e, c=c, xf=xf: e.tensor_tensor(out=xf[:, :n], in0=H[:, c, 0:n], in1=H[:, c, 128:128 + n],
                                                                              op=ALU.add), reads=[kH(c)], writes=[kf])
                            S.op('dve', lambda e, xf3=xf3, c3=c3: e.tensor_tensor(out=xf3[:, :, 1:64], in0=xf3[:, :, 1:64], in1=c3[:, :, 0:63],
                                                                                  op=ALU.add), reads=[kH(c), kf], writes=[kf])
                            S.op('dve', lambda e, xf3=xf3, c3=c3, xs3=xs3: e.tensor_tensor(out=xs3[:, :, 0:63], in0=xf3[:, :, 0:63],
                                                                                           in1=c3[:, :, 1:64], op=ALU.add),
                                 reads=[kH(c), kf], writes=[kX(c)])
                            S.op('pool', lambda e, xf3=xf3, xs3=xs3: e.tensor_copy(out=xs3[:, :, 63:64], in_=xf3[:, :, 63:64]),
                                 reads=[kf, kX(c)], writes=[kX(c)])
                    noc = 59 if s_ == 0 else 35
                    qi = 1 if s_ == 0 else 2
                    for oc in range(noc):
                        po = PO[it % 6]
                        kpo = 'PO%d' % (it % 6)
                        it += 1
                        if oc < 27:
                            b0 = (oc % 2) * 2
                            p1, p2 = ps[b0], ps[b0 + 1]
                            k1, k2 = 'ps%d' % b0, 'ps%d' % (b0 + 1)
                            for k in range(8):
                                S.mm(p1[:, :n], lhsT=WIN[:, k, oc * 128:(oc + 1) * 128], rhs=H[:, k, 64:64 + n], start=(k == 0), stop=(k == 7),
                                     reads=[kH(k)], writes=[k1], inc=(k == 7))
                            for k in range(8):
                                S.mm(p2[:, :n], lhsT=WIN[:, k, oc * 128:(oc + 1) * 128], rhs=Xs[:, k, :n], start=(k == 0), stop=(k == 7),
                                     reads=[kX(k)], writes=[k2], inc=(k == 7))
                            tz = TZ[iz % 2]
                            ktz = 'TZ%d' % (iz % 2)
                            iz += 1
                            S.op('act', lambda e, p2=p2, tz=tz, oc=oc: e.activation(out=tz[:, :n], in_=p2[:, :n], func=AF.Identity,
                                                                                    scale=MUSG[:, qi, oc:oc + 1]), reads=[k2, 'MUSG'],
                                 writes=[ktz])
                            S.op('dve', lambda e, p1=p1, tz=tz, po=po, oc=oc: e.scalar_tensor_tensor(
                                out=po[:, :n], in0=p1[:, :n], scalar=MUSG[:, 0, oc:oc + 1], in1=tz[:, :n], op0=ALU.mult, op1=ALU.add),
                                reads=[k1, ktz, 'MUSG'], writes=[kpo])
                        else:
                            pp = ps[4 + oc % 4]
                            kp = 'ps%d' % (4 + oc % 4)
                            for k in range(8):
                                S.mm(pp[:, :n], lhsT=WIN[:, k, oc * 128:(oc + 1) * 128], rhs=H[:, k, 64:64 + n], start=(k == 0), stop=(k == 7),
                                     reads=[kH(k)], writes=[kp], inc=(k == 7))
                            if oc >= 43:
                                S.op('act', lambda e, pp=pp, po=po: e.activation(out=po[:, :n], in_=pp[:, :n], func=AF.Sigmoid),
                                     reads=[kp], writes=[kpo])
                            elif oc >= 35:
                                S.op('act', lambda e, pp=pp, po=po: e.activation(out=po[:, :n], in_=pp[:, :n], func=AF.Gelu),
                                     reads=[kp], writes=[kpo])
                            elif oc % 2 == 0:
                                S.op('act', lambda e, pp=pp, po=po: e.copy(out=po[:, :n], in_=pp[:, :n]), reads=[kp], writes=[kpo])
                            else:
                                S.op('dve', lambda e, pp=pp, po=po: e.tensor_copy(out=po[:, :n], in_=pp[:, :n]), reads=[kp],
                                     writes=[kpo])
                        S.dma('pool', Pv[:, oc, g0:g0 + n], po[:, :n], reads=[kpo], writes=[('P', oc, g0)])
                S.barrier()

        if phases is None or "B" in phases:
            phase_inproj()
        if phases is not None and "L" not in phases and "R" not in phases:
            return nc

        ONEB = sb("ONEB", [128, 1])
        S.op('pool', lambda e: e.memset(ONEB[:], 1.0), writes=['ONEB'])
        LC = sb("LC", [128, 2, 16])
        HST = sb("HST", [128, 16])
        lo = VEC_OFF["lru_lam"]
        S.op('act', lambda e: e.activation(out=LC[:, 0, :], in_=PV[:, lo:lo + 16], func=AF.Exp, scale=-1.0),
             reads=['PV'], writes=['LC'])
        S.op('act', lambda e: e.activation(out=LC[:, 0, :], in_=LC[:, 0, :], func=AF.Ln, bias=ONEB[:, 0:1], scale=1.0),
             reads=['LC', 'ONEB'], writes=['LC'])
        S.op('dve', lambda e: e.tensor_scalar(out=LC[:, 1, :], in0=LC[:, 0, :], scalar1=-16.0, scalar2=None, op0=ALU.mult),
             reads=['LC'], writes=['LC'])
        S.op('dve', lambda e: e.tensor_scalar(out=LC[:, 0, :], in0=LC[:, 0, :], scalar1=-8.0, scalar2=None, op0=ALU.mult),
             reads=['LC'], writes=['LC'])

        def phase_lru():
            with ExitStack() as ph:
                WAX = sb("WAX", [128, 2, 16, 256], BF16, st=ph)
                STG = sb("STGL", [128, 16, 256], st=ph)
                XL = sb("XL", [128, 2, T + 4], BF16, st=ph)
                XC = sb("XC", [128, 2, T], st=ph)
                XCB = sb("XCB", [128, 2, T], BF16, st=ph)
                AB = sb("AB", [128, T], st=ph)
                UB = sb("UB", [128, T], st=ph)
                HB = [sb("HB%d" % i, [128, T], st=ph) for i in range(2)]
                GL = sb("GLg", [128, T], BF16, st=ph)
                YB = sb("YBl", [128, T], BF16, st=ph)
                TL = [sb("TL%d" % i, [128, 512], st=ph) for i in range(5)]
                for gi_, wsrc in enumerate((lru_wa_d, lru_wx_d)):
                    S.dma('sp', STG[:], wsrc.rearrange("d n (k p) j -> p (d n k) j", p=128), writes=['STGL'])
                    S.op('dve', lambda e, gi_=gi_: e.tensor_copy(out=WAX[:, gi_, :, :], in_=STG[:]), reads=['STGL'],
                         writes=['WAX'])
                S.op('pool', lambda e: e.memset(XL[:], 0.0), writes=['XL'])
                cw, cb = VEC_OFF["lru_conv_w"], VEC_OFF["lru_conv_b"]
                bao, bxo = VEC_OFF["lru_ba"], VEC_OFF["lru_bx"]
                for s_ in (1, 0):
                    n = TC if s_ == 1 else T
                    g0 = 0 if s_ == 1 else TC
                    if s_ == 0:
                        S.op('pool', lambda e: e.memset(XL[:], 0.0), reads=['XL'], writes=['XL'])
                    for blk in range(4):
                        for k in range(2):
                            cc = 2 * blk + k
                            S.dma('sp', XL[:, k, 1:1 + n], Pv[:, 27 + cc, g0:g0 + n], reads=[('P', 27 + cc, g0)],
                                  writes=['XL'])
                            S.op('act', lambda e, k=k, cc=cc: e.activation(
                                out=XC[:, k, :n], in_=XL[:, k, 0:n], func=AF.Identity, bias=PV[:, cb + cc:cb + cc + 1],
                                scale=PV[:, cw + cc:cw + cc + 1]), reads=['XL'], writes=[('XC', k)])
                            for j in range(1, 4):
                                S.op('dve', lambda e, k=k, cc=cc, j=j: e.scalar_tensor_tensor(
                                    out=XC[:, k, :n], in0=XL[:, k, j:j + n], scalar=PV[:, cw + j * 8 + cc:cw + j * 8 + cc + 1],
                                    in1=XC[:, k, :n], op0=ALU.mult, op1=ALU.add), reads=['XL', ('XC', k)], writes=[('XC', k)])
                            S.op('act', lambda e, k=k: e.copy(out=XCB[:, k, :n], in_=XC[:, k, :n]), reads=[('XC', k)],
                                 writes=[('XCB', k)])
                        for oc in range(2):
                            cc = 2 * blk + oc
                            if s_ == 0:
                                S.dma('sp', GL[:, :n], Pv[:, 35 + cc, g0:g0 + n], reads=[('P', 35 + cc, g0)], writes=['GL'])
                            for d in range(2):
                                col = d * 8 + cc
                                for (t0, m) in seq_tiles(n):
                                    for gi_ in range(2):
                                        pp = ps[gi_]
                                        for kin in range(2):
                                            S.mm(
                                                pp[:, :m], lhsT=WAX[:, gi_, (d * 4 + blk) * 2 + kin, oc * 128:(oc + 1) * 128],
                                                rhs=XCB[:, kin, t0:t0 + m], start=(kin == 0), stop=(kin == 1),
                                                reads=[('XCB', kin), 'WAX'], writes=['ps%d' % gi_], inc=(kin == 1))
                                    S.op('act', lambda e, m=m, col=col: e.activation(
                                        out=TL[0][:, :m], in_=ps[0][:, :m], func=AF.Sigmoid, bias=PV[:, bao + col:bao + col + 1],
                                        scale=1.0), reads=['ps0'], writes=['TL0'])
                                    S.op('act', lambda e, m=m, col=col: e.activation(
                                        out=TL[1][:, :m], in_=ps[1][:, :m], func=AF.Sigmoid, bias=PV[:, bxo + col:bxo + col + 1],
                                        scale=1.0), reads=['ps1'], writes=['TL1'])
                                    S.op('act', lambda e, m=m, col=col, t0=t0: e.activation(
                                        out=AB[:, t0:t0 + m], in_=TL[0][:, :m], func=AF.Exp, scale=LC[:, 0, col:col + 1]),
                                        reads=['TL0', 'LC'], writes=[('AB', t0)])
                                    S.op('act', lambda e, m=m, col=col: e.activation(
                                        out=TL[2][:, :m], in_=TL[0][:, :m], func=AF.Exp, scale=LC[:, 1, col:col + 1]),
                                        reads=['TL0', 'LC'], writes=['TL2'])
                                    S.op('act', lambda e, m=m: e.activation(
                                        out=TL[3][:, :m], in_=TL[2][:, :m], func=AF.Sqrt, bias=ONEB[:, 0:1], scale=-1.0),
                                        reads=['TL2', 'ONEB'], writes=['TL3'])
                                    S.op('pool', lambda e, m=m, t0=t0: e.tensor_tensor(
                                        out=TL[4][:, :m], in0=TL[1][:, :m], in1=XC[:, oc, t0:t0 + m], op=ALU.mult),
                                        reads=['TL1', ('XC', oc)], writes=['TL4'])
                                    S.op('dve', lambda e, m=m, t0=t0: e.tensor_tensor(
                                        out=UB[:, t0:t0 + m], in0=TL[3][:, :m], in1=TL[4][:, :m], op=ALU.mult),
                                        reads=['TL3', 'TL4'], writes=[('UB', t0)])
                                rk = [('AB', t0) for (t0, m) in seq_tiles(n)] + [('UB', t0) for (t0, m) in seq_tiles(n)]
                                init = 0.0 if s_ == 1 else HST[:, col:col + 1]
                                if d == 0:
                                    S.op('dve', lambda e, init=init: e.tensor_tensor_scan(
                                        out=HB[0][:, :n], data0=AB[:, :n], data1=UB[:, :n], initial=init, op0=ALU.mult,
                                        op1=ALU.add), reads=rk + ['HST'], writes=['HB0'])
                                    if s_ == 1:
                                        S.op('dve', lambda e, col=col: e.tensor_copy(out=HST[:, col:col + 1], in_=HB[0][:, n - 1:n]),
                                             reads=['HB0'], writes=['HST'])
                                else:
                                    S.op('dve', lambda e, init=init: e.tensor_tensor_scan(
                                        out=HB[1][:, n - 1::-1] if False else HB[1][:, :n][:, ::-1], data0=AB[:, :n][:, ::-1],
                                        data1=UB[:, :n][:, ::-1], initial=init, op0=ALU.mult, op1=ALU.add),
                                        reads=rk + ['HST'], writes=['HB1'])
                                    if s_ == 1:
                                        S.op('dve', lambda e, col=col: e.tensor_copy(out=HST[:, col:col + 1], in_=HB[1][:, 0:1]),
                                             reads=['HB1'], writes=['HST'])
                            if s_ == 0:
                                S.op('pool', lambda e: e.tensor_tensor(out=HB[0][:, :n], in0=HB[0][:, :n], in1=HB[1][:, :n],
                                                                       op=ALU.add), reads=['HB0', 'HB1'], writes=['HB0'])
                                S.op('dve', lambda e: e.tensor_tensor(out=YB[:, :n], in0=HB[0][:, :n], in1=GL[:, :n], op=ALU.mult),
                                     reads=['HB0', 'GL'], writes=['YBl'])
                                S.dma('pool', YLv[:, cc, :], YB[:, :n], reads=['YBl'], writes=[('YL', cc)])
                if debug:
                    S.dma('sp', HSTD[:, :], HST[:], reads=['HST'], writes=['HSTD'])
                S.barrier()

        if phases is None or "L" in phases:
            phase_lru()
        if phases is not None and "R" not in phases:
            return nc
        DS = float(np.exp(-0.5))
        LN_X_EPS = 64e-5
        NCH = TT // 64

        def _rwkv_body(ph):
            if True:
                WUP = sb("WUP", [128, D], BF16, st=ph)
                AUP = sb("AUP", [128, D], BF16, st=ph)
                GUP = sb("GUP", [128, D], BF16, st=ph)
                IDB = sb("IDB", [128, 128], BF16, st=ph)
                BONES = sb("BONES", [128, 128], st=ph)
                MK = [sb("MK%d" % d, [128, 2, 256], st=ph) for d in range(2)]
                AMK = [sb("AMK%d" % d, [128, 2, 64], st=ph) for d in range(2)]
                BMK = [sb("BMK%d" % d, [128, 4, 64], st=ph) for d in range(2)]
                RMF = sb("RMF", [128, 512], st=ph)
                RMB = sb("RMB", [128, 512], st=ph)
                TW = sb("TW", [128, TT], BF16, st=ph)
                ZA = sb("ZA", [128, TT], BF16, st=ph)
                PL = sb("PL", [128, TT], BF16, st=ph)
                SH = sb("SH", [128, TT], st=ph)
                ZR = sb("ZR", [128, TT], BF16, st=ph)
                ZK = sb("ZK", [128, TT], BF16, st=ph)
                ZV = sb("ZV", [128, TT], BF16, st=ph)
                KK = sb("KK", [128, TT], BF16, st=ph)
                ART = [[sb("ART%d%d" % (d, i), [128, 8, 2, 64], BF16, st=ph) for i in range(2)] for d in range(2)]
                KBT = [[sb("KBT%d%d" % (d, i), [128, 8, 2, 64], BF16, st=ph) for i in range(2)] for d in range(2)]
                WC = [sb("WC%d" % d, [128, NCH], st=ph) for d in range(2)]
                TRP = [[sb("TRP%d%d" % (d, i), [128, 512], F32 if i < 5 else BF16, st=ph) for i in range(7)] for d in range(2)]
                YB = sb("YBr", [128, 64, 64], st=ph)
                TR = [sb("TR%d" % i, [128, 512], st=ph) for i in range(3)]
                TRS = sb("TRS", [128, 256], st=ph)
                MT = [[sb("MT%d%d" % (d, i), [128, 8, 256], BF16, st=ph) for i in range(2)] for d in range(2)]
                ABt = [[sb("ABt%d%d" % (d, i), [128, 4, 2, 128], BF16, st=ph) for i in range(2)] for d in range(2)]
                ACC = [[sb("ACC%d%d" % (d, i), [128, 4, 128], BF16, st=ph) for i in range(2)] for d in range(2)]
                TTt = [[sb("TTt%d%d" % (d, i), [128, 8, 128], BF16, st=ph) for i in range(2)] for d in range(2)]
                PT = [[sb("PT%d%d" % (d, i), [128, 3, 64], BF16, st=ph) for i in range(2)] for d in range(2)]
                XB = [[sb("XB%d%d" % (d, i), [128, 64], BF16, st=ph) for i in range(2)] for d in range(2)]
                UB_ = [[sb("UBr%d%d" % (d, i), [128, 64], BF16, st=ph) for i in range(2)] for d in range(2)]
                SS = [sb("SS%d" % d, [128, 64], st=ph) for d in range(2)]
                SSb = [sb("SSb%d" % d, [128, 64], BF16, st=ph) for d in range(2)]
                MUS = sb("MUS", [128, 3, 27], st=ph)
                KA1 = sb("KA1", [128, 8], st=ph)
                GO = [sb("GO%d" % i, [128, 512], BF16, st=ph) for i in range(2)]
                ST8 = sb("ST8", [128, 2, 8, 64], st=ph) if debug else None

                for i_, (dst, src) in enumerate(((WUP, rw_w_up_d), (AUP, rw_a_up_d), (GUP, rw_g_up_d))):
                    S.dma('sp', SH[:, 0:D], src[:, :], writes=['SH'])
                    S.op('dve', lambda e, dst=dst: e.tensor_copy(out=dst[:], in_=SH[:, 0:D]), reads=['SH'], writes=['LW_' + str(i_)])
                S.op('dve', lambda e: e.tensor_copy(out=IDB[:], in_=ident[:]), reads=['ident'], writes=['IDB'])
                S.op('pool', lambda e: e.memset(BONES[:], 0.0), writes=['BONES'])
                S.op('pool', lambda e: e.memset(BONES[0:64, 0:64], 1.0), reads=['BONES'], writes=['BONES'])
                S.op('pool', lambda e: e.memset(BONES[64:128, 64:128], 1.0), reads=['BONES'], writes=['BONES'])
                S.op('pool', lambda e: e.memset(RMF[:], 1.0), writes=['RMF'])
                S.op('pool', lambda e: e.memset(RMF[:, 0:512:64], 0.0), reads=['RMF'], writes=['RMF'])
                S.op('pool', lambda e: e.memset(RMB[:], 1.0), writes=['RMB'])
                S.op('pool', lambda e: e.memset(RMB[:, 63:512:64], 0.0), reads=['RMB'], writes=['RMB'])

                def tri(dst_view_fn, kind):
                    pat, cm = ([[1, 64]], -1) if kind[0] == 'L' else ([[-1, 64]], 1)
                    cmp_ = ALU.is_gt if kind[2] == 's' else ALU.is_ge
                    for hp in (0, 64):
                        v = dst_view_fn(hp)
                        S.op('pool', lambda e, v=v: e.memset(v, 1.0), reads=['MASKS'], writes=['MASKS'])
                        S.op('pool', lambda e, v=v: e.affine_select(out=v, in_=v, pattern=pat, compare_op=cmp_, fill=0.0, base=0,
                                                                   channel_multiplier=cm), reads=['MASKS'], writes=['MASKS'])
                for d in range(2):
                    ks, ki = ('LTs', 'LTi') if d == 0 else ('GTs', 'GTi')
                    kt = 'GTs' if d == 0 else 'LTs'
                    for q in range(2):
                        for blk, kd_ in enumerate((ks, ki, ks, ki)):
                            tri(lambda hp, d=d, q=q, blk=blk: MK[d][hp:hp + 64, q, blk * 64:(blk + 1) * 64], kd_)
                        tri(lambda hp, d=d, q=q: AMK[d][hp:hp + 64, q, :], ks)
                    for q in range(4):
                        tri(lambda hp, d=d, q=q: BMK[d][hp:hp + 64, q, :], kt)
                mo = VEC_OFF["rw_mu"]
                S.op('dve', lambda e: e.tensor_scalar(out=MUS[:, 0, :], in0=PV[:, mo:mo + 27], scalar1=-1.0, scalar2=1.0, op0=ALU.mult,
                                                      op1=ALU.add), reads=['PV'], writes=['MUS'])
                S.op('dve', lambda e: e.tensor_scalar(out=MUS[:, 1, :], in0=PV[:, mo:mo + 27], scalar1=0.25, scalar2=None, op0=ALU.mult),
                     reads=['PV'], writes=['MUS'])
                S.op('dve', lambda e: e.tensor_scalar(out=MUS[:, 2, :], in0=PV[:, mo:mo + 27], scalar1=0.5, scalar2=None, op0=ALU.mult),
                     reads=['PV'], writes=['MUS'])
                kao = VEC_OFF["rw_k_a"]
                S.op('dve', lambda e: e.tensor_scalar(out=KA1[:], in0=PV[:, kao:kao + 8], scalar1=-1.0, scalar2=1.0, op0=ALU.mult,
                                                      op1=ALU.add), reads=['PV'], writes=['KA1'])
                for t_ in ABt[0] + ABt[1] + ACC[0] + ACC[1] + TTt[0] + TTt[1]:
                    S.op('pool', lambda e, t_=t_: e.memset(t_[:], 0.0), writes=['ABZ'])
                S.barrier()

                cut(1)
                PLx = PL[:, TC:TT].rearrange("p (r c) -> p r c", c=64)
                SHx = SH[:, TC:TT].rearrange("p (r c) -> p r c", c=64)

                def zlerp(pc, dst, dkey):
                    S.dma('sp', dst[:, 0:TC], Pv[:, pc, 0:TC], reads=[dkey], writes=[dkey])
                    for t0 in range(0, T, 1024):
                        S.dma('sp' if (t0 // 1024) % 2 == 0 else 'pool', dst[:, TC + t0:TC + t0 + 1024], Pv[:, pc, TC + t0:TC + t0 + 1024],
                              reads=[dkey], writes=[dkey])

                tiles_all = [(0, TC)] + [(TC + t0, m) for (t0, m) in seq_tiles(T)]

                zlerp(24, ZK, 'ZK')
                cut(2)
                S.op('act', lambda e: e.activation(out=TW[:, :], in_=ZK[:, :], func=AF.Tanh), reads=['ZK'], writes=['TW'])
                zlerp(25, ZA, 'ZA')
                zlerp(26, ZK, 'ZK')
                S.op('act', lambda e: e.activation(out=ZR[:, :], in_=ZK[:, :], func=AF.Sigmoid), reads=['ZK'], writes=['ZR'])
                gi_ = 0
                for c in range(8):
                    for (t0, m) in seq_tiles(T):
                        pp = ps[gi_ % 2]
                        go = GO[gi_ % 2]
                        S.mm(pp[:, :m], lhsT=GUP[:, c * 128:(c + 1) * 128],
                                                                              rhs=ZR[:, TC + t0:TC + t0 + m], start=True, stop=True,
                             reads=['ZR', 'LW_2'], writes=['ps%d' % (gi_ % 2)])
                        S.op('act', lambda e, pp=pp, go=go, m=m: e.copy(out=go[:, :m], in_=pp[:, :m]), reads=['ps%d' % (gi_ % 2)],
                             writes=['GO%d' % (gi_ % 2)])
                        S.dma('pool', GSv[:, c, t0:t0 + m], go[:, :m], reads=['GO%d' % (gi_ % 2)], writes=[('GS', c, t0)])
                        gi_ += 1

                cut(3)
                kko, rko = VEC_OFF["rw_k_k"], VEC_OFF["rw_r_k"]
                w0o, a0o = VEC_OFF["rw_w0"], VEC_OFF["rw_a0"]
                lwo, lbo = VEC_OFF["rw_ln_w"], VEC_OFF["rw_ln_b"]
                v3 = lambda ap, m: ap.rearrange("p (j s) -> p j s", s=64)

                for c in range(8):
                    zlerp(c, ZR, 'ZR')
                    zlerp(8 + c, ZK, 'ZK')
                    zlerp(16 + c, ZV, 'ZV')
                    for (g0, m) in tiles_all:
                        S.op('act', lambda e, g0=g0, m=m: e.activation(out=TR[0][:, :m], in_=ZK[:, g0:g0 + m], func=AF.Square,
                                                                       scale=PV[:, kko + c:kko + c + 1]), reads=['ZK'], writes=['TR0'])
                        S.mm(ps[0][:, :m], lhsT=BONES[:], rhs=TR[0][:, :m], start=True, stop=True,
                             reads=['TR0', 'BONES'], writes=['ps0'])
                        S.op('act', lambda e, m=m: e.activation(out=TR[1][:, :m], in_=ps[0][:, :m], func=AF.Sqrt), reads=['ps0'],
                             writes=['TR1'])
                        S.op('dve', lambda e, m=m: e.tensor_scalar(out=TR[1][:, :m], in0=TR[1][:, :m], scalar1=1e-12, scalar2=None,
                                                                   op0=ALU.max), reads=['TR1'], writes=['TR1'])
                        S.op('dve', lambda e, m=m: e.reciprocal(out=TR[1][:, :m], in_=TR[1][:, :m]), reads=['TR1'], writes=['TR1'])
                        S.op('dve', lambda e, g0=g0, m=m: e.scalar_tensor_tensor(
                            out=KK[:, g0:g0 + m], in0=ZK[:, g0:g0 + m], scalar=PV[:, kko + c:kko + c + 1], in1=TR[1][:, :m],
                            op0=ALU.mult, op1=ALU.mult), reads=['ZK', 'TR1'], writes=['KK'])

                    cut(4)
                    batches = [
                        [[0, 1, 2, 3]] + [list(range(4 + 8 * i, 12 + 8 * i)) for i in range(8)],
                        [[3, 2, 1, 0]] + [list(range(11 + 8 * i, 3 + 8 * i, -1)) for i in range(7, -1, -1)],
                    ]
                    NB = 9
                    for d in range(2):
                        S.op('pool', lambda e, d=d: e.memset(SS[d][:], 0.0), reads=[('SS', d, 0), ('SS', d, 1)],
                             writes=[('SS', d, 0), ('SS', d, 1)])
                        S.op('pool', lambda e, d=d: e.memset(SSb[d][:], 0.0), reads=[('SSb', d, 0), ('SSb', d, 1)],
                             writes=[('SSb', d, 0), ('SSb', d, 1)])
                    yb_written = set()

                    def prep_gen(d, n):
                        js = batches[d][n]
                        par = n % 2
                        jmin = min(js)
                        g0, m = jmin * 64, 64 * len(js)
                        nch = len(js)
                        dh = d * 64
                        col = d * 8 + c
                        LW, AS, CL, E1, E2, TB, TK = TRP[d]
                        kT = lambda i: ('TRP', d, i)
                        A_, K_ = ART[d][par], KBT[d][par]
                        kA, kK = ('ART', d, par), ('KBT', d, par)
                        pd_, kpd = ps[d], 'ps%d' % d
                        S.mm(pd_[:, :m], lhsT=WUP[dh:dh + 64, c * 128:(c + 1) * 128], rhs=TW[dh:dh + 64, g0:g0 + m], start=True, stop=True,
                             reads=['TW', 'LW_0'], writes=[kpd])
                        S.op('act', lambda e: e.activation(out=LW[:, :m], in_=pd_[:, :m], func=AF.Sigmoid,
                                                           bias=PV[:, w0o + col:w0o + col + 1], scale=1.0), reads=[kpd], writes=[kT(0)])
                        S.mm(pd_[:, :m], lhsT=AUP[dh:dh + 64, c * 128:(c + 1) * 128], rhs=ZA[dh:dh + 64, g0:g0 + m], start=True, stop=True,
                             reads=['ZA', 'LW_1'], writes=[kpd])
                        S.op('act', lambda e: e.activation(out=AS[:, :m], in_=pd_[:, :m], func=AF.Sigmoid,
                                                           bias=PV[:, a0o + col:a0o + col + 1], scale=1.0), reads=[kpd], writes=[kT(1)])
                        yield
                        if d == 0:
                            S.op('dve', lambda e: e.tensor_tensor_scan(out=CL[:, :m], data0=RMF[:, :m], data1=LW[:, :m], initial=0.0,
                                                                       op0=ALU.mult, op1=ALU.add), reads=[kT(0), 'RMF'], writes=[kT(2)])
                        else:
                            S.op('dve', lambda e: e.tensor_tensor_scan(out=CL[:, :m][:, ::-1], data0=RMB[:, :m][:, ::-1],
                                                                       data1=LW[:, :m][:, ::-1], initial=0.0, op0=ALU.mult, op1=ALU.add),
                                 reads=[kT(0), 'RMB'], writes=[kT(2)])
                        S.op('act', lambda e: e.activation(out=E1[:, :m], in_=CL[:, :m], func=AF.Exp, scale=-DS), reads=[kT(2)], writes=[kT(3)])
                        S.op('act', lambda e: e.activation(out=E2[:, :m], in_=CL[:, :m], func=AF.Exp, scale=DS), reads=[kT(2)], writes=[kT(4)])
                        S.op('pool', lambda e: e.tensor_tensor(out=LW[:, :m], in0=CL[:, :m], in1=LW[:, :m], op=ALU.subtract),
                             reads=[kT(2), kT(0)], writes=[kT(0)])
                        yield
                        S.op('act', lambda e: e.activation(out=CL[:, :m], in_=LW[:, :m], func=AF.Exp, scale=-DS), reads=[kT(0), kT(2)],
                             writes=[kT(2)])
                        S.op('dve', lambda e: e.tensor_tensor(out=A_[:, 0:nch, 1, :], in0=v3(ZR[:, g0:g0 + m], m), in1=v3(E1[:, :m], m),
                                                              op=ALU.mult), reads=['ZR', kT(3)], writes=[kA])
                        S.op('pool', lambda e: e.tensor_tensor(out=TB[:, :m], in0=KK[:, g0:g0 + m], in1=AS[:, :m], op=ALU.mult),
                             reads=['KK', kT(1)], writes=[kT(5)])
                        S.op('pool', lambda e: e.tensor_scalar(out=TK[:, :m], in0=AS[:, :m], scalar1=PV[:, kao + c:kao + c + 1],
                                                               scalar2=KA1[:, c:c + 1], op0=ALU.mult, op1=ALU.add),
                             reads=[kT(1), 'KA1'], writes=[kT(6)])
                        yield
                        S.op('dve', lambda e: e.scalar_tensor_tensor(out=A_[:, 0:nch, 0, :], in0=v3(KK[:, g0:g0 + m], m), scalar=-1.0,
                                                                     in1=v3(CL[:, :m], m), op0=ALU.mult, op1=ALU.mult),
                             reads=['KK', kT(2)], writes=[kA])
                        S.op('dve', lambda e: e.tensor_tensor(out=K_[:, 0:nch, 1, :], in0=v3(TB[:, :m], m), in1=v3(E2[:, :m], m), op=ALU.mult),
                             reads=[kT(5), kT(4)], writes=[kK])
                        S.op('pool', lambda e: e.tensor_tensor(out=TK[:, :m], in0=TK[:, :m], in1=ZK[:, g0:g0 + m], op=ALU.mult),
                             reads=[kT(6), 'ZK'], writes=[kT(6)])
                        yield
                        S.op('dve', lambda e: e.tensor_tensor(out=K_[:, 0:nch, 0, :], in0=v3(TK[:, :m], m), in1=v3(E2[:, :m], m), op=ALU.mult),
                             reads=[kT(6), kT(4)], writes=[kK])
                        ecol = 63 if d == 0 else 0
                        S.op('dve', lambda e: e.tensor_copy(out=WC[d][:, jmin:jmin + nch], in_=E1[:, ecol:m:64]), reads=[kT(3)],
                             writes=[('WC', d)])
                        yield

                    def stage1_gen(d, n):
                        js = batches[d][n]
                        par = n % 2
                        jmin = min(js)
                        A_, K_ = ART[d][par], KBT[d][par]
                        kA, kK = ('ART', d, par), ('KBT', d, par)
                        MT_, TT_ = MT[d][par], TTt[d][par]
                        AB_, AC_ = ABt[d], ACC[d]
                        for h0 in range(0, len(js), 4):
                            for q0 in range(0, 4, 2):
                                for hp in (0, 64):
                                    pm = ps[2 + hp // 64]
                                    for q in range(2):
                                        jl = js[h0 + q0 + q] - jmin
                                        for w_ in range(2):
                                            S.mm(pm[hp:hp + 64, q * 256 + w_ * 128:q * 256 + (w_ + 1) * 128],
                                                 lhsT=K_[hp:hp + 64, jl, w_, :], rhs=A_[hp:hp + 64, jl, :, :], start=True, stop=True,
                                                 reads=[kK, kA], writes=['ps%d' % (2 + hp // 64)])
                                lq = h0 + q0
                                for hp in (0, 64):
                                    pm, kpm = ps[2 + hp // 64], 'ps%d' % (2 + hp // 64)
                                    S.op('dve', lambda e, lq=lq, hp=hp, pm=pm: e.tensor_tensor(
                                        out=MT_[hp:hp + 64, lq:lq + 2, :], in0=pm[hp:hp + 64, 0:512].rearrange("p (q w) -> p q w", w=256),
                                        in1=MK[d][hp:hp + 64, :, :], op=ALU.mult), reads=[kpm], writes=[('MT', d, par, lq, hp)])
                                    S.op('dve', lambda e, q0=q0, hp=hp, pm=pm: e.tensor_tensor(
                                        out=AB_[0][hp:hp + 64, q0:q0 + 2, 0, hp:hp + 64],
                                        in0=pm[hp:hp + 64, 0:512].rearrange("p (q w) -> p q w", w=256)[:, :, 128:192],
                                        in1=AMK[d][hp:hp + 64, :, :], op=ALU.mult), reads=[kpm], writes=[('AB0', d, q0)])
                                yield
                            for hp in (0, 64):
                                pm = ps[2 + hp // 64]
                                for q in range(4):
                                    jl = js[h0 + q] - jmin
                                    S.mm(pm[hp:hp + 64, q * 128 + hp:q * 128 + hp + 64], lhsT=A_[hp:hp + 64, jl, 0, :],
                                         rhs=K_[hp:hp + 64, jl, 1, :], start=True, stop=True, reads=[kK, kA], writes=['ps%d' % (2 + hp // 64)])
                            for hp in (0, 64):
                                pm, kpm = ps[2 + hp // 64], 'ps%d' % (2 + hp // 64)
                                S.op('dve', lambda e, hp=hp, pm=pm: e.tensor_tensor(
                                    out=AB_[0][hp:hp + 64, 0:4, 1, hp:hp + 64],
                                    in0=pm[hp:hp + 64, 0:512].rearrange("p (q w) -> p q w", w=128)[:, :, hp:hp + 64],
                                    in1=BMK[d][hp:hp + 64, :, :], op=ALU.mult), reads=[kpm], writes=[('AB0', d, 0), ('AB0', d, 2)])
                            S.op('pool', lambda e: e.tensor_tensor(
                                out=AC_[0][:, 0:4, :], in0=AB_[0][:, 0:4, 0, :], in1=IDB[:].unsqueeze(1).to_broadcast([128, 4, 128]),
                                op=ALU.add), reads=[('AB0', d, 0), ('AB0', d, 2), 'IDB'], writes=[('ACC0', d)])
                            yield
                            for l in range(5):
                                cur, nxt = l % 2, 1 - (l % 2)
                                kc, kn = 'AB%d' % cur, 'AB%d' % nxt
                                for q0 in range(0, 4, 2):
                                    for q in range(2):
                                        jq = q0 + q
                                        if l < 4:
                                            S.mm(ps[2][:, q * 256:q * 256 + 128], lhsT=AB_[cur][:, jq, 1, :], rhs=AB_[cur][:, jq, 0, :],
                                                 start=True, stop=True, reads=[(kc, d, q0)], writes=['ps2'], inc=False)
                                        S.mm(ps[2][:, q * 256 + 128:q * 256 + 256], lhsT=AB_[cur][:, jq, 0, :], rhs=AB_[cur][:, jq, 1, :],
                                             start=True, stop=True, reads=[(kc, d, q0)], writes=['ps2'], inc=(q == 1))
                                    if l < 4:
                                        S.op('act', lambda e, q0=q0, nxt=nxt: e.copy(
                                            out=AB_[nxt][:, q0:q0 + 2, :, :].rearrange("p q a b -> p (q a b)"), in_=ps[2][:, 0:512]),
                                            reads=['ps2'], writes=[(kn, d, q0)])
                                    else:
                                        S.op('act', lambda e, q0=q0, nxt=nxt: e.copy(
                                            out=AB_[nxt][:, q0:q0 + 2, 1, :],
                                            in_=ps[2][:, 0:512].rearrange("p (q w) -> p q w", w=256)[:, :, 128:256]),
                                            reads=['ps2'], writes=[(kn, d, q0)])
                                    yield
                                for q in range(4):
                                    S.mm(ps[3][:, q * 128:(q + 1) * 128], lhsT=AB_[nxt][:, q, 1, :], rhs=AC_[cur][:, q, :], start=True, stop=True,
                                         reads=[(kn, d, (q // 2) * 2), ('ACC%d' % cur, d)], writes=['ps3'], inc=(q == 3))
                                if l == 4:
                                    S.op('dve', lambda e, cur=cur: e.tensor_tensor(
                                        out=TT_[:, h0:h0 + 4, :], in0=ps[3][:, 0:512].rearrange("p (q w) -> p q w", w=128),
                                        in1=AC_[cur][:, 0:4, :], op=ALU.add), reads=['ps3', ('ACC%d' % cur, d)], writes=[('TTt', d, par, h0)])
                                else:
                                    S.op('dve', lambda e, cur=cur, nxt=nxt: e.tensor_tensor(
                                        out=AC_[nxt][:, 0:4, :], in0=ps[3][:, 0:512].rearrange("p (q w) -> p q w", w=128),
                                        in1=AC_[cur][:, 0:4, :], op=ALU.add), reads=['ps3', ('ACC%d' % cur, d)], writes=[('ACC%d' % nxt, d)])
                                yield

                    def chain_gen(d, n):
                        js = batches[d][n]
                        par = n % 2
                        jmin = min(js)
                        A_, K_ = ART[d][par], KBT[d][par]
                        kA, kK = ('ART', d, par), ('KBT', d, par)
                        MT_, TT_ = MT[d][par], TTt[d][par]
                        SS_, SSb_ = SS[d], SSb[d]
                        for step, j in enumerate(js):
                            jl = j - jmin
                            tb = step % 2
                            isx = j >= 4
                            jx = j - 4
                            pt, xb, ub = PT[d][tb], XB[d][tb], UB_[d][tb]
                            ktt = ('TTt', d, par, (step // 4) * 4)
                            first_y = isx and (jx not in yb_written)
                            if isx:
                                yb_written.add(jx)
                            H = []
                            for h in range(2):
                                hp = 64 * h
                                H.append(dict(hp=hp, pb=ps[4 + 2 * d + h], kpb='ps%d' % (4 + 2 * d + h), kpt=('PT', d, tb, h), kxb=('XB', d, tb, h),
                                              kub=('UB', d, tb, h), kS=('SS', d, h), kSb=('SSb', d, h),
                                              kmt=('MT', d, par, (step // 2) * 2, hp), e0=('act', 'dve')[h], e1=('dve', 'act')[h]))

                            def cp(eng, out, in_, reads, writes):
                                if eng == 'act':
                                    S.op('act', lambda e: e.copy(out=out, in_=in_), reads=reads, writes=writes)
                                else:
                                    S.op('dve', lambda e: e.tensor_copy(out=out, in_=in_), reads=reads, writes=writes)
                            for x in H:
                                hp, pb = x['hp'], x['pb']
                                S.mm(pb[hp:hp + 64, 0:64], lhsT=ZV[hp:hp + 64, j * 64:(j + 1) * 64], rhs=IDB[hp:hp + 64, hp:hp + 64],
                                     start=True, stop=True, reads=['ZV', 'IDB'], writes=[x['kpb']])
                                S.mm(pb[hp:hp + 64, 64:128], lhsT=K_[hp:hp + 64, jl, 1, :], rhs=IDB[hp:hp + 64, hp:hp + 64],
                                     start=True, stop=True, reads=[kK, 'IDB'], writes=[x['kpb']])
                                S.mm(pb[hp:hp + 64, 128:192], lhsT=K_[hp:hp + 64, jl, 0, :], rhs=IDB[hp:hp + 64, hp:hp + 64],
                                     start=True, stop=True, reads=[kK, 'IDB'], writes=[x['kpb']])
                            for x in H:
                                hp, pb = x['hp'], x['pb']
                                cp(x['e0'], pt[hp:hp + 64, :, :].rearrange("p a b -> p (a b)"), pb[hp:hp + 64, 0:192], [x['kpb']], [x['kpt']])
                            yield
                            for x in H:
                                hp, pb = x['hp'], x['pb']
                                S.mm(pb[hp:hp + 64, 192:256], lhsT=A_[hp:hp + 64, jl, 0, :], rhs=SSb_[hp:hp + 64, :], start=True, stop=False,
                                     reads=[kA, x['kSb']], writes=[x['kpb']])
                                S.mm(pb[hp:hp + 64, 192:256], lhsT=MT_[hp:hp + 64, step, 0:64], rhs=pt[hp:hp + 64, 0, :], start=False, stop=True,
                                     reads=[x['kmt'], x['kpt']], writes=[x['kpb']])
                            for x in H:
                                hp, pb = x['hp'], x['pb']
                                cp(x['e0'], xb[hp:hp + 64, :], pb[hp:hp + 64, 192:256], [x['kpb']], [x['kxb']])
                            yield
                            for x in H:
                                hp, pb = x['hp'], x['pb']
                                S.mm(pb[hp:hp + 64, 256:320], lhsT=TT_[hp:hp + 64, step, hp:hp + 64], rhs=xb[hp:hp + 64, :], start=True, stop=True,
                                     reads=[ktt, x['kxb']], writes=[x['kpb']])
                            for x in H:
                                hp, pb = x['hp'], x['pb']
                                cp(x['e1'], ub[hp:hp + 64, :], pb[hp:hp + 64, 256:320], [x['kpb']], [x['kub']])
                            yield
                            if isx:
                                for x in H:
                                    hp, pb = x['hp'], x['pb']
                                    S.mm(pb[hp:hp + 64, 320:384], lhsT=A_[hp:hp + 64, jl, 1, :], rhs=SSb_[hp:hp + 64, :], start=True, stop=False,
                                         reads=[kA, x['kSb']], writes=[x['kpb']])
                                    S.mm(pb[hp:hp + 64, 320:384], lhsT=MT_[hp:hp + 64, step, 192:256], rhs=ub[hp:hp + 64, :], start=False, stop=False,
                                         reads=[x['kmt'], x['kub']], writes=[x['kpb']])
                                    S.mm(pb[hp:hp + 64, 320:384], lhsT=MT_[hp:hp + 64, step, 64:128], rhs=pt[hp:hp + 64, 0, :], start=False, stop=True,
                                         reads=[x['kmt'], x['kpt']], writes=[x['kpb']])
                                for x in H:
                                    hp, pb = x['hp'], x['pb']
                                    if first_y:
                                        cp(x['e1'], YB[hp:hp + 64, jx, :], pb[hp:hp + 64, 320:384], [x['kpb']], [('YB', jx, hp)])
                                    else:
                                        S.op('dve', lambda e, hp=hp, pb=pb: e.tensor_tensor(out=YB[hp:hp + 64, jx, :], in0=pb[hp:hp + 64, 320:384],
                                                                                          in1=YB[hp:hp + 64, jx, :], op=ALU.add),
                                             reads=[x['kpb'], ('YB', jx, hp)], writes=[('YB', jx, hp)])
                            for x in H:
                                hp, pb = x['hp'], x['pb']
                                S.mm(pb[hp:hp + 64, 384:448], lhsT=pt[hp:hp + 64, 1, :], rhs=ub[hp:hp + 64, :], start=True, stop=False,
                                     reads=[x['kpt'], x['kub']], writes=[x['kpb']])
                                S.mm(pb[hp:hp + 64, 384:448], lhsT=pt[hp:hp + 64, 2, :], rhs=pt[hp:hp + 64, 0, :], start=False, stop=True,
                                     reads=[x['kpt']], writes=[x['kpb']])
                            for x in H:
                                hp, pb = x['hp'], x['pb']
                                S.op('dve', lambda e, hp=hp: e.tensor_scalar(out=SS_[hp:hp + 64, :], in0=SS_[hp:hp + 64, :],
                                                                             scalar1=WC[d][hp:hp + 64, j:j + 1], scalar2=None, op0=ALU.mult),
                                     reads=[x['kS'], ('WC', d)], writes=[x['kS']])
                                S.op('dve', lambda e, hp=hp, pb=pb: e.scalar_tensor_tensor(
                                    out=SS_[hp:hp + 64, :], in0=pb[hp:hp + 64, 384:448], scalar=WC[d][hp:hp + 64, j:j + 1],
                                    in1=SS_[hp:hp + 64, :], op0=ALU.mult, op1=ALU.add), reads=[x['kpb'], x['kS'], ('WC', d)], writes=[x['kS']])
                                S.op('act', lambda e, hp=hp: e.copy(out=SSb_[hp:hp + 64, :], in_=SS_[hp:hp + 64, :]), reads=[x['kS']],
                                     writes=[x['kSb']])
                            if debug and j == (3 if d == 0 else 0):
                                S.op('dve', lambda e: e.tensor_copy(out=ST8[:, d, c, :], in_=SS_[:]), reads=[('SS', d, 0), ('SS', d, 1)],
                                     writes=['ST8'])
                            yield

                    def run_threads(ths):
                        ths = list(ths)
                        while ths:
                            for g in list(ths):
                                try:
                                    next(g)
                                except StopIteration:
                                    ths.remove(g)

                    import itertools as _it
                    run_threads([_it.chain(prep_gen(d, 0), stage1_gen(d, 0)) for d in range(2)])
                    for n in range(NB):
                        ths = []
                        for d in range(2):
                            ths.append(chain_gen(d, n))
                            if n + 1 < NB:
                                ths.append(_it.chain(prep_gen(d, n + 1), stage1_gen(d, n + 1)))
                        run_threads(ths)
                    cut(8)
                    YSQ = SH[:, 0:4096].rearrange("p (j v) -> p j v", v=64)
                    YNb = PL[:, 0:4096].rearrange("p (j v) -> p j v", v=64)
                    SUM, SSQ, MEAN, RSTD = TRS[:, 0:64], TRS[:, 64:128], TRS[:, 128:192], TRS[:, 192:256]
                    S.op('dve', lambda e: e.tensor_reduce(out=SUM, in_=YB[:], axis=AX.X, op=ALU.add),
                         reads=[('YB', jx, hp_) for jx in range(64) for hp_ in (0, 64)], writes=['TR8'])
                    S.op('act', lambda e: e.activation(out=YSQ, in_=YB[:], func=AF.Square), reads=[('YB', jx, hp_) for jx in range(64) for hp_ in (0, 64)],
                         writes=['SH'])
                    S.op('dve', lambda e: e.tensor_reduce(out=SSQ, in_=YSQ, axis=AX.X, op=ALU.add), reads=['SH'], writes=['TR8'])
                    S.op('dve', lambda e: e.tensor_scalar(out=MEAN, in0=SUM, scalar1=1.0 / 64, scalar2=None, op0=ALU.mult),
                         reads=['TR8'], writes=['TR8'])
                    S.op('dve', lambda e: e.tensor_tensor(out=SUM, in0=MEAN, in1=MEAN, op=ALU.mult), reads=['TR8'], writes=['TR8'])
                    S.op('dve', lambda e: e.scalar_tensor_tensor(out=SSQ, in0=SSQ, scalar=1.0 / 64, in1=SUM, op0=ALU.mult,
                                                                 op1=ALU.subtract), reads=['TR8'], writes=['TR8'])
                    S.op('dve', lambda e: e.tensor_scalar(out=SSQ, in0=SSQ, scalar1=LN_X_EPS, scalar2=None, op0=ALU.add),
                         reads=['TR8'], writes=['TR8'])
                    S.op('act', lambda e: e.activation(out=RSTD, in_=SSQ, func=AF.Sqrt), reads=['TR8'], writes=['TR8'])
                    S.op('dve', lambda e: e.reciprocal(out=RSTD, in_=RSTD), reads=['TR8'], writes=['TR8'])
                    S.op('dve', lambda e: e.tensor_tensor(out=YB[:], in0=YB[:], in1=MEAN.unsqueeze(2).to_broadcast([128, 64, 64]),
                                                          op=ALU.subtract), reads=['TR8'] + [('YB', jx, hp_) for jx in range(64) for hp_ in (0, 64)],
                         writes=[('YB', jx, hp_) for jx in range(64) for hp_ in (0, 64)])
                    S.op('dve', lambda e: e.tensor_tensor(out=YNb, in0=YB[:], in1=RSTD.unsqueeze(2).to_broadcast([128, 64, 64]),
                                                          op=ALU.mult), reads=['TR8'] + [('YB', jx, hp_) for jx in range(64) for hp_ in (0, 64)],
                         writes=['PL'])
                    for ti, (t0, m) in enumerate(seq_tiles(T)):
                        g0 = TC + t0
                        for hp in (0, 64):
                            for q in range(8):
                                jx = ti * 8 + q
                                S.mm(
                                    ps[0][hp:hp + 64, q * 64:(q + 1) * 64], lhsT=YNb[hp:hp + 64, jx, :], rhs=IDB[hp:hp + 64, hp:hp + 64],
                                    start=True, stop=True, reads=['PL', 'IDB'], writes=['ps0'], inc=(q == 7 and hp == 64))
                        S.op('dve', lambda e, g0=g0, m=m: e.scalar_tensor_tensor(
                            out=TR[0][:, :m], in0=ZR[:, g0:g0 + m], scalar=PV[:, rko + c:rko + c + 1], in1=ZK[:, g0:g0 + m],
                            op0=ALU.mult, op1=ALU.mult), reads=['ZR', 'ZK'], writes=['TR0'])
                        S.mm(ps[1][:, :m], lhsT=BONES[:], rhs=TR[0][:, :m], start=True, stop=True,
                             reads=['TR0', 'BONES'], writes=['ps1'])
                        S.op('dve', lambda e, g0=g0, m=m: e.tensor_tensor(out=TR[1][:, :m], in0=ps[1][:, :m], in1=ZV[:, g0:g0 + m],
                                                                          op=ALU.mult), reads=['ps1', 'ZV'], writes=['TR1'])
                        S.dma('sp', GO[0][:, :m], GSv[:, c, t0:t0 + m], reads=[('GS', c, t0)], writes=['GO0'])
                        S.op('dve', lambda e, m=m: e.tensor_scalar(out=TR[2][:, :m], in0=ps[0][:, :m], scalar1=PV[:, lwo + c:lwo + c + 1],
                                                                   scalar2=PV[:, lbo + c:lbo + c + 1], op0=ALU.mult, op1=ALU.add),
                             reads=['ps0'], writes=['TR2'])
                        S.op('pool', lambda e, m=m: e.tensor_tensor(out=TR[2][:, :m], in0=TR[2][:, :m], in1=TR[1][:, :m], op=ALU.add),
                             reads=['TR2', 'TR1'], writes=['TR2'])
                        S.op('pool', lambda e, m=m: e.tensor_tensor(out=GO[1][:, :m], in0=TR[2][:, :m], in1=GO[0][:, :m], op=ALU.mult),
                             reads=['TR2', 'GO0'], writes=['GO1'])
                        S.dma('pool', YRWv[:, c, t0:t0 + m], GO[1][:, :m], reads=['GO1'], writes=[('YRW', c, t0)])
                if debug:
                    S.dma('sp', SFD[:, :, :, :], ST8[:], reads=['ST8'], writes=['SFD'])
                S.barrier()

        with ExitStack() as ph_r:
            try:
                _rwkv_body(ph_r)
            except _Cut:
                print("CUT at", RCUT, "ops", S.nops)
            S.barrier()
        if RCUT:
            return nc

        if phases is not None and "C" not in phases:
            return nc

        def phase_merge():
            with ExitStack() as ph:
                WP = [sb("WP%d" % i, [128, 8, D], BF16, st=ph) for i in range(3)]
                STG = sb("STGC", [128, 2, D], st=ph)
                YR = sb("YRt", [128, 8, 512], BF16, st=ph)
                YLt = sb("YLt", [128, 8, 512], BF16, st=ph)
                SM = sb("SMt", [128, 16, 512], BF16, st=ph)
                X1t = sb("X1t", [128, 8, 512], st=ph)
                MG = sb("MGt", [128, 8, 512], BF16, st=ph)
                TA = [sb("TAc%d" % i, [128, 512], st=ph) for i in range(2)]
                ci = 0
                for wi, wsrc in enumerate((wpr_d, wpl_d, wo_d)):
                    for k in range(8):
                        i = ci % 2
                        ci += 1
                        S.dma('sp' if i == 0 else 'pool', STG[:, i, :], wsrc[k * 128:(k + 1) * 128, :], writes=['stg%d' % i])
                        S.op('dve' if i == 0 else 'act',
                             (lambda e, wi=wi, k=k, i=i: e.tensor_copy(out=WP[wi][:, k, :], in_=STG[:, i, :])) if i == 0 else
                             (lambda e, wi=wi, k=k, i=i: e.copy(out=WP[wi][:, k, :], in_=STG[:, i, :])), reads=['stg%d' % i])
                S.barrier()
                for (t0, n) in seq_tiles(T):
                    g0 = TC + t0
                    for c in range(8):
                        S.dma('sp', YR[:, c, :n], YRWv[:, c, t0:t0 + n], writes=[('YR', c)])
                        S.dma('pool', YLt[:, c, :n], YLv[:, c, t0:t0 + n], writes=[('YLt', c)])
                        S.dma('sp', X1t[:, c, :n], X1v[:, c, g0:g0 + n], writes=[('X1t', c)])
                    for c in range(16):
                        S.dma('pool' if c % 2 else 'sp', SM[:, c, :n], Pv[:, 43 + c, g0:g0 + n], writes=[('SM', c)])
                    for o in range(8):
                        pa, pb = ps[(o % 2) * 2], ps[(o % 2) * 2 + 1]
                        ka, kb = 'ps%d' % ((o % 2) * 2), 'ps%d' % ((o % 2) * 2 + 1)
                        for k in range(8):
                            S.mm(pa[:, :n], lhsT=WP[0][:, k, o * 128:(o + 1) * 128], rhs=YR[:, k, :n], start=(k == 0), stop=(k == 7),
                                 reads=[('YR', k)], writes=[ka], inc=(k == 7))
                        for k in range(8):
                            S.mm(pb[:, :n], lhsT=WP[1][:, k, o * 128:(o + 1) * 128], rhs=YLt[:, k, :n], start=(k == 0), stop=(k == 7),
                                 reads=[('YLt', k)], writes=[kb], inc=(k == 7))
                        S.op('dve', lambda e, pa=pa, o=o: e.tensor_tensor(out=TA[0][:, :n], in0=pa[:, :n], in1=SM[:, o, :n], op=ALU.mult),
                             reads=[ka, ('SM', o)], writes=['TA0'])
                        S.op('dve', lambda e, pb=pb, o=o: e.tensor_tensor(out=TA[1][:, :n], in0=pb[:, :n], in1=SM[:, 8 + o, :n], op=ALU.mult),
                             reads=[kb, ('SM', 8 + o)], writes=['TA1'])
                        S.op('pool', lambda e, o=o: e.tensor_tensor(out=MG[:, o, :n], in0=TA[0][:, :n], in1=TA[1][:, :n], op=ALU.add),
                             reads=['TA0', 'TA1'], writes=[('MG', o)])
                    for o in range(8):
                        pc_ = ps[4 + o % 2]
                        kc_ = 'ps%d' % (4 + o % 2)
                        for k in range(8):
                            S.mm(pc_[:, :n], lhsT=WP[2][:, k, o * 128:(o + 1) * 128], rhs=MG[:, k, :n], start=(k == 0), stop=(k == 7),
                                 reads=[('MG', k)], writes=[kc_], inc=(k == 7))
                        S.op('dve', lambda e, pc_=pc_, o=o: e.scalar_tensor_tensor(
                            out=X1t[:, o, :n], in0=pc_[:, :n], scalar=SCAL[:, 0, 5, o:o + 1], in1=X1t[:, o, :n], op0=ALU.mult,
                            op1=ALU.add), reads=[kc_, ('X1t', o)], writes=[('X1t', o)])
                        S.dma('pool', X2v[:, o, t0:t0 + n], X1t[:, o, :n], reads=[('X1t', o)], writes=[('X2', o, t0)])
                S.barrier()

        phase_merge()

        def load_feat_major(S, SCR, XT, s_, t0, n):
            for c in range(8):
                S.dma('sp' if c % 2 == 0 else 'pool', XT[:, c, :n], X2v[:, c, t0:t0 + n], writes=[('XT', c)])

        def epi_final(S, XT, HN, TM, RS, rmsnorm_mod, s_, t0, n, SCR=None):
            rmsnorm_mod(n, s_, 0, XT, ia=9, ib=10, dk='XT')
            for b_ in range(n // 128):
                for c in range(8):
                    pp = ps[6 + (c // 4) % 2]
                    S.mm(pp[:, (c % 4) * 128:(c % 4 + 1) * 128], lhsT=XT[:, c, b_ * 128:(b_ + 1) * 128], rhs=ident[:],
                         start=True, stop=True, reads=[('XT', c), 'ident'], writes=['ps%d' % (6 + (c // 4) % 2)], inc=(c % 4 == 3))
                    if c % 4 == 3:
                        h_ = c // 4
                        S.op('act' if h_ == 0 else 'dve',
                             (lambda e, pp=pp, b_=b_, h_=h_: e.copy(out=SCR[:, b_ * 1024 + h_ * 512:b_ * 1024 + (h_ + 1) * 512], in_=pp[:, 0:512]))
                             if h_ == 0 else
                             (lambda e, pp=pp, b_=b_, h_=h_: e.tensor_copy(out=SCR[:, b_ * 1024 + h_ * 512:b_ * 1024 + (h_ + 1) * 512], in_=pp[:, 0:512])),
                             reads=['ps%d' % (6 + h_)], writes=[('XIN', b_)])
                S.dma('pool', out_d[t0 + b_ * 128:t0 + (b_ + 1) * 128, :], SCR[:, b_ * 1024:(b_ + 1) * 1024], reads=[('XIN', b_)],
                      writes=[('OUT', t0, b_)])

        tiles2 = [(0, i * 512, 512) for i in range(T // 512)]
        ffn_phase("ffn2", tiles2, load_feat_major, epi_final)

        S.barrier()
        print("ops emitted", S.nops, S.cnt, S.dn)
    return nc


_CACHE = {}


def _prep_inputs(inputs, b):
    f = lambda a: np.ascontiguousarray(a, dtype=np.float32)
    m = {"x": f(inputs["x"][b]), "ctx": f(inputs["ctx"][b])}
    src = dict(inputs)
    src["c"] = inputs["c"][b]
    for n, r in VEC_ROWS:
        m[n] = f(np.asarray(src[n]).reshape(r, 128))
    m["w_mod"] = f(inputs["w_mod"][0])
    m["w_in"] = f(inputs["w_in"][0])
    m["lru_wa"] = f(inputs["lru_wa"][0])
    m["lru_wx"] = f(inputs["lru_wx"][0])
    m["rw_w_up"] = f(inputs["rw_w_up"][0].reshape(128, D))
    m["w_proj_rw"] = f(inputs["w_proj_rw"][0])
    m["w_proj_lru"] = f(inputs["w_proj_lru"][0])
    m["w_out"] = f(inputs["w_out"][0])
    m["rw_a_up"] = f(inputs["rw_a_up"][0].reshape(128, D))
    m["rw_g_up"] = f(inputs["rw_g_up"][0])
    for fn_ in ("ffn1", "ffn2"):
        for s in ("_wg", "_wu", "_wd"):
            m[fn_ + s] = f(inputs[fn_ + s][0])
    return m


def kernel(**inputs):
    if "nc" not in _CACHE:
        _CACHE["nc"] = build()
    nc = _CACHE["nc"]
    in_maps = [_prep_inputs(inputs, b % 4) for b in range(N_CORES)]
    res = run_bass_kernel_spmd(nc, in_maps, core_ids=list(range(N_CORES)))
    out = np.stack([res.results[b]["out"] for b in range(4)], axis=0)
    return out.astype(np.float32)
```

```python
import numpy as np
from contextlib import ExitStack
import concourse.bass as bass
import concourse.mybir as mybir
from concourse.bass_utils import run_bass_kernel_spmd

F32 = mybir.dt.float32
BF16 = mybir.dt.bfloat16
AF = mybir.ActivationFunctionType
ALU = mybir.AluOpType
AX = mybir.AxisListType

D = 1024
T = 4096
TC = 256
TT = T + TC
DFF = 2816
NJ = DFF // 128
NIN = 7552
NRW = 3456
EPS = 1e-6
N_CORES = 8


class Sched:
    R = 8

    def __init__(self, nc, es):
        self.nc = nc
        self.eng = {'pe': nc.tensor, 'act': nc.scalar, 'dve': nc.vector, 'pool': nc.gpsimd, 'sp': nc.sync}
        self.sem = {k: es.enter_context(nc.semaphore('s_' + k)) for k in ('pe', 'act', 'dve', 'pool')}
        self.cnt = {k: 0 for k in self.sem}
        self.dsem = {q: [es.enter_context(nc.semaphore('d_%s%d' % (q, i))) for i in range(self.R)]
                     for q in ('sp', 'pool')}
        self.dn = {q: 0 for q in self.dsem}
        self.waited = {e: {} for e in self.eng}
        self.lastw = {}
        self.readers = {}
        self.nops = 0
        self.pe_last = {}

    def _semof(self, k):
        return self.sem[k] if isinstance(k, str) else self.dsem[k[0]][k[1]]

    def _wait(self, e, k, val):
        if val <= 0 or self.waited[e].get(k, 0) >= val:
            return
        self.waited[e][k] = val
        self.eng[e].wait_ge(self._semof(k), val)

    def _deps(self, e, reads, writes):
        need = {}
        raw = {}
        for r in reads:
            t = self.lastw.get(r)
            if t is not None:
                if need.get(t[0], 0) < t[1]:
                    need[t[0]] = t[1]
                if raw.get(t[0], 0) < t[1]:
                    raw[t[0]] = t[1]
            if isinstance(r, str) and r.startswith('ps'):
                for k, v in self.readers.get(r, {}).items():
                    if k != e and need.get(k, 0) < v:
                        need[k] = v
        for w in writes:
            t = self.lastw.get(w)
            if t is not None and need.get(t[0], 0) < t[1]:
                need[t[0]] = t[1]
            for k, v in self.readers.get(w, {}).items():
                if need.get(k, 0) < v:
                    need[k] = v
        for k, v in need.items():
            if k == e:
                if e == 'pe':
                    continue
                if e in ('act', 'dve'):
                    v = raw.get(k, 0)
                    if v <= 0:
                        continue
            self._wait(e, k, v)

    def _commit(self, tok, reads, writes):
        for w in writes:
            self.lastw[w] = tok
            self.readers[w] = {}
        ws = set(writes)
        for r in reads:
            if r in ws:
                continue
            d = self.readers.setdefault(r, {})
            if d.get(tok[0], 0) < tok[1]:
                d[tok[0]] = tok[1]

    def mm(self, out, lhsT, rhs, start=True, stop=True, reads=(), writes=(), inc=True):
        rg = lhsT.base_partition() if lhsT.partition_size() < 128 else None
        self.op('pe', lambda e: e.matmul(out, lhsT=lhsT, rhs=rhs, start=start, stop=stop), reads, writes, inc, rg=rg)

    def op(self, e, fn, reads=(), writes=(), inc=True, rg=None):
        if e == 'pe':
            if rg is not None:
                inc = True
                for w in writes:
                    pl = self.pe_last.get(w)
                    if pl is not None and pl[0] != rg:
                        self._wait('pe', 'pe', pl[1])
            else:
                for w in writes:
                    self.pe_last.pop(w, None)
        self._deps(e, reads, writes)
        ins = fn(self.eng[e])
        self.nops += 1
        if inc:
            self.cnt[e] += 1
            ins.then_inc(self.sem[e], 1)
            tok = (e, self.cnt[e])
        else:
            assert e == 'pe'
            tok = (e, self.cnt[e] + 1)
        if e == 'pe' and rg is not None:
            for w in writes:
                self.pe_last[w] = (rg, tok[1])
        self._commit(tok, reads, writes)

    def dma(self, q, out, in_, reads=(), writes=(), **kw):
        n = self.dn[q]
        i = n % self.R
        tgt = 16 * (n // self.R + 1)
        self._wait(q, (q, i), tgt - 16)
        self._deps(q, reads, writes)
        self.eng[q].dma_start(out=out, in_=in_, **kw).then_inc(self.dsem[q][i], 16)
        self.nops += 1
        self.dn[q] += 1
        self._commit(((q, i), tgt), reads, writes)

    def barrier(self):
        for e in self.eng:
            for k in self.sem:
                self._wait(e, k, self.cnt[k])
            for q in self.dsem:
                for i in range(self.R):
                    n_i = (self.dn[q] - i + self.R - 1) // self.R
                    self._wait(e, (q, i), 16 * n_i)
        self.lastw = {}
        self.readers = {}


VEC_ROWS = [("b_mod", 72), ("g_ffn1", 8), ("g_mix", 8), ("g_ffn2", 8), ("g_final", 8), ("rw_mu", 27),
            ("rw_w0", 16), ("rw_a0", 16), ("rw_k_k", 8), ("rw_k_a", 8), ("rw_r_k", 8), ("rw_ln_w", 8),
            ("rw_ln_b", 8), ("lru_conv_w", 32), ("lru_conv_b", 8), ("lru_lam", 16), ("lru_ba", 16),
            ("lru_bx", 16), ("c", 8), ("c_ctx", 8)]
VEC_OFF = {}
_o = 0
for _n, _r in VEC_ROWS:
    VEC_OFF[_n] = _o
    _o += _r
NVEC = _o
NVT = (NVEC + 127) // 128


def build(debug=False, phases=None):
    nc = bass.Bass("TRN2", target_bir_lowering=False)
    dram_in = {}

    def din(name, shape, dt=F32):
        dram_in[name] = nc.dram_tensor(name, list(shape), dt, kind="ExternalInput").ap()
        return dram_in[name]

    x_d = din("x", [T, D])
    ctx_d = din("ctx", [TC, D])
    for n, r in VEC_ROWS:
        din(n, [r, 128])
    w_mod = din("w_mod", [D, 9 * D])
    wts = {}
    for f in ("ffn1", "ffn2"):
        wts[f + "_wg"] = din(f + "_wg", [D, DFF])
        wts[f + "_wu"] = din(f + "_wu", [D, DFF])
        wts[f + "_wd"] = din(f + "_wd", [DFF, D])
    w_in_d = din("w_in", [D, NIN])
    lru_wa_d = din("lru_wa", [2, 4, 256, 256])
    lru_wx_d = din("lru_wx", [2, 4, 256, 256])
    wpr_d = din("w_proj_rw", [D, D])
    wpl_d = din("w_proj_lru", [D, D])
    wo_d = din("w_out", [D, D])
    rw_w_up_d = din("rw_w_up", [128, D])
    rw_a_up_d = din("rw_a_up", [128, D])
    rw_g_up_d = din("rw_g_up", [128, D])
    out_d = nc.dram_tensor("out", [T, D], F32, kind="ExternalOutput").ap()
    skind = "ExternalOutput" if debug else "Internal"
    X1 = nc.dram_tensor("X1", [D, TT], F32, kind=skind).ap()
    XN = nc.dram_tensor("XN", [D, TT], BF16, kind=skind).ap()
    MODD = nc.dram_tensor("MODD", [128, 144], F32, kind=skind).ap()
    P = nc.dram_tensor("P", [NIN, TT], BF16, kind=skind).ap()
    YL = nc.dram_tensor("YL", [D, T], BF16, kind=skind).ap()
    HSTD = nc.dram_tensor("HSTD", [128, 16], F32, kind=skind).ap()
    GS = nc.dram_tensor("GS", [D, T], BF16, kind=skind).ap()
    YRW = nc.dram_tensor("YRW", [D, T], BF16, kind=skind).ap()
    SFD = nc.dram_tensor("SFD", [128, 2, 8, 64], F32, kind=skind).ap()
    X2 = nc.dram_tensor("X2", [D, T], F32, kind=skind).ap()
    X2v = X2.rearrange("(c p) t -> p c t", p=128)
    GSv = GS.rearrange("(c p) t -> p c t", p=128)
    YRWv = YRW.rearrange("(c p) t -> p c t", p=128)
    Pv = P.rearrange("(c p) t -> p c t", p=128)
    YLv = YL.rearrange("(c p) t -> p c t", p=128)

    with ExitStack() as es:
        S = Sched(nc, es)
        ps = [es.enter_context(nc.psum_tensor("ps%d" % i, [128, 512], F32)) for i in range(8)]
        sb = lambda name, shape, dt=F32, st=es: st.enter_context(nc.sbuf_tensor(name, list(shape), dt))
        ident = sb("ident", [128, 128])
        ones = sb("ones", [128, 128])
        PV = sb("PV", [128, NVT * 128])
        MOD = sb("MOD", [128, 72, 2])
        SCAL = sb("SCAL", [128, 2, 11, 8])

        def pv(name, i=0):
            o = VEC_OFF[name] + i
            return PV[:, o:o + 1]

        S.op('pool', lambda e: e.memset(ident[:], 1.0), writes=['ident'])
        S.op('pool', lambda e: e.affine_select(out=ident[:], in_=ident[:], pattern=[[-1, 128]],
                                               compare_op=ALU.is_equal, fill=0.0, base=0, channel_multiplier=1),
             reads=['ident'], writes=['ident'])
        S.op('pool', lambda e: e.memset(ones[:], 1.0), writes=['ones'])

        with ExitStack() as p0:
            VST = sb("VST", [128, NVT, 128], st=p0)
            SC = sb("SC", [128, 8, 2], st=p0)
            slab = [sb("slab%d" % i, [128, 8, 1152], st=p0) for i in range(2)]
            S.op('dve', lambda e: e.memset(VST[:], 0.0), writes=['VST'])
            for n, r in VEC_ROWS:
                o = VEC_OFF[n]
                done = 0
                while done < r:
                    ti, ri = divmod(o + done, 128)
                    m = min(r - done, 128 - ri)
                    S.dma('sp', VST[ri:ri + m, ti, :], dram_in[n][done:done + m, :], reads=[], writes=['VST'])
                    done += m
            for ti in range(NVT):
                S.mm(ps[7][:, 0:128], lhsT=VST[:, ti, :], rhs=ident[:],
                                                     start=True, stop=True,
                     reads=['VST', 'ident'], writes=['ps7'])
                S.op('dve', lambda e, ti=ti: e.tensor_copy(out=PV[:, ti * 128:(ti + 1) * 128], in_=ps[7][:, 0:128]),
                     reads=['ps7'], writes=['PV'])
            oc_, occ = VEC_OFF["c"], VEC_OFF["c_ctx"]
            S.op('act', lambda e: e.activation(out=SC[:, :, 0], in_=PV[:, oc_:oc_ + 8], func=AF.Silu),
                 reads=['PV'], writes=['SC'])
            S.op('act', lambda e: e.activation(out=SC[:, :, 1], in_=PV[:, occ:occ + 8], func=AF.Silu),
                 reads=['PV'], writes=['SC'])
            wmv = w_mod.rearrange("(k p) n -> p k n", p=128)
            for si in range(8):
                sl = slab[si % 2]
                key = 'slab%d' % (si % 2)
                for k in range(8):
                    S.dma('sp' if k % 2 == 0 else 'pool', sl[:, k, :], wmv[:, k, si * 1152:(si + 1) * 1152],
                          writes=[key])
                for ol in range(9):
                    oc = si * 9 + ol
                    for k in range(8):
                        S.mm(
                            ps[6][:, oc * 2:oc * 2 + 2], lhsT=sl[:, k, ol * 128:(ol + 1) * 128], rhs=SC[:, k, :],
                            start=(k == 0), stop=(k == 7),
                            reads=[key, 'SC'], writes=['ps6'], inc=(k == 7))
            bo = VEC_OFF["b_mod"]
            psv = ps[6][:, 0:144].rearrange("p (a b) -> p a b", b=2)
            for s_ in range(2):
                S.op('dve', lambda e, s_=s_: e.tensor_tensor(out=MOD[:, :, s_], in0=psv[:, :, s_],
                                                             in1=PV[:, bo:bo + 72], op=ALU.add),
                     reads=['ps6', 'PV'], writes=['MOD'])
            for s_ in range(2):
                for gi, gname in enumerate(("g_ffn1", "g_mix", "g_ffn2")):
                    go = VEC_OFF[gname]
                    sh = MOD[:, (3 * gi) * 8:(3 * gi + 1) * 8, s_]
                    scl = MOD[:, (3 * gi + 1) * 8:(3 * gi + 2) * 8, s_]
                    gt = MOD[:, (3 * gi + 2) * 8:(3 * gi + 3) * 8, s_]
                    S.op('dve', lambda e, s_=s_, gi=gi, scl=scl, go=go: e.scalar_tensor_tensor(
                        out=SCAL[:, s_, 3 * gi, :], in0=scl, scalar=1.0, in1=PV[:, go:go + 8],
                        op0=ALU.add, op1=ALU.mult), reads=['MOD', 'PV'], writes=['SCAL'])
                    S.op('dve', lambda e, s_=s_, gi=gi, sh=sh: e.tensor_copy(out=SCAL[:, s_, 3 * gi + 1, :], in_=sh),
                         reads=['MOD'], writes=['SCAL'])
                    S.op('dve', lambda e, s_=s_, gi=gi, gt=gt: e.tensor_scalar(
                        out=SCAL[:, s_, 3 * gi + 2, :], in0=gt, scalar1=(1.0 if gi == 1 else 0.5), scalar2=None,
                        op0=ALU.mult), reads=['MOD'], writes=['SCAL'])
            gfo = VEC_OFF["g_final"]
            for s_ in range(2):
                S.op('dve', lambda e, s_=s_: e.tensor_copy(out=SCAL[:, s_, 9, :], in_=PV[:, gfo:gfo + 8]), reads=['PV'], writes=['SCAL'])
                S.op('dve', lambda e, s_=s_: e.memset(SCAL[:, s_, 10, :], 0.0), reads=['SCAL'], writes=['SCAL'])
            if debug:
                S.dma('sp', MODD[:, :], MOD[:].rearrange("p a b -> p (a b)"), reads=['MOD'], writes=['MODD'])
            S.barrier()

        def ffn_phase(which, tiles, load_tile, epilogue):
            with ExitStack() as ph:
                WG = sb("WG_" + which, [128, 8, DFF], BF16, st=ph)
                WU = sb("WU_" + which, [128, 8, DFF], BF16, st=ph)
                WD = sb("WD_" + which, [128, NJ, D], BF16, st=ph)
                SCR = sb("SCR_" + which, [128, 4096], st=ph)
                XT = sb("XT_" + which, [128, 8, 512], st=ph)
                HN = sb("HN_" + which, [128, 8, 512], BF16, st=ph)
                AA = sb("AA_" + which, [128, NJ, 512], BF16, st=ph)
                TM = [sb("TM_" + which + "%d" % i, [128, 512], st=ph) for i in range(2)]
                RS = sb("RS_" + which, [128, 512], st=ph)
                ci = [0]
                cast_eng = ('dve', 'pool', 'act')

                def load_cast(dst, src, w):
                    i = ci[0] % 2
                    ci[0] += 1
                    key = 'stg%d' % i
                    S.dma('sp' if i == 0 else 'pool', SCR[:, i * 2048:i * 2048 + w], src, writes=[key])
                    ce = cast_eng[ci[0] % 3]
                    if ce == 'act':
                        S.op('act', lambda e: e.copy(out=dst, in_=SCR[:, i * 2048:i * 2048 + w]), reads=[key])
                    else:
                        S.op(ce, lambda e: e.tensor_copy(out=dst, in_=SCR[:, i * 2048:i * 2048 + w]), reads=[key])
                for (Wt, wd_) in ((WG, wts[which + "_wg"]), (WU, wts[which + "_wu"])):
                    for k in range(8):
                        for h in range(2):
                            load_cast(Wt[:, k, h * 1408:(h + 1) * 1408], wd_[k * 128:(k + 1) * 128, h * 1408:(h + 1) * 1408],
                                      1408)
                wdd = wts[which + "_wd"]
                for j in range(NJ):
                    load_cast(WD[:, j, :], wdd[j * 128:(j + 1) * 128, :], 1024)
                S.barrier()

                def rmsnorm_mod(n, s_, gi, dst, ia=None, ib=None, dk='HN'):
                    ia = 3 * gi if ia is None else ia
                    ib = 3 * gi + 1 if ib is None else ib
                    for c in range(8):
                        tm = TM[c % 2]
                        S.op('act', lambda e, c=c, tm=tm: e.activation(out=tm[:, :n], in_=XT[:, c, :n], func=AF.Square),
                             reads=[('XT', c)], writes=['TM%d' % (c % 2)])
                        S.mm(ps[6][:, :n], lhsT=ones[:], rhs=tm[:, :n],
                                                                  start=(c == 0), stop=(c == 7),
                             reads=['TM%d' % (c % 2), 'ones'], writes=['ps6'], inc=True)
                    S.op('act', lambda e: e.activation(out=RS[:, :n], in_=ps[6][:, :n], func=AF.Sqrt, bias=EPSB[:, 0:1],
                                                       scale=1.0 / D), reads=['ps6'], writes=['RS'])
                    S.op('dve', lambda e: e.reciprocal(out=RS[:, :n], in_=RS[:, :n]), reads=['RS'], writes=['RS'])
                    for c in range(8):
                        tm = TM[c % 2]
                        S.op('dve', lambda e, c=c, tm=tm: e.scalar_tensor_tensor(
                            out=tm[:, :n], in0=XT[:, c, :n], scalar=SCAL[:, s_, ia, c:c + 1], in1=RS[:, :n],
                            op0=ALU.mult, op1=ALU.mult), reads=[('XT', c), 'RS'], writes=['TM%d' % (c % 2)])
                        S.op('act', lambda e, c=c, tm=tm: e.activation(
                            out=dst[:, c, :n], in_=tm[:, :n], func=AF.Identity, bias=SCAL[:, s_, ib, c:c + 1],
                            scale=1.0), reads=['TM%d' % (c % 2)], writes=[(dk, c)])

                gi = 0 if which == "ffn1" else 2
                for (s_, t0, n) in tiles:
                    load_tile(S, SCR, XT, s_, t0, n)
                    rmsnorm_mod(n, s_, gi, HN)
                    for j in range(NJ):
                        pg, pu = ps[(j % 2) * 2], ps[(j % 2) * 2 + 1]
                        kg, ku = 'ps%d' % ((j % 2) * 2), 'ps%d' % ((j % 2) * 2 + 1)
                        for (pp, kk_, Wt) in ((pg, kg, WG), (pu, ku, WU)):
                            for k in range(8):
                                S.mm(
                                    pp[:, :n], lhsT=Wt[:, k, j * 128:(j + 1) * 128], rhs=HN[:, k, :n],
                                    start=(k == 0), stop=(k == 7),
                                    reads=[('HN', k)], writes=[kk_], inc=(k == 7))
                        tm = TM[j % 2]
                        S.op('act', lambda e, pg=pg, tm=tm: e.activation(out=tm[:, :n], in_=pg[:, :n], func=AF.Silu),
                             reads=[kg], writes=['TM%d' % (j % 2)])
                        S.op('dve', lambda e, pu=pu, tm=tm, j=j: e.tensor_tensor(out=AA[:, j, :n], in0=pu[:, :n],
                                                                                 in1=tm[:, :n], op=ALU.mult),
                             reads=[ku, 'TM%d' % (j % 2)], writes=[('AA', j)])
                    for o in range(8):
                        pd = ps[4 + o % 2]
                        kd = 'ps%d' % (4 + o % 2)
                        for j in range(NJ):
                            S.mm(
                                pd[:, :n], lhsT=WD[:, j, o * 128:(o + 1) * 128], rhs=AA[:, j, :n],
                                start=(j == 0), stop=(j == NJ - 1),
                                reads=[('AA', j)], writes=[kd], inc=(j == NJ - 1))
                        S.op('dve', lambda e, pd=pd, o=o: e.scalar_tensor_tensor(
                            out=XT[:, o, :n], in0=pd[:, :n], scalar=SCAL[:, s_, 3 * gi + 2, o:o + 1], in1=XT[:, o, :n],
                            op0=ALU.mult, op1=ALU.add), reads=[kd, ('XT', o)], writes=[('XT', o)])
                    epilogue(S, XT, HN, TM, RS, rmsnorm_mod, s_, t0, n, SCR=SCR)
                S.barrier()

        EPSB = sb("EPSB", [128, 1])
        S.op('pool', lambda e: e.memset(EPSB[:], EPS), writes=['EPSB'])
        MUSG = sb("MUSG", [128, 3, 27])
        _mo = VEC_OFF["rw_mu"]
        S.op('dve', lambda e: e.tensor_scalar(out=MUSG[:, 0, :], in0=PV[:, _mo:_mo + 27], scalar1=-1.0, scalar2=1.0, op0=ALU.mult,
                                              op1=ALU.add), reads=['PV'], writes=['MUSG'])
        S.op('dve', lambda e: e.tensor_scalar(out=MUSG[:, 1, :], in0=PV[:, _mo:_mo + 27], scalar1=0.25, scalar2=None, op0=ALU.mult),
             reads=['PV'], writes=['MUSG'])
        S.op('dve', lambda e: e.tensor_scalar(out=MUSG[:, 2, :], in0=PV[:, _mo:_mo + 27], scalar1=0.5, scalar2=None, op0=ALU.mult),
             reads=['PV'], writes=['MUSG'])

        def load_tok_major(S, SCR, XT, s_, t0, n):
            src = x_d if s_ == 0 else ctx_d
            nb = n // 128
            for b_ in range(nb):
                S.dma('sp', SCR[:, b_ * 1024:(b_ + 1) * 1024], src[t0 + b_ * 128:t0 + (b_ + 1) * 128, :],
                      writes=[('XIN', b_)])
            for c in range(8):
                for b_ in range(nb):
                    S.mm(
                        ps[7][:, b_ * 128:(b_ + 1) * 128], lhsT=SCR[:, b_ * 1024 + c * 128:b_ * 1024 + (c + 1) * 128],
                        rhs=ident[:], start=True, stop=True, reads=[('XIN', b_), 'ident'], writes=['ps7'],
                        inc=(b_ == nb - 1))
                S.op('act' if c % 2 == 0 else 'dve',
                     (lambda e, c=c: e.copy(out=XT[:, c, :n], in_=ps[7][:, :n])) if c % 2 == 0 else
                     (lambda e, c=c: e.tensor_copy(out=XT[:, c, :n], in_=ps[7][:, :n])),
                     reads=['ps7'], writes=[('XT', c)])

        X1v = X1.rearrange("(c p) t -> p c t", p=128)
        XNv = XN.rearrange("(c p) t -> p c t", p=128)

        def epi_ffn1(S, XT, HN, TM, RS, rmsnorm_mod, s_, t0, n, SCR=None):
            g0 = (TC if s_ == 0 else 0) + t0
            for c in range(8):
                S.dma('pool', X1v[:, c, g0:g0 + n], XT[:, c, :n], reads=[('XT', c)], writes=[('X1', g0)])
            rmsnorm_mod(n, s_, 1, HN)
            for c in range(8):
                S.dma('pool', XNv[:, c, g0:g0 + n], HN[:, c, :n], reads=[('HN', c)], writes=[('XN', g0)])

        tiles1 = [(1, 0, TC)] + [(0, i * 512, 512) for i in range(T // 512)]
        if phases is None or "A" in phases:
            ffn_phase("ffn1", tiles1, load_tok_major, epi_ffn1)
        S.barrier()

        import os as _os
        RCUT = int(_os.environ.get("RCUT", "0"))

        class _Cut(Exception):
            pass

        def cut(k):
            if RCUT == k:
                raise _Cut()

        def seq_tiles(n):
            return [(t0, min(512, n - t0)) for t0 in range(0, n, 512)]

        def phase_inproj():
            with ExitStack() as ph:
                WIN = sb("WIN", [128, 8, NIN], BF16, st=ph)
                SCRB = sb("SCRB", [128, 4096], st=ph)
                XNH = [sb("XNH%d" % i, [128, 8, 640], BF16, st=ph) for i in range(2)]
                XS = [sb("XSn%d" % i, [128, 8, 512], BF16, st=ph) for i in range(2)]
                XSF = [sb("XSF%d" % i, [128, 512], st=ph) for i in range(2)]
                TZ = [sb("TZ%d" % i, [128, 512], st=ph) for i in range(2)]
                PO = [sb("PO%d" % i, [128, 512], BF16, st=ph) for i in range(6)]
                ci = 0
                for k in range(8):
                    for c0 in range(0, NIN, 2048):
                        w = min(2048, NIN - c0)
                        i = ci % 2
                        ci += 1
                        key = 'stg%d' % i
                        S.dma('sp' if i == 0 else 'pool', SCRB[:, i * 2048:i * 2048 + w],
                              w_in_d[k * 128:(k + 1) * 128, c0:c0 + w], writes=[key])
                        ce = ('dve', 'pool', 'act')[ci % 3]
                        if ce == 'act':
                            S.op('act', lambda e, i=i, w=w, k=k, c0=c0: e.copy(out=WIN[:, k, c0:c0 + w],
                                                                               in_=SCRB[:, i * 2048:i * 2048 + w]), reads=[key])
                        else:
                            S.op(ce, lambda e, i=i, w=w, k=k, c0=c0: e.tensor_copy(out=WIN[:, k, c0:c0 + w],
                                                                                   in_=SCRB[:, i * 2048:i * 2048 + w]), reads=[key])
                S.barrier()
                tl = [(1, 0, TC)] + [(0, t0, n) for (t0, n) in seq_tiles(T)]
                it = 0
                iz = 0
                for ti_, (s_, t0, n) in enumerate(tl):
                    g0 = (TC if s_ == 0 else 0) + t0
                    H = XNH[ti_ % 2]
                    Xs = XS[ti_ % 2]
                    kH = lambda c: ('XNH', ti_ % 2, c)
                    kX = lambda c: ('XS', ti_ % 2, c)
                    has_top = (s_ == 0 and t0 > 0)
                    has_bot = (s_ == 0 and t0 + n < T)
                    if not has_top:
                        S.op('pool', lambda e, H=H: e.memset(H[:, :, 0:64], 0.0), reads=[kH(c) for c in range(8)],
                             writes=[kH(c) for c in range(8)])
                    if not has_bot:
                        S.op('pool', lambda e, H=H: e.memset(H[:, :, 64 + n:128 + n], 0.0), reads=[kH(c) for c in range(8)],
                             writes=[kH(c) for c in range(8)])
                    for c in range(8):
                        lo = g0 - (64 if has_top else 0)
                        hi = g0 + n + (64 if has_bot else 0)
                        S.dma('sp', H[:, c, 64 - (g0 - lo):64 + n + (hi - g0 - n)], XNv[:, c, lo:hi], reads=[kH(c)], writes=[kH(c)])
                    for c in range(8):
                        if s_ == 1:
                            S.op('dve' if c % 2 else 'pool', lambda e, c=c: e.tensor_tensor(
                                out=Xs[:, c, :n], in0=H[:, c, 63:63 + n], in1=H[:, c, 65:65 + n], op=ALU.add),
                                reads=[kH(c)], writes=[kX(c)])
                        else:
                            xf = XSF[c % 2]
                            kf = 'XSF%d' % (c % 2)
                            xf3 = xf[:, :n].rearrange("p (r c) -> p r c", c=64)
                            c3 = H[:, c, 64:64 + n].rearrange("p (r c) -> p r c", c=64)
                            xs3 = Xs[:, c, :n].rearrange("p (r c) -> p r c", c=64)
                            S.op('pool', lambda e, c=c, xf=xf: e.tensor_tensor(out=xf[:, :n], in0=H[:, c, 0:n], in1=H[:, c, 128:128 + n],
                                                                              op=ALU.add), reads=[kH(c)], writes=[kf])
                            S.op('dve', lambda e, xf3=xf3, c3=c3: e.tensor_tensor(out=xf3[:, :, 1:64], in0=xf3[:, :, 1:64], in1=c3[:, :, 0:63],
                                                                                  op=ALU.add), reads=[kH(c), kf], writes=[kf])
                            S.op('dve', lambda e, xf3=xf3, c3=c3, xs3=xs3: e.tensor_tensor(out=xs3[:, :, 0:63], in0=xf3[:, :, 0:63],
                                                                                           in1=c3[:, :, 1:64], op=ALU.add),
                                 reads=[kH(c), kf], writes=[kX(c)])
                            S.op('pool', lambda e, xf3=xf3, xs3=xs3: e.tensor_copy(out=xs3[:, :, 63:64], in_=xf3[:, :, 63:64]),
                                 reads=[kf, kX(c)], writes=[kX(c)])
                    noc = 59 if s_ == 0 else 35
                    qi = 1 if s_ == 0 else 2
                    for oc in range(noc):
                        po = PO[it % 6]
                        kpo = 'PO%d' % (it % 6)
                        it += 1
                        if oc < 27:
                            b0 = (oc % 2) * 2
                            p1, p2 = ps[b0], ps[b0 + 1]
                            k1, k2 = 'ps%d' % b0, 'ps%d' % (b0 + 1)
                            for k in range(8):
                                S.mm(p1[:, :n], lhsT=WIN[:, k, oc * 128:(oc + 1) * 128], rhs=H[:, k, 64:64 + n], start=(k == 0), stop=(k == 7),
                                     reads=[kH(k)], writes=[k1], inc=(k == 7))
                            for k in range(8):
                                S.mm(p2[:, :n], lhsT=WIN[:, k, oc * 128:(oc + 1) * 128], rhs=Xs[:, k, :n], start=(k == 0), stop=(k == 7),
                                     reads=[kX(k)], writes=[k2], inc=(k == 7))
                            tz = TZ[iz % 2]
                            ktz = 'TZ%d' % (iz % 2)
                            iz += 1
                            S.op('act', lambda e, p2=p2, tz=tz, oc=oc: e.activation(out=tz[:, :n], in_=p2[:, :n], func=AF.Identity,
                                                                                    scale=MUSG[:, qi, oc:oc + 1]), reads=[k2, 'MUSG'],
                                 writes=[ktz])
                            S.op('dve', lambda e, p1=p1, tz=tz, po=po, oc=oc: e.scalar_tensor_tensor(
                                out=po[:, :n], in0=p1[:, :n], scalar=MUSG[:, 0, oc:oc + 1], in1=tz[:, :n], op0=ALU.mult, op1=ALU.add),
                                reads=[k1, ktz, 'MUSG'], writes=[kpo])
                        else:
                            pp = ps[4 + oc % 4]
                            kp = 'ps%d' % (4 + oc % 4)
                            for k in range(8):
                                S.mm(pp[:, :n], lhsT=WIN[:, k, oc * 128:(oc + 1) * 128], rhs=H[:, k, 64:64 + n], start=(k == 0), stop=(k == 7),
                                     reads=[kH(k)], writes=[kp], inc=(k == 7))
                            if oc >= 43:
                                S.op('act', lambda e, pp=pp, po=po: e.activation(out=po[:, :n], in_=pp[:, :n], func=AF.Sigmoid),
                                     reads=[kp], writes=[kpo])
                            elif oc >= 35:
                                S.op('act', lambda e, pp=pp, po=po: e.activation(out=po[:, :n], in_=pp[:, :n], func=AF.Gelu),
                                     reads=[kp], writes=[kpo])
                            elif oc % 2 == 0:
                                S.op('act', lambda e, pp=pp, po=po: e.copy(out=po[:, :n], in_=pp[:, :n]), reads=[kp], writes=[kpo])
                            else:
                                S.op('dve', lambda e, pp=pp, po=po: e.tensor_copy(out=po[:, :n], in_=pp[:, :n]), reads=[kp],
                                     writes=[kpo])
                        S.dma('pool', Pv[:, oc, g0:g0 + n], po[:, :n], reads=[kpo], writes=[('P', oc, g0)])
                S.barrier()

        if phases is None or "B" in phases:
            phase_inproj()
        if phases is not None and "L" not in phases and "R" not in phases:
            return nc

        ONEB = sb("ONEB", [128, 1])
        S.op('pool', lambda e: e.memset(ONEB[:], 1.0), writes=['ONEB'])
        LC = sb("LC", [128, 2, 16])
        HST = sb("HST", [128, 16])
        lo = VEC_OFF["lru_lam"]
        S.op('act', lambda e: e.activation(out=LC[:, 0, :], in_=PV[:, lo:lo + 16], func=AF.Exp, scale=-1.0),
             reads=['PV'], writes=['LC'])
        S.op('act', lambda e: e.activation(out=LC[:, 0, :], in_=LC[:, 0, :], func=AF.Ln, bias=ONEB[:, 0:1], scale=1.0),
             reads=['LC', 'ONEB'], writes=['LC'])
        S.op('dve', lambda e: e.tensor_scalar(out=LC[:, 1, :], in0=LC[:, 0, :], scalar1=-16.0, scalar2=None, op0=ALU.mult),
             reads=['LC'], writes=['LC'])
        S.op('dve', lambda e: e.tensor_scalar(out=LC[:, 0, :], in0=LC[:, 0, :], scalar1=-8.0, scalar2=None, op0=ALU.mult),
             reads=['LC'], writes=['LC'])

        def phase_lru():
            with ExitStack() as ph:
                WAX = sb("WAX", [128, 2, 16, 256], BF16, st=ph)
                STG = sb("STGL", [128, 16, 256], st=ph)
                XL = sb("XL", [128, 2, T + 4], BF16, st=ph)
                XC = sb("XC", [128, 2, T], st=ph)
                XCB = sb("XCB", [128, 2, T], BF16, st=ph)
                AB = sb("AB", [128, T], st=ph)
                UB = sb("UB", [128, T], st=ph)
                HB = [sb("HB%d" % i, [128, T], st=ph) for i in range(2)]
                GL = sb("GLg", [128, T], BF16, st=ph)
                YB = sb("YBl", [128, T], BF16, st=ph)
                TL = [sb("TL%d" % i, [128, 512], st=ph) for i in range(5)]
                for gi_, wsrc in enumerate((lru_wa_d, lru_wx_d)):
                    S.dma('sp', STG[:], wsrc.rearrange("d n (k p) j -> p (d n k) j", p=128), writes=['STGL'])
                    S.op('dve', lambda e, gi_=gi_: e.tensor_copy(out=WAX[:, gi_, :, :], in_=STG[:]), reads=['STGL'],
                         writes=['WAX'])
                S.op('pool', lambda e: e.memset(XL[:], 0.0), writes=['XL'])
                cw, cb = VEC_OFF["lru_conv_w"], VEC_OFF["lru_conv_b"]
                bao, bxo = VEC_OFF["lru_ba"], VEC_OFF["lru_bx"]
                for s_ in (1, 0):
                    n = TC if s_ == 1 else T
                    g0 = 0 if s_ == 1 else TC
                    if s_ == 0:
                        S.op('pool', lambda e: e.memset(XL[:], 0.0), reads=['XL'], writes=['XL'])
                    for blk in range(4):
                        for k in range(2):
                            cc = 2 * blk + k
                            S.dma('sp', XL[:, k, 1:1 + n], Pv[:, 27 + cc, g0:g0 + n], reads=[('P', 27 + cc, g0)],
                                  writes=['XL'])
                            S.op('act', lambda e, k=k, cc=cc: e.activation(
                                out=XC[:, k, :n], in_=XL[:, k, 0:n], func=AF.Identity, bias=PV[:, cb + cc:cb + cc + 1],
                                scale=PV[:, cw + cc:cw + cc + 1]), reads=['XL'], writes=[('XC', k)])
                            for j in range(1, 4):
                                S.op('dve', lambda e, k=k, cc=cc, j=j: e.scalar_tensor_tensor(
                                    out=XC[:, k, :n], in0=XL[:, k, j:j + n], scalar=PV[:, cw + j * 8 + cc:cw + j * 8 + cc + 1],
                                    in1=XC[:, k, :n], op0=ALU.mult, op1=ALU.add), reads=['XL', ('XC', k)], writes=[('XC', k)])
                            S.op('act', lambda e, k=k: e.copy(out=XCB[:, k, :n], in_=XC[:, k, :n]), reads=[('XC', k)],
                                 writes=[('XCB', k)])
                        for oc in range(2):
                            cc = 2 * blk + oc
                            if s_ == 0:
                                S.dma('sp', GL[:, :n], Pv[:, 35 + cc, g0:g0 + n], reads=[('P', 35 + cc, g0)], writes=['GL'])
                            for d in range(2):
                                col = d * 8 + cc
                                kHB = 'HB%d' % d
                                HBd = HB[d]
                                for (t0, m) in seq_tiles(n):
                                    for gi_ in range(2):
                                        pp = ps[gi_]
                                        for kin in range(2):
                                            S.mm(
                                                pp[:, :m], lhsT=WAX[:, gi_, (d * 4 + blk) * 2 + kin, oc * 128:(oc + 1) * 128],
                                                rhs=XCB[:, kin, t0:t0 + m], start=(kin == 0), stop=(kin == 1),
                                                reads=[('XCB', kin), 'WAX'], writes=['ps%d' % gi_], inc=(kin == 1))
                                    S.op('act', lambda e, m=m, col=col, t0=t0: e.activation(
                                        out=AB[:, t0:t0 + m], in_=ps[0][:, :m], func=AF.Sigmoid, bias=PV[:, bao + col:bao + col + 1],
                                        scale=1.0), reads=['ps0'], writes=[('AB', t0)])
                                    S.op('act', lambda e, m=m, col=col, t0=t0: e.activation(
                                        out=UB[:, t0:t0 + m], in_=ps[1][:, :m], func=AF.Sigmoid, bias=PV[:, bxo + col:bxo + col + 1],
                                        scale=1.0), reads=['ps1'], writes=[('UB', t0)])
                                for (t0, m) in seq_tiles(n):
                                    S.op('act', lambda e, m=m, col=col, t0=t0: e.activation(
                                        out=HBd[:, t0:t0 + m], in_=AB[:, t0:t0 + m], func=AF.Exp, scale=LC[:, 1, col:col + 1]),
                                        reads=[('AB', t0), 'LC'], writes=[kHB])
                                    S.op('act', lambda e, m=m, col=col, t0=t0: e.activation(
                                        out=AB[:, t0:t0 + m], in_=AB[:, t0:t0 + m], func=AF.Exp, scale=LC[:, 0, col:col + 1]),
                                        reads=[('AB', t0), 'LC'], writes=[('AB', t0)])
                                for (t0, m) in seq_tiles(n):
                                    S.op('act', lambda e, m=m, t0=t0: e.activation(
                                        out=HBd[:, t0:t0 + m], in_=HBd[:, t0:t0 + m], func=AF.Sqrt, bias=ONEB[:, 0:1], scale=-1.0),
                                        reads=[kHB, 'ONEB'], writes=[kHB])
                                for (t0, m) in seq_tiles(n):
                                    S.op('pool', lambda e, m=m, t0=t0: e.tensor_tensor(
                                        out=UB[:, t0:t0 + m], in0=UB[:, t0:t0 + m], in1=XC[:, oc, t0:t0 + m], op=ALU.mult),
                                        reads=[('UB', t0), ('XC', oc)], writes=[('UB', t0)])
                                    S.op('dve', lambda e, m=m, t0=t0: e.tensor_tensor(
                                        out=UB[:, t0:t0 + m], in0=UB[:, t0:t0 + m], in1=HBd[:, t0:t0 + m], op=ALU.mult),
                                        reads=[('UB', t0), kHB], writes=[('UB', t0)])
                                rk = [('AB', t0) for (t0, m) in seq_tiles(n)] + [('UB', t0) for (t0, m) in seq_tiles(n)]
                                init = 0.0 if s_ == 1 else HST[:, col:col + 1]
                                if d == 0:
                                    S.op('dve', lambda e, init=init: e.tensor_tensor_scan(
                                        out=HB[0][:, :n], data0=AB[:, :n], data1=UB[:, :n], initial=init, op0=ALU.mult,
                                        op1=ALU.add), reads=rk + ['HST'], writes=['HB0'])
                                    if s_ == 1:
                                        S.op('dve', lambda e, col=col: e.tensor_copy(out=HST[:, col:col + 1], in_=HB[0][:, n - 1:n]),
                                             reads=['HB0'], writes=['HST'])
                                else:
                                    S.op('dve', lambda e, init=init: e.tensor_tensor_scan(
                                        out=HB[1][:, n - 1::-1] if False else HB[1][:, :n][:, ::-1], data0=AB[:, :n][:, ::-1],
                                        data1=UB[:, :n][:, ::-1], initial=init, op0=ALU.mult, op1=ALU.add),
                                        reads=rk + ['HST'], writes=['HB1'])
                                    if s_ == 1:
                                        S.op('dve', lambda e, col=col: e.tensor_copy(out=HST[:, col:col + 1], in_=HB[1][:, 0:1]),
                                             reads=['HB1'], writes=['HST'])
                            if s_ == 0:
                                S.op('pool', lambda e: e.tensor_tensor(out=HB[0][:, :n], in0=HB[0][:, :n], in1=HB[1][:, :n],
                                                                       op=ALU.add), reads=['HB0', 'HB1'], writes=['HB0'])
                                S.op('dve', lambda e: e.tensor_tensor(out=YB[:, :n], in0=HB[0][:, :n], in1=GL[:, :n], op=ALU.mult),
                                     reads=['HB0', 'GL'], writes=['YBl'])
                                S.dma('pool', YLv[:, cc, :], YB[:, :n], reads=['YBl'], writes=[('YL', cc)])
                if debug:
                    S.dma('sp', HSTD[:, :], HST[:], reads=['HST'], writes=['HSTD'])
                S.barrier()

        if phases is None or "L" in phases:
            phase_lru()
        if phases is not None and "R" not in phases:
            return nc
        DS = float(np.exp(-0.5))
        LN_X_EPS = 64e-5
        NCH = TT // 64

        def _rwkv_body(ph):
            if True:
                WUP = sb("WUP", [128, D], BF16, st=ph)
                AUP = sb("AUP", [128, D], BF16, st=ph)
                GUP = sb("GUP", [128, D], BF16, st=ph)
                IDB = sb("IDB", [128, 128], BF16, st=ph)
                BONES = sb("BONES", [128, 128], st=ph)
                MK = [sb("MK%d" % d, [128, 2, 256], st=ph) for d in range(2)]
                AMK = [sb("AMK%d" % d, [128, 2, 64], st=ph) for d in range(2)]
                BMK = [sb("BMK%d" % d, [128, 4, 64], st=ph) for d in range(2)]
                RMF = sb("RMF", [128, 512], st=ph)
                RMB = sb("RMB", [128, 512], st=ph)
                TW = sb("TW", [128, TT], BF16, st=ph)
                ZA = sb("ZA", [128, TT], BF16, st=ph)
                PL = sb("PL", [128, TT], BF16, st=ph)
                SH = sb("SH", [128, TT], st=ph)
                ZR = sb("ZR", [128, TT], BF16, st=ph)
                ZK = sb("ZK", [128, TT], BF16, st=ph)
                ZV = sb("ZV", [128, TT], BF16, st=ph)
                KK = sb("KK", [128, TT], BF16, st=ph)
                ART = [[sb("ART%d%d" % (d, i), [128, 8, 2, 64], BF16, st=ph) for i in range(2)] for d in range(2)]
                KBT = [[sb("KBT%d%d" % (d, i), [128, 8, 2, 64], BF16, st=ph) for i in range(2)] for d in range(2)]
                WC = [sb("WC%d" % d, [128, NCH], st=ph) for d in range(2)]
                TRP = [[sb("TRP%d%d" % (d, i), [128, 512], F32 if i < 5 else BF16, st=ph) for i in range(7)] for d in range(2)]
                YB = sb("YBr", [128, 64, 64], st=ph)
                TR = [sb("TR%d" % i, [128, 512], st=ph) for i in range(3)]
                TRS = sb("TRS", [128, 256], st=ph)
                MT = [[sb("MT%d%d" % (d, i), [128, 8, 256], BF16, st=ph) for i in range(2)] for d in range(2)]
                ABt = [[sb("ABt%d%d" % (d, i), [128, 4, 2, 128], BF16, st=ph) for i in range(2)] for d in range(2)]
                ACC = [[sb("ACC%d%d" % (d, i), [128, 4, 128], BF16, st=ph) for i in range(2)] for d in range(2)]
                TTt = [[sb("TTt%d%d" % (d, i), [128, 8, 128], BF16, st=ph) for i in range(2)] for d in range(2)]
                PT = [[sb("PT%d%d" % (d, i), [128, 3, 64], BF16, st=ph) for i in range(2)] for d in range(2)]
                XB = [[sb("XB%d%d" % (d, i), [128, 64], BF16, st=ph) for i in range(2)] for d in range(2)]
                UB_ = [[sb("UBr%d%d" % (d, i), [128, 64], BF16, st=ph) for i in range(2)] for d in range(2)]
                SS = [sb("SS%d" % d, [128, 64], st=ph) for d in range(2)]
                SSb = [sb("SSb%d" % d, [128, 64], BF16, st=ph) for d in range(2)]
                MUS = sb("MUS", [128, 3, 27], st=ph)
                KA1 = sb("KA1", [128, 8], st=ph)
                GO = [sb("GO%d" % i, [128, 512], BF16, st=ph) for i in range(2)]
                ST8 = sb("ST8", [128, 2, 8, 64], st=ph) if debug else None

                for i_, (dst, src) in enumerate(((WUP, rw_w_up_d), (AUP, rw_a_up_d), (GUP, rw_g_up_d))):
                    S.dma('sp', SH[:, 0:D], src[:, :], writes=['SH'])
                    S.op('dve', lambda e, dst=dst: e.tensor_copy(out=dst[:], in_=SH[:, 0:D]), reads=['SH'], writes=['LW_' + str(i_)])
                S.op('dve', lambda e: e.tensor_copy(out=IDB[:], in_=ident[:]), reads=['ident'], writes=['IDB'])
                S.op('pool', lambda e: e.memset(BONES[:], 0.0), writes=['BONES'])
                S.op('pool', lambda e: e.memset(BONES[0:64, 0:64], 1.0), reads=['BONES'], writes=['BONES'])
                S.op('pool', lambda e: e.memset(BONES[64:128, 64:128], 1.0), reads=['BONES'], writes=['BONES'])
                S.op('pool', lambda e: e.memset(RMF[:], 1.0), writes=['RMF'])
                S.op('pool', lambda e: e.memset(RMF[:, 0:512:64], 0.0), reads=['RMF'], writes=['RMF'])
                S.op('pool', lambda e: e.memset(RMB[:], 1.0), writes=['RMB'])
                S.op('pool', lambda e: e.memset(RMB[:, 63:512:64], 0.0), reads=['RMB'], writes=['RMB'])

                def tri(dst_view_fn, kind):
                    pat, cm = ([[1, 64]], -1) if kind[0] == 'L' else ([[-1, 64]], 1)
                    cmp_ = ALU.is_gt if kind[2] == 's' else ALU.is_ge
                    for hp in (0, 64):
                        v = dst_view_fn(hp)
                        S.op('pool', lambda e, v=v: e.memset(v, 1.0), reads=['MASKS'], writes=['MASKS'])
                        S.op('pool', lambda e, v=v: e.affine_select(out=v, in_=v, pattern=pat, compare_op=cmp_, fill=0.0, base=0,
                                                                   channel_multiplier=cm), reads=['MASKS'], writes=['MASKS'])
                for d in range(2):
                    ks, ki = ('LTs', 'LTi') if d == 0 else ('GTs', 'GTi')
                    kt = 'GTs' if d == 0 else 'LTs'
                    for q in range(2):
                        for blk, kd_ in enumerate((ks, ki, ks, ki)):
                            tri(lambda hp, d=d, q=q, blk=blk: MK[d][hp:hp + 64, q, blk * 64:(blk + 1) * 64], kd_)
                        tri(lambda hp, d=d, q=q: AMK[d][hp:hp + 64, q, :], ks)
                    for q in range(4):
                        tri(lambda hp, d=d, q=q: BMK[d][hp:hp + 64, q, :], kt)
                mo = VEC_OFF["rw_mu"]
                S.op('dve', lambda e: e.tensor_scalar(out=MUS[:, 0, :], in0=PV[:, mo:mo + 27], scalar1=-1.0, scalar2=1.0, op0=ALU.mult,
                                                      op1=ALU.add), reads=['PV'], writes=['MUS'])
                S.op('dve', lambda e: e.tensor_scalar(out=MUS[:, 1, :], in0=PV[:, mo:mo + 27], scalar1=0.25, scalar2=None, op0=ALU.mult),
                     reads=['PV'], writes=['MUS'])
                S.op('dve', lambda e: e.tensor_scalar(out=MUS[:, 2, :], in0=PV[:, mo:mo + 27], scalar1=0.5, scalar2=None, op0=ALU.mult),
                     reads=['PV'], writes=['MUS'])
                kao = VEC_OFF["rw_k_a"]
                S.op('dve', lambda e: e.tensor_scalar(out=KA1[:], in0=PV[:, kao:kao + 8], scalar1=-1.0, scalar2=1.0, op0=ALU.mult,
                                                      op1=ALU.add), reads=['PV'], writes=['KA1'])
                for t_ in ABt[0] + ABt[1] + ACC[0] + ACC[1] + TTt[0] + TTt[1]:
                    S.op('pool', lambda e, t_=t_: e.memset(t_[:], 0.0), writes=['ABZ'])
                S.barrier()

                cut(1)
                PLx = PL[:, TC:TT].rearrange("p (r c) -> p r c", c=64)
                SHx = SH[:, TC:TT].rearrange("p (r c) -> p r c", c=64)

                def zlerp(pc, dst, dkey):
                    S.dma('sp', dst[:, 0:TC], Pv[:, pc, 0:TC], reads=[dkey], writes=[dkey])
                    for t0 in range(0, T, 1024):
                        S.dma('sp' if (t0 // 1024) % 2 == 0 else 'pool', dst[:, TC + t0:TC + t0 + 1024], Pv[:, pc, TC + t0:TC + t0 + 1024],
                              reads=[dkey], writes=[dkey])

                tiles_all = [(0, TC)] + [(TC + t0, m) for (t0, m) in seq_tiles(T)]

                zlerp(24, ZK, 'ZK')
                cut(2)
                S.op('act', lambda e: e.activation(out=TW[:, :], in_=ZK[:, :], func=AF.Tanh), reads=['ZK'], writes=['TW'])
                zlerp(25, ZA, 'ZA')
                zlerp(26, ZK, 'ZK')
                S.op('act', lambda e: e.activation(out=ZR[:, :], in_=ZK[:, :], func=AF.Sigmoid), reads=['ZK'], writes=['ZR'])
                gi_ = 0
                for c in range(8):
                    for (t0, m) in seq_tiles(T):
                        pp = ps[gi_ % 2]
                        go = GO[gi_ % 2]
                        S.mm(pp[:, :m], lhsT=GUP[:, c * 128:(c + 1) * 128],
                                                                              rhs=ZR[:, TC + t0:TC + t0 + m], start=True, stop=True,
                             reads=['ZR', 'LW_2'], writes=['ps%d' % (gi_ % 2)])
                        S.op('act', lambda e, pp=pp, go=go, m=m: e.copy(out=go[:, :m], in_=pp[:, :m]), reads=['ps%d' % (gi_ % 2)],
                             writes=['GO%d' % (gi_ % 2)])
                        S.dma('pool', GSv[:, c, t0:t0 + m], go[:, :m], reads=['GO%d' % (gi_ % 2)], writes=[('GS', c, t0)])
                        gi_ += 1

                cut(3)
                kko, rko = VEC_OFF["rw_k_k"], VEC_OFF["rw_r_k"]
                w0o, a0o = VEC_OFF["rw_w0"], VEC_OFF["rw_a0"]
                lwo, lbo = VEC_OFF["rw_ln_w"], VEC_OFF["rw_ln_b"]
                v3 = lambda ap, m: ap.rearrange("p (j s) -> p j s", s=64)

                for c in range(8):
                    zlerp(c, ZR, 'ZR')
                    zlerp(8 + c, ZK, 'ZK')
                    zlerp(16 + c, ZV, 'ZV')
                    for (g0, m) in tiles_all:
                        S.op('act', lambda e, g0=g0, m=m: e.activation(out=TR[0][:, :m], in_=ZK[:, g0:g0 + m], func=AF.Square,
                                                                       scale=PV[:, kko + c:kko + c + 1]), reads=['ZK'], writes=['TR0'])
                        S.mm(ps[0][:, :m], lhsT=BONES[:], rhs=TR[0][:, :m], start=True, stop=True,
                             reads=['TR0', 'BONES'], writes=['ps0'])
                        S.op('act', lambda e, m=m: e.activation(out=TR[1][:, :m], in_=ps[0][:, :m], func=AF.Sqrt), reads=['ps0'],
                             writes=['TR1'])
                        S.op('dve', lambda e, m=m: e.tensor_scalar(out=TR[1][:, :m], in0=TR[1][:, :m], scalar1=1e-12, scalar2=None,
                                                                   op0=ALU.max), reads=['TR1'], writes=['TR1'])
                        S.op('dve', lambda e, m=m: e.reciprocal(out=TR[1][:, :m], in_=TR[1][:, :m]), reads=['TR1'], writes=['TR1'])
                        S.op('dve', lambda e, g0=g0, m=m: e.scalar_tensor_tensor(
                            out=KK[:, g0:g0 + m], in0=ZK[:, g0:g0 + m], scalar=PV[:, kko + c:kko + c + 1], in1=TR[1][:, :m],
                            op0=ALU.mult, op1=ALU.mult), reads=['ZK', 'TR1'], writes=['KK'])

                    cut(4)
                    batches = [
                        [[0, 1, 2, 3]] + [list(range(4 + 8 * i, 12 + 8 * i)) for i in range(8)],
                        [[3, 2, 1, 0]] + [list(range(11 + 8 * i, 3 + 8 * i, -1)) for i in range(7, -1, -1)],
                    ]
                    NB = 9
                    for d in range(2):
                        S.op('pool', lambda e, d=d: e.memset(SS[d][:], 0.0), reads=[('SS', d, 0), ('SS', d, 1)],
                             writes=[('SS', d, 0), ('SS', d, 1)])
                        S.op('pool', lambda e, d=d: e.memset(SSb[d][:], 0.0), reads=[('SSb', d, 0), ('SSb', d, 1)],
                             writes=[('SSb', d, 0), ('SSb', d, 1)])
                    yb_written = set()

                    def prep_gen(d, n):
                        js = batches[d][n]
                        par = n % 2
                        jmin = min(js)
                        g0, m = jmin * 64, 64 * len(js)
                        nch = len(js)
                        dh = d * 64
                        col = d * 8 + c
                        LW, AS, CL, E1, E2, TB, TK = TRP[d]
                        kT = lambda i: ('TRP', d, i)
                        A_, K_ = ART[d][par], KBT[d][par]
                        kA, kK = ('ART', d, par), ('KBT', d, par)
                        pd_, kpd = ps[d], 'ps%d' % d
                        S.mm(pd_[:, :m], lhsT=WUP[dh:dh + 64, c * 128:(c + 1) * 128], rhs=TW[dh:dh + 64, g0:g0 + m], start=True, stop=True,
                             reads=['TW', 'LW_0'], writes=[kpd])
                        S.op('act', lambda e: e.activation(out=LW[:, :m], in_=pd_[:, :m], func=AF.Sigmoid,
                                                           bias=PV[:, w0o + col:w0o + col + 1], scale=1.0), reads=[kpd], writes=[kT(0)])
                        S.mm(pd_[:, :m], lhsT=AUP[dh:dh + 64, c * 128:(c + 1) * 128], rhs=ZA[dh:dh + 64, g0:g0 + m], start=True, stop=True,
                             reads=['ZA', 'LW_1'], writes=[kpd])
                        S.op('act', lambda e: e.activation(out=AS[:, :m], in_=pd_[:, :m], func=AF.Sigmoid,
                                                           bias=PV[:, a0o + col:a0o + col + 1], scale=1.0), reads=[kpd], writes=[kT(1)])
                        yield
                        if d == 0:
                            S.op('dve', lambda e: e.tensor_tensor_scan(out=CL[:, :m], data0=RMF[:, :m], data1=LW[:, :m], initial=0.0,
                                                                       op0=ALU.mult, op1=ALU.add), reads=[kT(0), 'RMF'], writes=[kT(2)])
                        else:
                            S.op('dve', lambda e: e.tensor_tensor_scan(out=CL[:, :m][:, ::-1], data0=RMB[:, :m][:, ::-1],
                                                                       data1=LW[:, :m][:, ::-1], initial=0.0, op0=ALU.mult, op1=ALU.add),
                                 reads=[kT(0), 'RMB'], writes=[kT(2)])
                        S.op('act', lambda e: e.activation(out=E1[:, :m], in_=CL[:, :m], func=AF.Exp, scale=-DS), reads=[kT(2)], writes=[kT(3)])
                        S.op('act', lambda e: e.activation(out=E2[:, :m], in_=CL[:, :m], func=AF.Exp, scale=DS), reads=[kT(2)], writes=[kT(4)])
                        S.op('pool', lambda e: e.tensor_tensor(out=LW[:, :m], in0=CL[:, :m], in1=LW[:, :m], op=ALU.subtract),
                             reads=[kT(2), kT(0)], writes=[kT(0)])
                        yield
                        S.op('act', lambda e: e.activation(out=CL[:, :m], in_=LW[:, :m], func=AF.Exp, scale=-DS), reads=[kT(0), kT(2)],
                             writes=[kT(2)])
                        S.op('dve', lambda e: e.tensor_tensor(out=A_[:, 0:nch, 1, :], in0=v3(ZR[:, g0:g0 + m], m), in1=v3(E1[:, :m], m),
                                                              op=ALU.mult), reads=['ZR', kT(3)], writes=[kA])
                        S.op('pool', lambda e: e.tensor_tensor(out=TB[:, :m], in0=KK[:, g0:g0 + m], in1=AS[:, :m], op=ALU.mult),
                             reads=['KK', kT(1)], writes=[kT(5)])
                        S.op('pool', lambda e: e.tensor_scalar(out=TK[:, :m], in0=AS[:, :m], scalar1=PV[:, kao + c:kao + c + 1],
                                                               scalar2=KA1[:, c:c + 1], op0=ALU.mult, op1=ALU.add),
                             reads=[kT(1), 'KA1'], writes=[kT(6)])
                        yield
                        S.op('dve', lambda e: e.scalar_tensor_tensor(out=A_[:, 0:nch, 0, :], in0=v3(KK[:, g0:g0 + m], m), scalar=-1.0,
                                                                     in1=v3(CL[:, :m], m), op0=ALU.mult, op1=ALU.mult),
                             reads=['KK', kT(2)], writes=[kA])
                        S.op('dve', lambda e: e.tensor_tensor(out=K_[:, 0:nch, 1, :], in0=v3(TB[:, :m], m), in1=v3(E2[:, :m], m), op=ALU.mult),
                             reads=[kT(5), kT(4)], writes=[kK])
                        S.op('pool', lambda e: e.tensor_tensor(out=TK[:, :m], in0=TK[:, :m], in1=ZK[:, g0:g0 + m], op=ALU.mult),
                             reads=[kT(6), 'ZK'], writes=[kT(6)])
                        yield
                        S.op('dve', lambda e: e.tensor_tensor(out=K_[:, 0:nch, 0, :], in0=v3(TK[:, :m], m), in1=v3(E2[:, :m], m), op=ALU.mult),
                             reads=[kT(6), kT(4)], writes=[kK])
                        ecol = 63 if d == 0 else 0
                        S.op('dve', lambda e: e.tensor_copy(out=WC[d][:, jmin:jmin + nch], in_=E1[:, ecol:m:64]), reads=[kT(3)],
                             writes=[('WC', d)])
                        yield

                    def stage1_gen(d, n):
                        js = batches[d][n]
                        par = n % 2
                        jmin = min(js)
                        A_, K_ = ART[d][par], KBT[d][par]
                        kA, kK = ('ART', d, par), ('KBT', d, par)
                        MT_, TT_ = MT[d][par], TTt[d][par]
                        AB_, AC_ = ABt[d], ACC[d]
                        for h0 in range(0, len(js), 4):
                            for q0 in range(0, 4, 2):
                                for hp in (0, 64):
                                    pm = ps[2 + hp // 64]
                                    for q in range(2):
                                        jl = js[h0 + q0 + q] - jmin
                                        for w_ in range(2):
                                            S.mm(pm[hp:hp + 64, q * 256 + w_ * 128:q * 256 + (w_ + 1) * 128],
                                                 lhsT=K_[hp:hp + 64, jl, w_, :], rhs=A_[hp:hp + 64, jl, :, :], start=True, stop=True,
                                                 reads=[kK, kA], writes=['ps%d' % (2 + hp // 64)])
                                lq = h0 + q0
                                for hp in (0, 64):
                                    pm, kpm = ps[2 + hp // 64], 'ps%d' % (2 + hp // 64)
                                    S.op('dve', lambda e, lq=lq, hp=hp, pm=pm: e.tensor_tensor(
                                        out=MT_[hp:hp + 64, lq:lq + 2, :], in0=pm[hp:hp + 64, 0:512].rearrange("p (q w) -> p q w", w=256),
                                        in1=MK[d][hp:hp + 64, :, :], op=ALU.mult), reads=[kpm], writes=[('MT', d, par, lq, hp)])
                                    S.op('dve', lambda e, q0=q0, hp=hp, pm=pm: e.tensor_tensor(
                                        out=AB_[0][hp:hp + 64, q0:q0 + 2, 0, hp:hp + 64],
                                        in0=pm[hp:hp + 64, 0:512].rearrange("p (q w) -> p q w", w=256)[:, :, 128:192],
                                        in1=AMK[d][hp:hp + 64, :, :], op=ALU.mult), reads=[kpm], writes=[('AB0', d, q0)])
                                yield
                            for hp in (0, 64):
                                pm = ps[2 + hp // 64]
                                for q in range(4):
                                    jl = js[h0 + q] - jmin
                                    S.mm(pm[hp:hp + 64, q * 128 + hp:q * 128 + hp + 64], lhsT=A_[hp:hp + 64, jl, 0, :],
                                         rhs=K_[hp:hp + 64, jl, 1, :], start=True, stop=True, reads=[kK, kA], writes=['ps%d' % (2 + hp // 64)])
                            for hp in (0, 64):
                                pm, kpm = ps[2 + hp // 64], 'ps%d' % (2 + hp // 64)
                                S.op('dve', lambda e, hp=hp, pm=pm: e.tensor_tensor(
                                    out=AB_[0][hp:hp + 64, 0:4, 1, hp:hp + 64],
                                    in0=pm[hp:hp + 64, 0:512].rearrange("p (q w) -> p q w", w=128)[:, :, hp:hp + 64],
                                    in1=BMK[d][hp:hp + 64, :, :], op=ALU.mult), reads=[kpm], writes=[('AB0', d, 0), ('AB0', d, 2)])
                            S.op('pool', lambda e: e.tensor_tensor(
                                out=AC_[0][:, 0:4, :], in0=AB_[0][:, 0:4, 0, :], in1=IDB[:].unsqueeze(1).to_broadcast([128, 4, 128]),
                                op=ALU.add), reads=[('AB0', d, 0), ('AB0', d, 2), 'IDB'], writes=[('ACC0', d)])
                            yield
                            for l in range(5):
                                cur, nxt = l % 2, 1 - (l % 2)
                                kc, kn = 'AB%d' % cur, 'AB%d' % nxt
                                for q0 in range(0, 4, 2):
                                    for q in range(2):
                                        jq = q0 + q
                                        if l < 4:
                                            S.mm(ps[2][:, q * 256:q * 256 + 128], lhsT=AB_[cur][:, jq, 1, :], rhs=AB_[cur][:, jq, 0, :],
                                                 start=True, stop=True, reads=[(kc, d, q0)], writes=['ps2'], inc=False)
                                        S.mm(ps[2][:, q * 256 + 128:q * 256 + 256], lhsT=AB_[cur][:, jq, 0, :], rhs=AB_[cur][:, jq, 1, :],
                                             start=True, stop=True, reads=[(kc, d, q0)], writes=['ps2'], inc=(q == 1))
                                    if l < 4:
                                        S.op('act', lambda e, q0=q0, nxt=nxt: e.copy(
                                            out=AB_[nxt][:, q0:q0 + 2, :, :].rearrange("p q a b -> p (q a b)"), in_=ps[2][:, 0:512]),
                                            reads=['ps2'], writes=[(kn, d, q0)])
                                    else:
                                        S.op('act', lambda e, q0=q0, nxt=nxt: e.copy(
                                            out=AB_[nxt][:, q0:q0 + 2, 1, :],
                                            in_=ps[2][:, 0:512].rearrange("p (q w) -> p q w", w=256)[:, :, 128:256]),
                                            reads=['ps2'], writes=[(kn, d, q0)])
                                    yield
                                for q in range(4):
                                    S.mm(ps[3][:, q * 128:(q + 1) * 128], lhsT=AB_[nxt][:, q, 1, :], rhs=AC_[cur][:, q, :], start=True, stop=True,
                                         reads=[(kn, d, (q // 2) * 2), ('ACC%d' % cur, d)], writes=['ps3'], inc=(q == 3))
                                if l == 4:
                                    S.op('dve', lambda e, cur=cur: e.tensor_tensor(
                                        out=TT_[:, h0:h0 + 4, :], in0=ps[3][:, 0:512].rearrange("p (q w) -> p q w", w=128),
                                        in1=AC_[cur][:, 0:4, :], op=ALU.add), reads=['ps3', ('ACC%d' % cur, d)], writes=[('TTt', d, par, h0)])
                                else:
                                    S.op('dve', lambda e, cur=cur, nxt=nxt: e.tensor_tensor(
                                        out=AC_[nxt][:, 0:4, :], in0=ps[3][:, 0:512].rearrange("p (q w) -> p q w", w=128),
                                        in1=AC_[cur][:, 0:4, :], op=ALU.add), reads=['ps3', ('ACC%d' % cur, d)], writes=[('ACC%d' % nxt, d)])
                                yield

                    def chain_gen(d, n):
                        js = batches[d][n]
                        par = n % 2
                        jmin = min(js)
                        A_, K_ = ART[d][par], KBT[d][par]
                        kA, kK = ('ART', d, par), ('KBT', d, par)
                        MT_, TT_ = MT[d][par], TTt[d][par]
                        SS_, SSb_ = SS[d], SSb[d]
                        for step, j in enumerate(js):
                            jl = j - jmin
                            tb = step % 2
                            isx = j >= 4
                            jx = j - 4
                            pt, xb, ub = PT[d][tb], XB[d][tb], UB_[d][tb]
                            ktt = ('TTt', d, par, (step // 4) * 4)
                            first_y = isx and (jx not in yb_written)
                            if isx:
                                yb_written.add(jx)
                            H = []
                            for h in range(2):
                                hp = 64 * h
                                H.append(dict(hp=hp, pb=ps[4 + 2 * d + h], kpb='ps%d' % (4 + 2 * d + h), kpt=('PT', d, tb, h), kxb=('XB', d, tb, h),
                                              kub=('UB', d, tb, h), kS=('SS', d, h), kSb=('SSb', d, h),
                                              kmt=('MT', d, par, (step // 2) * 2, hp), e0=('act', 'dve')[h], e1=('dve', 'act')[h]))

                            def cp(eng, out, in_, reads, writes):
                                if eng == 'act':
                                    S.op('act', lambda e: e.copy(out=out, in_=in_), reads=reads, writes=writes)
                                else:
                                    S.op('dve', lambda e: e.tensor_copy(out=out, in_=in_), reads=reads, writes=writes)
                            for x in H:
                                hp, pb = x['hp'], x['pb']
                                S.mm(pb[hp:hp + 64, 0:64], lhsT=ZV[hp:hp + 64, j * 64:(j + 1) * 64], rhs=IDB[hp:hp + 64, hp:hp + 64],
                                     start=True, stop=True, reads=['ZV', 'IDB'], writes=[x['kpb']])
                                S.mm(pb[hp:hp + 64, 64:128], lhsT=K_[hp:hp + 64, jl, 1, :], rhs=IDB[hp:hp + 64, hp:hp + 64],
                                     start=True, stop=True, reads=[kK, 'IDB'], writes=[x['kpb']])
                                S.mm(pb[hp:hp + 64, 128:192], lhsT=K_[hp:hp + 64, jl, 0, :], rhs=IDB[hp:hp + 64, hp:hp + 64],
                                     start=True, stop=True, reads=[kK, 'IDB'], writes=[x['kpb']])
                            for x in H:
                                hp, pb = x['hp'], x['pb']
                                cp(x['e0'], pt[hp:hp + 64, :, :].rearrange("p a b -> p (a b)"), pb[hp:hp + 64, 0:192], [x['kpb']], [x['kpt']])
                            yield
                            for x in H:
                                hp, pb = x['hp'], x['pb']
                                S.mm(pb[hp:hp + 64, 192:256], lhsT=A_[hp:hp + 64, jl, 0, :], rhs=SSb_[hp:hp + 64, :], start=True, stop=False,
                                     reads=[kA, x['kSb']], writes=[x['kpb']])
                                S.mm(pb[hp:hp + 64, 192:256], lhsT=MT_[hp:hp + 64, step, 0:64], rhs=pt[hp:hp + 64, 0, :], start=False, stop=True,
                                     reads=[x['kmt'], x['kpt']], writes=[x['kpb']])
                            for x in H:
                                hp, pb = x['hp'], x['pb']
                                cp(x['e0'], xb[hp:hp + 64, :], pb[hp:hp + 64, 192:256], [x['kpb']], [x['kxb']])
                            yield
                            for x in H:
                                hp, pb = x['hp'], x['pb']
                                S.mm(pb[hp:hp + 64, 256:320], lhsT=TT_[hp:hp + 64, step, hp:hp + 64], rhs=xb[hp:hp + 64, :], start=True, stop=True,
                                     reads=[ktt, x['kxb']], writes=[x['kpb']])
                            for x in H:
                                hp, pb = x['hp'], x['pb']
                                cp(x['e1'], ub[hp:hp + 64, :], pb[hp:hp + 64, 256:320], [x['kpb']], [x['kub']])
                            yield
                            if isx:
                                for x in H:
                                    hp, pb = x['hp'], x['pb']
                                    S.mm(pb[hp:hp + 64, 320:384], lhsT=A_[hp:hp + 64, jl, 1, :], rhs=SSb_[hp:hp + 64, :], start=True, stop=False,
                                         reads=[kA, x['kSb']], writes=[x['kpb']])
                                    S.mm(pb[hp:hp + 64, 320:384], lhsT=MT_[hp:hp + 64, step, 192:256], rhs=ub[hp:hp + 64, :], start=False, stop=False,
                                         reads=[x['kmt'], x['kub']], writes=[x['kpb']])
                                    S.mm(pb[hp:hp + 64, 320:384], lhsT=MT_[hp:hp + 64, step, 64:128], rhs=pt[hp:hp + 64, 0, :], start=False, stop=True,
                                         reads=[x['kmt'], x['kpt']], writes=[x['kpb']])
                                for x in H:
                                    hp, pb = x['hp'], x['pb']
                                    if first_y:
                                        cp(x['e1'], YB[hp:hp + 64, jx, :], pb[hp:hp + 64, 320:384], [x['kpb']], [('YB', jx, hp)])
                                    else:
                                        S.op('dve', lambda e, hp=hp, pb=pb: e.tensor_tensor(out=YB[hp:hp + 64, jx, :], in0=pb[hp:hp + 64, 320:384],
                                                                                          in1=YB[hp:hp + 64, jx, :], op=ALU.add),
                                             reads=[x['kpb'], ('YB', jx, hp)], writes=[('YB', jx, hp)])
                            for x in H:
                                hp, pb = x['hp'], x['pb']
                                S.mm(pb[hp:hp + 64, 384:448], lhsT=pt[hp:hp + 64, 1, :], rhs=ub[hp:hp + 64, :], start=True, stop=False,
                                     reads=[x['kpt'], x['kub']], writes=[x['kpb']])
                                S.mm(pb[hp:hp + 64, 384:448], lhsT=pt[hp:hp + 64, 2, :], rhs=pt[hp:hp + 64, 0, :], start=False, stop=True,
                                     reads=[x['kpt']], writes=[x['kpb']])
                            for x in H:
                                hp, pb = x['hp'], x['pb']
                                S.op('dve', lambda e, hp=hp: e.tensor_scalar(out=SS_[hp:hp + 64, :], in0=SS_[hp:hp + 64, :],
                                                                             scalar1=WC[d][hp:hp + 64, j:j + 1], scalar2=None, op0=ALU.mult),
                                     reads=[x['kS'], ('WC', d)], writes=[x['kS']])
                                S.op('dve', lambda e, hp=hp, pb=pb: e.scalar_tensor_tensor(
                                    out=SS_[hp:hp + 64, :], in0=pb[hp:hp + 64, 384:448], scalar=WC[d][hp:hp + 64, j:j + 1],
                                    in1=SS_[hp:hp + 64, :], op0=ALU.mult, op1=ALU.add), reads=[x['kpb'], x['kS'], ('WC', d)], writes=[x['kS']])
                                S.op('act', lambda e, hp=hp: e.copy(out=SSb_[hp:hp + 64, :], in_=SS_[hp:hp + 64, :]), reads=[x['kS']],
                                     writes=[x['kSb']])
                            if debug and j == (3 if d == 0 else 0):
                                S.op('dve', lambda e: e.tensor_copy(out=ST8[:, d, c, :], in_=SS_[:]), reads=[('SS', d, 0), ('SS', d, 1)],
                                     writes=['ST8'])
                            yield

                    def run_threads(ths):
                        ths = list(ths)
                        while ths:
                            for g in list(ths):
                                try:
                                    next(g)
                                except StopIteration:
                                    ths.remove(g)

                    import itertools as _it
                    run_threads([_it.chain(prep_gen(d, 0), stage1_gen(d, 0)) for d in range(2)])
                    for n in range(NB):
                        ths = []
                        for d in range(2):
                            ths.append(chain_gen(d, n))
                            if n + 1 < NB:
                                ths.append(_it.chain(prep_gen(d, n + 1), stage1_gen(d, n + 1)))
                        run_threads(ths)
                    cut(8)
                    YSQ = SH[:, 0:4096].rearrange("p (j v) -> p j v", v=64)
                    YNb = PL[:, 0:4096].rearrange("p (j v) -> p j v", v=64)
                    SUM, SSQ, MEAN, RSTD = TRS[:, 0:64], TRS[:, 64:128], TRS[:, 128:192], TRS[:, 192:256]
                    S.op('dve', lambda e: e.tensor_reduce(out=SUM, in_=YB[:], axis=AX.X, op=ALU.add),
                         reads=[('YB', jx, hp_) for jx in range(64) for hp_ in (0, 64)], writes=['TR8'])
                    S.op('act', lambda e: e.activation(out=YSQ, in_=YB[:], func=AF.Square), reads=[('YB', jx, hp_) for jx in range(64) for hp_ in (0, 64)],
                         writes=['SH'])
                    S.op('dve', lambda e: e.tensor_reduce(out=SSQ, in_=YSQ, axis=AX.X, op=ALU.add), reads=['SH'], writes=['TR8'])
                    S.op('dve', lambda e: e.tensor_scalar(out=MEAN, in0=SUM, scalar1=1.0 / 64, scalar2=None, op0=ALU.mult),
                         reads=['TR8'], writes=['TR8'])
                    S.op('dve', lambda e: e.tensor_tensor(out=SUM, in0=MEAN, in1=MEAN, op=ALU.mult), reads=['TR8'], writes=['TR8'])
                    S.op('dve', lambda e: e.scalar_tensor_tensor(out=SSQ, in0=SSQ, scalar=1.0 / 64, in1=SUM, op0=ALU.mult,
                                                                 op1=ALU.subtract), reads=['TR8'], writes=['TR8'])
                    S.op('dve', lambda e: e.tensor_scalar(out=SSQ, in0=SSQ, scalar1=LN_X_EPS, scalar2=None, op0=ALU.add),
                         reads=['TR8'], writes=['TR8'])
                    S.op('act', lambda e: e.activation(out=RSTD, in_=SSQ, func=AF.Sqrt), reads=['TR8'], writes=['TR8'])
                    S.op('dve', lambda e: e.reciprocal(out=RSTD, in_=RSTD), reads=['TR8'], writes=['TR8'])
                    S.op('dve', lambda e: e.tensor_tensor(out=YB[:], in0=YB[:], in1=MEAN.unsqueeze(2).to_broadcast([128, 64, 64]),
                                                          op=ALU.subtract), reads=['TR8'] + [('YB', jx, hp_) for jx in range(64) for hp_ in (0, 64)],
                         writes=[('YB', jx, hp_) for jx in range(64) for hp_ in (0, 64)])
                    S.op('dve', lambda e: e.tensor_tensor(out=YNb, in0=YB[:], in1=RSTD.unsqueeze(2).to_broadcast([128, 64, 64]),
                                                          op=ALU.mult), reads=['TR8'] + [('YB', jx, hp_) for jx in range(64) for hp_ in (0, 64)],
                         writes=['PL'])
                    for ti, (t0, m) in enumerate(seq_tiles(T)):
                        g0 = TC + t0
                        for hp in (0, 64):
                            for q in range(8):
                                jx = ti * 8 + q
                                S.mm(
                                    ps[0][hp:hp + 64, q * 64:(q + 1) * 64], lhsT=YNb[hp:hp + 64, jx, :], rhs=IDB[hp:hp + 64, hp:hp + 64],
                                    start=True, stop=True, reads=['PL', 'IDB'], writes=['ps0'], inc=(q == 7 and hp == 64))
                        S.op('dve', lambda e, g0=g0, m=m: e.scalar_tensor_tensor(
                            out=TR[0][:, :m], in0=ZR[:, g0:g0 + m], scalar=PV[:, rko + c:rko + c + 1], in1=ZK[:, g0:g0 + m],
                            op0=ALU.mult, op1=ALU.mult), reads=['ZR', 'ZK'], writes=['TR0'])
                        S.mm(ps[1][:, :m], lhsT=BONES[:], rhs=TR[0][:, :m], start=True, stop=True,
                             reads=['TR0', 'BONES'], writes=['ps1'])
                        S.op('dve', lambda e, g0=g0, m=m: e.tensor_tensor(out=TR[1][:, :m], in0=ps[1][:, :m], in1=ZV[:, g0:g0 + m],
                                                                          op=ALU.mult), reads=['ps1', 'ZV'], writes=['TR1'])
                        S.dma('sp', GO[0][:, :m], GSv[:, c, t0:t0 + m], reads=[('GS', c, t0)], writes=['GO0'])
                        S.op('dve', lambda e, m=m: e.tensor_scalar(out=TR[2][:, :m], in0=ps[0][:, :m], scalar1=PV[:, lwo + c:lwo + c + 1],
                                                                   scalar2=PV[:, lbo + c:lbo + c + 1], op0=ALU.mult, op1=ALU.add),
                             reads=['ps0'], writes=['TR2'])
                        S.op('pool', lambda e, m=m: e.tensor_tensor(out=TR[2][:, :m], in0=TR[2][:, :m], in1=TR[1][:, :m], op=ALU.add),
                             reads=['TR2', 'TR1'], writes=['TR2'])
                        S.op('pool', lambda e, m=m: e.tensor_tensor(out=GO[1][:, :m], in0=TR[2][:, :m], in1=GO[0][:, :m], op=ALU.mult),
                             reads=['TR2', 'GO0'], writes=['GO1'])
                        S.dma('pool', YRWv[:, c, t0:t0 + m], GO[1][:, :m], reads=['GO1'], writes=[('YRW', c, t0)])
                if debug:
                    S.dma('sp', SFD[:, :, :, :], ST8[:], reads=['ST8'], writes=['SFD'])
                S.barrier()

        with ExitStack() as ph_r:
            try:
                _rwkv_body(ph_r)
            except _Cut:
                print("CUT at", RCUT, "ops", S.nops)
            S.barrier()
        if RCUT:
            return nc

        if phases is not None and "C" not in phases:
            return nc

        def phase_merge():
            with ExitStack() as ph:
                WP = [sb("WP%d" % i, [128, 8, D], BF16, st=ph) for i in range(3)]
                STG = sb("STGC", [128, 2, D], st=ph)
                YR = sb("YRt", [128, 8, 512], BF16, st=ph)
                YLt = sb("YLt", [128, 8, 512], BF16, st=ph)
                SM = sb("SMt", [128, 16, 512], BF16, st=ph)
                X1t = sb("X1t", [128, 8, 512], st=ph)
                MG = sb("MGt", [128, 8, 512], BF16, st=ph)
                TA = [sb("TAc%d" % i, [128, 512], st=ph) for i in range(2)]
                ci = 0
                for wi, wsrc in enumerate((wpr_d, wpl_d, wo_d)):
                    for k in range(8):
                        i = ci % 2
                        ci += 1
                        S.dma('sp' if i == 0 else 'pool', STG[:, i, :], wsrc[k * 128:(k + 1) * 128, :], writes=['stg%d' % i])
                        S.op('dve' if i == 0 else 'act',
                             (lambda e, wi=wi, k=k, i=i: e.tensor_copy(out=WP[wi][:, k, :], in_=STG[:, i, :])) if i == 0 else
                             (lambda e, wi=wi, k=k, i=i: e.copy(out=WP[wi][:, k, :], in_=STG[:, i, :])), reads=['stg%d' % i])
                S.barrier()
                for (t0, n) in seq_tiles(T):
                    g0 = TC + t0
                    for c in range(8):
                        S.dma('sp', YR[:, c, :n], YRWv[:, c, t0:t0 + n], writes=[('YR', c)])
                        S.dma('pool', YLt[:, c, :n], YLv[:, c, t0:t0 + n], writes=[('YLt', c)])
                        S.dma('sp', X1t[:, c, :n], X1v[:, c, g0:g0 + n], writes=[('X1t', c)])
                    for c in range(16):
                        S.dma('pool' if c % 2 else 'sp', SM[:, c, :n], Pv[:, 43 + c, g0:g0 + n], writes=[('SM', c)])
                    for o in range(8):
                        pa, pb = ps[(o % 2) * 2], ps[(o % 2) * 2 + 1]
                        ka, kb = 'ps%d' % ((o % 2) * 2), 'ps%d' % ((o % 2) * 2 + 1)
                        for k in range(8):
                            S.mm(pa[:, :n], lhsT=WP[0][:, k, o * 128:(o + 1) * 128], rhs=YR[:, k, :n], start=(k == 0), stop=(k == 7),
                                 reads=[('YR', k)], writes=[ka], inc=(k == 7))
                        for k in range(8):
                            S.mm(pb[:, :n], lhsT=WP[1][:, k, o * 128:(o + 1) * 128], rhs=YLt[:, k, :n], start=(k == 0), stop=(k == 7),
                                 reads=[('YLt', k)], writes=[kb], inc=(k == 7))
                        S.op('dve', lambda e, pa=pa, o=o: e.tensor_tensor(out=TA[0][:, :n], in0=pa[:, :n], in1=SM[:, o, :n], op=ALU.mult),
                             reads=[ka, ('SM', o)], writes=['TA0'])
                        S.op('dve', lambda e, pb=pb, o=o: e.tensor_tensor(out=TA[1][:, :n], in0=pb[:, :n], in1=SM[:, 8 + o, :n], op=ALU.mult),
                             reads=[kb, ('SM', 8 + o)], writes=['TA1'])
                        S.op('pool', lambda e, o=o: e.tensor_tensor(out=MG[:, o, :n], in0=TA[0][:, :n], in1=TA[1][:, :n], op=ALU.add),
                             reads=['TA0', 'TA1'], writes=[('MG', o)])
                    for o in range(8):
                        pc_ = ps[4 + o % 2]
                        kc_ = 'ps%d' % (4 + o % 2)
                        for k in range(8):
                            S.mm(pc_[:, :n], lhsT=WP[2][:, k, o * 128:(o + 1) * 128], rhs=MG[:, k, :n], start=(k == 0), stop=(k == 7),
                                 reads=[('MG', k)], writes=[kc_], inc=(k == 7))
                        S.op('dve', lambda e, pc_=pc_, o=o: e.scalar_tensor_tensor(
                            out=X1t[:, o, :n], in0=pc_[:, :n], scalar=SCAL[:, 0, 5, o:o + 1], in1=X1t[:, o, :n], op0=ALU.mult,
                            op1=ALU.add), reads=[kc_, ('X1t', o)], writes=[('X1t', o)])
                        S.dma('pool', X2v[:, o, t0:t0 + n], X1t[:, o, :n], reads=[('X1t', o)], writes=[('X2', o, t0)])
                S.barrier()

        phase_merge()

        def load_feat_major(S, SCR, XT, s_, t0, n):
            for c in range(8):
                S.dma('sp' if c % 2 == 0 else 'pool', XT[:, c, :n], X2v[:, c, t0:t0 + n], writes=[('XT', c)])

        def epi_final(S, XT, HN, TM, RS, rmsnorm_mod, s_, t0, n, SCR=None):
            rmsnorm_mod(n, s_, 0, XT, ia=9, ib=10, dk='XT')
            for b_ in range(n // 128):
                for c in range(8):
                    pp = ps[6 + (c // 4) % 2]
                    S.mm(pp[:, (c % 4) * 128:(c % 4 + 1) * 128], lhsT=XT[:, c, b_ * 128:(b_ + 1) * 128], rhs=ident[:],
                         start=True, stop=True, reads=[('XT', c), 'ident'], writes=['ps%d' % (6 + (c // 4) % 2)], inc=(c % 4 == 3))
                    if c % 4 == 3:
                        h_ = c // 4
                        S.op('act' if h_ == 0 else 'dve',
                             (lambda e, pp=pp, b_=b_, h_=h_: e.copy(out=SCR[:, b_ * 1024 + h_ * 512:b_ * 1024 + (h_ + 1) * 512], in_=pp[:, 0:512]))
                             if h_ == 0 else
                             (lambda e, pp=pp, b_=b_, h_=h_: e.tensor_copy(out=SCR[:, b_ * 1024 + h_ * 512:b_ * 1024 + (h_ + 1) * 512], in_=pp[:, 0:512])),
                             reads=['ps%d' % (6 + h_)], writes=[('XIN', b_)])
                S.dma('pool', out_d[t0 + b_ * 128:t0 + (b_ + 1) * 128, :], SCR[:, b_ * 1024:(b_ + 1) * 1024], reads=[('XIN', b_)],
                      writes=[('OUT', t0, b_)])

        tiles2 = [(0, i * 512, 512) for i in range(T // 512)]
        ffn_phase("ffn2", tiles2, load_feat_major, epi_final)

        S.barrier()
        print("ops emitted", S.nops, S.cnt, S.dn)
    return nc


_CACHE = {}


def _prep_inputs(inputs, b):
    f = lambda a: np.ascontiguousarray(a, dtype=np.float32)
    m = {"x": f(inputs["x"][b]), "ctx": f(inputs["ctx"][b])}
    src = dict(inputs)
    src["c"] = inputs["c"][b]
    for n, r in VEC_ROWS:
        m[n] = f(np.asarray(src[n]).reshape(r, 128))
    m["w_mod"] = f(inputs["w_mod"][0])
    m["w_in"] = f(inputs["w_in"][0])
    m["lru_wa"] = f(inputs["lru_wa"][0])
    m["lru_wx"] = f(inputs["lru_wx"][0])
    m["rw_w_up"] = f(inputs["rw_w_up"][0].reshape(128, D))
    m["w_proj_rw"] = f(inputs["w_proj_rw"][0])
    m["w_proj_lru"] = f(inputs["w_proj_lru"][0])
    m["w_out"] = f(inputs["w_out"][0])
    m["rw_a_up"] = f(inputs["rw_a_up"][0].reshape(128, D))
    m["rw_g_up"] = f(inputs["rw_g_up"][0])
    for fn_ in ("ffn1", "ffn2"):
        for s in ("_wg", "_wu", "_wd"):
            m[fn_ + s] = f(inputs[fn_ + s][0])
    return m


def kernel(**inputs):
    if "nc" not in _CACHE:
        _CACHE["nc"] = build()
    nc = _CACHE["nc"]
    in_maps = [_prep_inputs(inputs, b % 4) for b in range(N_CORES)]
    res = run_bass_kernel_spmd(nc, in_maps, core_ids=list(range(N_CORES)))
    out = np.stack([res.results[b]["out"] for b in range(4)], axis=0)
    return out.astype(np.float32)
```

```python
import numpy as np
from contextlib import ExitStack
import concourse.bass as bass
import concourse.mybir as mybir
from concourse.bass_utils import run_bass_kernel_spmd

F32 = mybir.dt.float32
BF16 = mybir.dt.bfloat16
AF = mybir.ActivationFunctionType
ALU = mybir.AluOpType
AX = mybir.AxisListType

D = 1024
T = 4096
TC = 256
TT = T + TC
DFF = 2816
NJ = DFF // 128
NIN = 7552
NRW = 3456
EPS = 1e-6
N_CORES = 8


class Sched:
    R = 8

    def __init__(self, nc, es):
        self.nc = nc
        self.eng = {'pe': nc.tensor, 'act': nc.scalar, 'dve': nc.vector, 'pool': nc.gpsimd, 'sp': nc.sync}
        self.sem = {k: es.enter_context(nc.semaphore('s_' + k)) for k in ('pe', 'act', 'dve', 'pool')}
        self.cnt = {k: 0 for k in self.sem}
        self.dsem = {q: [es.enter_context(nc.semaphore('d_%s%d' % (q, i))) for i in range(self.R)]
                     for q in ('sp', 'pool')}
        self.dn = {q: 0 for q in self.dsem}
        self.waited = {e: {} for e in self.eng}
        self.lastw = {}
        self.readers = {}
        self.nops = 0
        self.pe_last = {}

    def _semof(self, k):
        return self.sem[k] if isinstance(k, str) else self.dsem[k[0]][k[1]]

    def _wait(self, e, k, val):
        if val <= 0 or self.waited[e].get(k, 0) >= val:
            return
        self.waited[e][k] = val
        self.eng[e].wait_ge(self._semof(k), val)

    def _deps(self, e, reads, writes):
        need = {}
        raw = {}
        for r in reads:
            t = self.lastw.get(r)
            if t is not None:
                if need.get(t[0], 0) < t[1]:
                    need[t[0]] = t[1]
                if raw.get(t[0], 0) < t[1]:
                    raw[t[0]] = t[1]
            if isinstance(r, str) and r.startswith('ps'):
                for k, v in self.readers.get(r, {}).items():
                    if k != e and need.get(k, 0) < v:
                        need[k] = v
        for w in writes:
            t = self.lastw.get(w)
            if t is not None and need.get(t[0], 0) < t[1]:
                need[t[0]] = t[1]
            for k, v in self.readers.get(w, {}).items():
                if need.get(k, 0) < v:
                    need[k] = v
        for k, v in need.items():
            if k == e:
                if e == 'pe':
                    continue
                if e in ('act', 'dve'):
                    v = raw.get(k, 0)
                    if v <= 0:
                        continue
            self._wait(e, k, v)

    def _commit(self, tok, reads, writes):
        for w in writes:
            self.lastw[w] = tok
            self.readers[w] = {}
        ws = set(writes)
        for r in reads:
            if r in ws:
                continue
            d = self.readers.setdefault(r, {})
            if d.get(tok[0], 0) < tok[1]:
                d[tok[0]] = tok[1]

    def mm(self, out, lhsT, rhs, start=True, stop=True, reads=(), writes=(), inc=True):
        rg = lhsT.base_partition() if lhsT.partition_size() < 128 else None
        self.op('pe', lambda e: e.matmul(out, lhsT=lhsT, rhs=rhs, start=start, stop=stop), reads, writes, inc, rg=rg)

    def op(self, e, fn, reads=(), writes=(), inc=True, rg=None):
        if e == 'pe':
            if rg is not None:
                inc = True
                for w in writes:
                    pl = self.pe_last.get(w)
                    if pl is not None and pl[0] != rg:
                        self._wait('pe', 'pe', pl[1])
            else:
                for w in writes:
                    self.pe_last.pop(w, None)
        self._deps(e, reads, writes)
        ins = fn(self.eng[e])
        self.nops += 1
        if inc:
            self.cnt[e] += 1
            ins.then_inc(self.sem[e], 1)
            tok = (e, self.cnt[e])
        else:
            assert e == 'pe'
            tok = (e, self.cnt[e] + 1)
        if e == 'pe' and rg is not None:
            for w in writes:
                self.pe_last[w] = (rg, tok[1])
        self._commit(tok, reads, writes)

    def dma(self, q, out, in_, reads=(), writes=(), **kw):
        n = self.dn[q]
        i = n % self.R
        tgt = 16 * (n // self.R + 1)
        self._wait(q, (q, i), tgt - 16)
        self._deps(q, reads, writes)
        self.eng[q].dma_start(out=out, in_=in_, **kw).then_inc(self.dsem[q][i], 16)
        self.nops += 1
        self.dn[q] += 1
        self._commit(((q, i), tgt), reads, writes)

    def barrier(self):
        for e in self.eng:
            for k in self.sem:
                self._wait(e, k, self.cnt[k])
            for q in self.dsem:
                for i in range(self.R):
                    n_i = (self.dn[q] - i + self.R - 1) // self.R
                    self._wait(e, (q, i), 16 * n_i)
        self.lastw = {}
        self.readers = {}


VEC_ROWS = [("b_mod", 72), ("g_ffn1", 8), ("g_mix", 8), ("g_ffn2", 8), ("g_final", 8), ("rw_mu", 27),
            ("rw_w0", 16), ("rw_a0", 16), ("rw_k_k", 8), ("rw_k_a", 8), ("rw_r_k", 8), ("rw_ln_w", 8),
            ("rw_ln_b", 8), ("lru_conv_w", 32), ("lru_conv_b", 8), ("lru_lam", 16), ("lru_ba", 16),
            ("lru_bx", 16), ("c", 8), ("c_ctx", 8)]
VEC_OFF = {}
_o = 0
for _n, _r in VEC_ROWS:
    VEC_OFF[_n] = _o
    _o += _r
NVEC = _o
NVT = (NVEC + 127) // 128


def build(debug=False, phases=None):
    nc = bass.Bass("TRN2", target_bir_lowering=False)
    dram_in = {}

    def din(name, shape, dt=F32):
        dram_in[name] = nc.dram_tensor(name, list(shape), dt, kind="ExternalInput").ap()
        return dram_in[name]

    x_d = din("x", [T, D])
    ctx_d = din("ctx", [TC, D])
    for n, r in VEC_ROWS:
        din(n, [r, 128])
    w_mod = din("w_mod", [D, 9 * D])
    wts = {}
    for f in ("ffn1", "ffn2"):
        wts[f + "_wg"] = din(f + "_wg", [D, DFF])
        wts[f + "_wu"] = din(f + "_wu", [D, DFF])
        wts[f + "_wd"] = din(f + "_wd", [DFF, D])
    w_in_d = din("w_in", [D, NIN])
    lru_wa_d = din("lru_wa", [2, 4, 256, 256])
    lru_wx_d = din("lru_wx", [2, 4, 256, 256])
    wpr_d = din("w_proj_rw", [D, D])
    wpl_d = din("w_proj_lru", [D, D])
    wo_d = din("w_out", [D, D])
    rw_w_up_d = din("rw_w_up", [128, D])
    rw_a_up_d = din("rw_a_up", [128, D])
    rw_g_up_d = din("rw_g_up", [128, D])
    out_d = nc.dram_tensor("out", [T, D], F32, kind="ExternalOutput").ap()
    skind = "ExternalOutput" if debug else "Internal"
    X1 = nc.dram_tensor("X1", [D, TT], F32, kind=skind).ap()
    XN = nc.dram_tensor("XN", [D, TT], BF16, kind=skind).ap()
    MODD = nc.dram_tensor("MODD", [128, 144], F32, kind=skind).ap()
    P = nc.dram_tensor("P", [NIN, TT], BF16, kind=skind).ap()
    YL = nc.dram_tensor("YL", [D, T], BF16, kind=skind).ap()
    HSTD = nc.dram_tensor("HSTD", [128, 16], F32, kind=skind).ap()
    GS = nc.dram_tensor("GS", [D, T], BF16, kind=skind).ap()
    YRW = nc.dram_tensor("YRW", [D, T], BF16, kind=skind).ap()
    SFD = nc.dram_tensor("SFD", [128, 2, 8, 64], F32, kind=skind).ap()
    X2 = nc.dram_tensor("X2", [D, T], F32, kind=skind).ap()
    X2v = X2.rearrange("(c p) t -> p c t", p=128)
    GSv = GS.rearrange("(c p) t -> p c t", p=128)
    YRWv = YRW.rearrange("(c p) t -> p c t", p=128)
    Pv = P.rearrange("(c p) t -> p c t", p=128)
    YLv = YL.rearrange("(c p) t -> p c t", p=128)

    with ExitStack() as es:
        S = Sched(nc, es)
        ps = [es.enter_context(nc.psum_tensor("ps%d" % i, [128, 512], F32)) for i in range(8)]
        sb = lambda name, shape, dt=F32, st=es: st.enter_context(nc.sbuf_tensor(name, list(shape), dt))
        ident = sb("ident", [128, 128])
        ones = sb("ones", [128, 128])
        PV = sb("PV", [128, NVT * 128])
        MOD = sb("MOD", [128, 72, 2])
        SCAL = sb("SCAL", [128, 2, 11, 8])

        def pv(name, i=0):
            o = VEC_OFF[name] + i
            return PV[:, o:o + 1]

        S.op('pool', lambda e: e.memset(ident[:], 1.0), writes=['ident'])
        S.op('pool', lambda e: e.affine_select(out=ident[:], in_=ident[:], pattern=[[-1, 128]],
                                               compare_op=ALU.is_equal, fill=0.0, base=0, channel_multiplier=1),
             reads=['ident'], writes=['ident'])
        S.op('pool', lambda e: e.memset(ones[:], 1.0), writes=['ones'])
        onesb = sb("onesb", [128, 128], BF16)
        S.op('pool', lambda e: e.memset(onesb[:], 1.0), writes=['onesb'])

        with ExitStack() as p0:
            VST = sb("VST", [128, NVT, 128], st=p0)
            SC = sb("SC", [128, 8, 2], st=p0)
            slab = [sb("slab%d" % i, [128, 8, 1152], st=p0) for i in range(2)]
            S.op('dve', lambda e: e.memset(VST[:], 0.0), writes=['VST'])
            for n, r in VEC_ROWS:
                o = VEC_OFF[n]
                done = 0
                while done < r:
                    ti, ri = divmod(o + done, 128)
                    m = min(r - done, 128 - ri)
                    S.dma('sp', VST[ri:ri + m, ti, :], dram_in[n][done:done + m, :], reads=[], writes=['VST'])
                    done += m
            for ti in range(NVT):
                S.mm(ps[7][:, 0:128], lhsT=VST[:, ti, :], rhs=ident[:],
                                                     start=True, stop=True,
                     reads=['VST', 'ident'], writes=['ps7'])
                S.op('dve', lambda e, ti=ti: e.tensor_copy(out=PV[:, ti * 128:(ti + 1) * 128], in_=ps[7][:, 0:128]),
                     reads=['ps7'], writes=['PV'])
            oc_, occ = VEC_OFF["c"], VEC_OFF["c_ctx"]
            S.op('act', lambda e: e.activation(out=SC[:, :, 0], in_=PV[:, oc_:oc_ + 8], func=AF.Silu),
                 reads=['PV'], writes=['SC'])
            S.op('act', lambda e: e.activation(out=SC[:, :, 1], in_=PV[:, occ:occ + 8], func=AF.Silu),
                 reads=['PV'], writes=['SC'])
            wmv = w_mod.rearrange("(k p) n -> p k n", p=128)
            for si in range(8):
                sl = slab[si % 2]
                key = 'slab%d' % (si % 2)
                for k in range(8):
                    S.dma('sp' if k % 2 == 0 else 'pool', sl[:, k, :], wmv[:, k, si * 1152:(si + 1) * 1152],
                          writes=[key])
                for ol in range(9):
                    oc = si * 9 + ol
                    for k in range(8):
                        S.mm(
                            ps[6][:, oc * 2:oc * 2 + 2], lhsT=sl[:, k, ol * 128:(ol + 1) * 128], rhs=SC[:, k, :],
                            start=(k == 0), stop=(k == 7),
                            reads=[key, 'SC'], writes=['ps6'], inc=(k == 7))
            bo = VEC_OFF["b_mod"]
            psv = ps[6][:, 0:144].rearrange("p (a b) -> p a b", b=2)
            for s_ in range(2):
                S.op('dve', lambda e, s_=s_: e.tensor_tensor(out=MOD[:, :, s_], in0=psv[:, :, s_],
                                                             in1=PV[:, bo:bo + 72], op=ALU.add),
                     reads=['ps6', 'PV'], writes=['MOD'])
            for s_ in range(2):
                for gi, gname in enumerate(("g_ffn1", "g_mix", "g_ffn2")):
                    go = VEC_OFF[gname]
                    sh = MOD[:, (3 * gi) * 8:(3 * gi + 1) * 8, s_]
                    scl = MOD[:, (3 * gi + 1) * 8:(3 * gi + 2) * 8, s_]
                    gt = MOD[:, (3 * gi + 2) * 8:(3 * gi + 3) * 8, s_]
                    S.op('dve', lambda e, s_=s_, gi=gi, scl=scl, go=go: e.scalar_tensor_tensor(
                        out=SCAL[:, s_, 3 * gi, :], in0=scl, scalar=1.0, in1=PV[:, go:go + 8],
                        op0=ALU.add, op1=ALU.mult), reads=['MOD', 'PV'], writes=['SCAL'])
                    S.op('dve', lambda e, s_=s_, gi=gi, sh=sh: e.tensor_copy(out=SCAL[:, s_, 3 * gi + 1, :], in_=sh),
                         reads=['MOD'], writes=['SCAL'])
                    S.op('dve', lambda e, s_=s_, gi=gi, gt=gt: e.tensor_scalar(
                        out=SCAL[:, s_, 3 * gi + 2, :], in0=gt, scalar1=(1.0 if gi == 1 else 0.5), scalar2=None,
                        op0=ALU.mult), reads=['MOD'], writes=['SCAL'])
            gfo = VEC_OFF["g_final"]
            for s_ in range(2):
                S.op('dve', lambda e, s_=s_: e.tensor_copy(out=SCAL[:, s_, 9, :], in_=PV[:, gfo:gfo + 8]), reads=['PV'], writes=['SCAL'])
                S.op('dve', lambda e, s_=s_: e.memset(SCAL[:, s_, 10, :], 0.0), reads=['SCAL'], writes=['SCAL'])
            if debug:
                S.dma('sp', MODD[:, :], MOD[:].rearrange("p a b -> p (a b)"), reads=['MOD'], writes=['MODD'])
            S.barrier()

        def ffn_phase(which, tiles, load_tile, epilogue):
            with ExitStack() as ph:
                WG = sb("WG_" + which, [128, 8, DFF], BF16, st=ph)
                WU = sb("WU_" + which, [128, 8, DFF], BF16, st=ph)
                WD = sb("WD_" + which, [128, NJ, D], BF16, st=ph)
                SCR = sb("SCR_" + which, [128, 4096], st=ph)
                XT = sb("XT_" + which, [128, 8, 512], st=ph)
                HN = sb("HN_" + which, [128, 8, 512], BF16, st=ph)
                AA = sb("AA_" + which, [128, NJ, 512], BF16, st=ph)
                TM = [sb("TM_" + which + "%d" % i, [128, 512], st=ph) for i in range(2)]
                RS = sb("RS_" + which, [128, 512], st=ph)
                SQB = [sb("SQB_" + which + "%d" % i, [128, 512], BF16, st=ph) for i in range(2)]
                ci = [0]
                cast_eng = ('dve', 'pool', 'act')

                def load_cast(dst, src, w):
                    i = ci[0] % 2
                    ci[0] += 1
                    key = 'stg%d' % i
                    S.dma('sp' if i == 0 else 'pool', SCR[:, i * 2048:i * 2048 + w], src, writes=[key])
                    ce = cast_eng[ci[0] % 3]
                    if ce == 'act':
                        S.op('act', lambda e: e.copy(out=dst, in_=SCR[:, i * 2048:i * 2048 + w]), reads=[key])
                    else:
                        S.op(ce, lambda e: e.tensor_copy(out=dst, in_=SCR[:, i * 2048:i * 2048 + w]), reads=[key])
                for (Wt, wd_) in ((WG, wts[which + "_wg"]), (WU, wts[which + "_wu"])):
                    for k in range(8):
                        for h in range(2):
                            load_cast(Wt[:, k, h * 1408:(h + 1) * 1408], wd_[k * 128:(k + 1) * 128, h * 1408:(h + 1) * 1408],
                                      1408)
                wdd = wts[which + "_wd"]
                for j in range(NJ):
                    load_cast(WD[:, j, :], wdd[j * 128:(j + 1) * 128, :], 1024)
                S.barrier()

                def rmsnorm_mod(n, s_, gi, dst, ia=None, ib=None, dk='HN'):
                    ia = 3 * gi if ia is None else ia
                    ib = 3 * gi + 1 if ib is None else ib
                    for c in range(8):
                        sq = SQB[c % 2]
                        S.op('act', lambda e, c=c, sq=sq: e.activation(out=sq[:, :n], in_=XT[:, c, :n], func=AF.Square),
                             reads=[('XT', c)], writes=['SQB%d' % (c % 2)])
                        S.mm(ps[6][:, :n], lhsT=onesb[:], rhs=sq[:, :n], start=(c == 0), stop=(c == 7),
                             reads=['SQB%d' % (c % 2), 'onesb'], writes=['ps6'], inc=True)
                    S.op('act', lambda e: e.activation(out=RS[:, :n], in_=ps[6][:, :n], func=AF.Sqrt, bias=EPSB[:, 0:1],
                                                       scale=1.0 / D), reads=['ps6'], writes=['RS'])
                    S.op('dve', lambda e: e.reciprocal(out=RS[:, :n], in_=RS[:, :n]), reads=['RS'], writes=['RS'])
                    for c in range(8):
                        tm = TM[c % 2]
                        S.op('dve', lambda e, c=c, tm=tm: e.scalar_tensor_tensor(
                            out=tm[:, :n], in0=XT[:, c, :n], scalar=SCAL[:, s_, ia, c:c + 1], in1=RS[:, :n],
                            op0=ALU.mult, op1=ALU.mult), reads=[('XT', c), 'RS'], writes=['TM%d' % (c % 2)])
                        S.op('act', lambda e, c=c, tm=tm: e.activation(
                            out=dst[:, c, :n], in_=tm[:, :n], func=AF.Identity, bias=SCAL[:, s_, ib, c:c + 1],
                            scale=1.0), reads=['TM%d' % (c % 2)], writes=[(dk, c)])

                gi = 0 if which == "ffn1" else 2
                for (s_, t0, n) in tiles:
                    load_tile(S, SCR, XT, s_, t0, n)
                    rmsnorm_mod(n, s_, gi, HN)
                    for j in range(NJ):
                        pg, pu = ps[(j % 2) * 2], ps[(j % 2) * 2 + 1]
                        kg, ku = 'ps%d' % ((j % 2) * 2), 'ps%d' % ((j % 2) * 2 + 1)
                        for (pp, kk_, Wt) in ((pg, kg, WG), (pu, ku, WU)):
                            for k in range(8):
                                S.mm(
                                    pp[:, :n], lhsT=Wt[:, k, j * 128:(j + 1) * 128], rhs=HN[:, k, :n],
                                    start=(k == 0), stop=(k == 7),
                                    reads=[('HN', k)], writes=[kk_], inc=(k == 7))
                        tm = TM[j % 2]
                        S.op('act', lambda e, pg=pg, tm=tm: e.activation(out=tm[:, :n], in_=pg[:, :n], func=AF.Silu),
                             reads=[kg], writes=['TM%d' % (j % 2)])
                        S.op('dve', lambda e, pu=pu, tm=tm, j=j: e.tensor_tensor(out=AA[:, j, :n], in0=pu[:, :n],
                                                                                 in1=tm[:, :n], op=ALU.mult),
                             reads=[ku, 'TM%d' % (j % 2)], writes=[('AA', j)])
                    for o in range(8):
                        pd = ps[4 + o % 2]
                        kd = 'ps%d' % (4 + o % 2)
                        for j in range(NJ):
                            S.mm(
                                pd[:, :n], lhsT=WD[:, j, o * 128:(o + 1) * 128], rhs=AA[:, j, :n],
                                start=(j == 0), stop=(j == NJ - 1),
                                reads=[('AA', j)], writes=[kd], inc=(j == NJ - 1))
                        S.op('dve', lambda e, pd=pd, o=o: e.scalar_tensor_tensor(
                            out=XT[:, o, :n], in0=pd[:, :n], scalar=SCAL[:, s_, 3 * gi + 2, o:o + 1], in1=XT[:, o, :n],
                            op0=ALU.mult, op1=ALU.add), reads=[kd, ('XT', o)], writes=[('XT', o)])
                    epilogue(S, XT, HN, TM, RS, rmsnorm_mod, s_, t0, n, SCR=SCR)
                S.barrier()

        EPSB = sb("EPSB", [128, 1])
        S.op('pool', lambda e: e.memset(EPSB[:], EPS), writes=['EPSB'])
        MUSG = sb("MUSG", [128, 3, 27])
        _mo = VEC_OFF["rw_mu"]
        S.op('dve', lambda e: e.tensor_scalar(out=MUSG[:, 0, :], in0=PV[:, _mo:_mo + 27], scalar1=-1.0, scalar2=1.0, op0=ALU.mult,
                                              op1=ALU.add), reads=['PV'], writes=['MUSG'])
        S.op('dve', lambda e: e.tensor_scalar(out=MUSG[:, 1, :], in0=PV[:, _mo:_mo + 27], scalar1=0.25, scalar2=None, op0=ALU.mult),
             reads=['PV'], writes=['MUSG'])
        S.op('dve', lambda e: e.tensor_scalar(out=MUSG[:, 2, :], in0=PV[:, _mo:_mo + 27], scalar1=0.5, scalar2=None, op0=ALU.mult),
             reads=['PV'], writes=['MUSG'])

        def load_tok_major(S, SCR, XT, s_, t0, n):
            src = x_d if s_ == 0 else ctx_d
            nb = n // 128
            for b_ in range(nb):
                S.dma('sp', SCR[:, b_ * 1024:(b_ + 1) * 1024], src[t0 + b_ * 128:t0 + (b_ + 1) * 128, :],
                      writes=[('XIN', b_)])
            for c in range(8):
                for b_ in range(nb):
                    S.mm(
                        ps[7][:, b_ * 128:(b_ + 1) * 128], lhsT=SCR[:, b_ * 1024 + c * 128:b_ * 1024 + (c + 1) * 128],
                        rhs=ident[:], start=True, stop=True, reads=[('XIN', b_), 'ident'], writes=['ps7'],
                        inc=(b_ == nb - 1))
                S.op('act' if c % 2 == 0 else 'dve',
                     (lambda e, c=c: e.copy(out=XT[:, c, :n], in_=ps[7][:, :n])) if c % 2 == 0 else
                     (lambda e, c=c: e.tensor_copy(out=XT[:, c, :n], in_=ps[7][:, :n])),
                     reads=['ps7'], writes=[('XT', c)])

        X1v = X1.rearrange("(c p) t -> p c t", p=128)
        XNv = XN.rearrange("(c p) t -> p c t", p=128)

        def epi_ffn1(S, XT, HN, TM, RS, rmsnorm_mod, s_, t0, n, SCR=None):
            g0 = (TC if s_ == 0 else 0) + t0
            for c in range(8):
                S.dma('pool', X1v[:, c, g0:g0 + n], XT[:, c, :n], reads=[('XT', c)], writes=[('X1', g0)])
            rmsnorm_mod(n, s_, 1, HN)
            for c in range(8):
                S.dma('pool', XNv[:, c, g0:g0 + n], HN[:, c, :n], reads=[('HN', c)], writes=[('XN', g0)])

        tiles1 = [(1, 0, TC)] + [(0, i * 512, 512) for i in range(T // 512)]
        if phases is None or "A" in phases:
            ffn_phase("ffn1", tiles1, load_tok_major, epi_ffn1)
        S.barrier()

        import os as _os
        RCUT = int(_os.environ.get("RCUT", "0"))

        class _Cut(Exception):
            pass

        def cut(k):
            if RCUT == k:
                raise _Cut()

        def seq_tiles(n):
            return [(t0, min(512, n - t0)) for t0 in range(0, n, 512)]

        def phase_inproj():
            with ExitStack() as ph:
                WIN = sb("WIN", [128, 8, NIN], BF16, st=ph)
                SCRB = sb("SCRB", [128, 4096], st=ph)
                XNH = [sb("XNH%d" % i, [128, 8, 640], BF16, st=ph) for i in range(2)]
                XS = [sb("XSn%d" % i, [128, 8, 512], BF16, st=ph) for i in range(2)]
                XSF = [sb("XSF%d" % i, [128, 512], st=ph) for i in range(2)]
                TZ = [sb("TZ%d" % i, [128, 512], st=ph) for i in range(2)]
                PO = [sb("PO%d" % i, [128, 512], BF16, st=ph) for i in range(6)]
                ci = 0
                for k in range(8):
                    for c0 in range(0, NIN, 2048):
                        w = min(2048, NIN - c0)
                        i = ci % 2
                        ci += 1
                        key = 'stg%d' % i
                        S.dma('sp' if i == 0 else 'pool', SCRB[:, i * 2048:i * 2048 + w],
                              w_in_d[k * 128:(k + 1) * 128, c0:c0 + w], writes=[key])
                        ce = ('dve', 'pool', 'act')[ci % 3]
                        if ce == 'act':
                            S.op('act', lambda e, i=i, w=w, k=k, c0=c0: e.copy(out=WIN[:, k, c0:c0 + w],
                                                                               in_=SCRB[:, i * 2048:i * 2048 + w]), reads=[key])
                        else:
                            S.op(ce, lambda e, i=i, w=w, k=k, c0=c0: e.tensor_copy(out=WIN[:, k, c0:c0 + w],
                                                                                   in_=SCRB[:, i * 2048:i * 2048 + w]), reads=[key])
                S.barrier()
                tl = [(1, 0, TC)] + [(0, t0, n) for (t0, n) in seq_tiles(T)]
                it = 0
                iz = 0
                for ti_, (s_, t0, n) in enumerate(tl):
                    g0 = (TC if s_ == 0 else 0) + t0
                    H = XNH[ti_ % 2]
                    Xs = XS[ti_ % 2]
                    kH = lambda c: ('XNH', ti_ % 2, c)
                    kX = lambda c: ('XS', ti_ % 2, c)
                    has_top = (s_ == 0 and t0 > 0)
                    has_bot = (s_ == 0 and t0 + n < T)
                    if not has_top:
                        S.op('pool', lambda e, H=H: e.memset(H[:, :, 0:64], 0.0), reads=[kH(c) for c in range(8)],
                             writes=[kH(c) for c in range(8)])
                    if not has_bot:
                        S.op('pool', lambda e, H=H: e.memset(H[:, :, 64 + n:128 + n], 0.0), reads=[kH(c) for c in range(8)],
                             writes=[kH(c) for c in range(8)])
                    for c in range(8):
                        lo = g0 - (64 if has_top else 0)
                        hi = g0 + n + (64 if has_bot else 0)
                        S.dma('sp', H[:, c, 64 - (g0 - lo):64 + n + (hi - g0 - n)], XNv[:, c, lo:hi], reads=[kH(c)], writes=[kH(c)])
                    for c in range(8):
                        if s_ == 1:
                            S.op('dve' if c % 2 else 'pool', lambda e, c=c: e.tensor_tensor(
                                out=Xs[:, c, :n], in0=H[:, c, 63:63 + n], in1=H[:, c, 65:65 + n], op=ALU.add),
                                reads=[kH(c)], writes=[kX(c)])
                        else:
                            xf = XSF[c % 2]
                            kf = 'XSF%d' % (c % 2)
                            xf3 = xf[:, :n].rearrange("p (r c) -> p r c", c=64)
                            c3 = H[:, c, 64:64 + n].rearrange("p (r c) -> p r c", c=64)
                            xs3 = Xs[:, c, :n].rearrange("p (r c) -> p r c", c=64)
                            S.op('pool', lambda e, c=c, xf=xf: e.tensor_tensor(out=xf[:, :n], in0=H[:, c, 0:n], in1=H[:, c, 128:128 + n],
                                                                              op=ALU.add), reads=[kH(c)], writes=[kf])
                            S.op('dve', lambda e, xf3=xf3, c3=c3: e.tensor_tensor(out=xf3[:, :, 1:64], in0=xf3[:, :, 1:64], in1=c3[:, :, 0:63],
                                                                                  op=ALU.add), reads=[kH(c), kf], writes=[kf])
                            S.op('dve', lambda e, xf3=xf3, c3=c3, xs3=xs3: e.tensor_tensor(out=xs3[:, :, 0:63], in0=xf3[:, :, 0:63],
                                                                                           in1=c3[:, :, 1:64], op=ALU.add),
                                 reads=[kH(c), kf], writes=[kX(c)])
                            S.op('pool', lambda e, xf3=xf3, xs3=xs3: e.tensor_copy(out=xs3[:, :, 63:64], in_=xf3[:, :, 63:64]),
                                 reads=[kf, kX(c)], writes=[kX(c)])
                    noc = 59 if s_ == 0 else 35
                    qi = 1 if s_ == 0 else 2
                    for oc in range(noc):
                        po = PO[it % 6]
                        kpo = 'PO%d' % (it % 6)
                        it += 1
                        if oc < 27:
                            b0 = (oc % 2) * 2
                            p1, p2 = ps[b0], ps[b0 + 1]
                            k1, k2 = 'ps%d' % b0, 'ps%d' % (b0 + 1)
                            for k in range(8):
                                S.mm(p1[:, :n], lhsT=WIN[:, k, oc * 128:(oc + 1) * 128], rhs=H[:, k, 64:64 + n], start=(k == 0), stop=(k == 7),
                                     reads=[kH(k)], writes=[k1], inc=(k == 7))
                            for k in range(8):
                                S.mm(p2[:, :n], lhsT=WIN[:, k, oc * 128:(oc + 1) * 128], rhs=Xs[:, k, :n], start=(k == 0), stop=(k == 7),
                                     reads=[kX(k)], writes=[k2], inc=(k == 7))
                            tz = TZ[iz % 2]
                            ktz = 'TZ%d' % (iz % 2)
                            iz += 1
                            S.op('act', lambda e, p2=p2, tz=tz, oc=oc: e.activation(out=tz[:, :n], in_=p2[:, :n], func=AF.Identity,
                                                                                    scale=MUSG[:, qi, oc:oc + 1]), reads=[k2, 'MUSG'],
                                 writes=[ktz])
                            S.op('dve', lambda e, p1=p1, tz=tz, po=po, oc=oc: e.scalar_tensor_tensor(
                                out=po[:, :n], in0=p1[:, :n], scalar=MUSG[:, 0, oc:oc + 1], in1=tz[:, :n], op0=ALU.mult, op1=ALU.add),
                                reads=[k1, ktz, 'MUSG'], writes=[kpo])
                        else:
                            pp = ps[4 + oc % 4]
                            kp = 'ps%d' % (4 + oc % 4)
                            for k in range(8):
                                S.mm(pp[:, :n], lhsT=WIN[:, k, oc * 128:(oc + 1) * 128], rhs=H[:, k, 64:64 + n], start=(k == 0), stop=(k == 7),
                                     reads=[kH(k)], writes=[kp], inc=(k == 7))
                            if oc >= 43:
                                S.op('act', lambda e, pp=pp, po=po: e.activation(out=po[:, :n], in_=pp[:, :n], func=AF.Sigmoid),
                                     reads=[kp], writes=[kpo])
                            elif oc >= 35:
                                S.op('act', lambda e, pp=pp, po=po: e.activation(out=po[:, :n], in_=pp[:, :n], func=AF.Gelu),
                                     reads=[kp], writes=[kpo])
                            elif oc % 2 == 0:
                                S.op('act', lambda e, pp=pp, po=po: e.copy(out=po[:, :n], in_=pp[:, :n]), reads=[kp], writes=[kpo])
                            else:
                                S.op('dve', lambda e, pp=pp, po=po: e.tensor_copy(out=po[:, :n], in_=pp[:, :n]), reads=[kp],
                                     writes=[kpo])
                        S.dma('pool', Pv[:, oc, g0:g0 + n], po[:, :n], reads=[kpo], writes=[('P', oc, g0)])
                S.barrier()

        if phases is None or "B" in phases:
            phase_inproj()
        if phases is not None and "L" not in phases and "R" not in phases:
            return nc

        ONEB = sb("ONEB", [128, 1])
        S.op('pool', lambda e: e.memset(ONEB[:], 1.0), writes=['ONEB'])
        LC = sb("LC", [128, 2, 16])
        HST = sb("HST", [128, 16])
        lo = VEC_OFF["lru_lam"]
        S.op('act', lambda e: e.activation(out=LC[:, 0, :], in_=PV[:, lo:lo + 16], func=AF.Exp, scale=-1.0),
             reads=['PV'], writes=['LC'])
        S.op('act', lambda e: e.activation(out=LC[:, 0, :], in_=LC[:, 0, :], func=AF.Ln, bias=ONEB[:, 0:1], scale=1.0),
             reads=['LC', 'ONEB'], writes=['LC'])
        S.op('dve', lambda e: e.tensor_scalar(out=LC[:, 1, :], in0=LC[:, 0, :], scalar1=-16.0, scalar2=None, op0=ALU.mult),
             reads=['LC'], writes=['LC'])
        S.op('dve', lambda e: e.tensor_scalar(out=LC[:, 0, :], in0=LC[:, 0, :], scalar1=-8.0, scalar2=None, op0=ALU.mult),
             reads=['LC'], writes=['LC'])

        def phase_lru():
            with ExitStack() as ph:
                WAX = sb("WAX", [128, 2, 16, 256], BF16, st=ph)
                STG = sb("STGL", [128, 16, 256], st=ph)
                XL = sb("XL", [128, 2, T + 4], BF16, st=ph)
                XC = sb("XC", [128, 2, T], st=ph)
                XCB = sb("XCB", [128, 2, T], BF16, st=ph)
                AB = sb("AB", [128, T], st=ph)
                UB = sb("UB", [128, T], st=ph)
                HB = [sb("HB%d" % i, [128, T], st=ph) for i in range(2)]
                GL = sb("GLg", [128, T], BF16, st=ph)
                YB = sb("YBl", [128, T], BF16, st=ph)
                TL = [sb("TL%d" % i, [128, 512], st=ph) for i in range(5)]
                for gi_, wsrc in enumerate((lru_wa_d, lru_wx_d)):
                    S.dma('sp', STG[:], wsrc.rearrange("d n (k p) j -> p (d n k) j", p=128), writes=['STGL'])
                    S.op('dve', lambda e, gi_=gi_: e.tensor_copy(out=WAX[:, gi_, :, :], in_=STG[:]), reads=['STGL'],
                         writes=['WAX'])
                S.op('pool', lambda e: e.memset(XL[:], 0.0), writes=['XL'])
                cw, cb = VEC_OFF["lru_conv_w"], VEC_OFF["lru_conv_b"]
                bao, bxo = VEC_OFF["lru_ba"], VEC_OFF["lru_bx"]
                for s_ in (1, 0):
                    n = TC if s_ == 1 else T
                    g0 = 0 if s_ == 1 else TC
                    if s_ == 0:
                        S.op('pool', lambda e: e.memset(XL[:], 0.0), reads=['XL'], writes=['XL'])
                    for blk in range(4):
                        for k in range(2):
                            cc = 2 * blk + k
                            S.dma('sp', XL[:, k, 1:1 + n], Pv[:, 27 + cc, g0:g0 + n], reads=[('P', 27 + cc, g0)],
                                  writes=['XL'])
                            S.op('act', lambda e, k=k, cc=cc: e.activation(
                                out=XC[:, k, :n], in_=XL[:, k, 0:n], func=AF.Identity, bias=PV[:, cb + cc:cb + cc + 1],
                                scale=PV[:, cw + cc:cw + cc + 1]), reads=['XL'], writes=[('XC', k)])
                            for j in range(1, 4):
                                S.op('dve', lambda e, k=k, cc=cc, j=j: e.scalar_tensor_tensor(
                                    out=XC[:, k, :n], in0=XL[:, k, j:j + n], scalar=PV[:, cw + j * 8 + cc:cw + j * 8 + cc + 1],
                                    in1=XC[:, k, :n], op0=ALU.mult, op1=ALU.add), reads=['XL', ('XC', k)], writes=[('XC', k)])
                            S.op('act', lambda e, k=k: e.copy(out=XCB[:, k, :n], in_=XC[:, k, :n]), reads=[('XC', k)],
                                 writes=[('XCB', k)])
                        for oc in range(2):
                            cc = 2 * blk + oc
                            if s_ == 0:
                                S.dma('sp', GL[:, :n], Pv[:, 35 + cc, g0:g0 + n], reads=[('P', 35 + cc, g0)], writes=['GL'])
                            for d in range(2):
                                col = d * 8 + cc
                                kHB = 'HB%d' % d
                                HBd = HB[d]
                                for (t0, m) in seq_tiles(n):
                                    for gi_ in range(2):
                                        pp = ps[gi_]
                                        for kin in range(2):
                                            S.mm(
                                                pp[:, :m], lhsT=WAX[:, gi_, (d * 4 + blk) * 2 + kin, oc * 128:(oc + 1) * 128],
                                                rhs=XCB[:, kin, t0:t0 + m], start=(kin == 0), stop=(kin == 1),
                                                reads=[('XCB', kin), 'WAX'], writes=['ps%d' % gi_], inc=(kin == 1))
                                    S.op('act', lambda e, m=m, col=col, t0=t0: e.activation(
                                        out=AB[:, t0:t0 + m], in_=ps[0][:, :m], func=AF.Sigmoid, bias=PV[:, bao + col:bao + col + 1],
                                        scale=1.0), reads=['ps0'], writes=[('AB', t0)])
                                    S.op('act', lambda e, m=m, col=col, t0=t0: e.activation(
                                        out=UB[:, t0:t0 + m], in_=ps[1][:, :m], func=AF.Sigmoid, bias=PV[:, bxo + col:bxo + col + 1],
                                        scale=1.0), reads=['ps1'], writes=[('UB', t0)])
                                for (t0, m) in seq_tiles(n):
                                    S.op('act', lambda e, m=m, col=col, t0=t0: e.activation(
                                        out=HBd[:, t0:t0 + m], in_=AB[:, t0:t0 + m], func=AF.Exp, scale=LC[:, 1, col:col + 1]),
                                        reads=[('AB', t0), 'LC'], writes=[kHB])
                                    S.op('act', lambda e, m=m, col=col, t0=t0: e.activation(
                                        out=AB[:, t0:t0 + m], in_=AB[:, t0:t0 + m], func=AF.Exp, scale=LC[:, 0, col:col + 1]),
                                        reads=[('AB', t0), 'LC'], writes=[('AB', t0)])
                                for (t0, m) in seq_tiles(n):
                                    S.op('act', lambda e, m=m, t0=t0: e.activation(
                                        out=HBd[:, t0:t0 + m], in_=HBd[:, t0:t0 + m], func=AF.Sqrt, bias=ONEB[:, 0:1], scale=-1.0),
                                        reads=[kHB, 'ONEB'], writes=[kHB])
                                for (t0, m) in seq_tiles(n):
                                    S.op('pool', lambda e, m=m, t0=t0: e.tensor_tensor(
                                        out=UB[:, t0:t0 + m], in0=UB[:, t0:t0 + m], in1=XC[:, oc, t0:t0 + m], op=ALU.mult),
                                        reads=[('UB', t0), ('XC', oc)], writes=[('UB', t0)])
                                    S.op('dve', lambda e, m=m, t0=t0: e.tensor_tensor(
                                        out=UB[:, t0:t0 + m], in0=UB[:, t0:t0 + m], in1=HBd[:, t0:t0 + m], op=ALU.mult),
                                        reads=[('UB', t0), kHB], writes=[('UB', t0)])
                                rk = [('AB', t0) for (t0, m) in seq_tiles(n)] + [('UB', t0) for (t0, m) in seq_tiles(n)]
                                init = 0.0 if s_ == 1 else HST[:, col:col + 1]
                                if d == 0:
                                    S.op('dve', lambda e, init=init: e.tensor_tensor_scan(
                                        out=HB[0][:, :n], data0=AB[:, :n], data1=UB[:, :n], initial=init, op0=ALU.mult,
                                        op1=ALU.add), reads=rk + ['HST'], writes=['HB0'])
                                    if s_ == 1:
                                        S.op('dve', lambda e, col=col: e.tensor_copy(out=HST[:, col:col + 1], in_=HB[0][:, n - 1:n]),
                                             reads=['HB0'], writes=['HST'])
                                else:
                                    S.op('dve', lambda e, init=init: e.tensor_tensor_scan(
                                        out=HB[1][:, n - 1::-1] if False else HB[1][:, :n][:, ::-1], data0=AB[:, :n][:, ::-1],
                                        data1=UB[:, :n][:, ::-1], initial=init, op0=ALU.mult, op1=ALU.add),
                                        reads=rk + ['HST'], writes=['HB1'])
                                    if s_ == 1:
                                        S.op('dve', lambda e, col=col: e.tensor_copy(out=HST[:, col:col + 1], in_=HB[1][:, 0:1]),
                                             reads=['HB1'], writes=['HST'])
                            if s_ == 0:
                                S.op('pool', lambda e: e.tensor_tensor(out=HB[0][:, :n], in0=HB[0][:, :n], in1=HB[1][:, :n],
                                                                       op=ALU.add), reads=['HB0', 'HB1'], writes=['HB0'])
                                S.op('dve', lambda e: e.tensor_tensor(out=YB[:, :n], in0=HB[0][:, :n], in1=GL[:, :n], op=ALU.mult),
                                     reads=['HB0', 'GL'], writes=['YBl'])
                                S.dma('pool', YLv[:, cc, :], YB[:, :n], reads=['YBl'], writes=[('YL', cc)])
                if debug:
                    S.dma('sp', HSTD[:, :], HST[:], reads=['HST'], writes=['HSTD'])
                S.barrier()

        if phases is None or "L" in phases:
            phase_lru()
        if phases is not None and "R" not in phases:
            return nc
        DS = float(np.exp(-0.5))
        LN_X_EPS = 64e-5
        NCH = TT // 64

        def _rwkv_body(ph):
            if True:
                WUP = sb("WUP", [128, D], BF16, st=ph)
                AUP = sb("AUP", [128, D], BF16, st=ph)
                GUP = sb("GUP", [128, D], BF16, st=ph)
                IDB = sb("IDB", [128, 128], BF16, st=ph)
                BONES = sb("BONES", [128, 128], st=ph)
                MK = [sb("MK%d" % d, [128, 2, 256], st=ph) for d in range(2)]
                AMK = [sb("AMK%d" % d, [128, 2, 64], st=ph) for d in range(2)]
                BMK = [sb("BMK%d" % d, [128, 4, 64], st=ph) for d in range(2)]
                RMF = sb("RMF", [128, 512], st=ph)
                RMB = sb("RMB", [128, 512], st=ph)
                TW = sb("TW", [128, TT], BF16, st=ph)
                ZA = sb("ZA", [128, TT], BF16, st=ph)
                PL = sb("PL", [128, TT], BF16, st=ph)
                SH = sb("SH", [128, TT], st=ph)
                ZR = sb("ZR", [128, TT], BF16, st=ph)
                ZK = sb("ZK", [128, TT], BF16, st=ph)
                ZV = sb("ZV", [128, TT], BF16, st=ph)
                KK = sb("KK", [128, TT], BF16, st=ph)
                ART = [[sb("ART%d%d" % (d, i), [128, 8, 2, 64], BF16, st=ph) for i in range(2)] for d in range(2)]
                KBT = [[sb("KBT%d%d" % (d, i), [128, 8, 2, 64], BF16, st=ph) for i in range(2)] for d in range(2)]
                WC = [sb("WC%d" % d, [128, NCH], st=ph) for d in range(2)]
                TRP = [[sb("TRP%d%d" % (d, i), [128, 512], F32 if i < 5 else BF16, st=ph) for i in range(7)] for d in range(2)]
                YB = sb("YBr", [128, 64, 64], st=ph)
                TR = [sb("TR%d" % i, [128, 512], st=ph) for i in range(3)]
                TRS = sb("TRS", [128, 256], st=ph)
                MT = [[sb("MT%d%d" % (d, i), [128, 8, 256], BF16, st=ph) for i in range(2)] for d in range(2)]
                ABt = [[sb("ABt%d%d" % (d, i), [128, 4, 2, 128], BF16, st=ph) for i in range(2)] for d in range(2)]
                ACC = [[sb("ACC%d%d" % (d, i), [128, 4, 128], BF16, st=ph) for i in range(2)] for d in range(2)]
                TTt = [[sb("TTt%d%d" % (d, i), [128, 8, 128], BF16, st=ph) for i in range(2)] for d in range(2)]
                PT = [[sb("PT%d%d" % (d, i), [128, 3, 64], BF16, st=ph) for i in range(2)] for d in range(2)]
                XB = [[sb("XB%d%d" % (d, i), [128, 64], BF16, st=ph) for i in range(2)] for d in range(2)]
                UB_ = [[sb("UBr%d%d" % (d, i), [128, 64], BF16, st=ph) for i in range(2)] for d in range(2)]
                SS = [sb("SS%d" % d, [128, 64], st=ph) for d in range(2)]
                SSb = [sb("SSb%d" % d, [128, 64], BF16, st=ph) for d in range(2)]
                MUS = sb("MUS", [128, 3, 27], st=ph)
                KA1 = sb("KA1", [128, 8], st=ph)
                GO = [sb("GO%d" % i, [128, 512], BF16, st=ph) for i in range(2)]
                ST8 = sb("ST8", [128, 2, 8, 64], st=ph) if debug else None

                for i_, (dst, src) in enumerate(((WUP, rw_w_up_d), (AUP, rw_a_up_d), (GUP, rw_g_up_d))):
                    S.dma('sp', SH[:, 0:D], src[:, :], writes=['SH'])
                    S.op('dve', lambda e, dst=dst: e.tensor_copy(out=dst[:], in_=SH[:, 0:D]), reads=['SH'], writes=['LW_' + str(i_)])
                S.op('dve', lambda e: e.tensor_copy(out=IDB[:], in_=ident[:]), reads=['ident'], writes=['IDB'])
                S.op('pool', lambda e: e.memset(BONES[:], 0.0), writes=['BONES'])
                S.op('pool', lambda e: e.memset(BONES[0:64, 0:64], 1.0), reads=['BONES'], writes=['BONES'])
                S.op('pool', lambda e: e.memset(BONES[64:128, 64:128], 1.0), reads=['BONES'], writes=['BONES'])
                S.op('pool', lambda e: e.memset(RMF[:], 1.0), writes=['RMF'])
                S.op('pool', lambda e: e.memset(RMF[:, 0:512:64], 0.0), reads=['RMF'], writes=['RMF'])
                S.op('pool', lambda e: e.memset(RMB[:], 1.0), writes=['RMB'])
                S.op('pool', lambda e: e.memset(RMB[:, 63:512:64], 0.0), reads=['RMB'], writes=['RMB'])

                def tri(dst_view_fn, kind):
                    pat, cm = ([[1, 64]], -1) if kind[0] == 'L' else ([[-1, 64]], 1)
                    cmp_ = ALU.is_gt if kind[2] == 's' else ALU.is_ge
                    for hp in (0, 64):
                        v = dst_view_fn(hp)
                        S.op('pool', lambda e, v=v: e.memset(v, 1.0), reads=['MASKS'], writes=['MASKS'])
                        S.op('pool', lambda e, v=v: e.affine_select(out=v, in_=v, pattern=pat, compare_op=cmp_, fill=0.0, base=0,
                                                                   channel_multiplier=cm), reads=['MASKS'], writes=['MASKS'])
                for d in range(2):
                    ks, ki = ('LTs', 'LTi') if d == 0 else ('GTs', 'GTi')
                    kt = 'GTs' if d == 0 else 'LTs'
                    for q in range(2):
                        for blk, kd_ in enumerate((ks, ki, ks, ki)):
                            tri(lambda hp, d=d, q=q, blk=blk: MK[d][hp:hp + 64, q, blk * 64:(blk + 1) * 64], kd_)
                        tri(lambda hp, d=d, q=q: AMK[d][hp:hp + 64, q, :], ks)
                    for q in range(4):
                        tri(lambda hp, d=d, q=q: BMK[d][hp:hp + 64, q, :], kt)
                mo = VEC_OFF["rw_mu"]
                S.op('dve', lambda e: e.tensor_scalar(out=MUS[:, 0, :], in0=PV[:, mo:mo + 27], scalar1=-1.0, scalar2=1.0, op0=ALU.mult,
                                                      op1=ALU.add), reads=['PV'], writes=['MUS'])
                S.op('dve', lambda e: e.tensor_scalar(out=MUS[:, 1, :], in0=PV[:, mo:mo + 27], scalar1=0.25, scalar2=None, op0=ALU.mult),
                     reads=['PV'], writes=['MUS'])
                S.op('dve', lambda e: e.tensor_scalar(out=MUS[:, 2, :], in0=PV[:, mo:mo + 27], scalar1=0.5, scalar2=None, op0=ALU.mult),
                     reads=['PV'], writes=['MUS'])
                kao = VEC_OFF["rw_k_a"]
                S.op('dve', lambda e: e.tensor_scalar(out=KA1[:], in0=PV[:, kao:kao + 8], scalar1=-1.0, scalar2=1.0, op0=ALU.mult,
                                                      op1=ALU.add), reads=['PV'], writes=['KA1'])
                for t_ in ABt[0] + ABt[1] + ACC[0] + ACC[1] + TTt[0] + TTt[1]:
                    S.op('pool', lambda e, t_=t_: e.memset(t_[:], 0.0), writes=['ABZ'])
                S.barrier()

                cut(1)
                PLx = PL[:, TC:TT].rearrange("p (r c) -> p r c", c=64)
                SHx = SH[:, TC:TT].rearrange("p (r c) -> p r c", c=64)

                def zlerp(pc, dst, dkey):
                    S.dma('sp', dst[:, 0:TC], Pv[:, pc, 0:TC], reads=[dkey], writes=[dkey])
                    for t0 in range(0, T, 1024):
                        S.dma('sp' if (t0 // 1024) % 2 == 0 else 'pool', dst[:, TC + t0:TC + t0 + 1024], Pv[:, pc, TC + t0:TC + t0 + 1024],
                              reads=[dkey], writes=[dkey])

                tiles_all = [(0, TC)] + [(TC + t0, m) for (t0, m) in seq_tiles(T)]

                zlerp(24, ZK, 'ZK')
                cut(2)
                S.op('act', lambda e: e.activation(out=TW[:, :], in_=ZK[:, :], func=AF.Tanh), reads=['ZK'], writes=['TW'])
                zlerp(25, ZA, 'ZA')
                zlerp(26, ZK, 'ZK')
                S.op('act', lambda e: e.activation(out=ZR[:, :], in_=ZK[:, :], func=AF.Sigmoid), reads=['ZK'], writes=['ZR'])
                gi_ = 0
                for c in range(8):
                    for (t0, m) in seq_tiles(T):
                        pp = ps[gi_ % 2]
                        go = GO[gi_ % 2]
                        S.mm(pp[:, :m], lhsT=GUP[:, c * 128:(c + 1) * 128],
                                                                              rhs=ZR[:, TC + t0:TC + t0 + m], start=True, stop=True,
                             reads=['ZR', 'LW_2'], writes=['ps%d' % (gi_ % 2)])
                        S.op('act', lambda e, pp=pp, go=go, m=m: e.copy(out=go[:, :m], in_=pp[:, :m]), reads=['ps%d' % (gi_ % 2)],
                             writes=['GO%d' % (gi_ % 2)])
                        S.dma('pool', GSv[:, c, t0:t0 + m], go[:, :m], reads=['GO%d' % (gi_ % 2)], writes=[('GS', c, t0)])
                        gi_ += 1

                cut(3)
                kko, rko = VEC_OFF["rw_k_k"], VEC_OFF["rw_r_k"]
                w0o, a0o = VEC_OFF["rw_w0"], VEC_OFF["rw_a0"]
                lwo, lbo = VEC_OFF["rw_ln_w"], VEC_OFF["rw_ln_b"]
                v3 = lambda ap, m: ap.rearrange("p (j s) -> p j s", s=64)

                for c in range(8):
                    zlerp(c, ZR, 'ZR')
                    zlerp(8 + c, ZK, 'ZK')
                    zlerp(16 + c, ZV, 'ZV')
                    for (g0, m) in tiles_all:
                        S.op('act', lambda e, g0=g0, m=m: e.activation(out=TR[0][:, :m], in_=ZK[:, g0:g0 + m], func=AF.Square,
                                                                       scale=PV[:, kko + c:kko + c + 1]), reads=['ZK'], writes=['TR0'])
                        S.mm(ps[0][:, :m], lhsT=BONES[:], rhs=TR[0][:, :m], start=True, stop=True,
                             reads=['TR0', 'BONES'], writes=['ps0'])
                        S.op('act', lambda e, m=m: e.activation(out=TR[1][:, :m], in_=ps[0][:, :m], func=AF.Sqrt), reads=['ps0'],
                             writes=['TR1'])
                        S.op('dve', lambda e, m=m: e.tensor_scalar(out=TR[1][:, :m], in0=TR[1][:, :m], scalar1=1e-12, scalar2=None,
                                                                   op0=ALU.max), reads=['TR1'], writes=['TR1'])
                        S.op('dve', lambda e, m=m: e.reciprocal(out=TR[1][:, :m], in_=TR[1][:, :m]), reads=['TR1'], writes=['TR1'])
                        S.op('dve', lambda e, g0=g0, m=m: e.scalar_tensor_tensor(
                            out=KK[:, g0:g0 + m], in0=ZK[:, g0:g0 + m], scalar=PV[:, kko + c:kko + c + 1], in1=TR[1][:, :m],
                            op0=ALU.mult, op1=ALU.mult), reads=['ZK', 'TR1'], writes=['KK'])

                    cut(4)
                    batches = [
                        [[0, 1, 2, 3]] + [list(range(4 + 8 * i, 12 + 8 * i)) for i in range(8)],
                        [[3, 2, 1, 0]] + [list(range(11 + 8 * i, 3 + 8 * i, -1)) for i in range(7, -1, -1)],
                    ]
                    NB = 9
                    for d in range(2):
                        S.op('pool', lambda e, d=d: e.memset(SS[d][:], 0.0), reads=[('SS', d, 0), ('SS', d, 1)],
                             writes=[('SS', d, 0), ('SS', d, 1)])
                        S.op('pool', lambda e, d=d: e.memset(SSb[d][:], 0.0), reads=[('SSb', d, 0), ('SSb', d, 1)],
                             writes=[('SSb', d, 0), ('SSb', d, 1)])
                    yb_written = set()

                    def prep_gen(d, n):
                        js = batches[d][n]
                        par = n % 2
                        jmin = min(js)
                        g0, m = jmin * 64, 64 * len(js)
                        nch = len(js)
                        dh = d * 64
                        col = d * 8 + c
                        LW, AS, CL, E1, E2, TB, TK = TRP[d]
                        kT = lambda i: ('TRP', d, i)
                        A_, K_ = ART[d][par], KBT[d][par]
                        kA, kK = ('ART', d, par), ('KBT', d, par)
                        pd_, kpd = ps[d], 'ps%d' % d
                        S.mm(pd_[:, :m], lhsT=WUP[dh:dh + 64, c * 128:(c + 1) * 128], rhs=TW[dh:dh + 64, g0:g0 + m], start=True, stop=True,
                             reads=['TW', 'LW_0'], writes=[kpd])
                        S.op('act', lambda e: e.activation(out=LW[:, :m], in_=pd_[:, :m], func=AF.Sigmoid,
                                                           bias=PV[:, w0o + col:w0o + col + 1], scale=1.0), reads=[kpd], writes=[kT(0)])
                        S.mm(pd_[:, :m], lhsT=AUP[dh:dh + 64, c * 128:(c + 1) * 128], rhs=ZA[dh:dh + 64, g0:g0 + m], start=True, stop=True,
                             reads=['ZA', 'LW_1'], writes=[kpd])
                        S.op('act', lambda e: e.activation(out=AS[:, :m], in_=pd_[:, :m], func=AF.Sigmoid,
                                                           bias=PV[:, a0o + col:a0o + col + 1], scale=1.0), reads=[kpd], writes=[kT(1)])
                        yield
                        if d == 0:
                            S.op('dve', lambda e: e.tensor_tensor_scan(out=CL[:, :m], data0=RMF[:, :m], data1=LW[:, :m], initial=0.0,
                                                                       op0=ALU.mult, op1=ALU.add), reads=[kT(0), 'RMF'], writes=[kT(2)])
                        else:
                            S.op('dve', lambda e: e.tensor_tensor_scan(out=CL[:, :m][:, ::-1], data0=RMB[:, :m][:, ::-1],
                                                                       data1=LW[:, :m][:, ::-1], initial=0.0, op0=ALU.mult, op1=ALU.add),
                                 reads=[kT(0), 'RMB'], writes=[kT(2)])
                        S.op('act', lambda e: e.activation(out=E1[:, :m], in_=CL[:, :m], func=AF.Exp, scale=-DS), reads=[kT(2)], writes=[kT(3)])
                        S.op('act', lambda e: e.activation(out=E2[:, :m], in_=CL[:, :m], func=AF.Exp, scale=DS), reads=[kT(2)], writes=[kT(4)])
                        S.op('pool', lambda e: e.tensor_tensor(out=LW[:, :m], in0=CL[:, :m], in1=LW[:, :m], op=ALU.subtract),
                             reads=[kT(2), kT(0)], writes=[kT(0)])
                        yield
                        S.op('act', lambda e: e.activation(out=CL[:, :m], in_=LW[:, :m], func=AF.Exp, scale=-DS), reads=[kT(0), kT(2)],
                             writes=[kT(2)])
                        S.op('dve', lambda e: e.tensor_tensor(out=A_[:, 0:nch, 1, :], in0=v3(ZR[:, g0:g0 + m], m), in1=v3(E1[:, :m], m),
                                                              op=ALU.mult), reads=['ZR', kT(3)], writes=[kA])
                        S.op('pool', lambda e: e.tensor_tensor(out=TB[:, :m], in0=KK[:, g0:g0 + m], in1=AS[:, :m], op=ALU.mult),
                             reads=['KK', kT(1)], writes=[kT(5)])
                        S.op('pool', lambda e: e.tensor_scalar(out=TK[:, :m], in0=AS[:, :m], scalar1=PV[:, kao + c:kao + c + 1],
                                                               scalar2=KA1[:, c:c + 1], op0=ALU.mult, op1=ALU.add),
                             reads=[kT(1), 'KA1'], writes=[kT(6)])
                        yield
                        S.op('dve', lambda e: e.scalar_tensor_tensor(out=A_[:, 0:nch, 0, :], in0=v3(KK[:, g0:g0 + m], m), scalar=-1.0,
                                                                     in1=v3(CL[:, :m], m), op0=ALU.mult, op1=ALU.mult),
                             reads=['KK', kT(2)], writes=[kA])
                        S.op('dve', lambda e: e.tensor_tensor(out=K_[:, 0:nch, 1, :], in0=v3(TB[:, :m], m), in1=v3(E2[:, :m], m), op=ALU.mult),
                             reads=[kT(5), kT(4)], writes=[kK])
                        S.op('pool', lambda e: e.tensor_tensor(out=TK[:, :m], in0=TK[:, :m], in1=ZK[:, g0:g0 + m], op=ALU.mult),
                             reads=[kT(6), 'ZK'], writes=[kT(6)])
                        yield
                        S.op('dve', lambda e: e.tensor_tensor(out=K_[:, 0:nch, 0, :], in0=v3(TK[:, :m], m), in1=v3(E2[:, :m], m), op=ALU.mult),
                             reads=[kT(6), kT(4)], writes=[kK])
                        ecol = 63 if d == 0 else 0
                        S.op('dve', lambda e: e.tensor_copy(out=WC[d][:, jmin:jmin + nch], in_=E1[:, ecol:m:64]), reads=[kT(3)],
                             writes=[('WC', d)])
                        yield

                    def stage1_gen(d, n):
                        js = batches[d][n]
                        par = n % 2
                        jmin = min(js)
                        A_, K_ = ART[d][par], KBT[d][par]
                        kA, kK = ('ART', d, par), ('KBT', d, par)
                        MT_, TT_ = MT[d][par], TTt[d][par]
                        AB_, AC_ = ABt[d], ACC[d]
                        for h0 in range(0, len(js), 4):
                            for q0 in range(0, 4, 2):
                                for hp in (0, 64):
                                    pm = ps[2 + hp // 64]
                                    for q in range(2):
                                        jl = js[h0 + q0 + q] - jmin
                                        for w_ in range(2):
                                            S.mm(pm[hp:hp + 64, q * 256 + w_ * 128:q * 256 + (w_ + 1) * 128],
                                                 lhsT=K_[hp:hp + 64, jl, w_, :], rhs=A_[hp:hp + 64, jl, :, :], start=True, stop=True,
                                                 reads=[kK, kA], writes=['ps%d' % (2 + hp // 64)])
                                lq = h0 + q0
                                for hp in (0, 64):
                                    pm, kpm = ps[2 + hp // 64], 'ps%d' % (2 + hp // 64)
                                    S.op('dve', lambda e, lq=lq, hp=hp, pm=pm: e.tensor_tensor(
                                        out=MT_[hp:hp + 64, lq:lq + 2, :], in0=pm[hp:hp + 64, 0:512].rearrange("p (q w) -> p q w", w=256),
                                        in1=MK[d][hp:hp + 64, :, :], op=ALU.mult), reads=[kpm], writes=[('MT', d, par, lq, hp)])
                                    S.op('dve', lambda e, q0=q0, hp=hp, pm=pm: e.tensor_tensor(
                                        out=AB_[0][hp:hp + 64, q0:q0 + 2, 0, hp:hp + 64],
                                        in0=pm[hp:hp + 64, 0:512].rearrange("p (q w) -> p q w", w=256)[:, :, 128:192],
                                        in1=AMK[d][hp:hp + 64, :, :], op=ALU.mult), reads=[kpm], writes=[('AB0', d, q0)])
                                yield
                            for hp in (0, 64):
                                pm = ps[2 + hp // 64]
                                for q in range(4):
                                    jl = js[h0 + q] - jmin
                                    S.mm(pm[hp:hp + 64, q * 128 + hp:q * 128 + hp + 64], lhsT=A_[hp:hp + 64, jl, 0, :],
                                         rhs=K_[hp:hp + 64, jl, 1, :], start=True, stop=True, reads=[kK, kA], writes=['ps%d' % (2 + hp // 64)])
                            for hp in (0, 64):
                                pm, kpm = ps[2 + hp // 64], 'ps%d' % (2 + hp // 64)
                                S.op('dve', lambda e, hp=hp, pm=pm: e.tensor_tensor(
                                    out=AB_[0][hp:hp + 64, 0:4, 1, hp:hp + 64],
                                    in0=pm[hp:hp + 64, 0:512].rearrange("p (q w) -> p q w", w=128)[:, :, hp:hp + 64],
                                    in1=BMK[d][hp:hp + 64, :, :], op=ALU.mult), reads=[kpm], writes=[('AB0', d, 0), ('AB0', d, 2)])
                            S.op('pool', lambda e: e.tensor_tensor(
                                out=AC_[0][:, 0:4, :], in0=AB_[0][:, 0:4, 0, :], in1=IDB[:].unsqueeze(1).to_broadcast([128, 4, 128]),
                                op=ALU.add), reads=[('AB0', d, 0), ('AB0', d, 2), 'IDB'], writes=[('ACC0', d)])
                            yield
                            for l in range(5):
                                cur, nxt = l % 2, 1 - (l % 2)
                                kc, kn = 'AB%d' % cur, 'AB%d' % nxt
                                for q0 in range(0, 4, 2):
                                    for q in range(2):
                                        jq = q0 + q
                                        if l < 4:
                                            S.mm(ps[2][:, q * 256:q * 256 + 128], lhsT=AB_[cur][:, jq, 1, :], rhs=AB_[cur][:, jq, 0, :],
                                                 start=True, stop=True, reads=[(kc, d, q0)], writes=['ps2'], inc=False)
                                        S.mm(ps[2][:, q * 256 + 128:q * 256 + 256], lhsT=AB_[cur][:, jq, 0, :], rhs=AB_[cur][:, jq, 1, :],
                                             start=True, stop=True, reads=[(kc, d, q0)], writes=['ps2'], inc=(q == 1))
                                    if l < 4:
                                        S.op('act', lambda e, q0=q0, nxt=nxt: e.copy(
                                            out=AB_[nxt][:, q0:q0 + 2, :, :].rearrange("p q a b -> p (q a b)"), in_=ps[2][:, 0:512]),
                                            reads=['ps2'], writes=[(kn, d, q0)])
                                    else:
                                        S.op('act', lambda e, q0=q0, nxt=nxt: e.copy(
                                            out=AB_[nxt][:, q0:q0 + 2, 1, :],
                                            in_=ps[2][:, 0:512].rearrange("p (q w) -> p q w", w=256)[:, :, 128:256]),
                                            reads=['ps2'], writes=[(kn, d, q0)])
                                    yield
                                for q in range(4):
                                    S.mm(ps[3][:, q * 128:(q + 1) * 128], lhsT=AB_[nxt][:, q, 1, :], rhs=AC_[cur][:, q, :], start=True, stop=True,
                                         reads=[(kn, d, (q // 2) * 2), ('ACC%d' % cur, d)], writes=['ps3'], inc=(q == 3))
                                if l == 4:
                                    S.op('dve', lambda e, cur=cur: e.tensor_tensor(
                                        out=TT_[:, h0:h0 + 4, :], in0=ps[3][:, 0:512].rearrange("p (q w) -> p q w", w=128),
                                        in1=AC_[cur][:, 0:4, :], op=ALU.add), reads=['ps3', ('ACC%d' % cur, d)], writes=[('TTt', d, par, h0)])
                                else:
                                    S.op('dve', lambda e, cur=cur, nxt=nxt: e.tensor_tensor(
                                        out=AC_[nxt][:, 0:4, :], in0=ps[3][:, 0:512].rearrange("p (q w) -> p q w", w=128),
                                        in1=AC_[cur][:, 0:4, :], op=ALU.add), reads=['ps3', ('ACC%d' % cur, d)], writes=[('ACC%d' % nxt, d)])
                                yield

                    def chain_gen(d, n):
                        js = batches[d][n]
                        par = n % 2
                        jmin = min(js)
                        A_, K_ = ART[d][par], KBT[d][par]
                        kA, kK = ('ART', d, par), ('KBT', d, par)
                        MT_, TT_ = MT[d][par], TTt[d][par]
                        SS_, SSb_ = SS[d], SSb[d]
                        for step, j in enumerate(js):
                            jl = j - jmin
                            tb = step % 2
                            isx = j >= 4
                            jx = j - 4
                            pt, xb, ub = PT[d][tb], XB[d][tb], UB_[d][tb]
                            ktt = ('TTt', d, par, (step // 4) * 4)
                            first_y = isx and (jx not in yb_written)
                            if isx:
                                yb_written.add(jx)
                            H = []
                            for h in range(2):
                                hp = 64 * h
                                H.append(dict(hp=hp, pb=ps[4 + 2 * d + h], kpb='ps%d' % (4 + 2 * d + h), kpt=('PT', d, tb, h), kxb=('XB', d, tb, h),
                                              kub=('UB', d, tb, h), kS=('SS', d, h), kSb=('SSb', d, h),
                                              kmt=('MT', d, par, (step // 2) * 2, hp), e0=('act', 'dve')[h], e1=('dve', 'act')[h]))

                            def cp(eng, out, in_, reads, writes):
                                if eng == 'act':
                                    S.op('act', lambda e: e.copy(out=out, in_=in_), reads=reads, writes=writes)
                                else:
                                    S.op('dve', lambda e: e.tensor_copy(out=out, in_=in_), reads=reads, writes=writes)
                            for x in H:
                                hp, pb = x['hp'], x['pb']
                                S.mm(pb[hp:hp + 64, 0:64], lhsT=ZV[hp:hp + 64, j * 64:(j + 1) * 64], rhs=IDB[hp:hp + 64, hp:hp + 64],
                                     start=True, stop=True, reads=['ZV', 'IDB'], writes=[x['kpb']])
                                S.mm(pb[hp:hp + 64, 64:128], lhsT=K_[hp:hp + 64, jl, 1, :], rhs=IDB[hp:hp + 64, hp:hp + 64],
                                     start=True, stop=True, reads=[kK, 'IDB'], writes=[x['kpb']])
                                S.mm(pb[hp:hp + 64, 128:192], lhsT=K_[hp:hp + 64, jl, 0, :], rhs=IDB[hp:hp + 64, hp:hp + 64],
                                     start=True, stop=True, reads=[kK, 'IDB'], writes=[x['kpb']])
                            for x in H:
                                hp, pb = x['hp'], x['pb']
                                cp(x['e0'], pt[hp:hp + 64, :, :].rearrange("p a b -> p (a b)"), pb[hp:hp + 64, 0:192], [x['kpb']], [x['kpt']])
                            yield
                            for x in H:
                                hp, pb = x['hp'], x['pb']
                                S.mm(pb[hp:hp + 64, 192:256], lhsT=A_[hp:hp + 64, jl, 0, :], rhs=SSb_[hp:hp + 64, :], start=True, stop=False,
                                     reads=[kA, x['kSb']], writes=[x['kpb']])
                                S.mm(pb[hp:hp + 64, 192:256], lhsT=MT_[hp:hp + 64, step, 0:64], rhs=pt[hp:hp + 64, 0, :], start=False, stop=True,
                                     reads=[x['kmt'], x['kpt']], writes=[x['kpb']])
                            for x in H:
                                hp, pb = x['hp'], x['pb']
                                cp(x['e0'], xb[hp:hp + 64, :], pb[hp:hp + 64, 192:256], [x['kpb']], [x['kxb']])
                            yield
                            for x in H:
                                hp, pb = x['hp'], x['pb']
                                S.mm(pb[hp:hp + 64, 256:320], lhsT=TT_[hp:hp + 64, step, hp:hp + 64], rhs=xb[hp:hp + 64, :], start=True, stop=True,
                                     reads=[ktt, x['kxb']], writes=[x['kpb']])
                            for x in H:
                                hp, pb = x['hp'], x['pb']
                                cp(x['e1'], ub[hp:hp + 64, :], pb[hp:hp + 64, 256:320], [x['kpb']], [x['kub']])
                            yield
                            if isx:
                                for x in H:
                                    hp, pb = x['hp'], x['pb']
                                    S.mm(pb[hp:hp + 64, 320:384], lhsT=A_[hp:hp + 64, jl, 1, :], rhs=SSb_[hp:hp + 64, :], start=True, stop=False,
                                         reads=[kA, x['kSb']], writes=[x['kpb']])
                                    S.mm(pb[hp:hp + 64, 320:384], lhsT=MT_[hp:hp + 64, step, 192:256], rhs=ub[hp:hp + 64, :], start=False, stop=False,
                                         reads=[x['kmt'], x['kub']], writes=[x['kpb']])
                                    S.mm(pb[hp:hp + 64, 320:384], lhsT=MT_[hp:hp + 64, step, 64:128], rhs=pt[hp:hp + 64, 0, :], start=False, stop=True,
                                         reads=[x['kmt'], x['kpt']], writes=[x['kpb']])
                                for x in H:
                                    hp, pb = x['hp'], x['pb']
                                    if first_y:
                                        cp(x['e1'], YB[hp:hp + 64, jx, :], pb[hp:hp + 64, 320:384], [x['kpb']], [('YB', jx, hp)])
                                    else:
                                        S.op('dve', lambda e, hp=hp, pb=pb: e.tensor_tensor(out=YB[hp:hp + 64, jx, :], in0=pb[hp:hp + 64, 320:384],
                                                                                          in1=YB[hp:hp + 64, jx, :], op=ALU.add),
                                             reads=[x['kpb'], ('YB', jx, hp)], writes=[('YB', jx, hp)])
                            for x in H:
                                hp, pb = x['hp'], x['pb']
                                S.mm(pb[hp:hp + 64, 384:448], lhsT=pt[hp:hp + 64, 1, :], rhs=ub[hp:hp + 64, :], start=True, stop=False,
                                     reads=[x['kpt'], x['kub']], writes=[x['kpb']])
                                S.mm(pb[hp:hp + 64, 384:448], lhsT=pt[hp:hp + 64, 2, :], rhs=pt[hp:hp + 64, 0, :], start=False, stop=True,
                                     reads=[x['kpt']], writes=[x['kpb']])
                            for x in H:
                                hp, pb = x['hp'], x['pb']
                                S.op('dve', lambda e, hp=hp: e.tensor_scalar(out=SS_[hp:hp + 64, :], in0=SS_[hp:hp + 64, :],
                                                                             scalar1=WC[d][hp:hp + 64, j:j + 1], scalar2=None, op0=ALU.mult),
                                     reads=[x['kS'], ('WC', d)], writes=[x['kS']])
                                S.op('dve', lambda e, hp=hp, pb=pb: e.scalar_tensor_tensor(
                                    out=SS_[hp:hp + 64, :], in0=pb[hp:hp + 64, 384:448], scalar=WC[d][hp:hp + 64, j:j + 1],
                                    in1=SS_[hp:hp + 64, :], op0=ALU.mult, op1=ALU.add), reads=[x['kpb'], x['kS'], ('WC', d)], writes=[x['kS']])
                                S.op('act', lambda e, hp=hp: e.copy(out=SSb_[hp:hp + 64, :], in_=SS_[hp:hp + 64, :]), reads=[x['kS']],
                                     writes=[x['kSb']])
                            if debug and j == (3 if d == 0 else 0):
                                S.op('dve', lambda e: e.tensor_copy(out=ST8[:, d, c, :], in_=SS_[:]), reads=[('SS', d, 0), ('SS', d, 1)],
                                     writes=['ST8'])
                            yield

                    def run_threads(ths):
                        ths = list(ths)
                        while ths:
                            for g in list(ths):
                                try:
                                    next(g)
                                except StopIteration:
                                    ths.remove(g)

                    import itertools as _it
                    run_threads([_it.chain(prep_gen(d, 0), stage1_gen(d, 0)) for d in range(2)])
                    for n in range(NB):
                        ths = []
                        for d in range(2):
                            ths.append(chain_gen(d, n))
                            if n + 1 < NB:
                                ths.append(_it.chain(prep_gen(d, n + 1), stage1_gen(d, n + 1)))
                        run_threads(ths)
                    cut(8)
                    YSQ = SH[:, 0:4096].rearrange("p (j v) -> p j v", v=64)
                    YNb = PL[:, 0:4096].rearrange("p (j v) -> p j v", v=64)
                    SUM, SSQ, MEAN, RSTD = TRS[:, 0:64], TRS[:, 64:128], TRS[:, 128:192], TRS[:, 192:256]
                    S.op('dve', lambda e: e.tensor_reduce(out=SUM, in_=YB[:], axis=AX.X, op=ALU.add),
                         reads=[('YB', jx, hp_) for jx in range(64) for hp_ in (0, 64)], writes=['TR8'])
                    S.op('act', lambda e: e.activation(out=YSQ, in_=YB[:], func=AF.Square), reads=[('YB', jx, hp_) for jx in range(64) for hp_ in (0, 64)],
                         writes=['SH'])
                    S.op('dve', lambda e: e.tensor_reduce(out=SSQ, in_=YSQ, axis=AX.X, op=ALU.add), reads=['SH'], writes=['TR8'])
                    S.op('dve', lambda e: e.tensor_scalar(out=MEAN, in0=SUM, scalar1=1.0 / 64, scalar2=None, op0=ALU.mult),
                         reads=['TR8'], writes=['TR8'])
                    S.op('dve', lambda e: e.tensor_tensor(out=SUM, in0=MEAN, in1=MEAN, op=ALU.mult), reads=['TR8'], writes=['TR8'])
                    S.op('dve', lambda e: e.scalar_tensor_tensor(out=SSQ, in0=SSQ, scalar=1.0 / 64, in1=SUM, op0=ALU.mult,
                                                                 op1=ALU.subtract), reads=['TR8'], writes=['TR8'])
                    S.op('dve', lambda e: e.tensor_scalar(out=SSQ, in0=SSQ, scalar1=LN_X_EPS, scalar2=None, op0=ALU.add),
                         reads=['TR8'], writes=['TR8'])
                    S.op('act', lambda e: e.activation(out=RSTD, in_=SSQ, func=AF.Sqrt), reads=['TR8'], writes=['TR8'])
                    S.op('dve', lambda e: e.reciprocal(out=RSTD, in_=RSTD), reads=['TR8'], writes=['TR8'])
                    S.op('dve', lambda e: e.tensor_tensor(out=YB[:], in0=YB[:], in1=MEAN.unsqueeze(2).to_broadcast([128, 64, 64]),
                                                          op=ALU.subtract), reads=['TR8'] + [('YB', jx, hp_) for jx in range(64) for hp_ in (0, 64)],
                         writes=[('YB', jx, hp_) for jx in range(64) for hp_ in (0, 64)])
                    S.op('dve', lambda e: e.tensor_tensor(out=YNb, in0=YB[:], in1=RSTD.unsqueeze(2).to_broadcast([128, 64, 64]),
                                                          op=ALU.mult), reads=['TR8'] + [('YB', jx, hp_) for jx in range(64) for hp_ in (0, 64)],
                         writes=['PL'])
                    for ti, (t0, m) in enumerate(seq_tiles(T)):
                        g0 = TC + t0
                        for hp in (0, 64):
                            for q in range(8):
                                jx = ti * 8 + q
                                S.mm(
                                    ps[0][hp:hp + 64, q * 64:(q + 1) * 64], lhsT=YNb[hp:hp + 64, jx, :], rhs=IDB[hp:hp + 64, hp:hp + 64],
                                    start=True, stop=True, reads=['PL', 'IDB'], writes=['ps0'], inc=(q == 7 and hp == 64))
                        S.op('dve', lambda e, g0=g0, m=m: e.scalar_tensor_tensor(
                            out=TR[0][:, :m], in0=ZR[:, g0:g0 + m], scalar=PV[:, rko + c:rko + c + 1], in1=ZK[:, g0:g0 + m],
                            op0=ALU.mult, op1=ALU.mult), reads=['ZR', 'ZK'], writes=['TR0'])
                        S.mm(ps[1][:, :m], lhsT=BONES[:], rhs=TR[0][:, :m], start=True, stop=True,
                             reads=['TR0', 'BONES'], writes=['ps1'])
                        S.op('dve', lambda e, g0=g0, m=m: e.tensor_tensor(out=TR[1][:, :m], in0=ps[1][:, :m], in1=ZV[:, g0:g0 + m],
                                                                          op=ALU.mult), reads=['ps1', 'ZV'], writes=['TR1'])
                        S.dma('sp', GO[0][:, :m], GSv[:, c, t0:t0 + m], reads=[('GS', c, t0)], writes=['GO0'])
                        S.op('dve', lambda e, m=m: e.tensor_scalar(out=TR[2][:, :m], in0=ps[0][:, :m], scalar1=PV[:, lwo + c:lwo + c + 1],
                                                                   scalar2=PV[:, lbo + c:lbo + c + 1], op0=ALU.mult, op1=ALU.add),
                             reads=['ps0'], writes=['TR2'])
                        S.op('pool', lambda e, m=m: e.tensor_tensor(out=TR[2][:, :m], in0=TR[2][:, :m], in1=TR[1][:, :m], op=ALU.add),
                             reads=['TR2', 'TR1'], writes=['TR2'])
                        S.op('pool', lambda e, m=m: e.tensor_tensor(out=GO[1][:, :m], in0=TR[2][:, :m], in1=GO[0][:, :m], op=ALU.mult),
                             reads=['TR2', 'GO0'], writes=['GO1'])
                        S.dma('pool', YRWv[:, c, t0:t0 + m], GO[1][:, :m], reads=['GO1'], writes=[('YRW', c, t0)])
                if debug:
                    S.dma('sp', SFD[:, :, :, :], ST8[:], reads=['ST8'], writes=['SFD'])
                S.barrier()

        with ExitStack() as ph_r:
            try:
                _rwkv_body(ph_r)
            except _Cut:
                print("CUT at", RCUT, "ops", S.nops)
            S.barrier()
        if RCUT:
            return nc

        if phases is not None and "C" not in phases:
            return nc

        def phase_merge():
            with ExitStack() as ph:
                WP = [sb("WP%d" % i, [128, 8, D], BF16, st=ph) for i in range(3)]
                STG = sb("STGC", [128, 2, D], st=ph)
                YR = sb("YRt", [128, 8, 512], BF16, st=ph)
                YLt = sb("YLt", [128, 8, 512], BF16, st=ph)
                SM = sb("SMt", [128, 16, 512], BF16, st=ph)
                X1t = sb("X1t", [128, 8, 512], st=ph)
                MG = sb("MGt", [128, 8, 512], BF16, st=ph)
                TA = [sb("TAc%d" % i, [128, 512], st=ph) for i in range(2)]
                ci = 0
                for wi, wsrc in enumerate((wpr_d, wpl_d, wo_d)):
                    for k in range(8):
                        i = ci % 2
                        ci += 1
                        S.dma('sp' if i == 0 else 'pool', STG[:, i, :], wsrc[k * 128:(k + 1) * 128, :], writes=['stg%d' % i])
                        S.op('dve' if i == 0 else 'act',
                             (lambda e, wi=wi, k=k, i=i: e.tensor_copy(out=WP[wi][:, k, :], in_=STG[:, i, :])) if i == 0 else
                             (lambda e, wi=wi, k=k, i=i: e.copy(out=WP[wi][:, k, :], in_=STG[:, i, :])), reads=['stg%d' % i])
                S.barrier()
                for (t0, n) in seq_tiles(T):
                    g0 = TC + t0
                    for c in range(8):
                        S.dma('sp', YR[:, c, :n], YRWv[:, c, t0:t0 + n], writes=[('YR', c)])
                        S.dma('pool', YLt[:, c, :n], YLv[:, c, t0:t0 + n], writes=[('YLt', c)])
                        S.dma('sp', X1t[:, c, :n], X1v[:, c, g0:g0 + n], writes=[('X1t', c)])
                    for c in range(16):
                        S.dma('pool' if c % 2 else 'sp', SM[:, c, :n], Pv[:, 43 + c, g0:g0 + n], writes=[('SM', c)])
                    for o in range(8):
                        pa, pb = ps[(o % 2) * 2], ps[(o % 2) * 2 + 1]
                        ka, kb = 'ps%d' % ((o % 2) * 2), 'ps%d' % ((o % 2) * 2 + 1)
                        for k in range(8):
                            S.mm(pa[:, :n], lhsT=WP[0][:, k, o * 128:(o + 1) * 128], rhs=YR[:, k, :n], start=(k == 0), stop=(k == 7),
                                 reads=[('YR', k)], writes=[ka], inc=(k == 7))
                        for k in range(8):
                            S.mm(pb[:, :n], lhsT=WP[1][:, k, o * 128:(o + 1) * 128], rhs=YLt[:, k, :n], start=(k == 0), stop=(k == 7),
                                 reads=[('YLt', k)], writes=[kb], inc=(k == 7))
                        S.op('dve', lambda e, pa=pa, o=o: e.tensor_tensor(out=TA[0][:, :n], in0=pa[:, :n], in1=SM[:, o, :n], op=ALU.mult),
                             reads=[ka, ('SM', o)], writes=['TA0'])
                        S.op('dve', lambda e, pb=pb, o=o: e.tensor_tensor(out=TA[1][:, :n], in0=pb[:, :n], in1=SM[:, 8 + o, :n], op=ALU.mult),
                             reads=[kb, ('SM', 8 + o)], writes=['TA1'])
                        S.op('pool', lambda e, o=o: e.tensor_tensor(out=MG[:, o, :n], in0=TA[0][:, :n], in1=TA[1][:, :n], op=ALU.add),
                             reads=['TA0', 'TA1'], writes=[('MG', o)])
                    for o in range(8):
                        pc_ = ps[4 + o % 2]
                        kc_ = 'ps%d' % (4 + o % 2)
                        for k in range(8):
                            S.mm(pc_[:, :n], lhsT=WP[2][:, k, o * 128:(o + 1) * 128], rhs=MG[:, k, :n], start=(k == 0), stop=(k == 7),
                                 reads=[('MG', k)], writes=[kc_], inc=(k == 7))
                        S.op('dve', lambda e, pc_=pc_, o=o: e.scalar_tensor_tensor(
                            out=X1t[:, o, :n], in0=pc_[:, :n], scalar=SCAL[:, 0, 5, o:o + 1], in1=X1t[:, o, :n], op0=ALU.mult,
                            op1=ALU.add), reads=[kc_, ('X1t', o)], writes=[('X1t', o)])
                        S.dma('pool', X2v[:, o, t0:t0 + n], X1t[:, o, :n], reads=[('X1t', o)], writes=[('X2', o, t0)])
                S.barrier()

        phase_merge()

        def load_feat_major(S, SCR, XT, s_, t0, n):
            for c in range(8):
                S.dma('sp' if c % 2 == 0 else 'pool', XT[:, c, :n], X2v[:, c, t0:t0 + n], writes=[('XT', c)])

        def epi_final(S, XT, HN, TM, RS, rmsnorm_mod, s_, t0, n, SCR=None):
            rmsnorm_mod(n, s_, 0, XT, ia=9, ib=10, dk='XT')
            for b_ in range(n // 128):
                for c in range(8):
                    pp = ps[6 + (c // 4) % 2]
                    S.mm(pp[:, (c % 4) * 128:(c % 4 + 1) * 128], lhsT=XT[:, c, b_ * 128:(b_ + 1) * 128], rhs=ident[:],
                         start=True, stop=True, reads=[('XT', c), 'ident'], writes=['ps%d' % (6 + (c // 4) % 2)], inc=(c % 4 == 3))
                    if c % 4 == 3:
                        h_ = c // 4
                        S.op('act' if h_ == 0 else 'dve',
                             (lambda e, pp=pp, b_=b_, h_=h_: e.copy(out=SCR[:, b_ * 1024 + h_ * 512:b_ * 1024 + (h_ + 1) * 512], in_=pp[:, 0:512]))
                             if h_ == 0 else
                             (lambda e, pp=pp, b_=b_, h_=h_: e.tensor_copy(out=SCR[:, b_ * 1024 + h_ * 512:b_ * 1024 + (h_ + 1) * 512], in_=pp[:, 0:512])),
                             reads=['ps%d' % (6 + h_)], writes=[('XIN', b_)])
                S.dma('pool', out_d[t0 + b_ * 128:t0 + (b_ + 1) * 128, :], SCR[:, b_ * 1024:(b_ + 1) * 1024], reads=[('XIN', b_)],
                      writes=[('OUT', t0, b_)])

        tiles2 = [(0, i * 512, 512) for i in range(T // 512)]
        ffn_phase("ffn2", tiles2, load_feat_major, epi_final)

        S.barrier()
        print("ops emitted", S.nops, S.cnt, S.dn)
    return nc


_CACHE = {}


def _prep_inputs(inputs, b):
    f = lambda a: np.ascontiguousarray(a, dtype=np.float32)
    m = {"x": f(inputs["x"][b]), "ctx": f(inputs["ctx"][b])}
    src = dict(inputs)
    src["c"] = inputs["c"][b]
    for n, r in VEC_ROWS:
        m[n] = f(np.asarray(src[n]).reshape(r, 128))
    m["w_mod"] = f(inputs["w_mod"][0])
    m["w_in"] = f(inputs["w_in"][0])
    m["lru_wa"] = f(inputs["lru_wa"][0])
    m["lru_wx"] = f(inputs["lru_wx"][0])
    m["rw_w_up"] = f(inputs["rw_w_up"][0].reshape(128, D))
    m["w_proj_rw"] = f(inputs["w_proj_rw"][0])
    m["w_proj_lru"] = f(inputs["w_proj_lru"][0])
    m["w_out"] = f(inputs["w_out"][0])
    m["rw_a_up"] = f(inputs["rw_a_up"][0].reshape(128, D))
    m["rw_g_up"] = f(inputs["rw_g_up"][0])
    for fn_ in ("ffn1", "ffn2"):
        for s in ("_wg", "_wu", "_wd"):
            m[fn_ + s] = f(inputs[fn_ + s][0])
    return m


def kernel(**inputs):
    if "nc" not in _CACHE:
        _CACHE["nc"] = build()
    nc = _CACHE["nc"]
    in_maps = [_prep_inputs(inputs, b % 4) for b in range(N_CORES)]
    res = run_bass_kernel_spmd(nc, in_maps, core_ids=list(range(N_CORES)))
    out = np.stack([res.results[b]["out"] for b in range(4)], axis=0)
    return out.astype(np.float32)
```

```python
import numpy as np
from contextlib import ExitStack
import concourse.bass as bass
import concourse.mybir as mybir
from concourse.bass_utils import run_bass_kernel_spmd

F32 = mybir.dt.float32
BF16 = mybir.dt.bfloat16
AF = mybir.ActivationFunctionType
ALU = mybir.AluOpType
AX = mybir.AxisListType

D = 1024
T = 4096
TC = 256
TT = T + TC
DFF = 2816
NJ = DFF // 128
NIN = 7552
NRW = 3456
EPS = 1e-6
N_CORES = 8


class Sched:
    R = 8

    def __init__(self, nc, es):
        self.nc = nc
        self.eng = {'pe': nc.tensor, 'act': nc.scalar, 'dve': nc.vector, 'pool': nc.gpsimd, 'sp': nc.sync}
        self.sem = {k: es.enter_context(nc.semaphore('s_' + k)) for k in ('pe', 'act', 'dve', 'pool')}
        self.cnt = {k: 0 for k in self.sem}
        self.dsem = {q: [es.enter_context(nc.semaphore('d_%s%d' % (q, i))) for i in range(self.R)]
                     for q in ('sp', 'pool')}
        self.dn = {q: 0 for q in self.dsem}
        self.waited = {e: {} for e in self.eng}
        self.lastw = {}
        self.readers = {}
        self.nops = 0
        self.pe_last = {}

    def _semof(self, k):
        return self.sem[k] if isinstance(k, str) else self.dsem[k[0]][k[1]]

    def _wait(self, e, k, val):
        if val <= 0 or self.waited[e].get(k, 0) >= val:
            return
        self.waited[e][k] = val
        self.eng[e].wait_ge(self._semof(k), val)

    def _deps(self, e, reads, writes):
        need = {}
        raw = {}
        for r in reads:
            t = self.lastw.get(r)
            if t is not None:
                if need.get(t[0], 0) < t[1]:
                    need[t[0]] = t[1]
                if raw.get(t[0], 0) < t[1]:
                    raw[t[0]] = t[1]
            if isinstance(r, str) and r.startswith('ps'):
                for k, v in self.readers.get(r, {}).items():
                    if k != e and need.get(k, 0) < v:
                        need[k] = v
        for w in writes:
            t = self.lastw.get(w)
            if t is not None and need.get(t[0], 0) < t[1]:
                need[t[0]] = t[1]
            for k, v in self.readers.get(w, {}).items():
                if need.get(k, 0) < v:
                    need[k] = v
        for k, v in need.items():
            if k == e:
                if e == 'pe':
                    continue
                if e in ('act', 'dve'):
                    v = raw.get(k, 0)
                    if v <= 0:
                        continue
            self._wait(e, k, v)

    def _commit(self, tok, reads, writes):
        for w in writes:
            self.lastw[w] = tok
            self.readers[w] = {}
        ws = set(writes)
        for r in reads:
            if r in ws:
                continue
            d = self.readers.setdefault(r, {})
            if d.get(tok[0], 0) < tok[1]:
                d[tok[0]] = tok[1]

    def mm(self, out, lhsT, rhs, start=True, stop=True, reads=(), writes=(), inc=True):
        rg = lhsT.base_partition() if lhsT.partition_size() < 128 else None
        self.op('pe', lambda e: e.matmul(out, lhsT=lhsT, rhs=rhs, start=start, stop=stop), reads, writes, inc, rg=rg)

    def op(self, e, fn, reads=(), writes=(), inc=True, rg=None):
        if e == 'pe':
            if rg is not None:
                inc = True
                for w in writes:
                    pl = self.pe_last.get(w)
                    if pl is not None and pl[0] != rg:
                        self._wait('pe', 'pe', pl[1])
            else:
                for w in writes:
                    self.pe_last.pop(w, None)
        self._deps(e, reads, writes)
        ins = fn(self.eng[e])
        self.nops += 1
        if inc:
            self.cnt[e] += 1
            ins.then_inc(self.sem[e], 1)
            tok = (e, self.cnt[e])
        else:
            assert e == 'pe'
            tok = (e, self.cnt[e] + 1)
        if e == 'pe' and rg is not None:
            for w in writes:
                self.pe_last[w] = (rg, tok[1])
        self._commit(tok, reads, writes)

    def dma(self, q, out, in_, reads=(), writes=(), **kw):
        n = self.dn[q]
        i = n % self.R
        tgt = 16 * (n // self.R + 1)
        self._wait(q, (q, i), tgt - 16)
        self._deps(q, reads, writes)
        self.eng[q].dma_start(out=out, in_=in_, **kw).then_inc(self.dsem[q][i], 16)
        self.nops += 1
        self.dn[q] += 1
        self._commit(((q, i), tgt), reads, writes)

    def barrier(self):
        for e in self.eng:
            for k in self.sem:
                self._wait(e, k, self.cnt[k])
            for q in self.dsem:
                for i in range(self.R):
                    n_i = (self.dn[q] - i + self.R - 1) // self.R
                    self._wait(e, (q, i), 16 * n_i)
        self.lastw = {}
        self.readers = {}


VEC_ROWS = [("b_mod", 72), ("g_ffn1", 8), ("g_mix", 8), ("g_ffn2", 8), ("g_final", 8), ("rw_mu", 27),
            ("rw_w0", 16), ("rw_a0", 16), ("rw_k_k", 8), ("rw_k_a", 8), ("rw_r_k", 8), ("rw_ln_w", 8),
            ("rw_ln_b", 8), ("lru_conv_w", 32), ("lru_conv_b", 8), ("lru_lam", 16), ("lru_ba", 16),
            ("lru_bx", 16), ("c", 8), ("c_ctx", 8)]
VEC_OFF = {}
_o = 0
for _n, _r in VEC_ROWS:
    VEC_OFF[_n] = _o
    _o += _r
NVEC = _o
NVT = (NVEC + 127) // 128


def build(debug=False, phases=None):
    nc = bass.Bass("TRN2", target_bir_lowering=False)
    dram_in = {}

    def din(name, shape, dt=F32):
        dram_in[name] = nc.dram_tensor(name, list(shape), dt, kind="ExternalInput").ap()
        return dram_in[name]

    x_d = din("x", [T, D])
    ctx_d = din("ctx", [TC, D])
    for n, r in VEC_ROWS:
        din(n, [r, 128])
    w_mod = din("w_mod", [D, 9 * D])
    wts = {}
    for f in ("ffn1", "ffn2"):
        wts[f + "_wg"] = din(f + "_wg", [D, DFF])
        wts[f + "_wu"] = din(f + "_wu", [D, DFF])
        wts[f + "_wd"] = din(f + "_wd", [DFF, D])
    w_in_d = din("w_in", [D, NIN])
    lru_wa_d = din("lru_wa", [2, 4, 256, 256])
    lru_wx_d = din("lru_wx", [2, 4, 256, 256])
    wpr_d = din("w_proj_rw", [D, D])
    wpl_d = din("w_proj_lru", [D, D])
    wo_d = din("w_out", [D, D])
    rw_w_up_d = din("rw_w_up", [128, D])
    rw_a_up_d = din("rw_a_up", [128, D])
    rw_g_up_d = din("rw_g_up", [128, D])
    out_d = nc.dram_tensor("out", [T, D], F32, kind="ExternalOutput").ap()
    skind = "ExternalOutput" if debug else "Internal"
    X1 = nc.dram_tensor("X1", [D, TT], F32, kind=skind).ap()
    XN = nc.dram_tensor("XN", [D, TT], BF16, kind=skind).ap()
    MODD = nc.dram_tensor("MODD", [128, 144], F32, kind=skind).ap()
    P = nc.dram_tensor("P", [NIN, TT], BF16, kind=skind).ap()
    YL = nc.dram_tensor("YL", [D, T], BF16, kind=skind).ap()
    HSTD = nc.dram_tensor("HSTD", [128, 16], F32, kind=skind).ap()
    GS = nc.dram_tensor("GS", [D, T], BF16, kind=skind).ap()
    YRW = nc.dram_tensor("YRW", [D, T], BF16, kind=skind).ap()
    SFD = nc.dram_tensor("SFD", [128, 2, 8, 64], F32, kind=skind).ap()
    X2 = nc.dram_tensor("X2", [D, T], F32, kind=skind).ap()
    X2v = X2.rearrange("(c p) t -> p c t", p=128)
    GSv = GS.rearrange("(c p) t -> p c t", p=128)
    YRWv = YRW.rearrange("(c p) t -> p c t", p=128)
    Pv = P.rearrange("(c p) t -> p c t", p=128)
    YLv = YL.rearrange("(c p) t -> p c t", p=128)

    with ExitStack() as es:
        S = Sched(nc, es)
        ps = [es.enter_context(nc.psum_tensor("ps%d" % i, [128, 512], F32)) for i in range(8)]
        sb = lambda name, shape, dt=F32, st=es: st.enter_context(nc.sbuf_tensor(name, list(shape), dt))
        ident = sb("ident", [128, 128])
        ones = sb("ones", [128, 128])
        PV = sb("PV", [128, NVT * 128])
        MOD = sb("MOD", [128, 72, 2])
        SCAL = sb("SCAL", [128, 2, 11, 8])

        def pv(name, i=0):
            o = VEC_OFF[name] + i
            return PV[:, o:o + 1]

        S.op('pool', lambda e: e.memset(ident[:], 1.0), writes=['ident'])
        S.op('pool', lambda e: e.affine_select(out=ident[:], in_=ident[:], pattern=[[-1, 128]],
                                               compare_op=ALU.is_equal, fill=0.0, base=0, channel_multiplier=1),
             reads=['ident'], writes=['ident'])
        S.op('pool', lambda e: e.memset(ones[:], 1.0), writes=['ones'])

        with ExitStack() as p0:
            VST = sb("VST", [128, NVT, 128], st=p0)
            SC = sb("SC", [128, 8, 2], st=p0)
            slab = [sb("slab%d" % i, [128, 8, 1152], st=p0) for i in range(2)]
            S.op('dve', lambda e: e.memset(VST[:], 0.0), writes=['VST'])
            for n, r in VEC_ROWS:
                o = VEC_OFF[n]
                done = 0
                while done < r:
                    ti, ri = divmod(o + done, 128)
                    m = min(r - done, 128 - ri)
                    S.dma('sp', VST[ri:ri + m, ti, :], dram_in[n][done:done + m, :], reads=[], writes=['VST'])
                    done += m
            for ti in range(NVT):
                S.mm(ps[7][:, 0:128], lhsT=VST[:, ti, :], rhs=ident[:],
                                                     start=True, stop=True,
                     reads=['VST', 'ident'], writes=['ps7'])
                S.op('dve', lambda e, ti=ti: e.tensor_copy(out=PV[:, ti * 128:(ti + 1) * 128], in_=ps[7][:, 0:128]),
                     reads=['ps7'], writes=['PV'])
            oc_, occ = VEC_OFF["c"], VEC_OFF["c_ctx"]
            S.op('act', lambda e: e.activation(out=SC[:, :, 0], in_=PV[:, oc_:oc_ + 8], func=AF.Silu),
                 reads=['PV'], writes=['SC'])
            S.op('act', lambda e: e.activation(out=SC[:, :, 1], in_=PV[:, occ:occ + 8], func=AF.Silu),
                 reads=['PV'], writes=['SC'])
            wmv = w_mod.rearrange("(k p) n -> p k n", p=128)
            for si in range(8):
                sl = slab[si % 2]
                key = 'slab%d' % (si % 2)
                for k in range(8):
                    S.dma('sp' if k % 2 == 0 else 'pool', sl[:, k, :], wmv[:, k, si * 1152:(si + 1) * 1152],
                          writes=[key])
                for ol in range(9):
                    oc = si * 9 + ol
                    for k in range(8):
                        S.mm(
                            ps[6][:, oc * 2:oc * 2 + 2], lhsT=sl[:, k, ol * 128:(ol + 1) * 128], rhs=SC[:, k, :],
                            start=(k == 0), stop=(k == 7),
                            reads=[key, 'SC'], writes=['ps6'], inc=(k == 7))
            bo = VEC_OFF["b_mod"]
            psv = ps[6][:, 0:144].rearrange("p (a b) -> p a b", b=2)
            for s_ in range(2):
                S.op('dve', lambda e, s_=s_: e.tensor_tensor(out=MOD[:, :, s_], in0=psv[:, :, s_],
                                                             in1=PV[:, bo:bo + 72], op=ALU.add),
                     reads=['ps6', 'PV'], writes=['MOD'])
            for s_ in range(2):
                for gi, gname in enumerate(("g_ffn1", "g_mix", "g_ffn2")):
                    go = VEC_OFF[gname]
                    sh = MOD[:, (3 * gi) * 8:(3 * gi + 1) * 8, s_]
                    scl = MOD[:, (3 * gi + 1) * 8:(3 * gi + 2) * 8, s_]
                    gt = MOD[:, (3 * gi + 2) * 8:(3 * gi + 3) * 8, s_]
                    S.op('dve', lambda e, s_=s_, gi=gi, scl=scl, go=go: e.scalar_tensor_tensor(
                        out=SCAL[:, s_, 3 * gi, :], in0=scl, scalar=1.0, in1=PV[:, go:go + 8],
                        op0=ALU.add, op1=ALU.mult), reads=['MOD', 'PV'], writes=['SCAL'])
                    S.op('dve', lambda e, s_=s_, gi=gi, sh=sh: e.tensor_copy(out=SCAL[:, s_, 3 * gi + 1, :], in_=sh),
                         reads=['MOD'], writes=['SCAL'])
                    S.op('dve', lambda e, s_=s_, gi=gi, gt=gt: e.tensor_scalar(
                        out=SCAL[:, s_, 3 * gi + 2, :], in0=gt, scalar1=(1.0 if gi == 1 else 0.5), scalar2=None,
                        op0=ALU.mult), reads=['MOD'], writes=['SCAL'])
            gfo = VEC_OFF["g_final"]
            for s_ in range(2):
                S.op('dve', lambda e, s_=s_: e.tensor_copy(out=SCAL[:, s_, 9, :], in_=PV[:, gfo:gfo + 8]), reads=['PV'], writes=['SCAL'])
                S.op('dve', lambda e, s_=s_: e.memset(SCAL[:, s_, 10, :], 0.0), reads=['SCAL'], writes=['SCAL'])
            if debug:
                S.dma('sp', MODD[:, :], MOD[:].rearrange("p a b -> p (a b)"), reads=['MOD'], writes=['MODD'])
            S.barrier()

        def ffn_phase(which, tiles, load_tile, epilogue):
            with ExitStack() as ph:
                WG = sb("WG_" + which, [128, 8, DFF], BF16, st=ph)
                WU = sb("WU_" + which, [128, 8, DFF], BF16, st=ph)
                WD = sb("WD_" + which, [128, NJ, D], BF16, st=ph)
                SCR = sb("SCR_" + which, [128, 4096], st=ph)
                XT = sb("XT_" + which, [128, 8, 512], st=ph)
                HN = sb("HN_" + which, [128, 8, 512], BF16, st=ph)
                AA = sb("AA_" + which, [128, NJ, 512], BF16, st=ph)
                TM = [sb("TM_" + which + "%d" % i, [128, 512], st=ph) for i in range(2)]
                RS = sb("RS_" + which, [128, 512], st=ph)
                ci = [0]
                cast_eng = ('dve', 'pool', 'act')

                def load_cast(dst, src, w):
                    i = ci[0] % 4
                    ci[0] += 1
                    key = 'stg%d' % i
                    S.dma('sp' if i % 2 == 0 else 'pool', SCR[:, i * 1024:i * 1024 + w], src, writes=[key])
                    if ci[0] % 2 == 0:
                        S.op('act', lambda e: e.copy(out=dst, in_=SCR[:, i * 1024:i * 1024 + w]), reads=[key])
                    else:
                        S.op('dve', lambda e: e.tensor_copy(out=dst, in_=SCR[:, i * 1024:i * 1024 + w]), reads=[key])
                for (Wt, wd_) in ((WG, wts[which + "_wg"]), (WU, wts[which + "_wu"])):
                    for k in range(8):
                        for c0 in range(0, DFF, 1024):
                            w = min(1024, DFF - c0)
                            load_cast(Wt[:, k, c0:c0 + w], wd_[k * 128:(k + 1) * 128, c0:c0 + w], w)
                wdd = wts[which + "_wd"]
                for j in range(NJ):
                    load_cast(WD[:, j, :], wdd[j * 128:(j + 1) * 128, :], 1024)
                S.barrier()

                def rmsnorm_mod(n, s_, gi, dst, ia=None, ib=None, dk='HN'):
                    ia = 3 * gi if ia is None else ia
                    ib = 3 * gi + 1 if ib is None else ib
                    for c in range(8):
                        tm = TM[c % 2]
                        S.op('act', lambda e, c=c, tm=tm: e.activation(out=tm[:, :n], in_=XT[:, c, :n], func=AF.Square),
                             reads=[('XT', c)], writes=['TM%d' % (c % 2)])
                        S.mm(ps[6][:, :n], lhsT=ones[:], rhs=tm[:, :n],
                                                                  start=(c == 0), stop=(c == 7),
                             reads=['TM%d' % (c % 2), 'ones'], writes=['ps6'], inc=True)
                    S.op('act', lambda e: e.activation(out=RS[:, :n], in_=ps[6][:, :n], func=AF.Sqrt, bias=EPSB[:, 0:1],
                                                       scale=1.0 / D), reads=['ps6'], writes=['RS'])
                    S.op('dve', lambda e: e.reciprocal(out=RS[:, :n], in_=RS[:, :n]), reads=['RS'], writes=['RS'])
                    for c in range(8):
                        tm = TM[c % 2]
                        S.op('dve', lambda e, c=c, tm=tm: e.scalar_tensor_tensor(
                            out=tm[:, :n], in0=XT[:, c, :n], scalar=SCAL[:, s_, ia, c:c + 1], in1=RS[:, :n],
                            op0=ALU.mult, op1=ALU.mult), reads=[('XT', c), 'RS'], writes=['TM%d' % (c % 2)])
                        S.op('act', lambda e, c=c, tm=tm: e.activation(
                            out=dst[:, c, :n], in_=tm[:, :n], func=AF.Identity, bias=SCAL[:, s_, ib, c:c + 1],
                            scale=1.0), reads=['TM%d' % (c % 2)], writes=[(dk, c)])

                gi = 0 if which == "ffn1" else 2
                for (s_, t0, n) in tiles:
                    load_tile(S, SCR, XT, s_, t0, n)
                    rmsnorm_mod(n, s_, gi, HN)
                    for j in range(NJ):
                        pg, pu = ps[(j % 2) * 2], ps[(j % 2) * 2 + 1]
                        kg, ku = 'ps%d' % ((j % 2) * 2), 'ps%d' % ((j % 2) * 2 + 1)
                        for (pp, kk_, Wt) in ((pg, kg, WG), (pu, ku, WU)):
                            for k in range(8):
                                S.mm(
                                    pp[:, :n], lhsT=Wt[:, k, j * 128:(j + 1) * 128], rhs=HN[:, k, :n],
                                    start=(k == 0), stop=(k == 7),
                                    reads=[('HN', k)], writes=[kk_], inc=(k == 7))
                        tm = TM[j % 2]
                        S.op('act', lambda e, pg=pg, tm=tm: e.activation(out=tm[:, :n], in_=pg[:, :n], func=AF.Silu),
                             reads=[kg], writes=['TM%d' % (j % 2)])
                        S.op('dve', lambda e, pu=pu, tm=tm, j=j: e.tensor_tensor(out=AA[:, j, :n], in0=pu[:, :n],
                                                                                 in1=tm[:, :n], op=ALU.mult),
                             reads=[ku, 'TM%d' % (j % 2)], writes=[('AA', j)])
                    for o in range(8):
                        pd = ps[4 + o % 2]
                        kd = 'ps%d' % (4 + o % 2)
                        for j in range(NJ):
                            S.mm(
                                pd[:, :n], lhsT=WD[:, j, o * 128:(o + 1) * 128], rhs=AA[:, j, :n],
                                start=(j == 0), stop=(j == NJ - 1),
                                reads=[('AA', j)], writes=[kd], inc=(j == NJ - 1))
                        S.op('dve', lambda e, pd=pd, o=o: e.scalar_tensor_tensor(
                            out=XT[:, o, :n], in0=pd[:, :n], scalar=SCAL[:, s_, 3 * gi + 2, o:o + 1], in1=XT[:, o, :n],
                            op0=ALU.mult, op1=ALU.add), reads=[kd, ('XT', o)], writes=[('XT', o)])
                    epilogue(S, XT, HN, TM, RS, rmsnorm_mod, s_, t0, n, SCR=SCR)
                S.barrier()

        EPSB = sb("EPSB", [128, 1])
        S.op('pool', lambda e: e.memset(EPSB[:], EPS), writes=['EPSB'])
        MUSG = sb("MUSG", [128, 3, 27])
        _mo = VEC_OFF["rw_mu"]
        S.op('dve', lambda e: e.tensor_scalar(out=MUSG[:, 0, :], in0=PV[:, _mo:_mo + 27], scalar1=-1.0, scalar2=1.0, op0=ALU.mult,
                                              op1=ALU.add), reads=['PV'], writes=['MUSG'])
        S.op('dve', lambda e: e.tensor_scalar(out=MUSG[:, 1, :], in0=PV[:, _mo:_mo + 27], scalar1=0.25, scalar2=None, op0=ALU.mult),
             reads=['PV'], writes=['MUSG'])
        S.op('dve', lambda e: e.tensor_scalar(out=MUSG[:, 2, :], in0=PV[:, _mo:_mo + 27], scalar1=0.5, scalar2=None, op0=ALU.mult),
             reads=['PV'], writes=['MUSG'])

        def load_tok_major(S, SCR, XT, s_, t0, n):
            src = x_d if s_ == 0 else ctx_d
            nb = n // 128
            for b_ in range(nb):
                S.dma('sp', SCR[:, b_ * 1024:(b_ + 1) * 1024], src[t0 + b_ * 128:t0 + (b_ + 1) * 128, :],
                      writes=[('XIN', b_)])
            for c in range(8):
                for b_ in range(nb):
                    S.mm(
                        ps[7][:, b_ * 128:(b_ + 1) * 128], lhsT=SCR[:, b_ * 1024 + c * 128:b_ * 1024 + (c + 1) * 128],
                        rhs=ident[:], start=True, stop=True, reads=[('XIN', b_), 'ident'], writes=['ps7'],
                        inc=(b_ == nb - 1))
                S.op('act' if c % 2 == 0 else 'dve',
                     (lambda e, c=c: e.copy(out=XT[:, c, :n], in_=ps[7][:, :n])) if c % 2 == 0 else
                     (lambda e, c=c: e.tensor_copy(out=XT[:, c, :n], in_=ps[7][:, :n])),
                     reads=['ps7'], writes=[('XT', c)])

        X1v = X1.rearrange("(c p) t -> p c t", p=128)
        XNv = XN.rearrange("(c p) t -> p c t", p=128)

        def epi_ffn1(S, XT, HN, TM, RS, rmsnorm_mod, s_, t0, n, SCR=None):
            g0 = (TC if s_ == 0 else 0) + t0
            for c in range(8):
                S.dma('pool', X1v[:, c, g0:g0 + n], XT[:, c, :n], reads=[('XT', c)], writes=[('X1', g0)])
            rmsnorm_mod(n, s_, 1, HN)
            for c in range(8):
                S.dma('pool', XNv[:, c, g0:g0 + n], HN[:, c, :n], reads=[('HN', c)], writes=[('XN', g0)])

        tiles1 = [(1, 0, TC)] + [(0, i * 512, 512) for i in range(T // 512)]
        if phases is None or "A" in phases:
            ffn_phase("ffn1", tiles1, load_tok_major, epi_ffn1)
        S.barrier()

        import os as _os
        RCUT = int(_os.environ.get("RCUT", "0"))

        class _Cut(Exception):
            pass

        def cut(k):
            if RCUT == k:
                raise _Cut()

        def seq_tiles(n):
            return [(t0, min(512, n - t0)) for t0 in range(0, n, 512)]

        def phase_inproj():
            with ExitStack() as ph:
                WIN = sb("WIN", [128, 8, NIN], BF16, st=ph)
                SCRB = sb("SCRB", [128, 4096], st=ph)
                XNH = [sb("XNH%d" % i, [128, 8, 640], BF16, st=ph) for i in range(2)]
                XS = [sb("XSn%d" % i, [128, 8, 512], BF16, st=ph) for i in range(2)]
                XSF = [sb("XSF%d" % i, [128, 512], st=ph) for i in range(2)]
                TZ = [sb("TZ%d" % i, [128, 512], st=ph) for i in range(2)]
                PO = [sb("PO%d" % i, [128, 512], BF16, st=ph) for i in range(6)]
                ci = 0
                for k in range(8):
                    for c0 in range(0, NIN, 1024):
                        w = min(1024, NIN - c0)
                        i = ci % 4
                        ci += 1
                        key = 'stg%d' % i
                        S.dma('sp' if i % 2 == 0 else 'pool', SCRB[:, i * 1024:i * 1024 + w],
                              w_in_d[k * 128:(k + 1) * 128, c0:c0 + w], writes=[key])
                        if ci % 2 == 0:
                            S.op('act', lambda e, i=i, w=w, k=k, c0=c0: e.copy(out=WIN[:, k, c0:c0 + w],
                                                                               in_=SCRB[:, i * 1024:i * 1024 + w]), reads=[key])
                        else:
                            S.op('dve', lambda e, i=i, w=w, k=k, c0=c0: e.tensor_copy(out=WIN[:, k, c0:c0 + w],
                                                                                      in_=SCRB[:, i * 1024:i * 1024 + w]), reads=[key])
                S.barrier()
                tl = [(1, 0, TC)] + [(0, t0, n) for (t0, n) in seq_tiles(T)]
                it = 0
                iz = 0
                for ti_, (s_, t0, n) in enumerate(tl):
                    g0 = (TC if s_ == 0 else 0) + t0
                    H = XNH[ti_ % 2]
                    Xs = XS[ti_ % 2]
                    kH = lambda c: ('XNH', ti_ % 2, c)
                    kX = lambda c: ('XS', ti_ % 2, c)
                    has_top = (s_ == 0 and t0 > 0)
                    has_bot = (s_ == 0 and t0 + n < T)
                    if not has_top:
                        S.op('pool', lambda e, H=H: e.memset(H[:, :, 0:64], 0.0), reads=[kH(c) for c in range(8)],
                             writes=[kH(c) for c in range(8)])
                    if not has_bot:
                        S.op('pool', lambda e, H=H: e.memset(H[:, :, 64 + n:128 + n], 0.0), reads=[kH(c) for c in range(8)],
                             writes=[kH(c) for c in range(8)])
                    for c in range(8):
                        lo = g0 - (64 if has_top else 0)
                        hi = g0 + n + (64 if has_bot else 0)
                        S.dma('sp', H[:, c, 64 - (g0 - lo):64 + n + (hi - g0 - n)], XNv[:, c, lo:hi], reads=[kH(c)], writes=[kH(c)])
                    for c in range(8):
                        if s_ == 1:
                            S.op('dve' if c % 2 else 'pool', lambda e, c=c: e.tensor_tensor(
                                out=Xs[:, c, :n], in0=H[:, c, 63:63 + n], in1=H[:, c, 65:65 + n], op=ALU.add),
                                reads=[kH(c)], writes=[kX(c)])
                        else:
                            xf = XSF[c % 2]
                            kf = 'XSF%d' % (c % 2)
                            xf3 = xf[:, :n].rearrange("p (r c) -> p r c", c=64)
                            c3 = H[:, c, 64:64 + n].rearrange("p (r c) -> p r c", c=64)
                            xs3 = Xs[:, c, :n].rearrange("p (r c) -> p r c", c=64)
                            S.op('pool', lambda e, c=c, xf=xf: e.tensor_tensor(out=xf[:, :n], in0=H[:, c, 0:n], in1=H[:, c, 128:128 + n],
                                                                              op=ALU.add), reads=[kH(c)], writes=[kf])
                            S.op('dve', lambda e, xf3=xf3, c3=c3: e.tensor_tensor(out=xf3[:, :, 1:64], in0=xf3[:, :, 1:64], in1=c3[:, :, 0:63],
                                                                                  op=ALU.add), reads=[kH(c), kf], writes=[kf])
                            S.op('dve', lambda e, xf3=xf3, c3=c3, xs3=xs3: e.tensor_tensor(out=xs3[:, :, 0:63], in0=xf3[:, :, 0:63],
                                                                                           in1=c3[:, :, 1:64], op=ALU.add),
                                 reads=[kH(c), kf], writes=[kX(c)])
                            S.op('pool', lambda e, xf3=xf3, xs3=xs3: e.tensor_copy(out=xs3[:, :, 63:64], in_=xf3[:, :, 63:64]),
                                 reads=[kf, kX(c)], writes=[kX(c)])
                    noc = 59 if s_ == 0 else 35
                    qi = 1 if s_ == 0 else 2
                    for oc in range(noc):
                        po = PO[it % 6]
                        kpo = 'PO%d' % (it % 6)
                        it += 1
                        if oc < 27:
                            b0 = (oc % 2) * 2
                            p1, p2 = ps[b0], ps[b0 + 1]
                            k1, k2 = 'ps%d' % b0, 'ps%d' % (b0 + 1)
                            for k in range(8):
                                S.mm(p1[:, :n], lhsT=WIN[:, k, oc * 128:(oc + 1) * 128], rhs=H[:, k, 64:64 + n], start=(k == 0), stop=(k == 7),
                                     reads=[kH(k)], writes=[k1], inc=(k == 7))
                            for k in range(8):
                                S.mm(p2[:, :n], lhsT=WIN[:, k, oc * 128:(oc + 1) * 128], rhs=Xs[:, k, :n], start=(k == 0), stop=(k == 7),
                                     reads=[kX(k)], writes=[k2], inc=(k == 7))
                            tz = TZ[iz % 2]
                            ktz = 'TZ%d' % (iz % 2)
                            iz += 1
                            S.op('act', lambda e, p2=p2, tz=tz, oc=oc: e.activation(out=tz[:, :n], in_=p2[:, :n], func=AF.Identity,
                                                                                    scale=MUSG[:, qi, oc:oc + 1]), reads=[k2, 'MUSG'],
                                 writes=[ktz])
                            S.op('dve', lambda e, p1=p1, tz=tz, po=po, oc=oc: e.scalar_tensor_tensor(
                                out=po[:, :n], in0=p1[:, :n], scalar=MUSG[:, 0, oc:oc + 1], in1=tz[:, :n], op0=ALU.mult, op1=ALU.add),
                                reads=[k1, ktz, 'MUSG'], writes=[kpo])
                        else:
                            pp = ps[4 + oc % 4]
                            kp = 'ps%d' % (4 + oc % 4)
                            for k in range(8):
                                S.mm(pp[:, :n], lhsT=WIN[:, k, oc * 128:(oc + 1) * 128], rhs=H[:, k, 64:64 + n], start=(k == 0), stop=(k == 7),
                                     reads=[kH(k)], writes=[kp], inc=(k == 7))
                            if oc >= 43:
                                S.op('act', lambda e, pp=pp, po=po: e.activation(out=po[:, :n], in_=pp[:, :n], func=AF.Sigmoid),
                                     reads=[kp], writes=[kpo])
                            elif oc >= 35:
                                S.op('act', lambda e, pp=pp, po=po: e.activation(out=po[:, :n], in_=pp[:, :n], func=AF.Gelu),
                                     reads=[kp], writes=[kpo])
                            elif oc % 2 == 0:
                                S.op('act', lambda e, pp=pp, po=po: e.copy(out=po[:, :n], in_=pp[:, :n]), reads=[kp], writes=[kpo])
                            else:
                                S.op('dve', lambda e, pp=pp, po=po: e.tensor_copy(out=po[:, :n], in_=pp[:, :n]), reads=[kp],
                                     writes=[kpo])
                        S.dma('pool', Pv[:, oc, g0:g0 + n], po[:, :n], reads=[kpo], writes=[('P', oc, g0)])
                S.barrier()

        if phases is None or "B" in phases:
            phase_inproj()
        if phases is not None and "L" not in phases and "R" not in phases:
            return nc

        ONEB = sb("ONEB", [128, 1])
        S.op('pool', lambda e: e.memset(ONEB[:], 1.0), writes=['ONEB'])
        LC = sb("LC", [128, 2, 16])
        HST = sb("HST", [128, 16])
        lo = VEC_OFF["lru_lam"]
        S.op('act', lambda e: e.activation(out=LC[:, 0, :], in_=PV[:, lo:lo + 16], func=AF.Exp, scale=-1.0),
             reads=['PV'], writes=['LC'])
        S.op('act', lambda e: e.activation(out=LC[:, 0, :], in_=LC[:, 0, :], func=AF.Ln, bias=ONEB[:, 0:1], scale=1.0),
             reads=['LC', 'ONEB'], writes=['LC'])
        S.op('dve', lambda e: e.tensor_scalar(out=LC[:, 1, :], in0=LC[:, 0, :], scalar1=-16.0, scalar2=None, op0=ALU.mult),
             reads=['LC'], writes=['LC'])
        S.op('dve', lambda e: e.tensor_scalar(out=LC[:, 0, :], in0=LC[:, 0, :], scalar1=-8.0, scalar2=None, op0=ALU.mult),
             reads=['LC'], writes=['LC'])

        def phase_lru():
            with ExitStack() as ph:
                WAX = sb("WAX", [128, 2, 16, 256], BF16, st=ph)
                STG = sb("STGL", [128, 16, 256], st=ph)
                XL = sb("XL", [128, 2, T + 4], BF16, st=ph)
                XC = sb("XC", [128, 2, T], st=ph)
                XCB = sb("XCB", [128, 2, T], BF16, st=ph)
                AB = sb("AB", [128, T], st=ph)
                UB = sb("UB", [128, T], st=ph)
                HB = [sb("HB%d" % i, [128, T], st=ph) for i in range(2)]
                GL = sb("GLg", [128, T], BF16, st=ph)
                YB = sb("YBl", [128, T], BF16, st=ph)
                TL = [sb("TL%d" % i, [128, 512], st=ph) for i in range(5)]
                for gi_, wsrc in enumerate((lru_wa_d, lru_wx_d)):
                    S.dma('sp', STG[:], wsrc.rearrange("d n (k p) j -> p (d n k) j", p=128), writes=['STGL'])
                    S.op('dve', lambda e, gi_=gi_: e.tensor_copy(out=WAX[:, gi_, :, :], in_=STG[:]), reads=['STGL'],
                         writes=['WAX'])
                S.op('pool', lambda e: e.memset(XL[:], 0.0), writes=['XL'])
                cw, cb = VEC_OFF["lru_conv_w"], VEC_OFF["lru_conv_b"]
                bao, bxo = VEC_OFF["lru_ba"], VEC_OFF["lru_bx"]
                for s_ in (1, 0):
                    n = TC if s_ == 1 else T
                    g0 = 0 if s_ == 1 else TC
                    if s_ == 0:
                        S.op('pool', lambda e: e.memset(XL[:], 0.0), reads=['XL'], writes=['XL'])
                    for blk in range(4):
                        for k in range(2):
                            cc = 2 * blk + k
                            S.dma('sp', XL[:, k, 1:1 + n], Pv[:, 27 + cc, g0:g0 + n], reads=[('P', 27 + cc, g0)],
                                  writes=['XL'])
                            S.op('act', lambda e, k=k, cc=cc: e.activation(
                                out=XC[:, k, :n], in_=XL[:, k, 0:n], func=AF.Identity, bias=PV[:, cb + cc:cb + cc + 1],
                                scale=PV[:, cw + cc:cw + cc + 1]), reads=['XL'], writes=[('XC', k)])
                            for j in range(1, 4):
                                S.op('dve', lambda e, k=k, cc=cc, j=j: e.scalar_tensor_tensor(
                                    out=XC[:, k, :n], in0=XL[:, k, j:j + n], scalar=PV[:, cw + j * 8 + cc:cw + j * 8 + cc + 1],
                                    in1=XC[:, k, :n], op0=ALU.mult, op1=ALU.add), reads=['XL', ('XC', k)], writes=[('XC', k)])
                            S.op('act', lambda e, k=k: e.copy(out=XCB[:, k, :n], in_=XC[:, k, :n]), reads=[('XC', k)],
                                 writes=[('XCB', k)])
                        for oc in range(2):
                            cc = 2 * blk + oc
                            if s_ == 0:
                                S.dma('sp', GL[:, :n], Pv[:, 35 + cc, g0:g0 + n], reads=[('P', 35 + cc, g0)], writes=['GL'])
                            for d in range(2):
                                col = d * 8 + cc
                                kHB = 'HB%d' % d
                                HBd = HB[d]
                                for (t0, m) in seq_tiles(n):
                                    for gi_ in range(2):
                                        pp = ps[gi_]
                                        for kin in range(2):
                                            S.mm(
                                                pp[:, :m], lhsT=WAX[:, gi_, (d * 4 + blk) * 2 + kin, oc * 128:(oc + 1) * 128],
                                                rhs=XCB[:, kin, t0:t0 + m], start=(kin == 0), stop=(kin == 1),
                                                reads=[('XCB', kin), 'WAX'], writes=['ps%d' % gi_], inc=(kin == 1))
                                    S.op('act', lambda e, m=m, col=col, t0=t0: e.activation(
                                        out=AB[:, t0:t0 + m], in_=ps[0][:, :m], func=AF.Sigmoid, bias=PV[:, bao + col:bao + col + 1],
                                        scale=1.0), reads=['ps0'], writes=[('AB', t0)])
                                    S.op('act', lambda e, m=m, col=col, t0=t0: e.activation(
                                        out=UB[:, t0:t0 + m], in_=ps[1][:, :m], func=AF.Sigmoid, bias=PV[:, bxo + col:bxo + col + 1],
                                        scale=1.0), reads=['ps1'], writes=[('UB', t0)])
                                for (t0, m) in seq_tiles(n):
                                    S.op('act', lambda e, m=m, col=col, t0=t0: e.activation(
                                        out=HBd[:, t0:t0 + m], in_=AB[:, t0:t0 + m], func=AF.Exp, scale=LC[:, 1, col:col + 1]),
                                        reads=[('AB', t0), 'LC'], writes=[kHB])
                                    S.op('act', lambda e, m=m, col=col, t0=t0: e.activation(
                                        out=AB[:, t0:t0 + m], in_=AB[:, t0:t0 + m], func=AF.Exp, scale=LC[:, 0, col:col + 1]),
                                        reads=[('AB', t0), 'LC'], writes=[('AB', t0)])
                                for (t0, m) in seq_tiles(n):
                                    S.op('act', lambda e, m=m, t0=t0: e.activation(
                                        out=HBd[:, t0:t0 + m], in_=HBd[:, t0:t0 + m], func=AF.Sqrt, bias=ONEB[:, 0:1], scale=-1.0),
                                        reads=[kHB, 'ONEB'], writes=[kHB])
                                for (t0, m) in seq_tiles(n):
                                    S.op('pool', lambda e, m=m, t0=t0: e.tensor_tensor(
                                        out=UB[:, t0:t0 + m], in0=UB[:, t0:t0 + m], in1=XC[:, oc, t0:t0 + m], op=ALU.mult),
                                        reads=[('UB', t0), ('XC', oc)], writes=[('UB', t0)])
                                    S.op('dve', lambda e, m=m, t0=t0: e.tensor_tensor(
                                        out=UB[:, t0:t0 + m], in0=UB[:, t0:t0 + m], in1=HBd[:, t0:t0 + m], op=ALU.mult),
                                        reads=[('UB', t0), kHB], writes=[('UB', t0)])
                                rk = [('AB', t0) for (t0, m) in seq_tiles(n)] + [('UB', t0) for (t0, m) in seq_tiles(n)]
                                init = 0.0 if s_ == 1 else HST[:, col:col + 1]
                                if d == 0:
                                    S.op('dve', lambda e, init=init: e.tensor_tensor_scan(
                                        out=HB[0][:, :n], data0=AB[:, :n], data1=UB[:, :n], initial=init, op0=ALU.mult,
                                        op1=ALU.add), reads=rk + ['HST'], writes=['HB0'])
                                    if s_ == 1:
                                        S.op('dve', lambda e, col=col: e.tensor_copy(out=HST[:, col:col + 1], in_=HB[0][:, n - 1:n]),
                                             reads=['HB0'], writes=['HST'])
                                else:
                                    S.op('dve', lambda e, init=init: e.tensor_tensor_scan(
                                        out=HB[1][:, n - 1::-1] if False else HB[1][:, :n][:, ::-1], data0=AB[:, :n][:, ::-1],
                                        data1=UB[:, :n][:, ::-1], initial=init, op0=ALU.mult, op1=ALU.add),
                                        reads=rk + ['HST'], writes=['HB1'])
                                    if s_ == 1:
                                        S.op('dve', lambda e, col=col: e.tensor_copy(out=HST[:, col:col + 1], in_=HB[1][:, 0:1]),
                                             reads=['HB1'], writes=['HST'])
                            if s_ == 0:
                                S.op('pool', lambda e: e.tensor_tensor(out=HB[0][:, :n], in0=HB[0][:, :n], in1=HB[1][:, :n],
                                                                       op=ALU.add), reads=['HB0', 'HB1'], writes=['HB0'])
                                S.op('dve', lambda e: e.tensor_tensor(out=YB[:, :n], in0=HB[0][:, :n], in1=GL[:, :n], op=ALU.mult),
                                     reads=['HB0', 'GL'], writes=['YBl'])
                                S.dma('pool', YLv[:, cc, :], YB[:, :n], reads=['YBl'], writes=[('YL', cc)])
                if debug:
                    S.dma('sp', HSTD[:, :], HST[:], reads=['HST'], writes=['HSTD'])
                S.barrier()

        if phases is None or "L" in phases:
            phase_lru()
        if phases is not None and "R" not in phases:
            return nc
        DS = float(np.exp(-0.5))
        LN_X_EPS = 64e-5
        NCH = TT // 64

        def _rwkv_body(ph):
            if True:
                WUP = sb("WUP", [128, D], BF16, st=ph)
                AUP = sb("AUP", [128, D], BF16, st=ph)
                GUP = sb("GUP", [128, D], BF16, st=ph)
                IDB = sb("IDB", [128, 128], BF16, st=ph)
                BONES = sb("BONES", [128, 128], st=ph)
                MK = [sb("MK%d" % d, [128, 2, 256], st=ph) for d in range(2)]
                AMK = [sb("AMK%d" % d, [128, 2, 64], st=ph) for d in range(2)]
                BMK = [sb("BMK%d" % d, [128, 4, 64], st=ph) for d in range(2)]
                RMF = sb("RMF", [128, 512], st=ph)
                RMB = sb("RMB", [128, 512], st=ph)
                TW = sb("TW", [128, TT], BF16, st=ph)
                ZA = sb("ZA", [128, TT], BF16, st=ph)
                PL = sb("PL", [128, TT], BF16, st=ph)
                SH = sb("SH", [128, TT], st=ph)
                ZR = sb("ZR", [128, TT], BF16, st=ph)
                ZK = sb("ZK", [128, TT], BF16, st=ph)
                ZV = sb("ZV", [128, TT], BF16, st=ph)
                KK = sb("KK", [128, TT], BF16, st=ph)
                ART = [[sb("ART%d%d" % (d, i), [128, 8, 2, 64], BF16, st=ph) for i in range(2)] for d in range(2)]
                KBT = [[sb("KBT%d%d" % (d, i), [128, 8, 2, 64], BF16, st=ph) for i in range(2)] for d in range(2)]
                WC = [sb("WC%d" % d, [128, NCH], st=ph) for d in range(2)]
                TRP = [[sb("TRP%d%d" % (d, i), [128, 512], F32 if i < 5 else BF16, st=ph) for i in range(7)] for d in range(2)]
                YB = sb("YBr", [128, 64, 64], st=ph)
                TR = [sb("TR%d" % i, [128, 512], st=ph) for i in range(3)]
                TRS = sb("TRS", [128, 256], st=ph)
                MT = [[sb("MT%d%d" % (d, i), [128, 8, 256], BF16, st=ph) for i in range(2)] for d in range(2)]
                ABt = [[sb("ABt%d%d" % (d, i), [128, 4, 2, 128], BF16, st=ph) for i in range(2)] for d in range(2)]
                ACC = [[sb("ACC%d%d" % (d, i), [128, 4, 128], BF16, st=ph) for i in range(2)] for d in range(2)]
                TTt = [[sb("TTt%d%d" % (d, i), [128, 8, 128], BF16, st=ph) for i in range(2)] for d in range(2)]
                PT = [[sb("PT%d%d" % (d, i), [128, 3, 64], BF16, st=ph) for i in range(2)] for d in range(2)]
                XB = [[sb("XB%d%d" % (d, i), [128, 64], BF16, st=ph) for i in range(2)] for d in range(2)]
                UB_ = [[sb("UBr%d%d" % (d, i), [128, 64], BF16, st=ph) for i in range(2)] for d in range(2)]
                SS = [sb("SS%d" % d, [128, 64], st=ph) for d in range(2)]
                SSb = [sb("SSb%d" % d, [128, 64], BF16, st=ph) for d in range(2)]
                MUS = sb("MUS", [128, 3, 27], st=ph)
                KA1 = sb("KA1", [128, 8], st=ph)
                GO = [sb("GO%d" % i, [128, 512], BF16, st=ph) for i in range(2)]
                ST8 = sb("ST8", [128, 2, 8, 64], st=ph) if debug else None

                for i_, (dst, src) in enumerate(((WUP, rw_w_up_d), (AUP, rw_a_up_d), (GUP, rw_g_up_d))):
                    S.dma('sp', SH[:, 0:D], src[:, :], writes=['SH'])
                    S.op('dve', lambda e, dst=dst: e.tensor_copy(out=dst[:], in_=SH[:, 0:D]), reads=['SH'], writes=['LW_' + str(i_)])
                S.op('dve', lambda e: e.tensor_copy(out=IDB[:], in_=ident[:]), reads=['ident'], writes=['IDB'])
                S.op('pool', lambda e: e.memset(BONES[:], 0.0), writes=['BONES'])
                S.op('pool', lambda e: e.memset(BONES[0:64, 0:64], 1.0), reads=['BONES'], writes=['BONES'])
                S.op('pool', lambda e: e.memset(BONES[64:128, 64:128], 1.0), reads=['BONES'], writes=['BONES'])
                S.op('pool', lambda e: e.memset(RMF[:], 1.0), writes=['RMF'])
                S.op('pool', lambda e: e.memset(RMF[:, 0:512:64], 0.0), reads=['RMF'], writes=['RMF'])
                S.op('pool', lambda e: e.memset(RMB[:], 1.0), writes=['RMB'])
                S.op('pool', lambda e: e.memset(RMB[:, 63:512:64], 0.0), reads=['RMB'], writes=['RMB'])

                def tri(dst_view_fn, kind):
                    pat, cm = ([[1, 64]], -1) if kind[0] == 'L' else ([[-1, 64]], 1)
                    cmp_ = ALU.is_gt if kind[2] == 's' else ALU.is_ge
                    for hp in (0, 64):
                        v = dst_view_fn(hp)
                        S.op('pool', lambda e, v=v: e.memset(v, 1.0), reads=['MASKS'], writes=['MASKS'])
                        S.op('pool', lambda e, v=v: e.affine_select(out=v, in_=v, pattern=pat, compare_op=cmp_, fill=0.0, base=0,
                                                                   channel_multiplier=cm), reads=['MASKS'], writes=['MASKS'])
                for d in range(2):
                    ks, ki = ('LTs', 'LTi') if d == 0 else ('GTs', 'GTi')
                    kt = 'GTs' if d == 0 else 'LTs'
                    for q in range(2):
                        for blk, kd_ in enumerate((ks, ki, ks, ki)):
                            tri(lambda hp, d=d, q=q, blk=blk: MK[d][hp:hp + 64, q, blk * 64:(blk + 1) * 64], kd_)
                        tri(lambda hp, d=d, q=q: AMK[d][hp:hp + 64, q, :], ks)
                    for q in range(4):
                        tri(lambda hp, d=d, q=q: BMK[d][hp:hp + 64, q, :], kt)
                mo = VEC_OFF["rw_mu"]
                S.op('dve', lambda e: e.tensor_scalar(out=MUS[:, 0, :], in0=PV[:, mo:mo + 27], scalar1=-1.0, scalar2=1.0, op0=ALU.mult,
                                                      op1=ALU.add), reads=['PV'], writes=['MUS'])
                S.op('dve', lambda e: e.tensor_scalar(out=MUS[:, 1, :], in0=PV[:, mo:mo + 27], scalar1=0.25, scalar2=None, op0=ALU.mult),
                     reads=['PV'], writes=['MUS'])
                S.op('dve', lambda e: e.tensor_scalar(out=MUS[:, 2, :], in0=PV[:, mo:mo + 27], scalar1=0.5, scalar2=None, op0=ALU.mult),
                     reads=['PV'], writes=['MUS'])
                kao = VEC_OFF["rw_k_a"]
                S.op('dve', lambda e: e.tensor_scalar(out=KA1[:], in0=PV[:, kao:kao + 8], scalar1=-1.0, scalar2=1.0, op0=ALU.mult,
                                                      op1=ALU.add), reads=['PV'], writes=['KA1'])
                for t_ in ABt[0] + ABt[1] + ACC[0] + ACC[1] + TTt[0] + TTt[1]:
                    S.op('pool', lambda e, t_=t_: e.memset(t_[:], 0.0), writes=['ABZ'])
                S.barrier()

                cut(1)
                PLx = PL[:, TC:TT].rearrange("p (r c) -> p r c", c=64)
                SHx = SH[:, TC:TT].rearrange("p (r c) -> p r c", c=64)

                def zlerp(pc, dst, dkey):
                    S.dma('sp', dst[:, 0:TC], Pv[:, pc, 0:TC], reads=[dkey], writes=[dkey])
                    for t0 in range(0, T, 1024):
                        S.dma('sp' if (t0 // 1024) % 2 == 0 else 'pool', dst[:, TC + t0:TC + t0 + 1024], Pv[:, pc, TC + t0:TC + t0 + 1024],
                              reads=[dkey], writes=[dkey])

                tiles_all = [(0, TC)] + [(TC + t0, m) for (t0, m) in seq_tiles(T)]

                zlerp(24, ZK, 'ZK')
                cut(2)
                S.op('act', lambda e: e.activation(out=TW[:, :], in_=ZK[:, :], func=AF.Tanh), reads=['ZK'], writes=['TW'])
                zlerp(25, ZA, 'ZA')
                zlerp(26, ZK, 'ZK')
                S.op('act', lambda e: e.activation(out=ZR[:, :], in_=ZK[:, :], func=AF.Sigmoid), reads=['ZK'], writes=['ZR'])
                gi_ = 0
                for c in range(8):
                    for (t0, m) in seq_tiles(T):
                        pp = ps[gi_ % 2]
                        go = GO[gi_ % 2]
                        S.mm(pp[:, :m], lhsT=GUP[:, c * 128:(c + 1) * 128],
                                                                              rhs=ZR[:, TC + t0:TC + t0 + m], start=True, stop=True,
                             reads=['ZR', 'LW_2'], writes=['ps%d' % (gi_ % 2)])
                        S.op('act', lambda e, pp=pp, go=go, m=m: e.copy(out=go[:, :m], in_=pp[:, :m]), reads=['ps%d' % (gi_ % 2)],
                             writes=['GO%d' % (gi_ % 2)])
                        S.dma('pool', GSv[:, c, t0:t0 + m], go[:, :m], reads=['GO%d' % (gi_ % 2)], writes=[('GS', c, t0)])
                        gi_ += 1

                cut(3)
                kko, rko = VEC_OFF["rw_k_k"], VEC_OFF["rw_r_k"]
                w0o, a0o = VEC_OFF["rw_w0"], VEC_OFF["rw_a0"]
                lwo, lbo = VEC_OFF["rw_ln_w"], VEC_OFF["rw_ln_b"]
                v3 = lambda ap, m: ap.rearrange("p (j s) -> p j s", s=64)

                for c in range(8):
                    zlerp(c, ZR, 'ZR')
                    zlerp(8 + c, ZK, 'ZK')
                    zlerp(16 + c, ZV, 'ZV')
                    for (g0, m) in tiles_all:
                        S.op('act', lambda e, g0=g0, m=m: e.activation(out=TR[0][:, :m], in_=ZK[:, g0:g0 + m], func=AF.Square,
                                                                       scale=PV[:, kko + c:kko + c + 1]), reads=['ZK'], writes=['TR0'])
                        S.mm(ps[0][:, :m], lhsT=BONES[:], rhs=TR[0][:, :m], start=True, stop=True,
                             reads=['TR0', 'BONES'], writes=['ps0'])
                        S.op('act', lambda e, m=m: e.activation(out=TR[1][:, :m], in_=ps[0][:, :m], func=AF.Sqrt), reads=['ps0'],
                             writes=['TR1'])
                        S.op('dve', lambda e, m=m: e.tensor_scalar(out=TR[1][:, :m], in0=TR[1][:, :m], scalar1=1e-12, scalar2=None,
                                                                   op0=ALU.max), reads=['TR1'], writes=['TR1'])
                        S.op('dve', lambda e, m=m: e.reciprocal(out=TR[1][:, :m], in_=TR[1][:, :m]), reads=['TR1'], writes=['TR1'])
                        S.op('dve', lambda e, g0=g0, m=m: e.scalar_tensor_tensor(
                            out=KK[:, g0:g0 + m], in0=ZK[:, g0:g0 + m], scalar=PV[:, kko + c:kko + c + 1], in1=TR[1][:, :m],
                            op0=ALU.mult, op1=ALU.mult), reads=['ZK', 'TR1'], writes=['KK'])

                    cut(4)
                    batches = [
                        [[0, 1, 2, 3]] + [list(range(4 + 8 * i, 12 + 8 * i)) for i in range(8)],
                        [[3, 2, 1, 0]] + [list(range(11 + 8 * i, 3 + 8 * i, -1)) for i in range(7, -1, -1)],
                    ]
                    NB = 9
                    for d in range(2):
                        S.op('pool', lambda e, d=d: e.memset(SS[d][:], 0.0), reads=[('SS', d, 0), ('SS', d, 1)],
                             writes=[('SS', d, 0), ('SS', d, 1)])
                        S.op('pool', lambda e, d=d: e.memset(SSb[d][:], 0.0), reads=[('SSb', d, 0), ('SSb', d, 1)],
                             writes=[('SSb', d, 0), ('SSb', d, 1)])
                    yb_written = set()

                    def prep_gen(d, n):
                        js = batches[d][n]
                        par = n % 2
                        jmin = min(js)
                        g0, m = jmin * 64, 64 * len(js)
                        nch = len(js)
                        dh = d * 64
                        col = d * 8 + c
                        LW, AS, CL, E1, E2, TB, TK = TRP[d]
                        kT = lambda i: ('TRP', d, i)
                        A_, K_ = ART[d][par], KBT[d][par]
                        kA, kK = ('ART', d, par), ('KBT', d, par)
                        pd_, kpd = ps[d], 'ps%d' % d
                        S.mm(pd_[:, :m], lhsT=WUP[dh:dh + 64, c * 128:(c + 1) * 128], rhs=TW[dh:dh + 64, g0:g0 + m], start=True, stop=True,
                             reads=['TW', 'LW_0'], writes=[kpd])
                        S.op('act', lambda e: e.activation(out=LW[:, :m], in_=pd_[:, :m], func=AF.Sigmoid,
                                                           bias=PV[:, w0o + col:w0o + col + 1], scale=1.0), reads=[kpd], writes=[kT(0)])
                        S.mm(pd_[:, :m], lhsT=AUP[dh:dh + 64, c * 128:(c + 1) * 128], rhs=ZA[dh:dh + 64, g0:g0 + m], start=True, stop=True,
                             reads=['ZA', 'LW_1'], writes=[kpd])
                        S.op('act', lambda e: e.activation(out=AS[:, :m], in_=pd_[:, :m], func=AF.Sigmoid,
                                                           bias=PV[:, a0o + col:a0o + col + 1], scale=1.0), reads=[kpd], writes=[kT(1)])
                        yield
                        if d == 0:
                            S.op('dve', lambda e: e.tensor_tensor_scan(out=CL[:, :m], data0=RMF[:, :m], data1=LW[:, :m], initial=0.0,
                                                                       op0=ALU.mult, op1=ALU.add), reads=[kT(0), 'RMF'], writes=[kT(2)])
                        else:
                            S.op('dve', lambda e: e.tensor_tensor_scan(out=CL[:, :m][:, ::-1], data0=RMB[:, :m][:, ::-1],
                                                                       data1=LW[:, :m][:, ::-1], initial=0.0, op0=ALU.mult, op1=ALU.add),
                                 reads=[kT(0), 'RMB'], writes=[kT(2)])
                        S.op('act', lambda e: e.activation(out=E1[:, :m], in_=CL[:, :m], func=AF.Exp, scale=-DS), reads=[kT(2)], writes=[kT(3)])
                        S.op('act', lambda e: e.activation(out=E2[:, :m], in_=CL[:, :m], func=AF.Exp, scale=DS), reads=[kT(2)], writes=[kT(4)])
                        S.op('pool', lambda e: e.tensor_tensor(out=LW[:, :m], in0=CL[:, :m], in1=LW[:, :m], op=ALU.subtract),
                             reads=[kT(2), kT(0)], writes=[kT(0)])
                        yield
                        S.op('act', lambda e: e.activation(out=CL[:, :m], in_=LW[:, :m], func=AF.Exp, scale=-DS), reads=[kT(0), kT(2)],
                             writes=[kT(2)])
                        S.op('dve', lambda e: e.tensor_tensor(out=A_[:, 0:nch, 1, :], in0=v3(ZR[:, g0:g0 + m], m), in1=v3(E1[:, :m], m),
                                                              op=ALU.mult), reads=['ZR', kT(3)], writes=[kA])
                        S.op('pool', lambda e: e.tensor_tensor(out=TB[:, :m], in0=KK[:, g0:g0 + m], in1=AS[:, :m], op=ALU.mult),
                             reads=['KK', kT(1)], writes=[kT(5)])
                        S.op('pool', lambda e: e.tensor_scalar(out=TK[:, :m], in0=AS[:, :m], scalar1=PV[:, kao + c:kao + c + 1],
                                                               scalar2=KA1[:, c:c + 1], op0=ALU.mult, op1=ALU.add),
                             reads=[kT(1), 'KA1'], writes=[kT(6)])
                        yield
                        S.op('dve', lambda e: e.scalar_tensor_tensor(out=A_[:, 0:nch, 0, :], in0=v3(KK[:, g0:g0 + m], m), scalar=-1.0,
                                                                     in1=v3(CL[:, :m], m), op0=ALU.mult, op1=ALU.mult),
                             reads=['KK', kT(2)], writes=[kA])
                        S.op('dve', lambda e: e.tensor_tensor(out=K_[:, 0:nch, 1, :], in0=v3(TB[:, :m], m), in1=v3(E2[:, :m], m), op=ALU.mult),
                             reads=[kT(5), kT(4)], writes=[kK])
                        S.op('pool', lambda e: e.tensor_tensor(out=TK[:, :m], in0=TK[:, :m], in1=ZK[:, g0:g0 + m], op=ALU.mult),
                             reads=[kT(6), 'ZK'], writes=[kT(6)])
                        yield
                        S.op('dve', lambda e: e.tensor_tensor(out=K_[:, 0:nch, 0, :], in0=v3(TK[:, :m], m), in1=v3(E2[:, :m], m), op=ALU.mult),
                             reads=[kT(6), kT(4)], writes=[kK])
                        ecol = 63 if d == 0 else 0
                        S.op('dve', lambda e: e.tensor_copy(out=WC[d][:, jmin:jmin + nch], in_=E1[:, ecol:m:64]), reads=[kT(3)],
                             writes=[('WC', d)])
                        yield

                    def stage1_gen(d, n):
                        js = batches[d][n]
                        par = n % 2
                        jmin = min(js)
                        A_, K_ = ART[d][par], KBT[d][par]
                        kA, kK = ('ART', d, par), ('KBT', d, par)
                        MT_, TT_ = MT[d][par], TTt[d][par]
                        AB_, AC_ = ABt[d], ACC[d]
                        for h0 in range(0, len(js), 4):
                            for q0 in range(0, 4, 2):
                                for hp in (0, 64):
                                    pm = ps[2 + hp // 64]
                                    for q in range(2):
                                        jl = js[h0 + q0 + q] - jmin
                                        for w_ in range(2):
                                            S.mm(pm[hp:hp + 64, q * 256 + w_ * 128:q * 256 + (w_ + 1) * 128],
                                                 lhsT=K_[hp:hp + 64, jl, w_, :], rhs=A_[hp:hp + 64, jl, :, :], start=True, stop=True,
                                                 reads=[kK, kA], writes=['ps%d' % (2 + hp // 64)])
                                lq = h0 + q0
                                for hp in (0, 64):
                                    pm, kpm = ps[2 + hp // 64], 'ps%d' % (2 + hp // 64)
                                    S.op('dve', lambda e, lq=lq, hp=hp, pm=pm: e.tensor_tensor(
                                        out=MT_[hp:hp + 64, lq:lq + 2, :], in0=pm[hp:hp + 64, 0:512].rearrange("p (q w) -> p q w", w=256),
                                        in1=MK[d][hp:hp + 64, :, :], op=ALU.mult), reads=[kpm], writes=[('MT', d, par, lq, hp)])
                                    S.op('dve', lambda e, q0=q0, hp=hp, pm=pm: e.tensor_tensor(
                                        out=AB_[0][hp:hp + 64, q0:q0 + 2, 0, hp:hp + 64],
                                        in0=pm[hp:hp + 64, 0:512].rearrange("p (q w) -> p q w", w=256)[:, :, 128:192],
                                        in1=AMK[d][hp:hp + 64, :, :], op=ALU.mult), reads=[kpm], writes=[('AB0', d, q0)])
                                yield
                            for hp in (0, 64):
                                pm = ps[2 + hp // 64]
                                for q in range(4):
                                    jl = js[h0 + q] - jmin
                                    S.mm(pm[hp:hp + 64, q * 128 + hp:q * 128 + hp + 64], lhsT=A_[hp:hp + 64, jl, 0, :],
                                         rhs=K_[hp:hp + 64, jl, 1, :], start=True, stop=True, reads=[kK, kA], writes=['ps%d' % (2 + hp // 64)])
                            for hp in (0, 64):
                                pm, kpm = ps[2 + hp // 64], 'ps%d' % (2 + hp // 64)
                                S.op('dve', lambda e, hp=hp, pm=pm: e.tensor_tensor(
                                    out=AB_[0][hp:hp + 64, 0:4, 1, hp:hp + 64],
                                    in0=pm[hp:hp + 64, 0:512].rearrange("p (q w) -> p q w", w=128)[:, :, hp:hp + 64],
                                    in1=BMK[d][hp:hp + 64, :, :], op=ALU.mult), reads=[kpm], writes=[('AB0', d, 0), ('AB0', d, 2)])
                            S.op('pool', lambda e: e.tensor_tensor(
                                out=AC_[0][:, 0:4, :], in0=AB_[0][:, 0:4, 0, :], in1=IDB[:].unsqueeze(1).to_broadcast([128, 4, 128]),
                                op=ALU.add), reads=[('AB0', d, 0), ('AB0', d, 2), 'IDB'], writes=[('ACC0', d)])
                            yield
                            for l in range(5):
                                cur, nxt = l % 2, 1 - (l % 2)
                                kc, kn = 'AB%d' % cur, 'AB%d' % nxt
                                for q0 in range(0, 4, 2):
                                    for q in range(2):
                                        jq = q0 + q
                                        if l < 4:
                                            S.mm(ps[2][:, q * 256:q * 256 + 128], lhsT=AB_[cur][:, jq, 1, :], rhs=AB_[cur][:, jq, 0, :],
                                                 start=True, stop=True, reads=[(kc, d, q0)], writes=['ps2'], inc=False)
                                        S.mm(ps[2][:, q * 256 + 128:q * 256 + 256], lhsT=AB_[cur][:, jq, 0, :], rhs=AB_[cur][:, jq, 1, :],
                                             start=True, stop=True, reads=[(kc, d, q0)], writes=['ps2'], inc=(q == 1))
                                    if l < 4:
                                        S.op('act', lambda e, q0=q0, nxt=nxt: e.copy(
                                            out=AB_[nxt][:, q0:q0 + 2, :, :].rearrange("p q a b -> p (q a b)"), in_=ps[2][:, 0:512]),
                                            reads=['ps2'], writes=[(kn, d, q0)])
                                    else:
                                        S.op('act', lambda e, q0=q0, nxt=nxt: e.copy(
                                            out=AB_[nxt][:, q0:q0 + 2, 1, :],
                                            in_=ps[2][:, 0:512].rearrange("p (q w) -> p q w", w=256)[:, :, 128:256]),
                                            reads=['ps2'], writes=[(kn, d, q0)])
                                    yield
                                for q in range(4):
                                    S.mm(ps[3][:, q * 128:(q + 1) * 128], lhsT=AB_[nxt][:, q, 1, :], rhs=AC_[cur][:, q, :], start=True, stop=True,
                                         reads=[(kn, d, (q // 2) * 2), ('ACC%d' % cur, d)], writes=['ps3'], inc=(q == 3))
                                if l == 4:
                                    S.op('dve', lambda e, cur=cur: e.tensor_tensor(
                                        out=TT_[:, h0:h0 + 4, :], in0=ps[3][:, 0:512].rearrange("p (q w) -> p q w", w=128),
                                        in1=AC_[cur][:, 0:4, :], op=ALU.add), reads=['ps3', ('ACC%d' % cur, d)], writes=[('TTt', d, par, h0)])
                                else:
                                    S.op('dve', lambda e, cur=cur, nxt=nxt: e.tensor_tensor(
                                        out=AC_[nxt][:, 0:4, :], in0=ps[3][:, 0:512].rearrange("p (q w) -> p q w", w=128),
                                        in1=AC_[cur][:, 0:4, :], op=ALU.add), reads=['ps3', ('ACC%d' % cur, d)], writes=[('ACC%d' % nxt, d)])
                                yield

                    def chain_gen(d, n):
                        js = batches[d][n]
                        par = n % 2
                        jmin = min(js)
                        A_, K_ = ART[d][par], KBT[d][par]
                        kA, kK = ('ART', d, par), ('KBT', d, par)
                        MT_, TT_ = MT[d][par], TTt[d][par]
                        SS_, SSb_ = SS[d], SSb[d]
                        for step, j in enumerate(js):
                            jl = j - jmin
                            tb = step % 2
                            isx = j >= 4
                            jx = j - 4
                            pt, xb, ub = PT[d][tb], XB[d][tb], UB_[d][tb]
                            ktt = ('TTt', d, par, (step // 4) * 4)
                            first_y = isx and (jx not in yb_written)
                            if isx:
                                yb_written.add(jx)
                            H = []
                            for h in range(2):
                                hp = 64 * h
                                H.append(dict(hp=hp, pb=ps[4 + 2 * d + h], kpb='ps%d' % (4 + 2 * d + h), kpt=('PT', d, tb, h), kxb=('XB', d, tb, h),
                                              kub=('UB', d, tb, h), kS=('SS', d, h), kSb=('SSb', d, h),
                                              kmt=('MT', d, par, (step // 2) * 2, hp), e0=('act', 'dve')[h], e1=('dve', 'act')[h]))

                            def cp(eng, out, in_, reads, writes):
                                if eng == 'act':
                                    S.op('act', lambda e: e.copy(out=out, in_=in_), reads=reads, writes=writes)
                                else:
                                    S.op('dve', lambda e: e.tensor_copy(out=out, in_=in_), reads=reads, writes=writes)
                            for x in H:
                                hp, pb = x['hp'], x['pb']
                                S.mm(pb[hp:hp + 64, 0:64], lhsT=ZV[hp:hp + 64, j * 64:(j + 1) * 64], rhs=IDB[hp:hp + 64, hp:hp + 64],
                                     start=True, stop=True, reads=['ZV', 'IDB'], writes=[x['kpb']])
                                S.mm(pb[hp:hp + 64, 64:128], lhsT=K_[hp:hp + 64, jl, 1, :], rhs=IDB[hp:hp + 64, hp:hp + 64],
                                     start=True, stop=True, reads=[kK, 'IDB'], writes=[x['kpb']])
                                S.mm(pb[hp:hp + 64, 128:192], lhsT=K_[hp:hp + 64, jl, 0, :], rhs=IDB[hp:hp + 64, hp:hp + 64],
                                     start=True, stop=True, reads=[kK, 'IDB'], writes=[x['kpb']])
                            for x in H:
                                hp, pb = x['hp'], x['pb']
                                cp(x['e0'], pt[hp:hp + 64, :, :].rearrange("p a b -> p (a b)"), pb[hp:hp + 64, 0:192], [x['kpb']], [x['kpt']])
                            yield
                            for x in H:
                                hp, pb = x['hp'], x['pb']
                                S.mm(pb[hp:hp + 64, 192:256], lhsT=A_[hp:hp + 64, jl, 0, :], rhs=SSb_[hp:hp + 64, :], start=True, stop=False,
                                     reads=[kA, x['kSb']], writes=[x['kpb']])
                                S.mm(pb[hp:hp + 64, 192:256], lhsT=MT_[hp:hp + 64, step, 0:64], rhs=pt[hp:hp + 64, 0, :], start=False, stop=True,
                                     reads=[x['kmt'], x['kpt']], writes=[x['kpb']])
                            for x in H:
                                hp, pb = x['hp'], x['pb']
                                cp(x['e0'], xb[hp:hp + 64, :], pb[hp:hp + 64, 192:256], [x['kpb']], [x['kxb']])
                            yield
                            for x in H:
                                hp, pb = x['hp'], x['pb']
                                S.mm(pb[hp:hp + 64, 256:320], lhsT=TT_[hp:hp + 64, step, hp:hp + 64], rhs=xb[hp:hp + 64, :], start=True, stop=True,
                                     reads=[ktt, x['kxb']], writes=[x['kpb']])
                            for x in H:
                                hp, pb = x['hp'], x['pb']
                                cp(x['e1'], ub[hp:hp + 64, :], pb[hp:hp + 64, 256:320], [x['kpb']], [x['kub']])
                            yield
                            if isx:
                                for x in H:
                                    hp, pb = x['hp'], x['pb']
                                    S.mm(pb[hp:hp + 64, 320:384], lhsT=A_[hp:hp + 64, jl, 1, :], rhs=SSb_[hp:hp + 64, :], start=True, stop=False,
                                         reads=[kA, x['kSb']], writes=[x['kpb']])
                                    S.mm(pb[hp:hp + 64, 320:384], lhsT=MT_[hp:hp + 64, step, 192:256], rhs=ub[hp:hp + 64, :], start=False, stop=False,
                                         reads=[x['kmt'], x['kub']], writes=[x['kpb']])
                                    S.mm(pb[hp:hp + 64, 320:384], lhsT=MT_[hp:hp + 64, step, 64:128], rhs=pt[hp:hp + 64, 0, :], start=False, stop=True,
                                         reads=[x['kmt'], x['kpt']], writes=[x['kpb']])
                                for x in H:
                                    hp, pb = x['hp'], x['pb']
                                    if first_y:
                                        cp(x['e1'], YB[hp:hp + 64, jx, :], pb[hp:hp + 64, 320:384], [x['kpb']], [('YB', jx, hp)])
                                    else:
                                        S.op('dve', lambda e, hp=hp, pb=pb: e.tensor_tensor(out=YB[hp:hp + 64, jx, :], in0=pb[hp:hp + 64, 320:384],
                                                                                          in1=YB[hp:hp + 64, jx, :], op=ALU.add),
                                             reads=[x['kpb'], ('YB', jx, hp)], writes=[('YB', jx, hp)])
                            for x in H:
                                hp, pb = x['hp'], x['pb']
                                S.mm(pb[hp:hp + 64, 384:448], lhsT=pt[hp:hp + 64, 1, :], rhs=ub[hp:hp + 64, :], start=True, stop=False,
                                     reads=[x['kpt'], x['kub']], writes=[x['kpb']])
                                S.mm(pb[hp:hp + 64, 384:448], lhsT=pt[hp:hp + 64, 2, :], rhs=pt[hp:hp + 64, 0, :], start=False, stop=True,
                                     reads=[x['kpt']], writes=[x['kpb']])
                            for x in H:
                                hp, pb = x['hp'], x['pb']
                                S.op('dve', lambda e, hp=hp: e.tensor_scalar(out=SS_[hp:hp + 64, :], in0=SS_[hp:hp + 64, :],
                                                                             scalar1=WC[d][hp:hp + 64, j:j + 1], scalar2=None, op0=ALU.mult),
                                     reads=[x['kS'], ('WC', d)], writes=[x['kS']])
                                S.op('dve', lambda e, hp=hp, pb=pb: e.scalar_tensor_tensor(
                                    out=SS_[hp:hp + 64, :], in0=pb[hp:hp + 64, 384:448], scalar=WC[d][hp:hp + 64, j:j + 1],
                                    in1=SS_[hp:hp + 64, :], op0=ALU.mult, op1=ALU.add), reads=[x['kpb'], x['kS'], ('WC', d)], writes=[x['kS']])
                                S.op('act', lambda e, hp=hp: e.copy(out=SSb_[hp:hp + 64, :], in_=SS_[hp:hp + 64, :]), reads=[x['kS']],
                                     writes=[x['kSb']])
                            if debug and j == (3 if d == 0 else 0):
                                S.op('dve', lambda e: e.tensor_copy(out=ST8[:, d, c, :], in_=SS_[:]), reads=[('SS', d, 0), ('SS', d, 1)],
                                     writes=['ST8'])
                            yield

                    def run_threads(ths):
                        ths = list(ths)
                        while ths:
                            for g in list(ths):
                                try:
                                    next(g)
                                except StopIteration:
                                    ths.remove(g)

                    import itertools as _it
                    run_threads([_it.chain(prep_gen(d, 0), stage1_gen(d, 0)) for d in range(2)])
                    for n in range(NB):
                        ths = []
                        for d in range(2):
                            ths.append(chain_gen(d, n))
                            if n + 1 < NB:
                                ths.append(_it.chain(prep_gen(d, n + 1), stage1_gen(d, n + 1)))
                        run_threads(ths)
                    cut(8)
                    YSQ = SH[:, 0:4096].rearrange("p (j v) -> p j v", v=64)
                    YNb = PL[:, 0:4096].rearrange("p (j v) -> p j v", v=64)
                    SUM, SSQ, MEAN, RSTD = TRS[:, 0:64], TRS[:, 64:128], TRS[:, 128:192], TRS[:, 192:256]
                    S.op('dve', lambda e: e.tensor_reduce(out=SUM, in_=YB[:], axis=AX.X, op=ALU.add),
                         reads=[('YB', jx, hp_) for jx in range(64) for hp_ in (0, 64)], writes=['TR8'])
                    S.op('act', lambda e: e.activation(out=YSQ, in_=YB[:], func=AF.Square), reads=[('YB', jx, hp_) for jx in range(64) for hp_ in (0, 64)],
                         writes=['SH'])
                    S.op('dve', lambda e: e.tensor_reduce(out=SSQ, in_=YSQ, axis=AX.X, op=ALU.add), reads=['SH'], writes=['TR8'])
                    S.op('dve', lambda e: e.tensor_scalar(out=MEAN, in0=SUM, scalar1=1.0 / 64, scalar2=None, op0=ALU.mult),
                         reads=['TR8'], writes=['TR8'])
                    S.op('dve', lambda e: e.tensor_tensor(out=SUM, in0=MEAN, in1=MEAN, op=ALU.mult), reads=['TR8'], writes=['TR8'])
                    S.op('dve', lambda e: e.scalar_tensor_tensor(out=SSQ, in0=SSQ, scalar=1.0 / 64, in1=SUM, op0=ALU.mult,
                                                                 op1=ALU.subtract), reads=['TR8'], writes=['TR8'])
                    S.op('dve', lambda e: e.tensor_scalar(out=SSQ, in0=SSQ, scalar1=LN_X_EPS, scalar2=None, op0=ALU.add),
                         reads=['TR8'], writes=['TR8'])
                    S.op('act', lambda e: e.activation(out=RSTD, in_=SSQ, func=AF.Sqrt), reads=['TR8'], writes=['TR8'])
                    S.op('dve', lambda e: e.reciprocal(out=RSTD, in_=RSTD), reads=['TR8'], writes=['TR8'])
                    S.op('dve', lambda e: e.tensor_tensor(out=YB[:], in0=YB[:], in1=MEAN.unsqueeze(2).to_broadcast([128, 64, 64]),
                                                          op=ALU.subtract), reads=['TR8'] + [('YB', jx, hp_) for jx in range(64) for hp_ in (0, 64)],
                         writes=[('YB', jx, hp_) for jx in range(64) for hp_ in (0, 64)])
                    S.op('dve', lambda e: e.tensor_tensor(out=YNb, in0=YB[:], in1=RSTD.unsqueeze(2).to_broadcast([128, 64, 64]),
                                                          op=ALU.mult), reads=['TR8'] + [('YB', jx, hp_) for jx in range(64) for hp_ in (0, 64)],
                         writes=['PL'])
                    for ti, (t0, m) in enumerate(seq_tiles(T)):
                        g0 = TC + t0
                        for hp in (0, 64):
                            for q in range(8):
                                jx = ti * 8 + q
                                S.mm(
                                    ps[0][hp:hp + 64, q * 64:(q + 1) * 64], lhsT=YNb[hp:hp + 64, jx, :], rhs=IDB[hp:hp + 64, hp:hp + 64],
                                    start=True, stop=True, reads=['PL', 'IDB'], writes=['ps0'], inc=(q == 7 and hp == 64))
                        S.op('dve', lambda e, g0=g0, m=m: e.scalar_tensor_tensor(
                            out=TR[0][:, :m], in0=ZR[:, g0:g0 + m], scalar=PV[:, rko + c:rko + c + 1], in1=ZK[:, g0:g0 + m],
                            op0=ALU.mult, op1=ALU.mult), reads=['ZR', 'ZK'], writes=['TR0'])
                        S.mm(ps[1][:, :m], lhsT=BONES[:], rhs=TR[0][:, :m], start=True, stop=True,
                             reads=['TR0', 'BONES'], writes=['ps1'])
                        S.op('dve', lambda e, g0=g0, m=m: e.tensor_tensor(out=TR[1][:, :m], in0=ps[1][:, :m], in1=ZV[:, g0:g0 + m],
                                                                          op=ALU.mult), reads=['ps1', 'ZV'], writes=['TR1'])
                        S.dma('sp', GO[0][:, :m], GSv[:, c, t0:t0 + m], reads=[('GS', c, t0)], writes=['GO0'])
                        S.op('dve', lambda e, m=m: e.tensor_scalar(out=TR[2][:, :m], in0=ps[0][:, :m], scalar1=PV[:, lwo + c:lwo + c + 1],
                                                                   scalar2=PV[:, lbo + c:lbo + c + 1], op0=ALU.mult, op1=ALU.add),
                             reads=['ps0'], writes=['TR2'])
                        S.op('pool', lambda e, m=m: e.tensor_tensor(out=TR[2][:, :m], in0=TR[2][:, :m], in1=TR[1][:, :m], op=ALU.add),
                             reads=['TR2', 'TR1'], writes=['TR2'])
                        S.op('pool', lambda e, m=m: e.tensor_tensor(out=GO[1][:, :m], in0=TR[2][:, :m], in1=GO[0][:, :m], op=ALU.mult),
                             reads=['TR2', 'GO0'], writes=['GO1'])
                        S.dma('pool', YRWv[:, c, t0:t0 + m], GO[1][:, :m], reads=['GO1'], writes=[('YRW', c, t0)])
                if debug:
                    S.dma('sp', SFD[:, :, :, :], ST8[:], reads=['ST8'], writes=['SFD'])
                S.barrier()

        with ExitStack() as ph_r:
            try:
                _rwkv_body(ph_r)
            except _Cut:
                print("CUT at", RCUT, "ops", S.nops)
            S.barrier()
        if RCUT:
            return nc

        if phases is not None and "C" not in phases:
            return nc

        def phase_merge():
            with ExitStack() as ph:
                WP = [sb("WP%d" % i, [128, 8, D], BF16, st=ph) for i in range(3)]
                STG = sb("STGC", [128, 2, D], st=ph)
                YR = sb("YRt", [128, 8, 512], BF16, st=ph)
                YLt = sb("YLt", [128, 8, 512], BF16, st=ph)
                SM = sb("SMt", [128, 16, 512], BF16, st=ph)
                X1t = sb("X1t", [128, 8, 512], st=ph)
                MG = sb("MGt", [128, 8, 512], BF16, st=ph)
                TA = [sb("TAc%d" % i, [128, 512], st=ph) for i in range(2)]
                ci = 0
                for wi, wsrc in enumerate((wpr_d, wpl_d, wo_d)):
                    for k in range(8):
                        i = ci % 2
                        ci += 1
                        S.dma('sp' if i == 0 else 'pool', STG[:, i, :], wsrc[k * 128:(k + 1) * 128, :], writes=['stg%d' % i])
                        S.op('dve' if i == 0 else 'act',
                             (lambda e, wi=wi, k=k, i=i: e.tensor_copy(out=WP[wi][:, k, :], in_=STG[:, i, :])) if i == 0 else
                             (lambda e, wi=wi, k=k, i=i: e.copy(out=WP[wi][:, k, :], in_=STG[:, i, :])), reads=['stg%d' % i])
                S.barrier()
                for (t0, n) in seq_tiles(T):
                    g0 = TC + t0
                    for c in range(8):
                        S.dma('sp', YR[:, c, :n], YRWv[:, c, t0:t0 + n], writes=[('YR', c)])
                        S.dma('pool', YLt[:, c, :n], YLv[:, c, t0:t0 + n], writes=[('YLt', c)])
                        S.dma('sp', X1t[:, c, :n], X1v[:, c, g0:g0 + n], writes=[('X1t', c)])
                    for c in range(16):
                        S.dma('pool' if c % 2 else 'sp', SM[:, c, :n], Pv[:, 43 + c, g0:g0 + n], writes=[('SM', c)])
                    for o in range(8):
                        pa, pb = ps[(o % 2) * 2], ps[(o % 2) * 2 + 1]
                        ka, kb = 'ps%d' % ((o % 2) * 2), 'ps%d' % ((o % 2) * 2 + 1)
                        for k in range(8):
                            S.mm(pa[:, :n], lhsT=WP[0][:, k, o * 128:(o + 1) * 128], rhs=YR[:, k, :n], start=(k == 0), stop=(k == 7),
                                 reads=[('YR', k)], writes=[ka], inc=(k == 7))
                        for k in range(8):
                            S.mm(pb[:, :n], lhsT=WP[1][:, k, o * 128:(o + 1) * 128], rhs=YLt[:, k, :n], start=(k == 0), stop=(k == 7),
                                 reads=[('YLt', k)], writes=[kb], inc=(k == 7))
                        S.op('dve', lambda e, pa=pa, o=o: e.tensor_tensor(out=TA[0][:, :n], in0=pa[:, :n], in1=SM[:, o, :n], op=ALU.mult),
                             reads=[ka, ('SM', o)], writes=['TA0'])
                        S.op('dve', lambda e, pb=pb, o=o: e.tensor_tensor(out=TA[1][:, :n], in0=pb[:, :n], in1=SM[:, 8 + o, :n], op=ALU.mult),
                             reads=[kb, ('SM', 8 + o)], writes=['TA1'])
                        S.op('pool', lambda e, o=o: e.tensor_tensor(out=MG[:, o, :n], in0=TA[0][:, :n], in1=TA[1][:, :n], op=ALU.add),
                             reads=['TA0', 'TA1'], writes=[('MG', o)])
                    for o in range(8):
                        pc_ = ps[4 + o % 2]
                        kc_ = 'ps%d' % (4 + o % 2)
                        for k in range(8):
                            S.mm(pc_[:, :n], lhsT=WP[2][:, k, o * 128:(o + 1) * 128], rhs=MG[:, k, :n], start=(k == 0), stop=(k == 7),
                                 reads=[('MG', k)], writes=[kc_], inc=(k == 7))
                        S.op('dve', lambda e, pc_=pc_, o=o: e.scalar_tensor_tensor(
                            out=X1t[:, o, :n], in0=pc_[:, :n], scalar=SCAL[:, 0, 5, o:o + 1], in1=X1t[:, o, :n], op0=ALU.mult,
                            op1=ALU.add), reads=[kc_, ('X1t', o)], writes=[('X1t', o)])
                        S.dma('pool', X2v[:, o, t0:t0 + n], X1t[:, o, :n], reads=[('X1t', o)], writes=[('X2', o, t0)])
                S.barrier()

        phase_merge()

        def load_feat_major(S, SCR, XT, s_, t0, n):
            for c in range(8):
                S.dma('sp' if c % 2 == 0 else 'pool', XT[:, c, :n], X2v[:, c, t0:t0 + n], writes=[('XT', c)])

        def epi_final(S, XT, HN, TM, RS, rmsnorm_mod, s_, t0, n, SCR=None):
            rmsnorm_mod(n, s_, 0, XT, ia=9, ib=10, dk='XT')
            for b_ in range(n // 128):
                for c in range(8):
                    pp = ps[6 + (c // 4) % 2]
                    S.mm(pp[:, (c % 4) * 128:(c % 4 + 1) * 128], lhsT=XT[:, c, b_ * 128:(b_ + 1) * 128], rhs=ident[:],
                         start=True, stop=True, reads=[('XT', c), 'ident'], writes=['ps%d' % (6 + (c // 4) % 2)], inc=(c % 4 == 3))
                    if c % 4 == 3:
                        h_ = c // 4
                        S.op('act' if h_ == 0 else 'dve',
                             (lambda e, pp=pp, b_=b_, h_=h_: e.copy(out=SCR[:, b_ * 1024 + h_ * 512:b_ * 1024 + (h_ + 1) * 512], in_=pp[:, 0:512]))
                             if h_ == 0 else
                             (lambda e, pp=pp, b_=b_, h_=h_: e.tensor_copy(out=SCR[:, b_ * 1024 + h_ * 512:b_ * 1024 + (h_ + 1) * 512], in_=pp[:, 0:512])),
                             reads=['ps%d' % (6 + h_)], writes=[('XIN', b_)])
                S.dma('pool', out_d[t0 + b_ * 128:t0 + (b_ + 1) * 128, :], SCR[:, b_ * 1024:(b_ + 1) * 1024], reads=[('XIN', b_)],
                      writes=[('OUT', t0, b_)])

        tiles2 = [(0, i * 512, 512) for i in range(T // 512)]
        ffn_phase("ffn2", tiles2, load_feat_major, epi_final)

        S.barrier()
        print("ops emitted", S.nops, S.cnt, S.dn)
    return nc


_CACHE = {}


def _prep_inputs(inputs, b):
    f = lambda a: np.ascontiguousarray(a, dtype=np.float32)
    m = {"x": f(inputs["x"][b]), "ctx": f(inputs["ctx"][b])}
    src = dict(inputs)
    src["c"] = inputs["c"][b]
    for n, r in VEC_ROWS:
        m[n] = f(np.asarray(src[n]).reshape(r, 128))
    m["w_mod"] = f(inputs["w_mod"][0])
    m["w_in"] = f(inputs["w_in"][0])
    m["lru_wa"] = f(inputs["lru_wa"][0])
    m["lru_wx"] = f(inputs["lru_wx"][0])
    m["rw_w_up"] = f(inputs["rw_w_up"][0].reshape(128, D))
    m["w_proj_rw"] = f(inputs["w_proj_rw"][0])
    m["w_proj_lru"] = f(inputs["w_proj_lru"][0])
    m["w_out"] = f(inputs["w_out"][0])
    m["rw_a_up"] = f(inputs["rw_a_up"][0].reshape(128, D))
    m["rw_g_up"] = f(inputs["rw_g_up"][0])
    for fn_ in ("ffn1", "ffn2"):
        for s in ("_wg", "_wu", "_wd"):
            m[fn_ + s] = f(inputs[fn_ + s][0])
    return m


def kernel(**inputs):
    if "nc" not in _CACHE:
        _CACHE["nc"] = build()
    nc = _CACHE["nc"]
    in_maps = [_prep_inputs(inputs, b % 4) for b in range(N_CORES)]
    res = run_bass_kernel_spmd(nc, in_maps, core_ids=list(range(N_CORES)))
    out = np.stack([res.results[b]["out"] for b in range(4)], axis=0)
    return out.astype(np.float32)
```

```python
import numpy as np
from contextlib import ExitStack
import concourse.bass as bass
import concourse.mybir as mybir
from concourse.bass_utils import run_bass_kernel_spmd

F32 = mybir.dt.float32
BF16 = mybir.dt.bfloat16
AF = mybir.ActivationFunctionType
ALU = mybir.AluOpType
AX = mybir.AxisListType

D = 1024
T = 4096
TC = 256
TT = T + TC
DFF = 2816
NJ = DFF // 128
NIN = 7552
NRW = 3456
EPS = 1e-6
N_CORES = 8


class Sched:
    R = 8

    def __init__(self, nc, es):
        self.nc = nc
        self.eng = {'pe': nc.tensor, 'act': nc.scalar, 'dve': nc.vector, 'pool': nc.gpsimd, 'sp': nc.sync}
        self.sem = {k: es.enter_context(nc.semaphore('s_' + k)) for k in ('pe', 'act', 'dve', 'pool')}
        self.cnt = {k: 0 for k in self.sem}
        self.dsem = {q: [es.enter_context(nc.semaphore('d_%s%d' % (q, i))) for i in range(self.R)]
                     for q in ('sp', 'pool')}
        self.dn = {q: 0 for q in self.dsem}
        self.waited = {e: {} for e in self.eng}
        self.lastw = {}
        self.readers = {}
        self.nops = 0
        self.pe_last = {}

    def _semof(self, k):
        return self.sem[k] if isinstance(k, str) else self.dsem[k[0]][k[1]]

    def _wait(self, e, k, val):
        if val <= 0 or self.waited[e].get(k, 0) >= val:
            return
        self.waited[e][k] = val
        self.eng[e].wait_ge(self._semof(k), val)

    def _deps(self, e, reads, writes):
        need = {}
        raw = {}
        for r in reads:
            t = self.lastw.get(r)
            if t is not None:
                if need.get(t[0], 0) < t[1]:
                    need[t[0]] = t[1]
                if raw.get(t[0], 0) < t[1]:
                    raw[t[0]] = t[1]
            if isinstance(r, str) and r.startswith('ps'):
                for k, v in self.readers.get(r, {}).items():
                    if k != e and need.get(k, 0) < v:
                        need[k] = v
        for w in writes:
            t = self.lastw.get(w)
            if t is not None and need.get(t[0], 0) < t[1]:
                need[t[0]] = t[1]
            for k, v in self.readers.get(w, {}).items():
                if need.get(k, 0) < v:
                    need[k] = v
        for k, v in need.items():
            if k == e:
                if e == 'pe':
                    continue
                if e in ('act', 'dve'):
                    v = raw.get(k, 0)
                    if v <= 0:
                        continue
            self._wait(e, k, v)

    def _commit(self, tok, reads, writes):
        for w in writes:
            self.lastw[w] = tok
            self.readers[w] = {}
        ws = set(writes)
        for r in reads:
            if r in ws:
                continue
            d = self.readers.setdefault(r, {})
            if d.get(tok[0], 0) < tok[1]:
                d[tok[0]] = tok[1]

    def mm(self, out, lhsT, rhs, start=True, stop=True, reads=(), writes=(), inc=True):
        rg = lhsT.base_partition() if lhsT.partition_size() < 128 else None
        self.op('pe', lambda e: e.matmul(out, lhsT=lhsT, rhs=rhs, start=start, stop=stop), reads, writes, inc, rg=rg)

    def op(self, e, fn, reads=(), writes=(), inc=True, rg=None):
        if e == 'pe':
            if rg is not None:
                inc = True
                for w in writes:
                    pl = self.pe_last.get(w)
                    if pl is not None and pl[0] != rg:
                        self._wait('pe', 'pe', pl[1])
            else:
                for w in writes:
                    self.pe_last.pop(w, None)
        self._deps(e, reads, writes)
        ins = fn(self.eng[e])
        self.nops += 1
        if inc:
            self.cnt[e] += 1
            ins.then_inc(self.sem[e], 1)
            tok = (e, self.cnt[e])
        else:
            assert e == 'pe'
            tok = (e, self.cnt[e] + 1)
        if e == 'pe' and rg is not None:
            for w in writes:
                self.pe_last[w] = (rg, tok[1])
        self._commit(tok, reads, writes)

    def dma(self, q, out, in_, reads=(), writes=(), **kw):
        n = self.dn[q]
        i = n % self.R
        tgt = 16 * (n // self.R + 1)
        self._wait(q, (q, i), tgt - 16)
        self._deps(q, reads, writes)
        self.eng[q].dma_start(out=out, in_=in_, **kw).then_inc(self.dsem[q][i], 16)
        self.nops += 1
        self.dn[q] += 1
        self._commit(((q, i), tgt), reads, writes)

    def barrier(self):
        for e in self.eng:
            for k in self.sem:
                self._wait(e, k, self.cnt[k])
            for q in self.dsem:
                for i in range(self.R):
                    n_i = (self.dn[q] - i + self.R - 1) // self.R
                    self._wait(e, (q, i), 16 * n_i)
        self.lastw = {}
        self.readers = {}


VEC_ROWS = [("b_mod", 72), ("g_ffn1", 8), ("g_mix", 8), ("g_ffn2", 8), ("g_final", 8), ("rw_mu", 27),
            ("rw_w0", 16), ("rw_a0", 16), ("rw_k_k", 8), ("rw_k_a", 8), ("rw_r_k", 8), ("rw_ln_w", 8),
            ("rw_ln_b", 8), ("lru_conv_w", 32), ("lru_conv_b", 8), ("lru_lam", 16), ("lru_ba", 16),
            ("lru_bx", 16), ("c", 8), ("c_ctx", 8)]
VEC_OFF = {}
_o = 0
for _n, _r in VEC_ROWS:
    VEC_OFF[_n] = _o
    _o += _r
NVEC = _o
NVT = (NVEC + 127) // 128


def build(debug=False, phases=None):
    nc = bass.Bass("TRN2", target_bir_lowering=False)
    dram_in = {}

    def din(name, shape, dt=F32):
        dram_in[name] = nc.dram_tensor(name, list(shape), dt, kind="ExternalInput").ap()
        return dram_in[name]

    x_d = din("x", [T, D])
    ctx_d = din("ctx", [TC, D])
    for n, r in VEC_ROWS:
        din(n, [r, 128])
    w_mod = din("w_mod", [D, 9 * D])
    wts = {}
    for f in ("ffn1", "ffn2"):
        wts[f + "_wg"] = din(f + "_wg", [D, DFF])
        wts[f + "_wu"] = din(f + "_wu", [D, DFF])
        wts[f + "_wd"] = din(f + "_wd", [DFF, D])
    w_in_d = din("w_in", [D, NIN])
    lru_wa_d = din("lru_wa", [2, 4, 256, 256])
    lru_wx_d = din("lru_wx", [2, 4, 256, 256])
    wpr_d = din("w_proj_rw", [D, D])
    wpl_d = din("w_proj_lru", [D, D])
    wo_d = din("w_out", [D, D])
    rw_w_up_d = din("rw_w_up", [128, D])
    rw_a_up_d = din("rw_a_up", [128, D])
    rw_g_up_d = din("rw_g_up", [128, D])
    out_d = nc.dram_tensor("out", [T, D], F32, kind="ExternalOutput").ap()
    skind = "ExternalOutput" if debug else "Internal"
    X1 = nc.dram_tensor("X1", [D, TT], F32, kind=skind).ap()
    XN = nc.dram_tensor("XN", [D, TT], BF16, kind=skind).ap()
    MODD = nc.dram_tensor("MODD", [128, 144], F32, kind=skind).ap()
    P = nc.dram_tensor("P", [NIN, TT], BF16, kind=skind).ap()
    YL = nc.dram_tensor("YL", [D, T], BF16, kind=skind).ap()
    HSTD = nc.dram_tensor("HSTD", [128, 16], F32, kind=skind).ap()
    GS = nc.dram_tensor("GS", [D, T], BF16, kind=skind).ap()
    YRW = nc.dram_tensor("YRW", [D, T], BF16, kind=skind).ap()
    SFD = nc.dram_tensor("SFD", [128, 2, 8, 64], F32, kind=skind).ap()
    X2 = nc.dram_tensor("X2", [D, T], F32, kind=skind).ap()
    X2v = X2.rearrange("(c p) t -> p c t", p=128)
    GSv = GS.rearrange("(c p) t -> p c t", p=128)
    YRWv = YRW.rearrange("(c p) t -> p c t", p=128)
    Pv = P.rearrange("(c p) t -> p c t", p=128)
    YLv = YL.rearrange("(c p) t -> p c t", p=128)

    with ExitStack() as es:
        S = Sched(nc, es)
        ps = [es.enter_context(nc.psum_tensor("ps%d" % i, [128, 512], F32)) for i in range(8)]
        sb = lambda name, shape, dt=F32, st=es: st.enter_context(nc.sbuf_tensor(name, list(shape), dt))
        ident = sb("ident", [128, 128])
        ones = sb("ones", [128, 128])
        PV = sb("PV", [128, NVT * 128])
        MOD = sb("MOD", [128, 72, 2])
        SCAL = sb("SCAL", [128, 2, 11, 8])

        def pv(name, i=0):
            o = VEC_OFF[name] + i
            return PV[:, o:o + 1]

        S.op('pool', lambda e: e.memset(ident[:], 1.0), writes=['ident'])
        S.op('pool', lambda e: e.affine_select(out=ident[:], in_=ident[:], pattern=[[-1, 128]],
                                               compare_op=ALU.is_equal, fill=0.0, base=0, channel_multiplier=1),
             reads=['ident'], writes=['ident'])
        S.op('pool', lambda e: e.memset(ones[:], 1.0), writes=['ones'])

        with ExitStack() as p0:
            VST = sb("VST", [128, NVT, 128], st=p0)
            SC = sb("SC", [128, 8, 2], st=p0)
            slab = [sb("slab%d" % i, [128, 8, 1152], st=p0) for i in range(2)]
            S.op('dve', lambda e: e.memset(VST[:], 0.0), writes=['VST'])
            for n, r in VEC_ROWS:
                o = VEC_OFF[n]
                done = 0
                while done < r:
                    ti, ri = divmod(o + done, 128)
                    m = min(r - done, 128 - ri)
                    S.dma('sp', VST[ri:ri + m, ti, :], dram_in[n][done:done + m, :], reads=[], writes=['VST'])
                    done += m
            for ti in range(NVT):
                S.mm(ps[7][:, 0:128], lhsT=VST[:, ti, :], rhs=ident[:],
                                                     start=True, stop=True,
                     reads=['VST', 'ident'], writes=['ps7'])
                S.op('dve', lambda e, ti=ti: e.tensor_copy(out=PV[:, ti * 128:(ti + 1) * 128], in_=ps[7][:, 0:128]),
                     reads=['ps7'], writes=['PV'])
            oc_, occ = VEC_OFF["c"], VEC_OFF["c_ctx"]
            S.op('act', lambda e: e.activation(out=SC[:, :, 0], in_=PV[:, oc_:oc_ + 8], func=AF.Silu),
                 reads=['PV'], writes=['SC'])
            S.op('act', lambda e: e.activation(out=SC[:, :, 1], in_=PV[:, occ:occ + 8], func=AF.Silu),
                 reads=['PV'], writes=['SC'])
            wmv = w_mod.rearrange("(k p) n -> p k n", p=128)
            for si in range(8):
                sl = slab[si % 2]
                key = 'slab%d' % (si % 2)
                for k in range(8):
                    S.dma('sp' if k % 2 == 0 else 'pool', sl[:, k, :], wmv[:, k, si * 1152:(si + 1) * 1152],
                          writes=[key])
                for ol in range(9):
                    oc = si * 9 + ol
                    for k in range(8):
                        S.mm(
                            ps[6][:, oc * 2:oc * 2 + 2], lhsT=sl[:, k, ol * 128:(ol + 1) * 128], rhs=SC[:, k, :],
                            start=(k == 0), stop=(k == 7),
                            reads=[key, 'SC'], writes=['ps6'], inc=(k == 7))
            bo = VEC_OFF["b_mod"]
            psv = ps[6][:, 0:144].rearrange("p (a b) -> p a b", b=2)
            for s_ in range(2):
                S.op('dve', lambda e, s_=s_: e.tensor_tensor(out=MOD[:, :, s_], in0=psv[:, :, s_],
                                                             in1=PV[:, bo:bo + 72], op=ALU.add),
                     reads=['ps6', 'PV'], writes=['MOD'])
            for s_ in range(2):
                for gi, gname in enumerate(("g_ffn1", "g_mix", "g_ffn2")):
                    go = VEC_OFF[gname]
                    sh = MOD[:, (3 * gi) * 8:(3 * gi + 1) * 8, s_]
                    scl = MOD[:, (3 * gi + 1) * 8:(3 * gi + 2) * 8, s_]
                    gt = MOD[:, (3 * gi + 2) * 8:(3 * gi + 3) * 8, s_]
                    S.op('dve', lambda e, s_=s_, gi=gi, scl=scl, go=go: e.scalar_tensor_tensor(
                        out=SCAL[:, s_, 3 * gi, :], in0=scl, scalar=1.0, in1=PV[:, go:go + 8],
                        op0=ALU.add, op1=ALU.mult), reads=['MOD', 'PV'], writes=['SCAL'])
                    S.op('dve', lambda e, s_=s_, gi=gi, sh=sh: e.tensor_copy(out=SCAL[:, s_, 3 * gi + 1, :], in_=sh),
                         reads=['MOD'], writes=['SCAL'])
                    S.op('dve', lambda e, s_=s_, gi=gi, gt=gt: e.tensor_scalar(
                        out=SCAL[:, s_, 3 * gi + 2, :], in0=gt, scalar1=(1.0 if gi == 1 else 0.5), scalar2=None,
                        op0=ALU.mult), reads=['MOD'], writes=['SCAL'])
            gfo = VEC_OFF["g_final"]
            for s_ in range(2):
                S.op('dve', lambda e, s_=s_: e.tensor_copy(out=SCAL[:, s_, 9, :], in_=PV[:, gfo:gfo + 8]), reads=['PV'], writes=['SCAL'])
                S.op('dve', lambda e, s_=s_: e.memset(SCAL[:, s_, 10, :], 0.0), reads=['SCAL'], writes=['SCAL'])
            if debug:
                S.dma('sp', MODD[:, :], MOD[:].rearrange("p a b -> p (a b)"), reads=['MOD'], writes=['MODD'])
            S.barrier()

        def ffn_phase(which, tiles, load_tile, epilogue):
            with ExitStack() as ph:
                WG = sb("WG_" + which, [128, 8, DFF], BF16, st=ph)
                WU = sb("WU_" + which, [128, 8, DFF], BF16, st=ph)
                WD = sb("WD_" + which, [128, NJ, D], BF16, st=ph)
                SCR = sb("SCR_" + which, [128, 4096], st=ph)
                XT = sb("XT_" + which, [128, 8, 512], st=ph)
                HN = sb("HN_" + which, [128, 8, 512], BF16, st=ph)
                AA = sb("AA_" + which, [128, NJ, 512], BF16, st=ph)
                TM = [sb("TM_" + which + "%d" % i, [128, 512], st=ph) for i in range(2)]
                RS = sb("RS_" + which, [128, 512], st=ph)
                ci = [0]
                cast_eng = ('dve', 'pool', 'act')

                def load_cast(dst, src, w):
                    i = ci[0] % 4
                    ci[0] += 1
                    key = 'stg%d' % i
                    S.dma('sp' if i % 2 == 0 else 'pool', SCR[:, i * 1024:i * 1024 + w], src, writes=[key])
                    if ci[0] % 2 == 0:
                        S.op('act', lambda e: e.copy(out=dst, in_=SCR[:, i * 1024:i * 1024 + w]), reads=[key])
                    else:
                        S.op('dve', lambda e: e.tensor_copy(out=dst, in_=SCR[:, i * 1024:i * 1024 + w]), reads=[key])
                for (Wt, wd_) in ((WG, wts[which + "_wg"]), (WU, wts[which + "_wu"])):
                    for k in range(8):
                        for c0 in range(0, DFF, 1024):
                            w = min(1024, DFF - c0)
                            load_cast(Wt[:, k, c0:c0 + w], wd_[k * 128:(k + 1) * 128, c0:c0 + w], w)
                wdd = wts[which + "_wd"]
                for j in range(NJ):
                    load_cast(WD[:, j, :], wdd[j * 128:(j + 1) * 128, :], 1024)
                S.barrier()

                def rmsnorm_mod(n, s_, gi, dst, ia=None, ib=None, dk='HN'):
                    ia = 3 * gi if ia is None else ia
                    ib = 3 * gi + 1 if ib is None else ib
                    for c in range(8):
                        tm = TM[c % 2]
                        S.op('act', lambda e, c=c, tm=tm: e.activation(out=tm[:, :n], in_=XT[:, c, :n], func=AF.Square),
                             reads=[('XT', c)], writes=['TM%d' % (c % 2)])
                        S.mm(ps[6][:, :n], lhsT=ones[:], rhs=tm[:, :n],
                                                                  start=(c == 0), stop=(c == 7),
                             reads=['TM%d' % (c % 2), 'ones'], writes=['ps6'], inc=True)
                    S.op('act', lambda e: e.activation(out=RS[:, :n], in_=ps[6][:, :n], func=AF.Sqrt, bias=EPSB[:, 0:1],
                                                       scale=1.0 / D), reads=['ps6'], writes=['RS'])
                    S.op('dve', lambda e: e.reciprocal(out=RS[:, :n], in_=RS[:, :n]), reads=['RS'], writes=['RS'])
                    for c in range(8):
                        tm = TM[c % 2]
                        S.op('dve', lambda e, c=c, tm=tm: e.scalar_tensor_tensor(
                            out=tm[:, :n], in0=XT[:, c, :n], scalar=SCAL[:, s_, ia, c:c + 1], in1=RS[:, :n],
                            op0=ALU.mult, op1=ALU.mult), reads=[('XT', c), 'RS'], writes=['TM%d' % (c % 2)])
                        S.op('act', lambda e, c=c, tm=tm: e.activation(
                            out=dst[:, c, :n], in_=tm[:, :n], func=AF.Identity, bias=SCAL[:, s_, ib, c:c + 1],
                            scale=1.0), reads=['TM%d' % (c % 2)], writes=[(dk, c)])

                gi = 0 if which == "ffn1" else 2
                for (s_, t0, n) in tiles:
                    load_tile(S, SCR, XT, s_, t0, n)
                    rmsnorm_mod(n, s_, gi, HN)
                    for j in range(NJ):
                        pg, pu = ps[(j % 2) * 2], ps[(j % 2) * 2 + 1]
                        kg, ku = 'ps%d' % ((j % 2) * 2), 'ps%d' % ((j % 2) * 2 + 1)
                        for (pp, kk_, Wt) in ((pg, kg, WG), (pu, ku, WU)):
                            for k in range(8):
                                S.mm(
                                    pp[:, :n], lhsT=Wt[:, k, j * 128:(j + 1) * 128], rhs=HN[:, k, :n],
                                    start=(k == 0), stop=(k == 7),
                                    reads=[('HN', k)], writes=[kk_], inc=(k == 7))
                        tm = TM[j % 2]
                        S.op('act', lambda e, pg=pg, tm=tm: e.activation(out=tm[:, :n], in_=pg[:, :n], func=AF.Silu),
                             reads=[kg], writes=['TM%d' % (j % 2)])
                        S.op('dve', lambda e, pu=pu, tm=tm, j=j: e.tensor_tensor(out=AA[:, j, :n], in0=pu[:, :n],
                                                                                 in1=tm[:, :n], op=ALU.mult),
                             reads=[ku, 'TM%d' % (j % 2)], writes=[('AA', j)])
                    for o in range(8):
                        pd = ps[4 + o % 2]
                        kd = 'ps%d' % (4 + o % 2)
                        for j in range(NJ):
                            S.mm(
                                pd[:, :n], lhsT=WD[:, j, o * 128:(o + 1) * 128], rhs=AA[:, j, :n],
                                start=(j == 0), stop=(j == NJ - 1),
                                reads=[('AA', j)], writes=[kd], inc=(j == NJ - 1))
                        S.op('dve', lambda e, pd=pd, o=o: e.scalar_tensor_tensor(
                            out=XT[:, o, :n], in0=pd[:, :n], scalar=SCAL[:, s_, 3 * gi + 2, o:o + 1], in1=XT[:, o, :n],
                            op0=ALU.mult, op1=ALU.add), reads=[kd, ('XT', o)], writes=[('XT', o)])
                    epilogue(S, XT, HN, TM, RS, rmsnorm_mod, s_, t0, n, SCR=SCR)
                S.barrier()

        EPSB = sb("EPSB", [128, 1])
        S.op('pool', lambda e: e.memset(EPSB[:], EPS), writes=['EPSB'])
        MUSG = sb("MUSG", [128, 3, 27])
        _mo = VEC_OFF["rw_mu"]
        S.op('dve', lambda e: e.tensor_scalar(out=MUSG[:, 0, :], in0=PV[:, _mo:_mo + 27], scalar1=-1.0, scalar2=1.0, op0=ALU.mult,
                                              op1=ALU.add), reads=['PV'], writes=['MUSG'])
        S.op('dve', lambda e: e.tensor_scalar(out=MUSG[:, 1, :], in0=PV[:, _mo:_mo + 27], scalar1=0.25, scalar2=None, op0=ALU.mult),
             reads=['PV'], writes=['MUSG'])
        S.op('dve', lambda e: e.tensor_scalar(out=MUSG[:, 2, :], in0=PV[:, _mo:_mo + 27], scalar1=0.5, scalar2=None, op0=ALU.mult),
             reads=['PV'], writes=['MUSG'])

        def load_tok_major(S, SCR, XT, s_, t0, n):
            src = x_d if s_ == 0 else ctx_d
            nb = n // 128
            for b_ in range(nb):
                S.dma('sp', SCR[:, b_ * 1024:(b_ + 1) * 1024], src[t0 + b_ * 128:t0 + (b_ + 1) * 128, :],
                      writes=[('XIN', b_)])
            for c in range(8):
                for b_ in range(nb):
                    S.mm(
                        ps[7][:, b_ * 128:(b_ + 1) * 128], lhsT=SCR[:, b_ * 1024 + c * 128:b_ * 1024 + (c + 1) * 128],
                        rhs=ident[:], start=True, stop=True, reads=[('XIN', b_), 'ident'], writes=['ps7'],
                        inc=(b_ == nb - 1))
                S.op('act' if c % 2 == 0 else 'dve',
                     (lambda e, c=c: e.copy(out=XT[:, c, :n], in_=ps[7][:, :n])) if c % 2 == 0 else
                     (lambda e, c=c: e.tensor_copy(out=XT[:, c, :n], in_=ps[7][:, :n])),
                     reads=['ps7'], writes=[('XT', c)])

        X1v = X1.rearrange("(c p) t -> p c t", p=128)
        XNv = XN.rearrange("(c p) t -> p c t", p=128)

        def epi_ffn1(S, XT, HN, TM, RS, rmsnorm_mod, s_, t0, n, SCR=None):
            g0 = (TC if s_ == 0 else 0) + t0
            for c in range(8):
                S.dma('pool', X1v[:, c, g0:g0 + n], XT[:, c, :n], reads=[('XT', c)], writes=[('X1', g0)])
            rmsnorm_mod(n, s_, 1, HN)
            for c in range(8):
                S.dma('pool', XNv[:, c, g0:g0 + n], HN[:, c, :n], reads=[('HN', c)], writes=[('XN', g0)])

        tiles1 = [(1, 0, TC)] + [(0, i * 512, 512) for i in range(T // 512)]
        if phases is None or "A" in phases:
            ffn_phase("ffn1", tiles1, load_tok_major, epi_ffn1)
        S.barrier()

        import os as _os
        RCUT = int(_os.environ.get("RCUT", "0"))

        class _Cut(Exception):
            pass

        def cut(k):
            if RCUT == k:
                raise _Cut()

        def seq_tiles(n):
            return [(t0, min(512, n - t0)) for t0 in range(0, n, 512)]

        def phase_inproj():
            with ExitStack() as ph:
                WIN = sb("WIN", [128, 8, NIN], BF16, st=ph)
                SCRB = sb("SCRB", [128, 4096], st=ph)
                XNH = [sb("XNH%d" % i, [128, 8, 640], BF16, st=ph) for i in range(2)]
                XS = [sb("XSn%d" % i, [128, 8, 512], BF16, st=ph) for i in range(2)]
                XSF = [sb("XSF%d" % i, [128, 512], st=ph) for i in range(2)]
                TZ = [sb("TZ%d" % i, [128, 512], st=ph) for i in range(2)]
                PO = [sb("PO%d" % i, [128, 512], BF16, st=ph) for i in range(6)]
                ci = 0
                for k in range(8):
                    for c0 in range(0, NIN, 1024):
                        w = min(1024, NIN - c0)
                        i = ci % 4
                        ci += 1
                        key = 'stg%d' % i
                        S.dma('sp' if i % 2 == 0 else 'pool', SCRB[:, i * 1024:i * 1024 + w],
                              w_in_d[k * 128:(k + 1) * 128, c0:c0 + w], writes=[key])
                        if ci % 2 == 0:
                            S.op('act', lambda e, i=i, w=w, k=k, c0=c0: e.copy(out=WIN[:, k, c0:c0 + w],
                                                                               in_=SCRB[:, i * 1024:i * 1024 + w]), reads=[key])
                        else:
                            S.op('dve', lambda e, i=i, w=w, k=k, c0=c0: e.tensor_copy(out=WIN[:, k, c0:c0 + w],
                                                                                      in_=SCRB[:, i * 1024:i * 1024 + w]), reads=[key])
                S.barrier()
                tl = [(1, 0, TC)] + [(0, t0, n) for (t0, n) in seq_tiles(T)]
                it = 0
                iz = 0
                for ti_, (s_, t0, n) in enumerate(tl):
                    g0 = (TC if s_ == 0 else 0) + t0
                    H = XNH[ti_ % 2]
                    Xs = XS[ti_ % 2]
                    kH = lambda c: ('XNH', ti_ % 2, c)
                    kX = lambda c: ('XS', ti_ % 2, c)
                    has_top = (s_ == 0 and t0 > 0)
                    has_bot = (s_ == 0 and t0 + n < T)
                    if not has_top:
                        S.op('pool', lambda e, H=H: e.memset(H[:, :, 0:64], 0.0), reads=[kH(c) for c in range(8)],
                             writes=[kH(c) for c in range(8)])
                    if not has_bot:
                        S.op('pool', lambda e, H=H: e.memset(H[:, :, 64 + n:128 + n], 0.0), reads=[kH(c) for c in range(8)],
                             writes=[kH(c) for c in range(8)])
                    for c in range(8):
                        lo = g0 - (64 if has_top else 0)
                        hi = g0 + n + (64 if has_bot else 0)
                        S.dma('sp', H[:, c, 64 - (g0 - lo):64 + n + (hi - g0 - n)], XNv[:, c, lo:hi], reads=[kH(c)], writes=[kH(c)])
                    for c in range(8):
                        if s_ == 1:
                            S.op('dve' if c % 2 else 'pool', lambda e, c=c: e.tensor_tensor(
                                out=Xs[:, c, :n], in0=H[:, c, 63:63 + n], in1=H[:, c, 65:65 + n], op=ALU.add),
                                reads=[kH(c)], writes=[kX(c)])
                        else:
                            xf = XSF[c % 2]
                            kf = 'XSF%d' % (c % 2)
                            xf3 = xf[:, :n].rearrange("p (r c) -> p r c", c=64)
                            c3 = H[:, c, 64:64 + n].rearrange("p (r c) -> p r c", c=64)
                            xs3 = Xs[:, c, :n].rearrange("p (r c) -> p r c", c=64)
                            S.op('pool', lambda e, c=c, xf=xf: e.tensor_tensor(out=xf[:, :n], in0=H[:, c, 0:n], in1=H[:, c, 128:128 + n],
                                                                              op=ALU.add), reads=[kH(c)], writes=[kf])
                            S.op('dve', lambda e, xf3=xf3, c3=c3: e.tensor_tensor(out=xf3[:, :, 1:64], in0=xf3[:, :, 1:64], in1=c3[:, :, 0:63],
                                                                                  op=ALU.add), reads=[kH(c), kf], writes=[kf])
                            S.op('dve', lambda e, xf3=xf3, c3=c3, xs3=xs3: e.tensor_tensor(out=xs3[:, :, 0:63], in0=xf3[:, :, 0:63],
                                                                                           in1=c3[:, :, 1:64], op=ALU.add),
                                 reads=[kH(c), kf], writes=[kX(c)])
                            S.op('pool', lambda e, xf3=xf3, xs3=xs3: e.tensor_copy(out=xs3[:, :, 63:64], in_=xf3[:, :, 63:64]),
                                 reads=[kf, kX(c)], writes=[kX(c)])
                    noc = 59 if s_ == 0 else 35
                    qi = 1 if s_ == 0 else 2
                    for oc in range(noc):
                        po = PO[it % 6]
                        kpo = 'PO%d' % (it % 6)
                        it += 1
                        if oc < 27:
                            b0 = (oc % 2) * 2
                            p1, p2 = ps[b0], ps[b0 + 1]
                            k1, k2 = 'ps%d' % b0, 'ps%d' % (b0 + 1)
                            for k in range(8):
                                S.mm(p1[:, :n], lhsT=WIN[:, k, oc * 128:(oc + 1) * 128], rhs=H[:, k, 64:64 + n], start=(k == 0), stop=(k == 7),
                                     reads=[kH(k)], writes=[k1], inc=(k == 7))
                            for k in range(8):
                                S.mm(p2[:, :n], lhsT=WIN[:, k, oc * 128:(oc + 1) * 128], rhs=Xs[:, k, :n], start=(k == 0), stop=(k == 7),
                                     reads=[kX(k)], writes=[k2], inc=(k == 7))
                            tz = TZ[iz % 2]
                            ktz = 'TZ%d' % (iz % 2)
                            iz += 1
                            S.op('act', lambda e, p2=p2, tz=tz, oc=oc: e.activation(out=tz[:, :n], in_=p2[:, :n], func=AF.Identity,
                                                                                    scale=MUSG[:, qi, oc:oc + 1]), reads=[k2, 'MUSG'],
                                 writes=[ktz])
                            S.op('dve', lambda e, p1=p1, tz=tz, po=po, oc=oc: e.scalar_tensor_tensor(
                                out=po[:, :n], in0=p1[:, :n], scalar=MUSG[:, 0, oc:oc + 1], in1=tz[:, :n], op0=ALU.mult, op1=ALU.add),
                                reads=[k1, ktz, 'MUSG'], writes=[kpo])
                        else:
                            pp = ps[4 + oc % 4]
                            kp = 'ps%d' % (4 + oc % 4)
                            for k in range(8):
                                S.mm(pp[:, :n], lhsT=WIN[:, k, oc * 128:(oc + 1) * 128], rhs=H[:, k, 64:64 + n], start=(k == 0), stop=(k == 7),
                                     reads=[kH(k)], writes=[kp], inc=(k == 7))
                            if oc >= 43:
                                S.op('act', lambda e, pp=pp, po=po: e.activation(out=po[:, :n], in_=pp[:, :n], func=AF.Sigmoid),
                                     reads=[kp], writes=[kpo])
                            elif oc >= 35:
                                S.op('act', lambda e, pp=pp, po=po: e.activation(out=po[:, :n], in_=pp[:, :n], func=AF.Gelu),
                                     reads=[kp], writes=[kpo])
                            elif oc % 2 == 0:
                                S.op('act', lambda e, pp=pp, po=po: e.copy(out=po[:, :n], in_=pp[:, :n]), reads=[kp], writes=[kpo])
                            else:
                                S.op('dve', lambda e, pp=pp, po=po: e.tensor_copy(out=po[:, :n], in_=pp[:, :n]), reads=[kp],
                                     writes=[kpo])
                        S.dma('pool', Pv[:, oc, g0:g0 + n], po[:, :n], reads=[kpo], writes=[('P', oc, g0)])
                S.barrier()

        if phases is None or "B" in phases:
            phase_inproj()
        if phases is not None and "L" not in phases and "R" not in phases:
            return nc

        ONEB = sb("ONEB", [128, 1])
        S.op('pool', lambda e: e.memset(ONEB[:], 1.0), writes=['ONEB'])
        LC = sb("LC", [128, 2, 16])
        HST = sb("HST", [128, 16])
        lo = VEC_OFF["lru_lam"]
        S.op('act', lambda e: e.activation(out=LC[:, 0, :], in_=PV[:, lo:lo + 16], func=AF.Exp, scale=-1.0),
             reads=['PV'], writes=['LC'])
        S.op('act', lambda e: e.activation(out=LC[:, 0, :], in_=LC[:, 0, :], func=AF.Ln, bias=ONEB[:, 0:1], scale=1.0),
             reads=['LC', 'ONEB'], writes=['LC'])
        S.op('dve', lambda e: e.tensor_scalar(out=LC[:, 1, :], in0=LC[:, 0, :], scalar1=-16.0, scalar2=None, op0=ALU.mult),
             reads=['LC'], writes=['LC'])
        S.op('dve', lambda e: e.tensor_scalar(out=LC[:, 0, :], in0=LC[:, 0, :], scalar1=-8.0, scalar2=None, op0=ALU.mult),
             reads=['LC'], writes=['LC'])

        def phase_lru():
            with ExitStack() as ph:
                WAX = sb("WAX", [128, 2, 16, 256], BF16, st=ph)
                STG = sb("STGL", [128, 16, 256], st=ph)
                XL = sb("XL", [128, 2, T + 4], BF16, st=ph)
                XC = sb("XC", [128, 2, T], st=ph)
                XCB = sb("XCB", [128, 2, T], BF16, st=ph)
                AB = sb("AB", [128, T], st=ph)
                UB = sb("UB", [128, T], st=ph)
                HB = [sb("HB%d" % i, [128, T], st=ph) for i in range(2)]
                GL = sb("GLg", [128, T], BF16, st=ph)
                YB = sb("YBl", [128, T], BF16, st=ph)
                TL = [sb("TL%d" % i, [128, 512], st=ph) for i in range(5)]
                for gi_, wsrc in enumerate((lru_wa_d, lru_wx_d)):
                    S.dma('sp', STG[:], wsrc.rearrange("d n (k p) j -> p (d n k) j", p=128), writes=['STGL'])
                    S.op('dve', lambda e, gi_=gi_: e.tensor_copy(out=WAX[:, gi_, :, :], in_=STG[:]), reads=['STGL'],
                         writes=['WAX'])
                S.op('pool', lambda e: e.memset(XL[:], 0.0), writes=['XL'])
                cw, cb = VEC_OFF["lru_conv_w"], VEC_OFF["lru_conv_b"]
                bao, bxo = VEC_OFF["lru_ba"], VEC_OFF["lru_bx"]
                for s_ in (1, 0):
                    n = TC if s_ == 1 else T
                    g0 = 0 if s_ == 1 else TC
                    if s_ == 0:
                        S.op('pool', lambda e: e.memset(XL[:], 0.0), reads=['XL'], writes=['XL'])
                    for blk in range(4):
                        for k in range(2):
                            cc = 2 * blk + k
                            S.dma('sp', XL[:, k, 1:1 + n], Pv[:, 27 + cc, g0:g0 + n], reads=[('P', 27 + cc, g0)],
                                  writes=['XL'])
                            S.op('act', lambda e, k=k, cc=cc: e.activation(
                                out=XC[:, k, :n], in_=XL[:, k, 0:n], func=AF.Identity, bias=PV[:, cb + cc:cb + cc + 1],
                                scale=PV[:, cw + cc:cw + cc + 1]), reads=['XL'], writes=[('XC', k)])
                            for j in range(1, 4):
                                S.op('dve', lambda e, k=k, cc=cc, j=j: e.scalar_tensor_tensor(
                                    out=XC[:, k, :n], in0=XL[:, k, j:j + n], scalar=PV[:, cw + j * 8 + cc:cw + j * 8 + cc + 1],
                                    in1=XC[:, k, :n], op0=ALU.mult, op1=ALU.add), reads=['XL', ('XC', k)], writes=[('XC', k)])
                            S.op('act', lambda e, k=k: e.copy(out=XCB[:, k, :n], in_=XC[:, k, :n]), reads=[('XC', k)],
                                 writes=[('XCB', k)])
                        for oc in range(2):
                            cc = 2 * blk + oc
                            if s_ == 0:
                                S.dma('sp', GL[:, :n], Pv[:, 35 + cc, g0:g0 + n], reads=[('P', 35 + cc, g0)], writes=['GL'])
                            for d in range(2):
                                col = d * 8 + cc
                                kHB = 'HB%d' % d
                                HBd = HB[d]
                                for (t0, m) in seq_tiles(n):
                                    for gi_ in range(2):
                                        pp = ps[gi_]
                                        for kin in range(2):
                                            S.mm(
                                                pp[:, :m], lhsT=WAX[:, gi_, (d * 4 + blk) * 2 + kin, oc * 128:(oc + 1) * 128],
                                                rhs=XCB[:, kin, t0:t0 + m], start=(kin == 0), stop=(kin == 1),
                                                reads=[('XCB', kin), 'WAX'], writes=['ps%d' % gi_], inc=(kin == 1))
                                    S.op('act', lambda e, m=m, col=col, t0=t0: e.activation(
                                        out=AB[:, t0:t0 + m], in_=ps[0][:, :m], func=AF.Sigmoid, bias=PV[:, bao + col:bao + col + 1],
                                        scale=1.0), reads=['ps0'], writes=[('AB', t0)])
                                    S.op('act', lambda e, m=m, col=col, t0=t0: e.activation(
                                        out=UB[:, t0:t0 + m], in_=ps[1][:, :m], func=AF.Sigmoid, bias=PV[:, bxo + col:bxo + col + 1],
                                        scale=1.0), reads=['ps1'], writes=[('UB', t0)])
                                for (t0, m) in seq_tiles(n):
                                    S.op('act', lambda e, m=m, col=col, t0=t0: e.activation(
                                        out=HBd[:, t0:t0 + m], in_=AB[:, t0:t0 + m], func=AF.Exp, scale=LC[:, 1, col:col + 1]),
                                        reads=[('AB', t0), 'LC'], writes=[kHB])
                                    S.op('act', lambda e, m=m, col=col, t0=t0: e.activation(
                                        out=AB[:, t0:t0 + m], in_=AB[:, t0:t0 + m], func=AF.Exp, scale=LC[:, 0, col:col + 1]),
                                        reads=[('AB', t0), 'LC'], writes=[('AB', t0)])
                                for (t0, m) in seq_tiles(n):
                                    S.op('act', lambda e, m=m, t0=t0: e.activation(
                                        out=HBd[:, t0:t0 + m], in_=HBd[:, t0:t0 + m], func=AF.Sqrt, bias=ONEB[:, 0:1], scale=-1.0),
                                        reads=[kHB, 'ONEB'], writes=[kHB])
                                for (t0, m) in seq_tiles(n):
                                    S.op('pool', lambda e, m=m, t0=t0: e.tensor_tensor(
                                        out=UB[:, t0:t0 + m], in0=UB[:, t0:t0 + m], in1=XC[:, oc, t0:t0 + m], op=ALU.mult),
                                        reads=[('UB', t0), ('XC', oc)], writes=[('UB', t0)])
                                    S.op('dve', lambda e, m=m, t0=t0: e.tensor_tensor(
                                        out=UB[:, t0:t0 + m], in0=UB[:, t0:t0 + m], in1=HBd[:, t0:t0 + m], op=ALU.mult),
                                        reads=[('UB', t0), kHB], writes=[('UB', t0)])
                                rk = [('AB', t0) for (t0, m) in seq_tiles(n)] + [('UB', t0) for (t0, m) in seq_tiles(n)]
                                init = 0.0 if s_ == 1 else HST[:, col:col + 1]
                                if d == 0:
                                    S.op('dve', lambda e, init=init: e.tensor_tensor_scan(
                                        out=HB[0][:, :n], data0=AB[:, :n], data1=UB[:, :n], initial=init, op0=ALU.mult,
                                        op1=ALU.add), reads=rk + ['HST'], writes=['HB0'])
                                    if s_ == 1:
                                        S.op('dve', lambda e, col=col: e.tensor_copy(out=HST[:, col:col + 1], in_=HB[0][:, n - 1:n]),
                                             reads=['HB0'], writes=['HST'])
                                else:
                                    S.op('dve', lambda e, init=init: e.tensor_tensor_scan(
                                        out=HB[1][:, n - 1::-1] if False else HB[1][:, :n][:, ::-1], data0=AB[:, :n][:, ::-1],
                                        data1=UB[:, :n][:, ::-1], initial=init, op0=ALU.mult, op1=ALU.add),
                                        reads=rk + ['HST'], writes=['HB1'])
                                    if s_ == 1:
                                        S.op('dve', lambda e, col=col: e.tensor_copy(out=HST[:, col:col + 1], in_=HB[1][:, 0:1]),
                                             reads=['HB1'], writes=['HST'])
                            if s_ == 0:
                                S.op('pool', lambda e: e.tensor_tensor(out=HB[0][:, :n], in0=HB[0][:, :n], in1=HB[1][:, :n],
                                                                       op=ALU.add), reads=['HB0', 'HB1'], writes=['HB0'])
                                S.op('dve', lambda e: e.tensor_tensor(out=YB[:, :n], in0=HB[0][:, :n], in1=GL[:, :n], op=ALU.mult),
                                     reads=['HB0', 'GL'], writes=['YBl'])
                                S.dma('pool', YLv[:, cc, :], YB[:, :n], reads=['YBl'], writes=[('YL', cc)])
                if debug:
                    S.dma('sp', HSTD[:, :], HST[:], reads=['HST'], writes=['HSTD'])
                S.barrier()

        if phases is None or "L" in phases:
            phase_lru()
        if phases is not None and "R" not in phases:
            return nc
        DS = float(np.exp(-0.5))
        LN_X_EPS = 64e-5
        NCH = TT // 64

        def _rwkv_body(ph):
            if True:
                WUP = sb("WUP", [128, D], BF16, st=ph)
                AUP = sb("AUP", [128, D], BF16, st=ph)
                GUP = sb("GUP", [128, D], BF16, st=ph)
                IDB = sb("IDB", [128, 128], BF16, st=ph)
                BONES = sb("BONES", [128, 128], st=ph)
                MK = [sb("MK%d" % d, [128, 2, 256], st=ph) for d in range(2)]
                AMK = [sb("AMK%d" % d, [128, 2, 64], st=ph) for d in range(2)]
                BMK = [sb("BMK%d" % d, [128, 4, 64], st=ph) for d in range(2)]
                RMF = sb("RMF", [128, 512], st=ph)
                RMB = sb("RMB", [128, 512], st=ph)
                TW = sb("TW", [128, TT], BF16, st=ph)
                ZA = sb("ZA", [128, TT], BF16, st=ph)
                PL = sb("PL", [128, TT], BF16, st=ph)
                SH = sb("SH", [128, TT], st=ph)
                ZR = sb("ZR", [128, TT], BF16, st=ph)
                ZK = sb("ZK", [128, TT], BF16, st=ph)
                ZV = sb("ZV", [128, TT], BF16, st=ph)
                KK = sb("KK", [128, TT], BF16, st=ph)
                ART = [[sb("ART%d%d" % (d, i), [128, 8, 2, 64], BF16, st=ph) for i in range(2)] for d in range(2)]
                KBT = [[sb("KBT%d%d" % (d, i), [128, 8, 2, 64], BF16, st=ph) for i in range(2)] for d in range(2)]
                WC = [sb("WC%d" % d, [128, NCH], st=ph) for d in range(2)]
                TRP = [[sb("TRP%d%d" % (d, i), [128, 512], F32 if i < 5 else BF16, st=ph) for i in range(7)] for d in range(2)]
                YB = sb("YBr", [128, 64, 64], st=ph)
                TR = [sb("TR%d" % i, [128, 512], st=ph) for i in range(3)]
                TRS = sb("TRS", [128, 256], st=ph)
                MT = [[sb("MT%d%d" % (d, i), [128, 8, 256], BF16, st=ph) for i in range(2)] for d in range(2)]
                ABt = [[sb("ABt%d%d" % (d, i), [128, 4, 2, 128], BF16, st=ph) for i in range(2)] for d in range(2)]
                ACC = [[sb("ACC%d%d" % (d, i), [128, 4, 128], BF16, st=ph) for i in range(2)] for d in range(2)]
                TTt = [[sb("TTt%d%d" % (d, i), [128, 8, 128], BF16, st=ph) for i in range(2)] for d in range(2)]
                PT = [[sb("PT%d%d" % (d, i), [128, 3, 64], BF16, st=ph) for i in range(2)] for d in range(2)]
                XB = [[sb("XB%d%d" % (d, i), [128, 64], BF16, st=ph) for i in range(2)] for d in range(2)]
                UB_ = [[sb("UBr%d%d" % (d, i), [128, 64], BF16, st=ph) for i in range(2)] for d in range(2)]
                SS = [sb("SS%d" % d, [128, 64], st=ph) for d in range(2)]
                SSb = [sb("SSb%d" % d, [128, 64], BF16, st=ph) for d in range(2)]
                MUS = sb("MUS", [128, 3, 27], st=ph)
                KA1 = sb("KA1", [128, 8], st=ph)
                GO = [sb("GO%d" % i, [128, 512], BF16, st=ph) for i in range(2)]
                ST8 = sb("ST8", [128, 2, 8, 64], st=ph) if debug else None

                for i_, (dst, src) in enumerate(((WUP, rw_w_up_d), (AUP, rw_a_up_d), (GUP, rw_g_up_d))):
                    S.dma('sp', SH[:, 0:D], src[:, :], writes=['SH'])
                    S.op('dve', lambda e, dst=dst: e.tensor_copy(out=dst[:], in_=SH[:, 0:D]), reads=['SH'], writes=['LW_' + str(i_)])
                S.op('dve', lambda e: e.tensor_copy(out=IDB[:], in_=ident[:]), reads=['ident'], writes=['IDB'])
                S.op('pool', lambda e: e.memset(BONES[:], 0.0), writes=['BONES'])
                S.op('pool', lambda e: e.memset(BONES[0:64, 0:64], 1.0), reads=['BONES'], writes=['BONES'])
                S.op('pool', lambda e: e.memset(BONES[64:128, 64:128], 1.0), reads=['BONES'], writes=['BONES'])
                S.op('pool', lambda e: e.memset(RMF[:], 1.0), writes=['RMF'])
                S.op('pool', lambda e: e.memset(RMF[:, 0:512:64], 0.0), reads=['RMF'], writes=['RMF'])
                S.op('pool', lambda e: e.memset(RMB[:], 1.0), writes=['RMB'])
                S.op('pool', lambda e: e.memset(RMB[:, 63:512:64], 0.0), reads=['RMB'], writes=['RMB'])

                def tri(dst_view_fn, kind):
                    pat, cm = ([[1, 64]], -1) if kind[0] == 'L' else ([[-1, 64]], 1)
                    cmp_ = ALU.is_gt if kind[2] == 's' else ALU.is_ge
                    for hp in (0, 64):
                        v = dst_view_fn(hp)
                        S.op('pool', lambda e, v=v: e.memset(v, 1.0), reads=['MASKS'], writes=['MASKS'])
                        S.op('pool', lambda e, v=v: e.affine_select(out=v, in_=v, pattern=pat, compare_op=cmp_, fill=0.0, base=0,
                                                                   channel_multiplier=cm), reads=['MASKS'], writes=['MASKS'])
                for d in range(2):
                    ks, ki = ('LTs', 'LTi') if d == 0 else ('GTs', 'GTi')
                    kt = 'GTs' if d == 0 else 'LTs'
                    for q in range(2):
                        for blk, kd_ in enumerate((ks, ki, ks, ki)):
                            tri(lambda hp, d=d, q=q, blk=blk: MK[d][hp:hp + 64, q, blk * 64:(blk + 1) * 64], kd_)
                        tri(lambda hp, d=d, q=q: AMK[d][hp:hp + 64, q, :], ks)
                    for q in range(4):
                        tri(lambda hp, d=d, q=q: BMK[d][hp:hp + 64, q, :], kt)
                mo = VEC_OFF["rw_mu"]
                S.op('dve', lambda e: e.tensor_scalar(out=MUS[:, 0, :], in0=PV[:, mo:mo + 27], scalar1=-1.0, scalar2=1.0, op0=ALU.mult,
                                                      op1=ALU.add), reads=['PV'], writes=['MUS'])
                S.op('dve', lambda e: e.tensor_scalar(out=MUS[:, 1, :], in0=PV[:, mo:mo + 27], scalar1=0.25, scalar2=None, op0=ALU.mult),
                     reads=['PV'], writes=['MUS'])
                S.op('dve', lambda e: e.tensor_scalar(out=MUS[:, 2, :], in0=PV[:, mo:mo + 27], scalar1=0.5, scalar2=None, op0=ALU.mult),
                     reads=['PV'], writes=['MUS'])
                kao = VEC_OFF["rw_k_a"]
                S.op('dve', lambda e: e.tensor_scalar(out=KA1[:], in0=PV[:, kao:kao + 8], scalar1=-1.0, scalar2=1.0, op0=ALU.mult,
                                                      op1=ALU.add), reads=['PV'], writes=['KA1'])
                for t_ in ABt[0] + ABt[1] + ACC[0] + ACC[1] + TTt[0] + TTt[1]:
                    S.op('pool', lambda e, t_=t_: e.memset(t_[:], 0.0), writes=['ABZ'])
                S.barrier()

                cut(1)
                PLx = PL[:, TC:TT].rearrange("p (r c) -> p r c", c=64)
                SHx = SH[:, TC:TT].rearrange("p (r c) -> p r c", c=64)

                def zlerp(pc, dst, dkey):
                    S.dma('sp', dst[:, 0:TC], Pv[:, pc, 0:TC], reads=[dkey], writes=[dkey])
                    for t0 in range(0, T, 1024):
                        S.dma('sp' if (t0 // 1024) % 2 == 0 else 'pool', dst[:, TC + t0:TC + t0 + 1024], Pv[:, pc, TC + t0:TC + t0 + 1024],
                              reads=[dkey], writes=[dkey])

                tiles_all = [(0, TC)] + [(TC + t0, m) for (t0, m) in seq_tiles(T)]

                zlerp(24, ZK, 'ZK')
                cut(2)
                S.op('act', lambda e: e.activation(out=TW[:, :], in_=ZK[:, :], func=AF.Tanh), reads=['ZK'], writes=['TW'])
                zlerp(25, ZA, 'ZA')
                zlerp(26, ZK, 'ZK')
                S.op('act', lambda e: e.activation(out=ZR[:, :], in_=ZK[:, :], func=AF.Sigmoid), reads=['ZK'], writes=['ZR'])
                gi_ = 0
                for c in range(8):
                    for (t0, m) in seq_tiles(T):
                        pp = ps[gi_ % 2]
                        go = GO[gi_ % 2]
                        S.mm(pp[:, :m], lhsT=GUP[:, c * 128:(c + 1) * 128],
                                                                              rhs=ZR[:, TC + t0:TC + t0 + m], start=True, stop=True,
                             reads=['ZR', 'LW_2'], writes=['ps%d' % (gi_ % 2)])
                        S.op('act', lambda e, pp=pp, go=go, m=m: e.copy(out=go[:, :m], in_=pp[:, :m]), reads=['ps%d' % (gi_ % 2)],
                             writes=['GO%d' % (gi_ % 2)])
                        S.dma('pool', GSv[:, c, t0:t0 + m], go[:, :m], reads=['GO%d' % (gi_ % 2)], writes=[('GS', c, t0)])
                        gi_ += 1

                cut(3)
                kko, rko = VEC_OFF["rw_k_k"], VEC_OFF["rw_r_k"]
                w0o, a0o = VEC_OFF["rw_w0"], VEC_OFF["rw_a0"]
                lwo, lbo = VEC_OFF["rw_ln_w"], VEC_OFF["rw_ln_b"]
                v3 = lambda ap, m: ap.rearrange("p (j s) -> p j s", s=64)

                for c in range(8):
                    zlerp(c, ZR, 'ZR')
                    zlerp(8 + c, ZK, 'ZK')
                    zlerp(16 + c, ZV, 'ZV')
                    for (g0, m) in tiles_all:
                        S.op('act', lambda e, g0=g0, m=m: e.activation(out=TR[0][:, :m], in_=ZK[:, g0:g0 + m], func=AF.Square,
                                                                       scale=PV[:, kko + c:kko + c + 1]), reads=['ZK'], writes=['TR0'])
                        S.mm(ps[0][:, :m], lhsT=BONES[:], rhs=TR[0][:, :m], start=True, stop=True,
                             reads=['TR0', 'BONES'], writes=['ps0'])
                        S.op('act', lambda e, m=m: e.activation(out=TR[1][:, :m], in_=ps[0][:, :m], func=AF.Sqrt), reads=['ps0'],
                             writes=['TR1'])
                        S.op('dve', lambda e, m=m: e.tensor_scalar(out=TR[1][:, :m], in0=TR[1][:, :m], scalar1=1e-12, scalar2=None,
                                                                   op0=ALU.max), reads=['TR1'], writes=['TR1'])
                        S.op('dve', lambda e, m=m: e.reciprocal(out=TR[1][:, :m], in_=TR[1][:, :m]), reads=['TR1'], writes=['TR1'])
                        S.op('dve', lambda e, g0=g0, m=m: e.scalar_tensor_tensor(
                            out=KK[:, g0:g0 + m], in0=ZK[:, g0:g0 + m], scalar=PV[:, kko + c:kko + c + 1], in1=TR[1][:, :m],
                            op0=ALU.mult, op1=ALU.mult), reads=['ZK', 'TR1'], writes=['KK'])

                    cut(4)
                    batches = [
                        [[0, 1, 2, 3]] + [list(range(4 + 8 * i, 12 + 8 * i)) for i in range(8)],
                        [[3, 2, 1, 0]] + [list(range(11 + 8 * i, 3 + 8 * i, -1)) for i in range(7, -1, -1)],
                    ]
                    NB = 9
                    for d in range(2):
                        S.op('pool', lambda e, d=d: e.memset(SS[d][:], 0.0), reads=[('SS', d, 0), ('SS', d, 1)],
                             writes=[('SS', d, 0), ('SS', d, 1)])
                        S.op('pool', lambda e, d=d: e.memset(SSb[d][:], 0.0), reads=[('SSb', d, 0), ('SSb', d, 1)],
                             writes=[('SSb', d, 0), ('SSb', d, 1)])
                    yb_written = set()

                    def prep_gen(d, n):
                        js = batches[d][n]
                        par = n % 2
                        jmin = min(js)
                        g0, m = jmin * 64, 64 * len(js)
                        nch = len(js)
                        dh = d * 64
                        col = d * 8 + c
                        LW, AS, CL, E1, E2, TB, TK = TRP[d]
                        kT = lambda i: ('TRP', d, i)
                        A_, K_ = ART[d][par], KBT[d][par]
                        kA, kK = ('ART', d, par), ('KBT', d, par)
                        pd_, kpd = ps[d], 'ps%d' % d
                        S.mm(pd_[:, :m], lhsT=WUP[dh:dh + 64, c * 128:(c + 1) * 128], rhs=TW[dh:dh + 64, g0:g0 + m], start=True, stop=True,
                             reads=['TW', 'LW_0'], writes=[kpd])
                        S.op('act', lambda e: e.activation(out=LW[:, :m], in_=pd_[:, :m], func=AF.Sigmoid,
                                                           bias=PV[:, w0o + col:w0o + col + 1], scale=1.0), reads=[kpd], writes=[kT(0)])
                        S.mm(pd_[:, :m], lhsT=AUP[dh:dh + 64, c * 128:(c + 1) * 128], rhs=ZA[dh:dh + 64, g0:g0 + m], start=True, stop=True,
                             reads=['ZA', 'LW_1'], writes=[kpd])
                        S.op('act', lambda e: e.activation(out=AS[:, :m], in_=pd_[:, :m], func=AF.Sigmoid,
                                                           bias=PV[:, a0o + col:a0o + col + 1], scale=1.0), reads=[kpd], writes=[kT(1)])
                        yield
                        if d == 0:
                            S.op('dve', lambda e: e.tensor_tensor_scan(out=CL[:, :m], data0=RMF[:, :m], data1=LW[:, :m], initial=0.0,
                                                                       op0=ALU.mult, op1=ALU.add), reads=[kT(0), 'RMF'], writes=[kT(2)])
                        else:
                            S.op('dve', lambda e: e.tensor_tensor_scan(out=CL[:, :m][:, ::-1], data0=RMB[:, :m][:, ::-1],
                                                                       data1=LW[:, :m][:, ::-1], initial=0.0, op0=ALU.mult, op1=ALU.add),
                                 reads=[kT(0), 'RMB'], writes=[kT(2)])
                        S.op('act', lambda e: e.activation(out=E1[:, :m], in_=CL[:, :m], func=AF.Exp, scale=-DS), reads=[kT(2)], writes=[kT(3)])
                        S.op('act', lambda e: e.activation(out=E2[:, :m], in_=CL[:, :m], func=AF.Exp, scale=DS), reads=[kT(2)], writes=[kT(4)])
                        S.op('pool', lambda e: e.tensor_tensor(out=LW[:, :m], in0=CL[:, :m], in1=LW[:, :m], op=ALU.subtract),
                             reads=[kT(2), kT(0)], writes=[kT(0)])
                        yield
                        S.op('act', lambda e: e.activation(out=CL[:, :m], in_=LW[:, :m], func=AF.Exp, scale=-DS), reads=[kT(0), kT(2)],
                             writes=[kT(2)])
                        S.op('dve', lambda e: e.tensor_tensor(out=A_[:, 0:nch, 1, :], in0=v3(ZR[:, g0:g0 + m], m), in1=v3(E1[:, :m], m),
                                                              op=ALU.mult), reads=['ZR', kT(3)], writes=[kA])
                        S.op('pool', lambda e: e.tensor_tensor(out=TB[:, :m], in0=KK[:, g0:g0 + m], in1=AS[:, :m], op=ALU.mult),
                             reads=['KK', kT(1)], writes=[kT(5)])
                        S.op('pool', lambda e: e.tensor_scalar(out=TK[:, :m], in0=AS[:, :m], scalar1=PV[:, kao + c:kao + c + 1],
                                                               scalar2=KA1[:, c:c + 1], op0=ALU.mult, op1=ALU.add),
                             reads=[kT(1), 'KA1'], writes=[kT(6)])
                        yield
                        S.op('dve', lambda e: e.scalar_tensor_tensor(out=A_[:, 0:nch, 0, :], in0=v3(KK[:, g0:g0 + m], m), scalar=-1.0,
                                                                     in1=v3(CL[:, :m], m), op0=ALU.mult, op1=ALU.mult),
                             reads=['KK', kT(2)], writes=[kA])
                        S.op('dve', lambda e: e.tensor_tensor(out=K_[:, 0:nch, 1, :], in0=v3(TB[:, :m], m), in1=v3(E2[:, :m], m), op=ALU.mult),
                             reads=[kT(5), kT(4)], writes=[kK])
                        S.op('pool', lambda e: e.tensor_tensor(out=TK[:, :m], in0=TK[:, :m], in1=ZK[:, g0:g0 + m], op=ALU.mult),
                             reads=[kT(6), 'ZK'], writes=[kT(6)])
                        yield
                        S.op('dve', lambda e: e.tensor_tensor(out=K_[:, 0:nch, 0, :], in0=v3(TK[:, :m], m), in1=v3(E2[:, :m], m), op=ALU.mult),
                             reads=[kT(6), kT(4)], writes=[kK])
                        ecol = 63 if d == 0 else 0
                        S.op('dve', lambda e: e.tensor_copy(out=WC[d][:, jmin:jmin + nch], in_=E1[:, ecol:m:64]), reads=[kT(3)],
                             writes=[('WC', d)])
                        yield

                    def stage1_gen(d, n):
                        js = batches[d][n]
                        par = n % 2
                        jmin = min(js)
                        A_, K_ = ART[d][par], KBT[d][par]
                        kA, kK = ('ART', d, par), ('KBT', d, par)
                        MT_, TT_ = MT[d][par], TTt[d][par]
                        AB_, AC_ = ABt[d], ACC[d]
                        for h0 in range(0, len(js), 4):
                            for q0 in range(0, 4, 2):
                                for hp in (0, 64):
                                    pm = ps[2 + hp // 64]
                                    for q in range(2):
                                        jl = js[h0 + q0 + q] - jmin
                                        for w_ in range(2):
                                            S.mm(pm[hp:hp + 64, q * 256 + w_ * 128:q * 256 + (w_ + 1) * 128],
                                                 lhsT=K_[hp:hp + 64, jl, w_, :], rhs=A_[hp:hp + 64, jl, :, :], start=True, stop=True,
                                                 reads=[kK, kA], writes=['ps%d' % (2 + hp // 64)])
                                lq = h0 + q0
                                for hp in (0, 64):
                                    pm, kpm = ps[2 + hp // 64], 'ps%d' % (2 + hp // 64)
                                    S.op('dve', lambda e, lq=lq, hp=hp, pm=pm: e.tensor_tensor(
                                        out=MT_[hp:hp + 64, lq:lq + 2, :], in0=pm[hp:hp + 64, 0:512].rearrange("p (q w) -> p q w", w=256),
                                        in1=MK[d][hp:hp + 64, :, :], op=ALU.mult), reads=[kpm], writes=[('MT', d, par, lq, hp)])
                                    S.op('dve', lambda e, q0=q0, hp=hp, pm=pm: e.tensor_tensor(
                                        out=AB_[0][hp:hp + 64, q0:q0 + 2, 0, hp:hp + 64],
                                        in0=pm[hp:hp + 64, 0:512].rearrange("p (q w) -> p q w", w=256)[:, :, 128:192],
                                        in1=AMK[d][hp:hp + 64, :, :], op=ALU.mult), reads=[kpm], writes=[('AB0', d, q0)])
                                yield
                            for hp in (0, 64):
                                pm = ps[2 + hp // 64]
                                for q in range(4):
                                    jl = js[h0 + q] - jmin
                                    S.mm(pm[hp:hp + 64, q * 128 + hp:q * 128 + hp + 64], lhsT=A_[hp:hp + 64, jl, 0, :],
                                         rhs=K_[hp:hp + 64, jl, 1, :], start=True, stop=True, reads=[kK, kA], writes=['ps%d' % (2 + hp // 64)])
                            for hp in (0, 64):
                                pm, kpm = ps[2 + hp // 64], 'ps%d' % (2 + hp // 64)
                                S.op('dve', lambda e, hp=hp, pm=pm: e.tensor_tensor(
                                    out=AB_[0][hp:hp + 64, 0:4, 1, hp:hp + 64],
                                    in0=pm[hp:hp + 64, 0:512].rearrange("p (q w) -> p q w", w=128)[:, :, hp:hp + 64],
                                    in1=BMK[d][hp:hp + 64, :, :], op=ALU.mult), reads=[kpm], writes=[('AB0', d, 0), ('AB0', d, 2)])
                            S.op('pool', lambda e: e.tensor_tensor(
                                out=AC_[0][:, 0:4, :], in0=AB_[0][:, 0:4, 0, :], in1=IDB[:].unsqueeze(1).to_broadcast([128, 4, 128]),
                                op=ALU.add), reads=[('AB0', d, 0), ('AB0', d, 2), 'IDB'], writes=[('ACC0', d)])
                            yield
                            for l in range(5):
                                cur, nxt = l % 2, 1 - (l % 2)
                                kc, kn = 'AB%d' % cur, 'AB%d' % nxt
                                for q0 in range(0, 4, 2):
                                    for q in range(2):
                                        jq = q0 + q
                                        if l < 4:
                                            S.mm(ps[2][:, q * 256:q * 256 + 128], lhsT=AB_[cur][:, jq, 1, :], rhs=AB_[cur][:, jq, 0, :],
                                                 start=True, stop=True, reads=[(kc, d, q0)], writes=['ps2'], inc=False)
                                        S.mm(ps[2][:, q * 256 + 128:q * 256 + 256], lhsT=AB_[cur][:, jq, 0, :], rhs=AB_[cur][:, jq, 1, :],
                                             start=True, stop=True, reads=[(kc, d, q0)], writes=['ps2'], inc=(q == 1))
                                    if l < 4:
                                        S.op('act', lambda e, q0=q0, nxt=nxt: e.copy(
                                            out=AB_[nxt][:, q0:q0 + 2, :, :].rearrange("p q a b -> p (q a b)"), in_=ps[2][:, 0:512]),
                                            reads=['ps2'], writes=[(kn, d, q0)])
                                    else:
                                        S.op('act', lambda e, q0=q0, nxt=nxt: e.copy(
                                            out=AB_[nxt][:, q0:q0 + 2, 1, :],
                                            in_=ps[2][:, 0:512].rearrange("p (q w) -> p q w", w=256)[:, :, 128:256]),
                                            reads=['ps2'], writes=[(kn, d, q0)])
                                    yield
                                for q in range(4):
                                    S.mm(ps[3][:, q * 128:(q + 1) * 128], lhsT=AB_[nxt][:, q, 1, :], rhs=AC_[cur][:, q, :], start=True, stop=True,
                                         reads=[(kn, d, (q // 2) * 2), ('ACC%d' % cur, d)], writes=['ps3'], inc=(q == 3))
                                if l == 4:
                                    S.op('dve', lambda e, cur=cur: e.tensor_tensor(
                                        out=TT_[:, h0:h0 + 4, :], in0=ps[3][:, 0:512].rearrange("p (q w) -> p q w", w=128),
                                        in1=AC_[cur][:, 0:4, :], op=ALU.add), reads=['ps3', ('ACC%d' % cur, d)], writes=[('TTt', d, par, h0)])
                                else:
                                    S.op('dve', lambda e, cur=cur, nxt=nxt: e.tensor_tensor(
                                        out=AC_[nxt][:, 0:4, :], in0=ps[3][:, 0:512].rearrange("p (q w) -> p q w", w=128),
                                        in1=AC_[cur][:, 0:4, :], op=ALU.add), reads=['ps3', ('ACC%d' % cur, d)], writes=[('ACC%d' % nxt, d)])
                                yield

                    def chain_gen(d, n):
                        js = batches[d][n]
                        par = n % 2
                        jmin = min(js)
                        A_, K_ = ART[d][par], KBT[d][par]
                        kA, kK = ('ART', d, par), ('KBT', d, par)
                        MT_, TT_ = MT[d][par], TTt[d][par]
                        SS_, SSb_ = SS[d], SSb[d]
                        for step, j in enumerate(js):
                            jl = j - jmin
                            tb = step % 2
                            isx = j >= 4
                            jx = j - 4
                            pt, xb, ub = PT[d][tb], XB[d][tb], UB_[d][tb]
                            ktt = ('TTt', d, par, (step // 4) * 4)
                            first_y = isx and (jx not in yb_written)
                            if isx:
                                yb_written.add(jx)
                            H = []
                            for h in range(2):
                                hp = 64 * h
                                H.append(dict(hp=hp, pb=ps[4 + 2 * d + h], kpb='ps%d' % (4 + 2 * d + h), kpt=('PT', d, tb, h), kxb=('XB', d, tb, h),
                                              kub=('UB', d, tb, h), kS=('SS', d, h), kSb=('SSb', d, h),
                                              kmt=('MT', d, par, (step // 2) * 2, hp), e0=('act', 'dve')[h], e1=('dve', 'act')[h]))

                            def cp(eng, out, in_, reads, writes):
                                if eng == 'act':
                                    S.op('act', lambda e: e.copy(out=out, in_=in_), reads=reads, writes=writes)
                                else:
                                    S.op('dve', lambda e: e.tensor_copy(out=out, in_=in_), reads=reads, writes=writes)
                            for x in H:
                                hp, pb = x['hp'], x['pb']
                                S.mm(pb[hp:hp + 64, 0:64], lhsT=ZV[hp:hp + 64, j * 64:(j + 1) * 64], rhs=IDB[hp:hp + 64, hp:hp + 64],
                                     start=True, stop=True, reads=['ZV', 'IDB'], writes=[x['kpb']])
                                S.mm(pb[hp:hp + 64, 64:128], lhsT=K_[hp:hp + 64, jl, 1, :], rhs=IDB[hp:hp + 64, hp:hp + 64],
                                     start=True, stop=True, reads=[kK, 'IDB'], writes=[x['kpb']])
                                S.mm(pb[hp:hp + 64, 128:192], lhsT=K_[hp:hp + 64, jl, 0, :], rhs=IDB[hp:hp + 64, hp:hp + 64],
                                     start=True, stop=True, reads=[kK, 'IDB'], writes=[x['kpb']])
                            for x in H:
                                hp, pb = x['hp'], x['pb']
                                cp(x['e0'], pt[hp:hp + 64, :, :].rearrange("p a b -> p (a b)"), pb[hp:hp + 64, 0:192], [x['kpb']], [x['kpt']])
                            yield
                            for x in H:
                                hp, pb = x['hp'], x['pb']
                                S.mm(pb[hp:hp + 64, 192:256], lhsT=A_[hp:hp + 64, jl, 0, :], rhs=SSb_[hp:hp + 64, :], start=True, stop=False,
                                     reads=[kA, x['kSb']], writes=[x['kpb']])
                                S.mm(pb[hp:hp + 64, 192:256], lhsT=MT_[hp:hp + 64, step, 0:64], rhs=pt[hp:hp + 64, 0, :], start=False, stop=True,
                                     reads=[x['kmt'], x['kpt']], writes=[x['kpb']])
                            for x in H:
                                hp, pb = x['hp'], x['pb']
                                cp(x['e0'], xb[hp:hp + 64, :], pb[hp:hp + 64, 192:256], [x['kpb']], [x['kxb']])
                            yield
                            for x in H:
                                hp, pb = x['hp'], x['pb']
                                S.mm(pb[hp:hp + 64, 256:320], lhsT=TT_[hp:hp + 64, step, hp:hp + 64], rhs=xb[hp:hp + 64, :], start=True, stop=True,
                                     reads=[ktt, x['kxb']], writes=[x['kpb']])
                            for x in H:
                                hp, pb = x['hp'], x['pb']
                                cp(x['e1'], ub[hp:hp + 64, :], pb[hp:hp + 64, 256:320], [x['kpb']], [x['kub']])
                            yield
                            if isx:
                                for x in H:
                                    hp, pb = x['hp'], x['pb']
                                    S.mm(pb[hp:hp + 64, 320:384], lhsT=A_[hp:hp + 64, jl, 1, :], rhs=SSb_[hp:hp + 64, :], start=True, stop=False,
                                         reads=[kA, x['kSb']], writes=[x['kpb']])
                                    S.mm(pb[hp:hp + 64, 320:384], lhsT=MT_[hp:hp + 64, step, 192:256], rhs=ub[hp:hp + 64, :], start=False, stop=False,
                                         reads=[x['kmt'], x['kub']], writes=[x['kpb']])
                                    S.mm(pb[hp:hp + 64, 320:384], lhsT=MT_[hp:hp + 64, step, 64:128], rhs=pt[hp:hp + 64, 0, :], start=False, stop=True,
                                         reads=[x['kmt'], x['kpt']], writes=[x['kpb']])
                                for x in H:
                                    hp, pb = x['hp'], x['pb']
                                    if first_y:
                                        cp(x['e1'], YB[hp:hp + 64, jx, :], pb[hp:hp + 64, 320:384], [x['kpb']], [('YB', jx, hp)])
                                    else:
                                        S.op('dve', lambda e, hp=hp, pb=pb: e.tensor_tensor(out=YB[hp:hp + 64, jx, :], in0=pb[hp:hp + 64, 320:384],
                                                                                          in1=YB[hp:hp + 64, jx, :], op=ALU.add),
                                             reads=[x['kpb'], ('YB', jx, hp)], writes=[('YB', jx, hp)])
                            for x in H:
                                hp, pb = x['hp'], x['pb']
                                S.mm(pb[hp:hp + 64, 384:448], lhsT=pt[hp:hp + 64, 1, :], rhs=ub[hp:hp + 64, :], start=True, stop=False,
                                     reads=[x['kpt'], x['kub']], writes=[x['kpb']])
                                S.mm(pb[hp:hp + 64, 384:448], lhsT=pt[hp:hp + 64, 2, :], rhs=pt[hp:hp + 64, 0, :], start=False, stop=True,
                                     reads=[x['kpt']], writes=[x['kpb']])
                            for x in H:
                                hp, pb = x['hp'], x['pb']
                                S.op('dve', lambda e, hp=hp: e.tensor_scalar(out=SS_[hp:hp + 64, :], in0=SS_[hp:hp + 64, :],
                                                                             scalar1=WC[d][hp:hp + 64, j:j + 1], scalar2=None, op0=ALU.mult),
                                     reads=[x['kS'], ('WC', d)], writes=[x['kS']])
                                S.op('dve', lambda e, hp=hp, pb=pb: e.scalar_tensor_tensor(
                                    out=SS_[hp:hp + 64, :], in0=pb[hp:hp + 64, 384:448], scalar=WC[d][hp:hp + 64, j:j + 1],
                                    in1=SS_[hp:hp + 64, :], op0=ALU.mult, op1=ALU.add), reads=[x['kpb'], x['kS'], ('WC', d)], writes=[x['kS']])
                                S.op('act', lambda e, hp=hp: e.copy(out=SSb_[hp:hp + 64, :], in_=SS_[hp:hp + 64, :]), reads=[x['kS']],
                                     writes=[x['kSb']])
                            if debug and j == (3 if d == 0 else 0):
                                S.op('dve', lambda e: e.tensor_copy(out=ST8[:, d, c, :], in_=SS_[:]), reads=[('SS', d, 0), ('SS', d, 1)],
                                     writes=['ST8'])
                            yield

                    def run_threads(ths):
                        ths = list(ths)
                        while ths:
                            for g in list(ths):
                                try:
                                    next(g)
                                except StopIteration:
                                    ths.remove(g)

                    import itertools as _it
                    run_threads([_it.chain(prep_gen(d, 0), stage1_gen(d, 0)) for d in range(2)])
                    for n in range(NB):
                        ths = []
                        for d in range(2):
                            ths.append(chain_gen(d, n))
                            if n + 1 < NB:
                                ths.append(_it.chain(prep_gen(d, n + 1), stage1_gen(d, n + 1)))
                        run_threads(ths)
                    cut(8)
                    YSQ = SH[:, 0:4096].rearrange("p (j v) -> p j v", v=64)
                    YNb = PL[:, 0:4096].rearrange("p (j v) -> p j v", v=64)
                    SUM, SSQ, MEAN, RSTD = TRS[:, 0:64], TRS[:, 64:128], TRS[:, 128:192], TRS[:, 192:256]
                    S.op('dve', lambda e: e.tensor_reduce(out=SUM, in_=YB[:], axis=AX.X, op=ALU.add),
                         reads=[('YB', jx, hp_) for jx in range(64) for hp_ in (0, 64)], writes=['TR8'])
                    S.op('act', lambda e: e.activation(out=YSQ, in_=YB[:], func=AF.Square), reads=[('YB', jx, hp_) for jx in range(64) for hp_ in (0, 64)],
                         writes=['SH'])
                    S.op('dve', lambda e: e.tensor_reduce(out=SSQ, in_=YSQ, axis=AX.X, op=ALU.add), reads=['SH'], writes=['TR8'])
                    S.op('dve', lambda e: e.tensor_scalar(out=MEAN, in0=SUM, scalar1=1.0 / 64, scalar2=None, op0=ALU.mult),
                         reads=['TR8'], writes=['TR8'])
                    S.op('dve', lambda e: e.tensor_tensor(out=SUM, in0=MEAN, in1=MEAN, op=ALU.mult), reads=['TR8'], writes=['TR8'])
                    S.op('dve', lambda e: e.scalar_tensor_tensor(out=SSQ, in0=SSQ, scalar=1.0 / 64, in1=SUM, op0=ALU.mult,
                                                                 op1=ALU.subtract), reads=['TR8'], writes=['TR8'])
                    S.op('dve', lambda e: e.tensor_scalar(out=SSQ, in0=SSQ, scalar1=LN_X_EPS, scalar2=None, op0=ALU.add),
                         reads=['TR8'], writes=['TR8'])
                    S.op('act', lambda e: e.activation(out=RSTD, in_=SSQ, func=AF.Sqrt), reads=['TR8'], writes=['TR8'])
                    S.op('dve', lambda e: e.reciprocal(out=RSTD, in_=RSTD), reads=['TR8'], writes=['TR8'])
                    S.op('dve', lambda e: e.tensor_tensor(out=YB[:], in0=YB[:], in1=MEAN.unsqueeze(2).to_broadcast([128, 64, 64]),
                                                          op=ALU.subtract), reads=['TR8'] + [('YB', jx, hp_) for jx in range(64) for hp_ in (0, 64)],
                         writes=[('YB', jx, hp_) for jx in range(64) for hp_ in (0, 64)])
                    S.op('dve', lambda e: e.tensor_tensor(out=YNb, in0=YB[:], in1=RSTD.unsqueeze(2).to_broadcast([128, 64, 64]),
                                                          op=ALU.mult), reads=['TR8'] + [('YB', jx, hp_) for jx in range(64) for hp_ in (0, 64)],
                         writes=['PL'])
                    for ti, (t0, m) in enumerate(seq_tiles(T)):
                        g0 = TC + t0
                        for hp in (0, 64):
                            for q in range(8):
                                jx = ti * 8 + q
                                S.mm(
                                    ps[0][hp:hp + 64, q * 64:(q + 1) * 64], lhsT=YNb[hp:hp + 64, jx, :], rhs=IDB[hp:hp + 64, hp:hp + 64],
                                    start=True, stop=True, reads=['PL', 'IDB'], writes=['ps0'], inc=(q == 7 and hp == 64))
                        S.op('dve', lambda e, g0=g0, m=m: e.scalar_tensor_tensor(
                            out=TR[0][:, :m], in0=ZR[:, g0:g0 + m], scalar=PV[:, rko + c:rko + c + 1], in1=ZK[:, g0:g0 + m],
                            op0=ALU.mult, op1=ALU.mult), reads=['ZR', 'ZK'], writes=['TR0'])
                        S.mm(ps[1][:, :m], lhsT=BONES[:], rhs=TR[0][:, :m], start=True, stop=True,
                             reads=['TR0', 'BONES'], writes=['ps1'])
                        S.op('dve', lambda e, g0=g0, m=m: e.tensor_tensor(out=TR[1][:, :m], in0=ps[1][:, :m], in1=ZV[:, g0:g0 + m],
                                                                          op=ALU.mult), reads=['ps1', 'ZV'], writes=['TR1'])
                        S.dma('sp', GO[0][:, :m], GSv[:, c, t0:t0 + m], reads=[('GS', c, t0)], writes=['GO0'])
                        S.op('dve', lambda e, m=m: e.tensor_scalar(out=TR[2][:, :m], in0=ps[0][:, :m], scalar1=PV[:, lwo + c:lwo + c + 1],
                                                                   scalar2=PV[:, lbo + c:lbo + c + 1], op0=ALU.mult, op1=ALU.add),
                             reads=['ps0'], writes=['TR2'])
                        S.op('pool', lambda e, m=m: e.tensor_tensor(out=TR[2][:, :m], in0=TR[2][:, :m], in1=TR[1][:, :m], op=ALU.add),
                             reads=['TR2', 'TR1'], writes=['TR2'])
                        S.op('pool', lambda e, m=m: e.tensor_tensor(out=GO[1][:, :m], in0=TR[2][:, :m], in1=GO[0][:, :m], op=ALU.mult),
                             reads=['TR2', 'GO0'], writes=['GO1'])
                        S.dma('pool', YRWv[:, c, t0:t0 + m], GO[1][:, :m], reads=['GO1'], writes=[('YRW', c, t0)])
                if debug:
                    S.dma('sp', SFD[:, :, :, :], ST8[:], reads=['ST8'], writes=['SFD'])
                S.barrier()

        with ExitStack() as ph_r:
            try:
                _rwkv_body(ph_r)
            except _Cut:
                print("CUT at", RCUT, "ops", S.nops)
            S.barrier()
        if RCUT:
            return nc

        if phases is not None and "C" not in phases:
            return nc

        def phase_merge():
            with ExitStack() as ph:
                WP = [sb("WP%d" % i, [128, 8, D], BF16, st=ph) for i in range(3)]
                STG = sb("STGC", [128, 2, D], st=ph)
                YR2 = [sb("YRt%d" % i, [128, 8, 512], BF16, st=ph) for i in range(2)]
                YL2 = [sb("YLt%d" % i, [128, 8, 512], BF16, st=ph) for i in range(2)]
                SM2 = [sb("SMt%d" % i, [128, 16, 512], BF16, st=ph) for i in range(2)]
                X12 = [sb("X1t%d" % i, [128, 8, 512], st=ph) for i in range(2)]
                MG = sb("MGt", [128, 8, 512], BF16, st=ph)
                TA = [sb("TAc%d" % i, [128, 512], st=ph) for i in range(2)]
                ci = 0
                for wi, wsrc in enumerate((wpr_d, wpl_d, wo_d)):
                    for k in range(8):
                        i = ci % 2
                        ci += 1
                        S.dma('sp' if i == 0 else 'pool', STG[:, i, :], wsrc[k * 128:(k + 1) * 128, :], writes=['stg%d' % i])
                        S.op('dve' if i == 0 else 'act',
                             (lambda e, wi=wi, k=k, i=i: e.tensor_copy(out=WP[wi][:, k, :], in_=STG[:, i, :])) if i == 0 else
                             (lambda e, wi=wi, k=k, i=i: e.copy(out=WP[wi][:, k, :], in_=STG[:, i, :])), reads=['stg%d' % i])
                S.barrier()
                for ti_, (t0, n) in enumerate(seq_tiles(T)):
                    g0 = TC + t0
                    pr = ti_ % 2
                    YR, YLt, SM, X1t = YR2[pr], YL2[pr], SM2[pr], X12[pr]
                    for c in range(8):
                        S.dma('sp', YR[:, c, :n], YRWv[:, c, t0:t0 + n], writes=[('YR', pr, c)])
                        S.dma('sp', YLt[:, c, :n], YLv[:, c, t0:t0 + n], writes=[('YLt', pr, c)])
                        S.dma('sp', X1t[:, c, :n], X1v[:, c, g0:g0 + n], writes=[('X1t', pr, c)])
                    for c in range(16):
                        S.dma('sp', SM[:, c, :n], Pv[:, 43 + c, g0:g0 + n], writes=[('SM', pr, c)])
                    for o in range(8):
                        pa, pb = ps[(o % 2) * 2], ps[(o % 2) * 2 + 1]
                        ka, kb = 'ps%d' % ((o % 2) * 2), 'ps%d' % ((o % 2) * 2 + 1)
                        for k in range(8):
                            S.mm(pa[:, :n], lhsT=WP[0][:, k, o * 128:(o + 1) * 128], rhs=YR[:, k, :n], start=(k == 0), stop=(k == 7),
                                 reads=[('YR', pr, k)], writes=[ka], inc=(k == 7))
                        for k in range(8):
                            S.mm(pb[:, :n], lhsT=WP[1][:, k, o * 128:(o + 1) * 128], rhs=YLt[:, k, :n], start=(k == 0), stop=(k == 7),
                                 reads=[('YLt', pr, k)], writes=[kb], inc=(k == 7))
                        S.op('dve', lambda e, pa=pa, o=o: e.tensor_tensor(out=TA[0][:, :n], in0=pa[:, :n], in1=SM[:, o, :n], op=ALU.mult),
                             reads=[ka, ('SM', pr, o)], writes=['TA0'])
                        S.op('dve', lambda e, pb=pb, o=o: e.tensor_tensor(out=TA[1][:, :n], in0=pb[:, :n], in1=SM[:, 8 + o, :n], op=ALU.mult),
                             reads=[kb, ('SM', pr, 8 + o)], writes=['TA1'])
                        S.op('pool', lambda e, o=o: e.tensor_tensor(out=MG[:, o, :n], in0=TA[0][:, :n], in1=TA[1][:, :n], op=ALU.add),
                             reads=['TA0', 'TA1'], writes=[('MG', o)])
                    for o in range(8):
                        pc_ = ps[4 + o % 2]
                        kc_ = 'ps%d' % (4 + o % 2)
                        for k in range(8):
                            S.mm(pc_[:, :n], lhsT=WP[2][:, k, o * 128:(o + 1) * 128], rhs=MG[:, k, :n], start=(k == 0), stop=(k == 7),
                                 reads=[('MG', k)], writes=[kc_], inc=(k == 7))
                        S.op('dve', lambda e, pc_=pc_, o=o: e.scalar_tensor_tensor(
                            out=X1t[:, o, :n], in0=pc_[:, :n], scalar=SCAL[:, 0, 5, o:o + 1], in1=X1t[:, o, :n], op0=ALU.mult,
                            op1=ALU.add), reads=[kc_, ('X1t', pr, o)], writes=[('X1t', pr, o)])
                        S.dma('pool', X2v[:, o, t0:t0 + n], X1t[:, o, :n], reads=[('X1t', pr, o)], writes=[('X2', o, t0)])
                S.barrier()

        phase_merge()

        def load_feat_major(S, SCR, XT, s_, t0, n):
            for c in range(8):
                S.dma('sp' if c % 2 == 0 else 'pool', XT[:, c, :n], X2v[:, c, t0:t0 + n], writes=[('XT', c)])

        def epi_final(S, XT, HN, TM, RS, rmsnorm_mod, s_, t0, n, SCR=None):
            rmsnorm_mod(n, s_, 0, XT, ia=9, ib=10, dk='XT')
            for b_ in range(n // 128):
                for c in range(8):
                    pp = ps[6 + (c // 4) % 2]
                    S.mm(pp[:, (c % 4) * 128:(c % 4 + 1) * 128], lhsT=XT[:, c, b_ * 128:(b_ + 1) * 128], rhs=ident[:],
                         start=True, stop=True, reads=[('XT', c), 'ident'], writes=['ps%d' % (6 + (c // 4) % 2)], inc=(c % 4 == 3))
                    if c % 4 == 3:
                        h_ = c // 4
                        S.op('act' if h_ == 0 else 'dve',
                             (lambda e, pp=pp, b_=b_, h_=h_: e.copy(out=SCR[:, b_ * 1024 + h_ * 512:b_ * 1024 + (h_ + 1) * 512], in_=pp[:, 0:512]))
                             if h_ == 0 else
                             (lambda e, pp=pp, b_=b_, h_=h_: e.tensor_copy(out=SCR[:, b_ * 1024 + h_ * 512:b_ * 1024 + (h_ + 1) * 512], in_=pp[:, 0:512])),
                             reads=['ps%d' % (6 + h_)], writes=[('XIN', b_)])
                S.dma('pool', out_d[t0 + b_ * 128:t0 + (b_ + 1) * 128, :], SCR[:, b_ * 1024:(b_ + 1) * 1024], reads=[('XIN', b_)],
                      writes=[('OUT', t0, b_)])

        tiles2 = [(0, i * 512, 512) for i in range(T // 512)]
        ffn_phase("ffn2", tiles2, load_feat_major, epi_final)

        S.barrier()
        print("ops emitted", S.nops, S.cnt, S.dn)
    return nc


_CACHE = {}


def _prep_inputs(inputs, b):
    f = lambda a: np.ascontiguousarray(a, dtype=np.float32)
    m = {"x": f(inputs["x"][b]), "ctx": f(inputs["ctx"][b])}
    src = dict(inputs)
    src["c"] = inputs["c"][b]
    for n, r in VEC_ROWS:
        m[n] = f(np.asarray(src[n]).reshape(r, 128))
    m["w_mod"] = f(inputs["w_mod"][0])
    m["w_in"] = f(inputs["w_in"][0])
    m["lru_wa"] = f(inputs["lru_wa"][0])
    m["lru_wx"] = f(inputs["lru_wx"][0])
    m["rw_w_up"] = f(inputs["rw_w_up"][0].reshape(128, D))
    m["w_proj_rw"] = f(inputs["w_proj_rw"][0])
    m["w_proj_lru"] = f(inputs["w_proj_lru"][0])
    m["w_out"] = f(inputs["w_out"][0])
    m["rw_a_up"] = f(inputs["rw_a_up"][0].reshape(128, D))
    m["rw_g_up"] = f(inputs["rw_g_up"][0])
    for fn_ in ("ffn1", "ffn2"):
        for s in ("_wg", "_wu", "_wd"):
            m[fn_ + s] = f(inputs[fn_ + s][0])
    return m


def kernel(**inputs):
    if "nc" not in _CACHE:
        _CACHE["nc"] = build()
    nc = _CACHE["nc"]
    in_maps = [_prep_inputs(inputs, b % 4) for b in range(N_CORES)]
    res = run_bass_kernel_spmd(nc, in_maps, core_ids=list(range(N_CORES)))
    out = np.stack([res.results[b]["out"] for b in range(4)], axis=0)
    return out.astype(np.float32)
```
